# Optimizing a Trainium2 kernel written in Bass

```python
import jax
import jax.numpy as jnp
from jax import lax
import numpy as np


D_MODEL = 1024
BATCH = 2
SEQ = 8192
DEPTH = 2

GRID_W = 64
CTX_LEN = 256

N_HEADS = 8
N_KV_HEADS = 2
HEAD_DIM = 128
Q_PER_KV = N_HEADS // N_KV_HEADS
ROPE_THETA = 10000.0
Q_BLOCK = 128

GLA_HEADS = 4
GLA_DK = D_MODEL // 2 // GLA_HEADS
GLA_DV = D_MODEL // GLA_HEADS
GLA_RANK = 16
GLA_TAU = 16.0
GLA_CHUNK = 64

CONV_CH = D_MODEL
CONV_WIDTH = 31

D_FF = -(-8 * D_MODEL // (3 * 256)) * 256

ALPHA = (2.0 * DEPTH) ** 0.25
BETA = (8.0 * DEPTH) ** -0.25
NORM_EPS = 1e-6

ATT_Q = N_HEADS * HEAD_DIM
ATT_KV = N_KV_HEADS * HEAD_DIM
GLA_K = GLA_HEADS * GLA_DK
GLA_V = GLA_HEADS * GLA_DV
N_BRANCH = 3
MEM_SPLITS = (ATT_KV, ATT_KV, GLA_K, GLA_V, 2 * GLA_RANK)
REST_SPLITS = (ATT_Q, GLA_K, GLA_V, 2 * CONV_CH, N_BRANCH * D_MODEL)
MEM_COLS = sum(MEM_SPLITS)
IN_COLS = MEM_COLS + sum(REST_SPLITS)

kernel_name = 'hybrid_gla_conformer_gqa_diffusion_trunk'


def _split(a, sizes):
    return jnp.split(a, np.cumsum(sizes)[:-1].tolist(), axis=-1)


def _layernorm(x, g=None, b=None):
    xf = x.astype(jnp.float32)
    xc = xf - jnp.mean(xf, -1, keepdims=True)
    y = xc * lax.rsqrt(jnp.mean(xc * xc, -1, keepdims=True) + NORM_EPS)
    if g is not None:
        y = y * g.astype(jnp.float32) + b.astype(jnp.float32)
    return y.astype(x.dtype)


def _rmsnorm(x, g):
    xf = x.astype(jnp.float32)
    y = xf * lax.rsqrt(jnp.mean(xf * xf, -1, keepdims=True) + NORM_EPS)
    return (y * g.astype(jnp.float32)).astype(x.dtype)


def _modulate(x, shift, scale):
    return _layernorm(x) * (1.0 + scale) + shift


def _post(x, f, gate, g, b):
    return _layernorm(ALPHA * x + gate * f, g, b)


def _heads(a, n):
    B, L, _ = a.shape
    return a.reshape(B, L, n, -1).transpose(0, 2, 1, 3)


def _flip(a):
    return a[:, :, ::-1]


def _axial_rope(rows):
    t = jnp.arange(rows * GRID_W)
    row = (t // GRID_W).astype(jnp.float32)
    col = (t % GRID_W).astype(jnp.float32)
    half = HEAD_DIM // 2
    inv = ROPE_THETA ** (-jnp.arange(0, half, 2, dtype=jnp.float32) / half)
    ang = jnp.concatenate([row[:, None] * inv, col[:, None] * inv], axis=-1)
    return jnp.cos(ang), jnp.sin(ang)


def _apply_rope(x, cos, sin):
    xf = x.astype(jnp.float32)
    x1, x2 = xf[..., 0::2], xf[..., 1::2]
    y = jnp.stack([x1 * cos - x2 * sin, x1 * sin + x2 * cos], axis=-1)
    return y.reshape(x.shape).astype(x.dtype)


def _sdpa(q, k, v):
    s = jnp.einsum('bkgqd,bkld->bkgql', q, k, preferred_element_type=jnp.float32) * (HEAD_DIM ** -0.5)
    p = jax.nn.softmax(s, axis=-1)
    return jnp.einsum('bkgql,bkld->bkgqd', p.astype(v.dtype), v)


def _attend_blocks(q, k_all, v_all):
    B, Hk, G, S, Dh = q.shape
    nb = S // Q_BLOCK
    qb = jnp.moveaxis(q.reshape(B, Hk, G, nb, Q_BLOCK, Dh), 3, 0)
    o = lax.map(lambda blk: _sdpa(blk, k_all, v_all), qb)
    return jnp.moveaxis(o, 0, 3).reshape(B, Hk, G, S, Dh)


def _gla_log_gates(glr, w_a2, b_a):
    B, L, _ = glr.shape
    z = jnp.einsum('blnr,nrk->nblk', glr.reshape(B, L, 2, GLA_RANK), w_a2) + b_a[:, None, None, :]
    lg = jax.nn.log_sigmoid(z.astype(jnp.float32)) / GLA_TAU
    lg = lg.reshape(2, B, L, GLA_HEADS, GLA_DK).transpose(0, 1, 3, 2, 4)
    return lg[0], lg[1]


def _gla_scan(q, k, v, logg, s0):
    B, H, L, dk = q.shape
    dv = v.shape[-1]
    n = L // GLA_CHUNK

    def chunks(a):
        a = a.astype(jnp.float32)
        return a.reshape(B, H, n, GLA_CHUNK, a.shape[-1]).transpose(2, 0, 1, 3, 4)

    lower = jnp.tril(jnp.ones((GLA_CHUNK, GLA_CHUNK), dtype=bool))[:, :, None]

    def step(s, inp):
        qc, kc, vc, gc = inp
        b = jnp.cumsum(gc, axis=2)
        rel = jnp.where(lower, b[:, :, :, None, :] - b[:, :, None, :, :], -jnp.inf)
        a = jnp.einsum('bhid,bhjd,bhijd->bhij', qc, kc, jnp.exp(rel))
        o = jnp.einsum('bhid,bhde->bhie', qc * jnp.exp(b), s) + jnp.einsum('bhij,bhje->bhie', a, vc)
        b_end = b[:, :, -1:, :]
        s = jnp.exp(b_end[:, :, 0, :, None]) * s + jnp.einsum('bhjd,bhje->bhde', kc * jnp.exp(b_end - b), vc)
        return s, o

    s_end, o = lax.scan(step, s0, (chunks(q), chunks(k), chunks(v), chunks(logg)))
    return o.transpose(1, 2, 0, 3, 4).reshape(B, H, L, dv).astype(v.dtype), s_end


def _gla_final_state(k, v, logg):
    b = jnp.cumsum(logg, axis=2)
    w = jnp.exp(b[:, :, -1:, :] - b)
    return jnp.einsum('bhld,bhle->bhde', k.astype(jnp.float32) * w, v.astype(jnp.float32))


def _gla_bidir(qg, kg, vg, gf, gb, s_f, s_b):
    o_f, s_f_end = _gla_scan(qg, kg, vg, gf, s_f)
    o_b, s_b_end = _gla_scan(_flip(qg), _flip(kg), _flip(vg), _flip(gb), s_b)
    return o_f + _flip(o_b), s_f_end, s_b_end


def _mem_parts(pm, p):
    k, v, kg, vg, glr = _split(pm, MEM_SPLITS)
    k = _rmsnorm(_heads(k, N_KV_HEADS), p['k_norm'])
    v = _heads(v, N_KV_HEADS)
    gf, gb = _gla_log_gates(glr, p['gla_w_a2'], p['gla_b_a'])
    return k, v, _heads(kg, GLA_HEADS), _heads(vg, GLA_HEADS), gf, gb


def _rest_parts(pr, p):
    q, qg, r, glu, gates = _split(pr, REST_SPLITS)
    B, L, _ = q.shape
    q = _rmsnorm(q.reshape(B, L, N_KV_HEADS, Q_PER_KV, HEAD_DIM), p['q_norm']).transpose(0, 2, 3, 1, 4)
    qg = _heads(qg, GLA_HEADS) * (GLA_DK ** -0.5)
    return q, qg, r, glu, gates


def _conv_module(glu, p):
    a, g = jnp.split(glu, 2, axis=-1)
    y = a * jax.nn.sigmoid(g)
    pad = CONV_WIDTH // 2
    y = lax.conv_general_dilated(y, p['conv_w_dw'].astype(y.dtype), (1,), [(pad, pad)],
                                 dimension_numbers=('NWC', 'WIO', 'NWC'),
                                 feature_group_count=CONV_CH) + p['conv_b_dw']
    y = jax.nn.silu(_layernorm(y, p['conv_ln_g'], p['conv_ln_b']))
    return y @ p['w_conv_o']


def _merge(o_att, o_gla, r, glu, gates, p):
    B, L, _ = r.shape
    y_att = o_att.transpose(0, 3, 1, 2, 4).reshape(B, L, ATT_Q) @ p['w_att_o']
    o_gla = _rmsnorm(o_gla, p['gla_norm']).transpose(0, 2, 1, 3).reshape(B, L, GLA_V)
    y_gla = (o_gla * jax.nn.silu(r)) @ p['w_gla_o']
    y_conv = _conv_module(glu, p)
    g_att, g_gla, g_conv = jnp.split(jax.nn.sigmoid(gates), N_BRANCH, axis=-1)
    return (g_att * y_att + g_gla * y_gla + g_conv * y_conv) @ p['w_out']


def _context_mixer(h, p):
    proj = h @ p['w_in']
    k, v, kg, vg, gf, gb = _mem_parts(proj[..., :MEM_COLS], p)
    q, qg, r, glu, gates = _rest_parts(proj[..., MEM_COLS:], p)
    o_att = _sdpa(q, k, v)
    zero = jnp.zeros(kg.shape[:2] + (GLA_DK, GLA_DV), jnp.float32)
    o_gla, s_f, s_b = _gla_bidir(qg, kg, vg, gf, gb, zero, zero)
    return _merge(o_att, o_gla, r, glu, gates, p), (k, v, s_f, s_b)


def _context_memory(h, p):
    k, v, kg, vg, gf, gb = _mem_parts(h @ p['w_in'][:, :MEM_COLS], p)
    s_f = _gla_final_state(kg, vg, gf)
    s_b = _gla_final_state(_flip(kg), _flip(vg), _flip(gb))
    return k, v, s_f, s_b


def _latent_mixer(h, mem, cos, sin, p):
    proj = h @ p['w_in']
    k, v, kg, vg, gf, gb = _mem_parts(proj[..., :MEM_COLS], p)
    q, qg, r, glu, gates = _rest_parts(proj[..., MEM_COLS:], p)
    k_ctx, v_ctx, s_f, s_b = mem
    k_all = jnp.concatenate([_apply_rope(k, cos, sin), k_ctx], axis=2)
    v_all = jnp.concatenate([v, v_ctx], axis=2)
    o_att = _attend_blocks(_apply_rope(q, cos, sin), k_all, v_all)
    o_gla, _, _ = _gla_bidir(qg, kg, vg, gf, gb, s_f, s_b)
    return _merge(o_att, o_gla, r, glu, gates, p)


def _ffn(h, p):
    return (jax.nn.silu(h @ p['w_ff_gate']) * (h @ p['w_ff_up'])) @ p['w_ff_down']


def setup_inputs(seed: int = 0) -> dict:
    key = jax.random.key(seed)
    ks = jax.random.split(key, 27)

    def nrm(i, shape, scale):
        return jax.random.normal(ks[i], shape, jnp.float32) * scale

    D = D_MODEL
    L = DEPTH
    return {
        'x': nrm(0, (BATCH, SEQ, D), 1.0),
        'c': nrm(1, (BATCH, D), 1.0),
        'ctx': nrm(2, (BATCH, CTX_LEN, D), 1.0),
        'c_ctx': nrm(3, (D,), 1.0),
        'w_ada': nrm(4, (L, D, 6 * D), 0.3 * D ** -0.5),
        'b_ada': nrm(5, (L, 6 * D), 0.02),
        'w_in': nrm(6, (L, D, IN_COLS), D ** -0.5),
        'q_norm': 1.0 + nrm(7, (L, HEAD_DIM), 0.02),
        'k_norm': 1.0 + nrm(8, (L, HEAD_DIM), 0.02),
        'w_att_o': nrm(9, (L, ATT_Q, D), ATT_Q ** -0.5),
        'gla_w_a2': nrm(10, (L, 2, GLA_RANK, GLA_K), GLA_RANK ** -0.5),
        'gla_b_a': nrm(11, (L, 2, GLA_K), 0.5),
        'gla_norm': 1.0 + nrm(12, (L, GLA_DV), 0.02),
        'w_gla_o': nrm(13, (L, GLA_V, D), GLA_V ** -0.5),
        'conv_w_dw': nrm(14, (L, CONV_WIDTH, 1, CONV_CH), CONV_WIDTH ** -0.5),
        'conv_b_dw': nrm(15, (L, CONV_CH), 0.02),
        'conv_ln_g': 1.0 + nrm(16, (L, CONV_CH), 0.02),
        'conv_ln_b': nrm(17, (L, CONV_CH), 0.02),
        'w_conv_o': nrm(18, (L, CONV_CH, D), CONV_CH ** -0.5),
        'w_out': nrm(19, (L, D, D), BETA * D ** -0.5),
        'ln1_g': 1.0 + nrm(20, (L, D), 0.02),
        'ln1_b': nrm(21, (L, D), 0.02),
        'w_ff_gate': nrm(22, (L, D, D_FF), D ** -0.5),
        'w_ff_up': nrm(23, (L, D, D_FF), D ** -0.5),
        'w_ff_down': nrm(24, (L, D_FF, D), BETA * D_FF ** -0.5),
        'ln2_g': 1.0 + nrm(25, (L, D), 0.02),
        'ln2_b': nrm(26, (L, D), 0.02),
    }


def reference(x, c, ctx, c_ctx, w_ada, b_ada, w_in, q_norm, k_norm, w_att_o, gla_w_a2, gla_b_a,
              gla_norm, w_gla_o, conv_w_dw, conv_b_dw, conv_ln_g, conv_ln_b, w_conv_o, w_out,
              ln1_g, ln1_b, w_ff_gate, w_ff_up, w_ff_down, ln2_g, ln2_b):
    rows = x.shape[1] // GRID_W
    cos, sin = _axial_rope(rows)
    silu_c = jax.nn.silu(c)
    silu_cc = jax.nn.silu(c_ctx)
    xc = ctx
    for l in range(DEPTH):
        p = {'w_in': w_in[l], 'q_norm': q_norm[l], 'k_norm': k_norm[l], 'w_att_o': w_att_o[l],
             'gla_w_a2': gla_w_a2[l], 'gla_b_a': gla_b_a[l], 'gla_norm': gla_norm[l],
             'w_gla_o': w_gla_o[l], 'conv_w_dw': conv_w_dw[l], 'conv_b_dw': conv_b_dw[l],
             'conv_ln_g': conv_ln_g[l], 'conv_ln_b': conv_ln_b[l], 'w_conv_o': w_conv_o[l],
             'w_out': w_out[l], 'w_ff_gate': w_ff_gate[l], 'w_ff_up': w_ff_up[l],
             'w_ff_down': w_ff_down[l]}
        last = l == DEPTH - 1
        mod = (silu_c @ w_ada[l] + b_ada[l])[:, None, :]
        sh1, sc1, g1, sh2, sc2, g2 = jnp.split(mod, 6, axis=-1)
        n_cm = 2 if last else 6
        mod_c = jnp.split(silu_cc @ w_ada[l][:, :n_cm * D_MODEL] + b_ada[l][:n_cm * D_MODEL], n_cm)
        hc = _modulate(xc, mod_c[0], mod_c[1])
        if last:
            mem = _context_memory(hc, p)
        else:
            yc, mem = _context_mixer(hc, p)
            xc = _post(xc, yc, mod_c[2], ln1_g[l], ln1_b[l])
            xc = _post(xc, _ffn(_modulate(xc, mod_c[3], mod_c[4]), p), mod_c[5], ln2_g[l], ln2_b[l])
        x = _post(x, _latent_mixer(_modulate(x, sh1, sc1), mem, cos, sin, p), g1, ln1_g[l], ln1_b[l])
        x = _post(x, _ffn(_modulate(x, sh2, sc2), p), g2, ln2_g[l], ln2_b[l])
    return x
```

```python
import numpy as np
from contextlib import ExitStack
import concourse.bass as bass
import concourse.mybir as mybir
from concourse.bass_utils import run_bass_kernel_spmd

F32 = mybir.dt.float32
BF16 = mybir.dt.bfloat16
AF = mybir.ActivationFunctionType
ALU = mybir.AluOpType
AX = mybir.AxisListType


class V:
    def __init__(self, t, ap, key=None):
        self.t = t
        self.ap = ap
        self.key = key


class T:
    def __init__(self, prog, handle, name):
        self.p = prog
        self.h = handle
        self.name = name
        self.state = {}

    def __getitem__(self, idx):
        return V(self, self.h[idx], None)

    def k(self, key, idx=None):
        if idx is None:
            return V(self, self.h[:], key)
        return V(self, self.h[idx], key)

    def _keys(self, key):
        if key is None:
            return list(self.state.keys())
        ks = [key]
        if None in self.state:
            ks.append(None)
        return [k for k in ks if k in self.state]

    def deps_read(self, key):
        out = []
        for k in self._keys(key):
            w = self.state[k][0]
            if w is not None:
                out.append(w)
        return out

    def deps_write(self, key):
        out = []
        for k in self._keys(key):
            w, rs = self.state[k]
            if w is not None:
                out.append(w)
            out.extend(rs)
        return out

    def add_reader(self, key, tok):
        st = self.state.setdefault(key, [None, []])
        st[1].append(tok)
        if len(st[1]) > 64:
            best = {}
            for (s, v) in st[1]:
                best[s] = max(best.get(s, 0), v)
            st[1] = list(best.items())

    def set_writer(self, key, tok):
        if key is None:
            self.state = {None: [tok, []]}
        else:
            self.state[key] = [tok, []]


class Prog:
    ENG = ("pe", "dve", "act", "pool", "sp")

    def __init__(self, nc, es, n_dma_sems=16):
        self.nc = nc
        self.es = es
        self.engs = {"pe": nc.tensor, "dve": nc.vector, "act": nc.scalar,
                     "pool": nc.gpsimd, "sp": nc.sync}
        self.sems = []
        self.esem = {}
        self.cnt = {}
        for e in self.ENG:
            self.esem[e] = self._new_sem("prog_" + e)
            self.cnt[e] = 0
        self.dsem = {}
        self.dcnt = {}
        self.dnext = {}
        for q in ("sp", "pool", "act"):
            self.dsem[q] = [self._new_sem("dma_%s_%d" % (q, i)) for i in range(n_dma_sems)]
            self.dcnt[q] = [0] * n_dma_sems
            self.dnext[q] = 0
        self.waited = {e: {} for e in self.ENG}
        self.n_ops = 0
        self.n_waits = 0
        self.uid = 0

    def _new_sem(self, name):
        h = self.es.enter_context(self.nc.semaphore(name))
        self.sems.append(h)
        return len(self.sems) - 1

    def push_scope(self):
        if not hasattr(self, "scopes"):
            self.scopes = []
            self.scope_id = 0
        st = ExitStack()
        st.__enter__()
        self.scopes.append(st)
        self.scope_id += 1

    def pop_scope(self):
        self.barrier()
        st = self.scopes.pop()
        st.__exit__(None, None, None)

    def sb(self, name, shape, dtype):
        scopes = getattr(self, "scopes", [])
        es = scopes[-1] if scopes else self.es
        sid = getattr(self, "scope_id", 0)
        h = es.enter_context(self.nc.sbuf_tensor("s%d_%s" % (sid, name), list(shape), dtype))
        return T(self, h, name)

    def ps(self, name, shape, dtype):
        h = self.es.enter_context(self.nc.psum_tensor("p_" + name, list(shape), dtype))
        return T(self, h, name)

    def dram(self, name, shape, dtype, kind):
        h = self.nc.dram_tensor(name, list(shape), dtype, kind=kind)
        return T(self, h.ap(), name)

    def _wait(self, eng, toks):
        best = {}
        for (s, v) in toks:
            if v <= 0:
                continue
            if eng == "pe" and s == self.esem["pe"]:
                continue
            if v > best.get(s, 0):
                best[s] = v
        w = self.waited[eng]
        e = self.engs[eng]
        for s, v in best.items():
            if w.get(s, 0) >= v:
                continue
            e.wait_ge(self.sems[s], v)
            w[s] = v
            self.n_waits += 1

    def op(self, eng, reads, writes, emit):
        toks = []
        for v in reads:
            toks += v.t.deps_read(v.key)
        for v in writes:
            toks += v.t.deps_write(v.key)
        self._wait(eng, toks)
        ins = emit(self.engs[eng])
        self.cnt[eng] += 1
        ins.then_inc(self.sems[self.esem[eng]], 1)
        tok = (self.esem[eng], self.cnt[eng])
        for v in reads:
            v.t.add_reader(v.key, tok)
        for v in writes:
            v.t.set_writer(v.key, tok)
        self.n_ops += 1
        return tok

    def dma(self, q, out, in_, **kw):
        toks = in_.t.deps_read(in_.key) + out.t.deps_write(out.key)
        i = self.dnext[q]
        self.dnext[q] = (i + 1) % len(self.dsem[q])
        s = self.dsem[q][i]
        toks.append((s, self.dcnt[q][i]))
        self._wait(q, toks)
        ins = self.engs[q].dma_start(out=out.ap, in_=in_.ap, **kw)
        self.dcnt[q][i] += 16
        ins.then_inc(self.sems[s], 16)
        tok = (s, self.dcnt[q][i])
        in_.t.add_reader(in_.key, tok)
        out.t.set_writer(out.key, tok)
        self.n_ops += 1
        return tok

    def collective(self, kind, ins, outs, groups):
        q = "pool"
        if not hasattr(self, "ccsem"):
            self.ccsem = self._new_sem("cc_sem")
            self.cccnt = 0
        toks = []
        for v in ins:
            toks += v.t.deps_read(v.key)
        for v in outs:
            toks += v.t.deps_write(v.key)
        toks.append((self.ccsem, self.cccnt))
        self._wait(q, toks)
        ins_ = self.engs[q].collective_compute(kind, ALU.bypass, groups, [v.ap for v in ins], [v.ap for v in outs])
        self.cccnt += 1
        ins_.then_inc(self.sems[self.ccsem], 1)
        tok = (self.ccsem, self.cccnt)
        for v in ins:
            v.t.add_reader(v.key, tok)
        for v in outs:
            v.t.set_writer(v.key, tok)
        self.n_ops += 1
        return tok

    def barrier(self):
        for e in self.ENG:
            toks = [(self.esem[f], self.cnt[f]) for f in self.ENG if f != e]
            for q in self.dsem:
                toks += [(s, c) for s, c in zip(self.dsem[q], self.dcnt[q])]
            if hasattr(self, "ccsem"):
                toks.append((self.ccsem, self.cccnt))
            self._wait(e, toks)

    def wait_all(self, eng, views):
        toks = []
        for v in views:
            toks += v.t.deps_read(v.key)
        self._wait(eng, toks)

    def mm(self, out, pairs, reads_extra=()):
        reads = []
        for (l, r) in pairs:
            reads += [l, r]
        n = len(pairs)

        def emit(e):
            ins = None
            for i, (l, r) in enumerate(pairs):
                ins = e.matmul(out.ap, l.ap, r.ap, start=(i == 0), stop=(i == n - 1))
            return ins
        return self.op("pe", reads, [out], emit)

    def mm1(self, out, l, r, start, stop):
        return self.op("pe", [l, r], [out],
                       lambda e: e.matmul(out.ap, l.ap, r.ap, start=start, stop=stop))

    def tr(self, out, in_, ident):
        return self.op("pe", [in_, ident], [out],
                       lambda e: e.transpose(out.ap, in_.ap, ident.ap))

    def act(self, out, in_, func, bias=None, scale=None, accum=None, eng="act"):
        reads = [in_]
        kw = {}
        if bias is not None:
            if isinstance(bias, V):
                reads.append(bias)
                kw["bias"] = bias.ap
            else:
                kw["bias"] = bias
        if scale is not None:
            if isinstance(scale, V):
                reads.append(scale)
                kw["scale"] = scale.ap
            else:
                kw["scale"] = scale
        writes = [out]
        if accum is not None:
            writes.append(accum)
            kw["accum_out"] = accum.ap
        return self.op(eng, reads, writes,
                       lambda e: e.activation(out.ap, in_.ap, func, **kw))

    def tt(self, eng, out, a, b, op):
        return self.op(eng, [a, b], [out],
                       lambda e: e.tensor_tensor(out.ap, a.ap, b.ap, op))

    def ts(self, eng, out, a, s1, s2, op0, op1=None, accum=None):
        reads = [a]
        s1a = s1.ap if isinstance(s1, V) else s1
        s2a = s2.ap if isinstance(s2, V) else s2
        if isinstance(s1, V):
            reads.append(s1)
        if isinstance(s2, V):
            reads.append(s2)
        writes = [out]
        kw = {}
        if accum is not None:
            writes.append(accum)
            kw["accum_out"] = accum.ap
        if op1 is None:
            return self.op(eng, reads, writes,
                           lambda e: e.tensor_scalar(out.ap, a.ap, s1a, None, op0, **kw))
        return self.op(eng, reads, writes,
                       lambda e: e.tensor_scalar(out.ap, a.ap, s1a, s2a, op0, op1, **kw))

    def stt(self, eng, out, a, s, b, op0, op1):
        reads = [a, b]
        sa = s.ap if isinstance(s, V) else s
        if isinstance(s, V):
            reads.append(s)
        return self.op(eng, reads, [out],
                       lambda e: e.scalar_tensor_tensor(out.ap, a.ap, sa, b.ap, op0, op1))

    def copy(self, eng, out, in_):
        if eng == "act":
            return self.op(eng, [in_], [out], lambda e: e.copy(out.ap, in_.ap))
        return self.op(eng, [in_], [out], lambda e: e.tensor_copy(out.ap, in_.ap))

    def memset(self, eng, out, val):
        return self.op(eng, [], [out], lambda e: e.memset(out.ap, val))


D = 1024
B = 2
GRID_W = 64
CTX = 256
DEPTH = 2
HD = 128
NH = 8
NKV = 2
GH = 4
GDK = 128
GDV = 256
RANK = 16
TAU = 16.0
CHUNK = 64
CW = 31
DFF = 2816
ALPHA = (2.0 * DEPTH) ** 0.25
EPS = 1e-6
NCORES = 8
INC = 9760
OK_, OV_, OKG, OVG, OQ, OQG, OR, OY, OGT, C16 = 0, 256, 512, 1024, 2048, 3072, 3584, 4608, 5632, 8704


class Ctx:
    def __init__(self, p):
        self.p = p
        self.banks = [p.ps("bank%d" % i, [128, 512], F32) for i in range(8)]
        self.bi = 0
        self.uid = 0

    def bank(self):
        b = self.banks[self.bi % 8]
        self.bi += 1
        return b

    def name(self, s):
        self.uid += 1
        return "%s_%d" % (s, self.uid)


def bcast_mid(ap, n):
    a = ap.ap
    return bass.AP(ap.tensor, ap.offset, [list(a[0]), [0, n]] + [list(x) for x in a[1:]])


def load_const_eps(p, cx):
    eps = p.sb("eps_c", [128, 1], F32)
    p.memset("dve", eps[:], EPS)
    one = p.sb("one_c", [128, 1], F32)
    p.memset("dve", one[:], 1.0)
    cx.eps = eps
    cx.one = one


def ln_stats(p, cx, x, width, mean_rstd):
    nchunk = width // 512
    bn = cx.bn
    xr = x.ap.rearrange("p (c f) -> p c f", f=512)
    for c in range(nchunk):
        p.op("dve", [x], [bn.k(c, (slice(None), c, slice(None)))],
             lambda e, c=c: e.bn_stats(bn.h[:, c, :], xr[:, c, :]))
    p.op("dve", [bn[:, 0:nchunk, :]], [mean_rstd],
         lambda e: e.bn_aggr(mean_rstd.ap, bn.h[:, 0:nchunk, :]))
    r = V(mean_rstd.t, mean_rstd.ap[:, 1:2], mean_rstd.key)
    p.act(r, r, AF.Sqrt, bias=cx.eps[:, 0:1])
    p.op("dve", [r], [r], lambda e: e.reciprocal(r.ap, r.ap))


def mod_tiles(p, cx, scb, w_ada, b_ada, specs):
    wv = w_ada.h.rearrange("(kc p) c -> p kc c", p=128)
    for (vi, mi, plus1, out) in specs:
        for half in range(2):
            c0 = mi * 1024 + half * 512
            wt = cx.wts[cx.wi % 2]
            cx.wi += 1
            p.dma("pool", wt[:], V(w_ada, wv[:, :, c0:c0 + 512]))
            bb = cx.bbc
            p.dma("sp", bb[:, 0:512], V(b_ada, b_ada.h[c0:c0 + 512].partition_broadcast(128)))
            ps = cx.bank()
            p.mm(ps[:], [(scb[:, vi, kc, :], wt[:, kc, :]) for kc in range(8)])
            o = out[:, half * 512:(half + 1) * 512]
            if plus1:
                p.stt("dve", o, ps[:], 1.0, bb[:, 0:512], ALU.add, ALU.add)
            else:
                p.tt("dve", o, ps[:], bb[:, 0:512], ALU.add)


def make_scb(p, cx, cvT, nvec):
    cv = p.sb("cv", [128, nvec * 8], F32)
    p.dma("sp", cv[:], V(cvT, cvT.h.rearrange("p v k -> p (v k)")))
    p.act(cv[:], cv[:], AF.Silu)
    ones = p.sb("ones_bf", [128, 128], BF16)
    p.memset("dve", ones[:], 1.0)
    scb = p.sb("scb", [128, nvec, 8, 128], BF16)
    for v in range(nvec):
        for k in range(8):
            j = v * 8 + k
            p.ts("dve", scb[:, v, k, :], ones[:], cv[:, j:j + 1], None, ALU.mult)
    return scb


def ln_mod_transpose(p, cx, xsrc, t, scp, sh, hT, ident):
    xt = cx.xt[t % 2]
    p.dma("sp", xt[:], xsrc)
    mr = cx.mr[t % 2]
    ln_stats(p, cx, xt[:], 1024, mr[:])
    xn = cx.xn
    p.ts("dve", xn[:], xt[:], mr[:, 0:1], mr[:, 1:2], ALU.subtract, ALU.mult)
    p.tt("pool", xn[:], xn[:], scp[:], ALU.mult)
    hb = cx.hb[t % 2]
    p.tt("dve", hb[:], xn[:], sh[:], ALU.add)
    transpose_into(p, cx, hb, hT, t, ident, 8)


def transpose_into(p, cx, src, dstT, t, ident, nk, eng="act", col0=None, src0=0):
    for g0 in range(0, nk, 8):
        n = min(8, nk - g0)
        ps = cx.bank()
        pv = ps.h[:].bitcast(BF16)
        for k in range(n):
            p.tr(V(ps, pv[:, k * 128:(k + 1) * 128]), src[:, src0 + (g0 + k) * 128:src0 + (g0 + k + 1) * 128], ident[:])
        inv = V(ps, pv[:, 0:n * 128].rearrange("p (k f) -> p k f", f=128))
        c0_ = t * 128 if col0 is None else col0
        outv = dstT.k(("t", t), (slice(None), slice(g0, g0 + n), slice(c0_, c0_ + 128)))
        if eng == "act":
            p.act(outv, inv, AF.Identity)
        else:
            p.copy(eng, outv, inv)


def emit_p1(p, cx, NT, NCT, xs, cvT, w_ada, b_ada, w_in, qn_d, kn_d, wa2_0, wa2_1, ba, cos_d, sin_d,
            ident_d, o16, o32, layer_has_rope=True):
    NTA = NT + NCT
    load_const_eps(p, cx)
    cx.bn = p.sb("bn", [128, 2, 6], F32)
    cx.wts = [p.sb("wt%d" % i, [128, 8, 512], BF16) for i in range(2)]
    cx.wi = 0
    cx.bbc = p.sb("bbc", [128, 1024], F32)
    cx.xt = [p.sb("xt%d" % i, [128, D], F32) for i in range(2)]
    cx.mr = [p.sb("mr%d" % i, [128, 2], F32) for i in range(2)]
    cx.xn = p.sb("xn", [128, D], F32)
    cx.hb = [p.sb("hb%d" % i, [128, D], BF16) for i in range(2)]
    ident = p.sb("ident_sb", [128, 128], BF16)
    p.dma("pool", ident[:], ident_d[:])
    cos = p.sb("cos_sb", [128, NT, 64], F32)
    sin = p.sb("sin_sb", [128, NT, 64], F32)
    p.dma("sp", cos[:], cos_d[:])
    p.dma("sp", sin[:], sin_d[:])
    gq = p.sb("gq", [128, 128], F32)
    gk = p.sb("gk", [128, 128], F32)
    p.dma("sp", gq[:], V(qn_d, qn_d.h.partition_broadcast(128)))
    p.dma("sp", gk[:], V(kn_d, kn_d.h.partition_broadcast(128)))
    babc = p.sb("babc", [128, 1024], F32)
    p.dma("sp", babc[:], V(ba, ba.h.partition_broadcast(128)))
    w2bd = p.sb("w2bd", [32, 1024], BF16)
    p.memset("dve", w2bd[:], 0.0)
    p.dma("pool", w2bd[0:16, 0:512], wa2_0)
    p.dma("pool", w2bd[16:32, 512:1024], wa2_1)

    scb = make_scb(p, cx, cvT, 2)
    mods = {}
    for nm in ("shL", "scL", "shC", "scC"):
        mods[nm] = p.sb(nm, [128, D], F32)
    mod_tiles(p, cx, scb, w_ada, b_ada,
              [(0, 0, False, mods["shL"]), (0, 1, True, mods["scL"]),
               (1, 0, False, mods["shC"]), (1, 1, True, mods["scC"])])

    hT = p.sb("hT", [128, 8, NTA * 128], BF16)
    for t in range(NTA):
        lat = t < NT
        ln_mod_transpose(p, cx, xs[t], t, mods["scL" if lat else "scC"],
                         mods["shL" if lat else "shC"], hT, ident)

    wv = w_in.h.rearrange("(kc p) c -> p kc c", p=128)
    stg = [p.sb("stg%d" % i, [128, 512], BF16) for i in range(3)]
    stg32 = [p.sb("stgf%d" % i, [128, 512], F32) for i in range(2)]
    sq = p.sb("sq", [128, 512], F32)
    ss = p.sb("ss", [128, 4], F32)
    qn = p.sb("qn", [128, 512], F32)
    rt = [p.sb("rt%d" % i, [128, 4, 64], F32) for i in range(4)]
    si = [0]

    def load_w(col_ranges):
        wt = cx.wts[cx.wi % 2]
        cx.wi += 1
        o = 0
        for (c0, wd) in col_ranges:
            p.dma("pool", wt[:, :, o:o + wd], V(w_in, wv[:, :, c0:c0 + wd]))
            o += wd
        return wt, o

    def proj(t, wt, width):
        ps = cx.bank()
        p.mm(ps[:, 0:width], [(hT.k(("t", t), (slice(None), kc, slice(t * 128, (t + 1) * 128))),
                               wt[:, kc, 0:width]) for kc in range(8)])
        return ps

    def out16(t, st, col, width):
        p.dma("sp", V(o16, o16.h[t, :, col:col + width]), st[:, 0:width])

    def next_stg():
        s = stg[si[0] % 3]
        si[0] += 1
        return s

    def epi_simple(func, scale=None):
        def f(t, ps, width, col):
            st = next_stg()
            p.act(st[:, 0:width], ps[:, 0:width], func, scale=scale)
            out16(t, st, col, width)
        return f

    def epi_rmsrope(H, gain):
        def f(t, ps, width, col):
            lat = t < NT
            p.act(sq[:, 0:width], ps[:, 0:width], AF.Square)
            p.op("dve", [sq[:, 0:width]], [ss[:, 0:H]],
                 lambda e: e.reduce_sum(ss.h[:, 0:H], sq.h[:, 0:width].rearrange("p (h d) -> p h d", d=128), AX.X))
            p.act(ss[:, 0:H], ss[:, 0:H], AF.Sqrt, bias=cx.eps[:, 0:1], scale=1.0 / HD)
            p.op("dve", [ss[:, 0:H]], [ss[:, 0:H]], lambda e: e.reciprocal(ss.h[:, 0:H], ss.h[:, 0:H]))
            for h in range(H):
                p.stt("dve", qn[:, h * 128:(h + 1) * 128], ps[:, h * 128:(h + 1) * 128], ss[:, h:h + 1],
                      gain[:], ALU.mult, ALU.mult)
            st = next_stg()
            if lat and layer_has_rope:
                q4 = qn.h[:, 0:width].rearrange("p (h i two) -> p h i two", two=2, i=64)
                x1 = V(qn, q4[:, :, :, 0])
                x2 = V(qn, q4[:, :, :, 1])
                cb = V(cos, bcast_mid(cos.h[:, t, :], H))
                sb_ = V(sin, bcast_mid(sin.h[:, t, :], H))
                o4 = st.h[:, 0:width].rearrange("p (h i two) -> p h i two", two=2, i=64)
                a_, b_, c_, d_ = [V(r, r.h[:, 0:H, :]) for r in rt]
                p.tt("dve", a_, x1, cb, ALU.mult)
                p.tt("pool", b_, x2, sb_, ALU.mult)
                p.tt("dve", c_, x1, sb_, ALU.mult)
                p.tt("pool", d_, x2, cb, ALU.mult)
                p.tt("dve", V(st, o4[:, :, :, 0]), a_, b_, ALU.subtract)
                p.tt("pool", V(st, o4[:, :, :, 1]), c_, d_, ALU.add)
            else:
                p.copy("dve", st[:, 0:width], qn[:, 0:width])
            out16(t, st, col, width)
        return f

    def epi_glu(t, ps, width, col):
        sg = stg32[si[0] % 2]
        p.act(sg[:, 0:256], ps[:, 256:512], AF.Sigmoid)
        st = next_stg()
        p.tt("dve", st[:, 0:256], ps[:, 0:256], sg[:, 0:256], ALU.mult)
        out16(t, st, col, 256)

    groups = []
    groups.append(([(0, 256)], OK_, epi_rmsrope(2, gk)))
    groups.append(([(256, 256)], OV_, epi_simple(AF.Identity)))
    groups.append(([(512, 512)], OKG, epi_simple(AF.Identity)))
    groups.append(([(1024, 512)], OVG, epi_simple(AF.Identity)))
    groups.append(([(1536, 512)], OVG + 512, epi_simple(AF.Identity)))
    groups.append(([(2080, 512)], OQ, epi_rmsrope(4, gq)))
    groups.append(([(2592, 512)], OQ + 512, epi_rmsrope(4, gq)))
    groups.append(([(3104, 512)], OQG, epi_simple(AF.Identity, scale=GDK ** -0.5)))
    groups.append(([(3616, 512)], OR, epi_simple(AF.Silu)))
    groups.append(([(4128, 512)], OR + 512, epi_simple(AF.Silu)))
    for j in range(4):
        groups.append(([(4640 + 256 * j, 256), (5664 + 256 * j, 256)], OY + 256 * j, epi_glu))
    for j in range(6):
        groups.append(([(6688 + 512 * j, 512)], OGT + 512 * j, epi_simple(AF.Sigmoid)))

    for (cr, col, epi) in groups:
        wt, width = load_w(cr)
        for t in range(NTA):
            ps = proj(t, wt, width)
            epi(t, ps, width, col)

    wt, _ = load_w([(2048, 32)])
    glrT = p.sb("glrT", [32, NTA * 128], BF16)
    for t0 in range(0, NTA * 128, 512):
        n = min(512, NTA * 128 - t0)
        ps = cx.bank()
        t_lo, t_hi = t0 // 128, (t0 + n) // 128
        reads = []
        p.mm(ps[0:32, 0:n], [(wt[:, kc, 0:32], hT[:, kc, t0:t0 + n]) for kc in range(8)])
        p.act(glrT[:, t0:t0 + n], ps[0:32, 0:n], AF.Identity)
    for t in range(NTA):
        for dr in range(2):
            ps = cx.bank()
            p.mm(ps[:], [(glrT[:, t * 128:(t + 1) * 128], w2bd[:, dr * 512:(dr + 1) * 512])])
            sg = stg32[dr]
            p.tt("dve", sg[:], ps[:], babc[:, dr * 512:(dr + 1) * 512], ALU.add)
            p.act(sg[:], sg[:], AF.Exp, scale=-1.0)
            p.act(sg[:], sg[:], AF.Ln, bias=cx.one[:, 0:1])
            p.ts("dve", sg[:], sg[:], -1.0 / TAU, None, ALU.mult)
            p.dma("sp", V(o32, o32.h[t, :, dr * 512:(dr + 1) * 512]), sg[:])


def build_p1(NT, NCT, layer_has_rope=True):
    NTA = NT + NCT
    nc = bass.Bass("TRN2", target_bir_lowering=False)
    es = ExitStack()
    with es:
        p = Prog(nc, es)
        cx = Ctx(p)
        xs = p.dram("xs", [NTA, 128, D], F32, "ExternalInput")
        cvT = p.dram("cvT", [128, 2, 8], F32, "ExternalInput")
        w_ada = p.dram("w_ada", [D, 2 * D], F32, "ExternalInput")
        b_ada = p.dram("b_ada", [2 * D], F32, "ExternalInput")
        w_in = p.dram("w_in", [D, INC], F32, "ExternalInput")
        qn_d = p.dram("q_norm", [HD], F32, "ExternalInput")
        kn_d = p.dram("k_norm", [HD], F32, "ExternalInput")
        wa2 = p.dram("w_a2", [2, RANK, 512], F32, "ExternalInput")
        ba = p.dram("b_a", [2 * 512], F32, "ExternalInput")
        cos_d = p.dram("cos", [128, NT, 64], F32, "ExternalInput")
        sin_d = p.dram("sin", [128, NT, 64], F32, "ExternalInput")
        ident_d = p.dram("ident", [128, 128], F32, "ExternalInput")
        o16 = p.dram("o16", [NTA, 128, C16], BF16, "ExternalOutput")
        o32 = p.dram("o32", [NTA, 128, 1024], F32, "ExternalOutput")
        emit_p1(p, cx, NT, NCT, xs, cvT, w_ada, b_ada, w_in, qn_d, kn_d, wa2[0], wa2[1], ba, cos_d, sin_d,
                ident_d, o16, o32, layer_has_rope)
        p.wait_all("sp", [o16[:], o32[:]])
        print("P1 ops", p.n_ops, "waits", p.n_waits)
    return nc


_CACHE = {}


def _rope_tables(S):
    t = np.arange(S)
    row = (t // GRID_W).astype(np.float32)
    col = (t % GRID_W).astype(np.float32)
    half = HD // 2
    inv = (np.float32(10000.0) ** (-np.arange(0, half, 2, dtype=np.float32) / np.float32(half))).astype(np.float32)
    ang = np.concatenate([row[:, None] * inv, col[:, None] * inv], axis=-1).astype(np.float32)
    return np.cos(ang).astype(np.float32), np.sin(ang).astype(np.float32)


def _run(key, builder, in_maps):
    if key not in _CACHE:
        _CACHE[key] = builder()
    nc = _CACHE[key]
    res = run_bass_kernel_spmd(nc, in_maps, core_ids=list(range(NCORES)))
    return res.results


def host_p1(l, x, xc, c, c_ctx, P, S):
    TPC = S * B // NCORES
    NT = TPC // 128
    SEGS = NCORES // B
    cos, sin = _rope_tables(S)
    ctx_tiles = xc.reshape(B * CTX // 128, 128, D)
    in_maps = []
    for core in range(NCORES):
        b, seg = core // SEGS, core % SEGS
        xs = np.concatenate([x[b, seg * TPC:(seg + 1) * TPC].reshape(NT, 128, D),
                             ctx_tiles[core % 4][None]], axis=0)
        cv = np.stack([c[b], c_ctx]).reshape(2, 8, 128).transpose(2, 0, 1)
        cs = cos[seg * TPC:(seg + 1) * TPC].reshape(NT, 128, 64).transpose(1, 0, 2)
        sn = sin[seg * TPC:(seg + 1) * TPC].reshape(NT, 128, 64).transpose(1, 0, 2)
        in_maps.append({
            "xs": np.ascontiguousarray(xs), "cvT": np.ascontiguousarray(cv),
            "w_ada": np.ascontiguousarray(P["w_ada"][l][:, 0:2 * D]), "b_ada": np.ascontiguousarray(P["b_ada"][l][0:2 * D]), "w_in": P["w_in"][l],
            "q_norm": P["q_norm"][l], "k_norm": P["k_norm"][l],
            "w_a2": P["gla_w_a2"][l], "b_a": np.ascontiguousarray(P["gla_b_a"][l].reshape(-1)),
            "cos": np.ascontiguousarray(cs), "sin": np.ascontiguousarray(sn),
            "ident": np.eye(128, dtype=np.float32),
        })
    res = _run(("p1", NT), lambda: build_p1(NT, 1), in_maps)
    o16 = np.stack([r["o16"] for r in res])
    o32 = np.stack([r["o32"] for r in res])
    return o16, o32


def build_mix(S, with_ctx_q):
    L = S + CTX
    NKT = L // 128
    NQ = L if with_ctx_q else S
    nc = bass.Bass("TRN2", target_bir_lowering=False)
    es = ExitStack()
    with es:
        p = Prog(nc, es)
        banks = [p.ps("bank%d" % i, [128, 512], F32) for i in range(8)]
        qT_d = p.dram("qT", [2, 128, L], BF16, "ExternalInput")
        kT_d = p.dram("kT", [128, L], BF16, "ExternalInput")
        vE_d = p.dram("vE", [128, NKT, 129], BF16, "ExternalInput")
        oatt = p.dram("oatt", [2, NKT, 128, 128], BF16, "ExternalOutput")
        gq_d = p.dram("gqT", [2, 128, L], BF16, "ExternalInput")
        gkT_d = p.dram("gkT", [2, 128, L], BF16, "ExternalInput")
        gk_d = p.dram("gk", [2, NKT, 128, 128], BF16, "ExternalInput")
        gv_d = p.dram("gv", [2, NKT, 128, 256], BF16, "ExternalInput")
        gg_d = p.dram("gg", [2, NKT, 128, 128], F32, "ExternalInput")
        cst_d = p.dram("cst", [128, 128 * 2 + 2], F32, "ExternalInput")
        ogla = p.dram("ogla", [2, NKT, 128, 256], F32, "ExternalOutput")
        cy_d = p.dram("cy", [128, B, S + 30], BF16, "ExternalInput")
        cyc_d = p.dram("cyc", [128, B, CTX + 30], BF16, "ExternalInput")
        cw_d = p.dram("cw", [128, CW + 1], F32, "ExternalInput")
        oconv = p.dram("oconv", [128, B, S], F32, "ExternalOutput")
        oconvc = p.dram("oconvc", [128, B, CTX], F32, "ExternalOutput")

        cw = p.sb("cw", [128, CW + 1], F32)
        p.dma("sp", cw[:], cw_d[:])
        cy = p.sb("cy", [128, B, S + 30], BF16)
        cyc = p.sb("cyc", [128, B, CTX + 30], BF16)
        p.dma("sp", cy[:], cy_d[:])
        p.dma("sp", cyc[:], cyc_d[:])
        acc = p.sb("cacc", [128, B, S], F32)
        accc = p.sb("caccc", [128, B, CTX], F32)

        def conv_emit(tap):
            for b in range(B):
                eng = "dve"
                for (a, y, n, nm) in ((acc, cy, S, "l"), (accc, cyc, CTX, "c")):
                    av = a.k((nm, b), (slice(None), b, slice(None)))
                    yv = y[:, b, tap:tap + n]
                    if tap == 0:
                        p.ts(eng, av, yv, cw[:, 0:1], None, ALU.mult)
                    else:
                        p.stt(eng, av, yv, cw[:, tap:tap + 1], av, ALU.mult, ALU.add)
                    if tap == CW - 1:
                        p.ts(eng, av, av, cw[:, CW:CW + 1], None, ALU.add)
                        od = oconv if nm == "l" else oconvc
                        p.dma("sp", V(od, od.h[:, b, :]), av)

        cst = p.sb("cst", [128, 258], F32)
        p.dma("sp", cst[:], cst_d[:])
        Lm = cst[:, 0:128]
        Um = cst[:, 128:256]
        sel = cst[:, 256:258]
        Lmb = p.sb("Lmb", [128, 128], F32)
        p.copy("dve", Lmb[:], Lm)
        St = [p.sb("gS%d" % i, [128, 256], F32) for i in range(2)]
        Sb = [p.sb("gSb%d" % i, [128, 256], BF16) for i in range(2)]
        for i in range(2):
            p.memset("dve", St[i][:], 0.0)
            p.memset("dve", Sb[i][:], 0.0)
        gl = {}
        for nm, shp, dt in (("qT", [128, 128], BF16), ("kT", [128, 128], BF16), ("k", [128, 128], BF16),
                            ("v", [128, 256], BF16), ("g", [128, 128], F32), ("EbT", [128, 128], F32),
                            ("EnbT", [128, 128], F32), ("Erem", [128, 128], F32), ("Eend", [128, 2], F32),
                            ("qeT", [128, 128], BF16), ("keT", [128, 128], BF16), ("kend", [128, 128], BF16),
                            ("ATm", [128, 128], BF16), ("osb", [64, 2, 256], F32)):
            gl[nm] = [[p.sb("g_%s_%d_%d" % (nm, s, j), shp, dt) for j in range(2)] for s in range(2)]

        def gla_tile(s, t):
            j = t % 2
            T_ = {k: v[s][j] for k, v in gl.items()}
            pb = banks[4 + 2 * s:6 + 2 * s]
            p.dma("sp", T_["qT"][:], V(gq_d, gq_d.h[s, :, t * 128:(t + 1) * 128]))
            p.dma("sp", T_["kT"][:], V(gkT_d, gkT_d.h[s, :, t * 128:(t + 1) * 128]))
            p.dma("sp", T_["k"][:], V(gk_d, gk_d.h[s, t]))
            p.dma("sp", T_["v"][:], V(gv_d, gv_d.h[s, t]))
            p.dma("sp", T_["g"][:], V(gg_d, gg_d.h[s, t]))
            g = T_["g"]
            b0 = pb[0]
            p.mm(b0[:, 0:128], [(g[:], Lm)])
            p.mm(b0[:, 128:256], [(Um, g[:])])
            p.mm(b0[:, 256:258], [(g[:], sel)])
            p.act(T_["EbT"][:], b0[:, 0:128], AF.Exp)
            p.act(T_["EnbT"][:], b0[:, 0:128], AF.Exp, scale=-1.0)
            p.act(T_["Erem"][:], b0[:, 128:256], AF.Exp)
            p.act(T_["Eend"][:], b0[:, 256:258], AF.Exp)
            p.tt("dve", T_["qeT"][:], T_["qT"][:], T_["EbT"][:], ALU.mult)
            p.tt("dve", T_["keT"][:], T_["kT"][:], T_["EnbT"][:], ALU.mult)
            p.tt("dve", T_["kend"][:], T_["k"][:], T_["Erem"][:], ALU.mult)
            p.mm(b0[:, 384:512], [(T_["keT"][:], T_["qeT"][:])])
            p.tt("dve", T_["ATm"][:], b0[:, 384:512], Lmb[:], ALU.mult)
            b1 = pb[1]
            for c in range(2):
                cs = slice(c * 64, (c + 1) * 64)
                ov = b1[0:64, c * 256:(c + 1) * 256] if False else None
            for c in range(2):
                cs = slice(c * 64, (c + 1) * 64)
                ops_ = V(b1, b1.h[0:64, 0:256])
                p.mm(ops_, [(T_["qeT"][:, cs], Sb[s][:]), (T_["ATm"][:, cs], T_["v"][:])])
                p.act(T_["osb"][:, c, :], ops_, AF.Identity)
                ups = V(b1, b1.h[:, 256:512])
                p.mm(ups, [(T_["kend"][cs, :], T_["v"][cs, :])])
                p.stt("dve", St[s][:], St[s][:], T_["Eend"][:, c:c + 1], ups, ALU.mult, ALU.add)
                p.act(Sb[s][:], St[s][:], AF.Identity)
            p.dma("sp", V(ogla, ogla.h[s, t].rearrange("(c q) e -> q c e", q=64)), T_["osb"][:])

        kT = p.sb("kT", [128, L], BF16)
        vE = p.sb("vE", [128, NKT, 129], BF16)
        p.dma("sp", kT[:], kT_d[:])
        p.dma("sp", vE[:], vE_d[:])
        qTs = [p.sb("qTs%d" % i, [128, 512], BF16) for i in range(2)]
        pTs = [p.sb("pTs%d" % i, [128, 512], BF16) for i in range(3)]
        rcp = p.sb("rcp", [128, 4], F32)
        ost = [p.sb("ost%d" % i, [128, 128], BF16) for i in range(4)]
        SC = HD ** -0.5
        blocks = []
        for h in range(2):
            for q0 in range(0, S, 512):
                blocks.append((h, q0, min(512, S - q0), 0, NKT))
            if with_ctx_q:
                blocks.append((h, S, CTX, NKT - CTX // 128, NKT))
        state = {"bi": 0, "pi": 0, "sb": 0}

        def att_block(blk):
            h, q0, nq, kt0, kt1 = blk
            qt = qTs[state["bi"] % 2]
            state["bi"] += 1
            p.dma("sp", qt[:, 0:nq], V(qT_d, qT_d.h[h, :, q0:q0 + nq]))
            nsub = nq // 128
            for kt in range(kt0, kt1):
                sbk = banks[2 + state["sb"] % 2]
                state["sb"] += 1
                p.mm(sbk[:, 0:nq], [(kT[:, kt * 128:(kt + 1) * 128], qt[:, 0:nq])])
                pt = pTs[state["pi"] % 3]
                state["pi"] += 1
                p.act(pt[:, 0:nq], sbk[:, 0:nq], AF.Exp, scale=SC)
                for qs in range(nsub):
                    ob = banks[qs // 2]
                    ov = V(ob, ob.h[:, (qs % 2) * 256:(qs % 2) * 256 + 129])
                    p.mm1(ov, pt[:, qs * 128:(qs + 1) * 128], vE[:, kt, :], kt == kt0, kt == kt1 - 1)
            for qs in range(nsub):
                ob = banks[qs // 2]
                base = (qs % 2) * 256
                p.op("dve", [ob[:, base + 128:base + 129]], [rcp[:, qs:qs + 1]],
                     lambda e, ob=ob, base=base, qs=qs: e.reciprocal(rcp.h[:, qs:qs + 1], ob.h[:, base + 128:base + 129]))
                p.ts("dve", ost[qs][:], ob[:, base:base + 128], rcp[:, qs:qs + 1], None, ALU.mult)
                p.dma("sp", V(oatt, oatt.h[h, (q0 // 128) + qs]), ost[qs][:])

        nb = len(blocks)
        ngl = NKT
        total = max(nb, ngl, CW)
        gi = ci = ai = 0
        for step in range(total):
            while gi < ngl and gi * total <= step * ngl:
                gla_tile(0, gi)
                gla_tile(1, gi)
                gi += 1
            while ci < CW and ci * total <= step * CW:
                conv_emit(ci)
                ci += 1
            while ai < nb and ai * total <= step * nb:
                att_block(blocks[ai])
                ai += 1
        while gi < ngl:
            gla_tile(0, gi); gla_tile(1, gi); gi += 1
        while ci < CW:
            conv_emit(ci); ci += 1
        while ai < nb:
            att_block(blocks[ai]); ai += 1
        p.wait_all("sp", [oatt[:], ogla[:], oconv[:], oconvc[:]])
        print("MIX ops", p.n_ops, "waits", p.n_waits)
    return nc


def _post_common(p, cx, NTA):
    load_const_eps(p, cx)
    cx.bn = p.sb("bn", [128, 2, 6], F32)
    cx.wts = [p.sb("wt%d" % i, [128, 8, 512], BF16) for i in range(2)]
    cx.wi = 0
    cx.bbc = p.sb("bbc", [128, 1024], F32)
    cx.xt = [p.sb("xt%d" % i, [128, D], F32) for i in range(2)]
    cx.mr = [p.sb("mr%d" % i, [128, 2], F32) for i in range(2)]
    cx.xn = p.sb("xn", [128, D], F32)
    cx.hb = [p.sb("hb%d" % i, [128, D], BF16) for i in range(2)]


def _bc_vec(p, name, d, n):
    t = p.sb(name + "_bc", [128, n], F32)
    p.dma("sp", t[:], V(d, d.h.partition_broadcast(128)))
    return t


def deepnorm_out(p, cx, u, xo_view, lg, lb, i):
    mr = cx.mr[i % 2]
    ln_stats(p, cx, u[:], 1024, mr[:])
    p.ts("dve", u[:], u[:], mr[:, 0:1], mr[:, 1:2], ALU.subtract, ALU.mult)
    p.tt("pool", u[:], u[:], lg[:], ALU.mult)
    p.tt("dve", u[:], u[:], lb[:], ALU.add)
    p.dma("sp", xo_view, u[:])


class NS:
    pass


def emit_posta(p, cx, NT, NCT, A):
    NTA = NT + NCT
    _post_common(p, cx, NTA)
    ident = p.sb("ident_sb", [128, 128], BF16)
    p.dma("pool", ident[:], A.ident_d[:])
    bc = {n: _bc_vec(p, n, d, d.h.shape[0]) for n, d in A.vecs.items()}
    scb = make_scb(p, cx, A.cvT, 2)
    g1L = p.sb("g1L", [128, D], F32)
    g1C = p.sb("g1C", [128, D], F32)
    specs = [(0, A.g1_mi, False, g1L)]
    if NCT:
        specs.append((1, A.g1_mi, False, g1C))
    mod_tiles(p, cx, scb, A.w_ada, A.b_ada, specs)

    aT = p.sb("aT", [128, 8, NTA * 128], BF16)
    m = p.sb("m", [128, NTA, D], BF16)
    ld16 = [p.sb("ld16_%d" % i, [128, D], BF16) for i in range(2)]
    ld32 = [p.sb("ld32_%d" % i, [128, D], F32) for i in range(2)]
    ss = p.sb("ss", [128, 4], F32)
    sq = p.sb("sq", [128, D], F32)
    gtile = [p.sb("gtile%d" % i, [128, 512], BF16) for i in range(2)]
    tmp = p.sb("tmpm", [128, 512], F32)

    def fill_att(t):
        a = ld16[t % 2]
        p.dma("sp", a[:], A.oatt(t))
        transpose_into(p, cx, a, aT, t, ident, 8)

    def fill_gla(t):
        a, b_ = ld32[0], ld32[1]
        p.dma("sp", a[:], A.ogf(t))
        p.dma("sp", b_[:], A.ogb(t))
        p.tt("dve", a[:], a[:], b_[:], ALU.add)
        p.act(sq[:], a[:], AF.Square)
        p.op("dve", [sq[:]], [ss[:]],
             lambda e: e.reduce_sum(ss.h[:], sq.h[:].rearrange("p (h d) -> p h d", d=GDV), AX.X))
        p.act(ss[:], ss[:], AF.Sqrt, bias=cx.eps[:, 0:1], scale=1.0 / GDV)
        p.op("dve", [ss[:]], [ss[:]], lambda e: e.reciprocal(ss.h[:], ss.h[:]))
        for h in range(GH):
            hs = slice(h * GDV, (h + 1) * GDV)
            p.stt("dve", a[:, hs], a[:, hs], ss[:, h:h + 1], bc["gla_norm"][:], ALU.mult, ALU.mult)
        r = ld16[0]
        p.dma("sp", r[:], A.sr(t))
        hb = cx.hb[t % 2]
        p.tt("dve", hb[:], a[:], r[:], ALU.mult)
        transpose_into(p, cx, hb, aT, t, ident, 8)

    def fill_conv(t):
        a = ld32[t % 2]
        p.dma("sp", a[:], A.cv(t))
        mr = cx.mr[t % 2]
        ln_stats(p, cx, a[:], 1024, mr[:])
        p.ts("dve", a[:], a[:], mr[:, 0:1], mr[:, 1:2], ALU.subtract, ALU.mult)
        p.tt("pool", a[:], a[:], bc["conv_ln_g"][:], ALU.mult)
        p.tt("dve", a[:], a[:], bc["conv_ln_b"][:], ALU.add)
        hb = cx.hb[t % 2]
        p.act(hb[:], a[:], AF.Silu)
        transpose_into(p, cx, hb, aT, t, ident, 8)

    def load_wo(wd, half):
        wt = cx.wts[cx.wi % 2]
        cx.wi += 1
        wv = wd.h.rearrange("(kc p) c -> p kc c", p=128)
        p.dma("pool", wt[:], V(wd, wv[:, :, half * 512:(half + 1) * 512]))
        return wt

    for bi, (fill, wn) in enumerate(((fill_att, "w_att_o"), (fill_gla, "w_gla_o"), (fill_conv, "w_conv_o"))):
        for t in range(NTA):
            fill(t)
        for half in range(2):
            wt = load_wo(A.wo[wn], half)
            for t in range(NTA):
                ps = cx.bank()
                p.mm(ps[:], [(aT[:, kc, t * 128:(t + 1) * 128], wt[:, kc, :]) for kc in range(8)])
                g = gtile[t % 2]
                p.dma("sp", g[:], A.gt(t, bi * D + half * 512, bi * D + (half + 1) * 512))
                mv = m.k(("t", t, half), (slice(None), t, slice(half * 512, (half + 1) * 512)))
                if bi == 0:
                    p.tt("dve", mv, ps[:], g[:], ALU.mult)
                else:
                    p.tt("dve", tmp[:], ps[:], g[:], ALU.mult)
                    p.tt("pool", mv, mv, tmp[:], ALU.add)
    for t in range(NTA):
        hb = cx.hb[t % 2]
        p.copy("dve", hb[:], m[:, t, :])
        transpose_into(p, cx, hb, aT, t, ident, 8)
    w0 = load_wo(A.wo["w_out"], 0)
    w1 = load_wo(A.wo["w_out"], 1)
    us = [p.sb("u%d" % i, [128, D], F32) for i in range(2)]
    for t in range(NTA):
        lat = t < NT
        g1 = g1L if lat else g1C
        u = us[t % 2]
        xt = cx.xt[t % 2]
        p.dma("sp", xt[:], A.xs(t))
        for half, wt in enumerate((w0, w1)):
            ps = cx.bank()
            hs = slice(half * 512, (half + 1) * 512)
            p.mm(ps[:], [(aT[:, kc, t * 128:(t + 1) * 128], wt[:, kc, :]) for kc in range(8)])
            p.tt("dve", u[:, hs], ps[:], g1[:, hs], ALU.mult)
        p.stt("dve", u[:], xt[:], float(ALPHA), u[:], ALU.mult, ALU.add)
        deepnorm_out(p, cx, u, A.xo(t), bc["ln1_g"], bc["ln1_b"], t)


def build_posta(NT, NCT):
    NTA = NT + NCT
    nc = bass.Bass("TRN2", target_bir_lowering=False)
    es = ExitStack()
    with es:
        p = Prog(nc, es)
        cx = Ctx(p)
        oatt = p.dram("oatt_t", [NTA, 128, D], BF16, "ExternalInput")
        ogf = p.dram("ogf", [NTA, 128, D], F32, "ExternalInput")
        ogb = p.dram("ogb", [NTA, 128, D], F32, "ExternalInput")
        sr = p.dram("sr", [NTA, 128, D], BF16, "ExternalInput")
        cv = p.dram("cv", [NTA, 128, D], F32, "ExternalInput")
        gt = p.dram("gt", [NTA, 128, 3 * D], BF16, "ExternalInput")
        xs = p.dram("xs", [NTA, 128, D], F32, "ExternalInput")
        cvT = p.dram("cvT", [128, 2, 8], F32, "ExternalInput")
        w_ada = p.dram("w_ada", [D, D], F32, "ExternalInput")
        b_ada = p.dram("b_ada", [D], F32, "ExternalInput")
        wo = {n: p.dram(n, [D, D], F32, "ExternalInput") for n in ("w_att_o", "w_gla_o", "w_conv_o", "w_out")}
        vecs = {n: p.dram(n, [sz], F32, "ExternalInput") for n, sz in
                (("gla_norm", GDV), ("conv_ln_g", D), ("conv_ln_b", D), ("ln1_g", D), ("ln1_b", D))}
        ident_d = p.dram("ident", [128, 128], F32, "ExternalInput")
        xo = p.dram("xo", [NTA, 128, D], F32, "ExternalOutput")
        A = NS()
        A.oatt = lambda t: oatt[t]
        A.ogf = lambda t: ogf[t]
        A.ogb = lambda t: ogb[t]
        A.sr = lambda t: sr[t]
        A.cv = lambda t: cv[t]
        A.xs = lambda t: xs[t]
        A.xo = lambda t: xo[t]
        A.gt = lambda t, c0, c1: V(gt, gt.h[t, :, c0:c1])
        A.g1_mi = 0
        A.ident_d, A.vecs, A.cvT, A.w_ada, A.b_ada, A.wo = ident_d, vecs, cvT, w_ada, b_ada, wo
        emit_posta(p, cx, NT, NCT, A)
        p.wait_all("sp", [xo[:]])
        print("POSTA ops", p.n_ops, "waits", p.n_waits)
    return nc


def emit_postb(p, cx, NT, NCT, A):
    NTA = NT + NCT
    NFC = DFF // 128
    _post_common(p, cx, NTA)
    ident = p.sb("ident_sb", [128, 128], BF16)
    p.dma("pool", ident[:], A.ident_d[:])
    l2g = _bc_vec(p, "l2g", A.l2g_d, D)
    l2b = _bc_vec(p, "l2b", A.l2b_d, D)
    scb = make_scb(p, cx, A.cvT, 2)
    mods = {n: p.sb(n, [128, D], F32) for n in ("sh2L", "sc2L", "g2L")}
    m0 = A.mi0
    specs = [(0, m0, False, mods["sh2L"]), (0, m0 + 1, True, mods["sc2L"]), (0, m0 + 2, False, mods["g2L"])]
    if NCT:
        for n in ("sh2C", "sc2C", "g2C"):
            mods[n] = p.sb(n, [128, D], F32)
        specs += [(1, m0, False, mods["sh2C"]), (1, m0 + 1, True, mods["sc2C"]), (1, m0 + 2, False, mods["g2C"])]
    mod_tiles(p, cx, scb, A.w_ada, A.b_ada, specs)
    hT = p.sb("hT", [128, 8, NTA * 128], BF16)
    for t in range(NTA):
        lat = t < NT
        ln_mod_transpose(p, cx, A.xs(t), t, mods["sc2L" if lat else "sc2C"],
                         mods["sh2L" if lat else "sh2C"], hT, ident)
    wgs = [p.sb("wg%d" % i, [128, 8, 512], BF16) for i in range(2)]
    wus = cx.wts
    actT = p.sb("actT", [128, NFC, 512], BF16)
    wdt = p.sb("wdt", [128, NFC, 512], BF16)
    sg = [p.sb("sg%d" % i, [128, 512], F32) for i in range(2)]
    us = [p.sb("u%d" % i, [128, D], F32) for i in range(4)]
    wg_d, wu_d, wd_d = A.wg_d, A.wu_d, A.wd_d
    wgv = wg_d.h.rearrange("(kc p) c -> p kc c", p=128)
    wuv = wu_d.h.rearrange("(kc p) c -> p kc c", p=128)
    wdv = wd_d.h.rearrange("(c p) n -> p c n", p=128)
    wi = 0
    for g0 in range(0, NTA, 4):
        gt_ = list(range(g0, min(NTA, g0 + 4)))
        ntok = len(gt_) * 128
        tk = slice(g0 * 128, g0 * 128 + ntok)
        for c0 in range(0, DFF, 512):
            wd_ = min(512, DFF - c0)
            wgt, wut = wgs[wi % 2], wus[wi % 2]
            wi += 1
            p.dma("pool", wgt[:, :, 0:wd_], V(wg_d, wgv[:, :, c0:c0 + wd_]))
            p.dma("pool", wut[:, :, 0:wd_], V(wu_d, wuv[:, :, c0:c0 + wd_]))
            for sub in range(wd_ // 128):
                ffc = c0 // 128 + sub
                cs = slice(sub * 128, (sub + 1) * 128)
                pg = cx.bank()
                pu = cx.bank()
                p.mm(pg[:, 0:ntok], [(wgt[:, kc, cs], hT[:, kc, tk]) for kc in range(8)])
                p.mm(pu[:, 0:ntok], [(wut[:, kc, cs], hT[:, kc, tk]) for kc in range(8)])
                s_ = sg[ffc % 2]
                p.act(s_[:, 0:ntok], pg[:, 0:ntok], AF.Silu)
                p.tt("dve", actT.k(ffc, (slice(None), ffc, slice(0, ntok))), s_[:, 0:ntok], pu[:, 0:ntok], ALU.mult)
        for half in range(2):
            hs = slice(half * 512, (half + 1) * 512)
            p.dma("pool", wdt[:], V(wd_d, wdv[:, :, hs]))
            for i, t in enumerate(gt_):
                lat = t < NT
                g2 = mods["g2L" if lat else "g2C"]
                ps = cx.bank()
                p.mm(ps[:], [(actT[:, ffc, i * 128:(i + 1) * 128], wdt[:, ffc, :]) for ffc in range(NFC)])
                p.tt("dve", us[i][:, hs], ps[:], g2[:, hs], ALU.mult)
        for i, t in enumerate(gt_):
            xt = cx.xt[t % 2]
            p.dma("sp", xt[:], A.xs(t))
            p.stt("dve", us[i][:], xt[:], float(ALPHA), us[i][:], ALU.mult, ALU.add)
            deepnorm_out(p, cx, us[i], A.xo(t), l2g, l2b, t)


def emit_postb2(p, cx, NT, NCT, A):
    NTA = NT + NCT
    NFC = DFF // 128
    NTOK = NTA * 128
    act_s = A.act_s
    _post_common(p, cx, NTA)
    ident = p.sb("ident_sb", [128, 128], BF16)
    p.dma("pool", ident[:], A.ident_d[:])
    l2g = _bc_vec(p, "l2g", A.l2g_d, D)
    l2b = _bc_vec(p, "l2b", A.l2b_d, D)
    scb = make_scb(p, cx, A.cvT, 2)
    mods = {n: p.sb(n, [128, D], F32) for n in ("sh2L", "sc2L", "g2L")}
    m0 = A.mi0
    specs = [(0, m0, False, mods["sh2L"]), (0, m0 + 1, True, mods["sc2L"]), (0, m0 + 2, False, mods["g2L"])]
    if NCT:
        for n in ("sh2C", "sc2C", "g2C"):
            mods[n] = p.sb(n, [128, D], F32)
        specs += [(1, m0, False, mods["sh2C"]), (1, m0 + 1, True, mods["sc2C"]), (1, m0 + 2, False, mods["g2C"])]
    mod_tiles(p, cx, scb, A.w_ada, A.b_ada, specs)
    hT = p.sb("hT", [128, 8, NTOK], BF16)
    for t in range(NTA):
        lat = t < NT
        ln_mod_transpose(p, cx, A.xs(t), t, mods["sc2L" if lat else "sc2C"],
                         mods["sh2L" if lat else "sh2C"], hT, ident)
    wgs = [p.sb("wg%d" % i, [128, 8, 512], BF16) for i in range(2)]
    wus = cx.wts
    sg = [p.sb("sg%d" % i, [128, 512], F32) for i in range(2)]
    stg = [p.sb("astg%d" % i, [128, 512], BF16) for i in range(3)]
    wdt = p.sb("wdt", [128, NFC, D], BF16)
    wg_d, wu_d, wd_d = A.wg_d, A.wu_d, A.wd_d
    wgv = wg_d.h.rearrange("(kc q) c -> q kc c", q=128)
    wuv = wu_d.h.rearrange("(kc q) c -> q kc c", q=128)
    wdv = wd_d.h.rearrange("(c q) n -> q c n", q=128)
    wi = 0
    si = 0
    for c0 in range(0, DFF, 512):
        wd_ = min(512, DFF - c0)
        wgt, wut = wgs[wi % 2], wus[wi % 2]
        wi += 1
        p.dma("pool", wgt[:, :, 0:wd_], V(wg_d, wgv[:, :, c0:c0 + wd_]))
        p.dma("pool", wut[:, :, 0:wd_], V(wu_d, wuv[:, :, c0:c0 + wd_]))
        if c0 == 0:
            for half in range(2):
                p.dma("pool", wdt[:, :, half * 512:(half + 1) * 512], V(wd_d, wdv[:, :, half * 512:(half + 1) * 512]))
        for tk0 in range(0, NTOK, 512):
            ntok = min(512, NTOK - tk0)
            nt_ = ntok // 128
            tk = slice(tk0, tk0 + ntok)
            for sub in range(wd_ // 128):
                ffc = c0 // 128 + sub
                cs = slice(sub * 128, (sub + 1) * 128)
                pg = cx.bank()
                pu = cx.bank()
                p.mm(pg[:, 0:ntok], [(wgt[:, kc, cs], hT[:, kc, tk]) for kc in range(8)])
                p.mm(pu[:, 0:ntok], [(wut[:, kc, cs], hT[:, kc, tk]) for kc in range(8)])
                s_ = sg[si % 2]
                st = stg[si % 3]
                si += 1
                p.act(s_[:, 0:ntok], pg[:, 0:ntok], AF.Silu)
                p.tt("dve", st[:, 0:ntok], s_[:, 0:ntok], pu[:, 0:ntok], ALU.mult)
                t0_ = tk0 // 128
                p.dma("sp", V(act_s, act_s.h[t0_:t0_ + nt_, :, ffc, :].rearrange("t q k -> q t k")),
                      V(st, st.h[:, 0:ntok].rearrange("q (t k) -> q t k", k=128)))
    ats = [p.sb("at%d" % i, [128, NFC, 128], BF16) for i in range(2)]
    us = [p.sb("u%d" % i, [128, D], F32) for i in range(2)]
    for t in range(NTA):
        lat = t < NT
        g2 = mods["g2L" if lat else "g2C"]
        at = ats[t % 2]
        u = us[t % 2]
        p.dma("sp", at[:], act_s[t])
        for half in range(2):
            hs = slice(half * 512, (half + 1) * 512)
            ps = cx.bank()
            p.mm(ps[:], [(at[:, ffc, :], wdt[:, ffc, hs]) for ffc in range(NFC)])
            p.tt("dve", u[:, hs], ps[:], g2[:, hs], ALU.mult)
        xt = cx.xt[t % 2]
        p.dma("sp", xt[:], A.xs(t))
        p.stt("dve", u[:], xt[:], float(ALPHA), u[:], ALU.mult, ALU.add)
        deepnorm_out(p, cx, u, A.xo(t), l2g, l2b, t)


def build_postb(NT, NCT):
    NTA = NT + NCT
    NFC = DFF // 128
    nc = bass.Bass("TRN2", target_bir_lowering=False)
    es = ExitStack()
    with es:
        p = Prog(nc, es)
        cx = Ctx(p)
        xs = p.dram("xs", [NTA, 128, D], F32, "ExternalInput")
        cvT = p.dram("cvT", [128, 2, 8], F32, "ExternalInput")
        w_ada = p.dram("w_ada", [D, 3 * D], F32, "ExternalInput")
        b_ada = p.dram("b_ada", [3 * D], F32, "ExternalInput")
        wg_d = p.dram("w_ff_gate", [D, DFF], F32, "ExternalInput")
        wu_d = p.dram("w_ff_up", [D, DFF], F32, "ExternalInput")
        wd_d = p.dram("w_ff_down", [DFF, D], F32, "ExternalInput")
        l2g_d = p.dram("ln2_g", [D], F32, "ExternalInput")
        l2b_d = p.dram("ln2_b", [D], F32, "ExternalInput")
        ident_d = p.dram("ident", [128, 128], F32, "ExternalInput")
        xo = p.dram("xo", [NTA, 128, D], F32, "ExternalOutput")
        A = NS()
        A.xs = lambda t: xs[t]
        A.xo = lambda t: xo[t]
        A.mi0 = 0
        A.ident_d, A.cvT, A.w_ada, A.b_ada, A.l2g_d, A.l2b_d = ident_d, cvT, w_ada, b_ada, l2g_d, l2b_d
        A.wg_d, A.wu_d, A.wd_d = wg_d, wu_d, wd_d
        emit_postb(p, cx, NT, NCT, A)
        p.wait_all("sp", [xo[:]])
        print("POSTB ops", p.n_ops, "waits", p.n_waits)
    return nc


import ml_dtypes
NPBF = ml_dtypes.bfloat16


def _unpack(o, NT):
    C = o.shape[-1]
    lat = o[:, :NT].reshape(B, -1, C)
    ctx = o[0:4, NT].reshape(B, CTX, C) if o.shape[1] > NT else None
    return lat, ctx


def _pack(lat, ctx, core, NT, TPC, with_ctx):
    b, seg = core // 4, core % 4
    a = lat[b, seg * TPC:(seg + 1) * TPC].reshape(NT, 128, -1)
    if with_ctx:
        a = np.concatenate([a, ctx.reshape(4, 128, -1)[core % 4][None]], axis=0)
    return np.ascontiguousarray(a)


def _gla_consts():
    j = np.arange(128)[:, None]
    i = np.arange(128)[None, :]
    same = (j // CHUNK) == (i // CHUNK)
    Lm = (same & (j <= i)).astype(np.float32)
    Um = (same & (j > i)).astype(np.float32)
    sel = ((np.arange(128)[:, None] // CHUNK) == np.arange(2)[None, :]).astype(np.float32)
    return np.ascontiguousarray(np.concatenate([Lm, Um, sel], axis=1))


def host_mix(l, lat16, ctx16, latg, ctxg, P, S, with_ctx_q):
    L = S + CTX
    NKT = L // 128
    cst = _gla_consts()
    in_maps = []
    for core in range(NCORES):
        b, r = core // 4, core % 4
        h0, kv = 2 * r, r // 2
        Ql = lat16[b, :, OQ:OQ + 1024].reshape(S, NH, HD)
        Qc = ctx16[b, :, OQ:OQ + 1024].reshape(CTX, NH, HD)
        qT = np.stack([np.concatenate([Ql[:, h0 + hh], Qc[:, h0 + hh]], 0).T for hh in range(2)])
        Kl = lat16[b, :, OK_:OK_ + 256].reshape(S, NKV, HD)[:, kv]
        Kc = ctx16[b, :, OK_:OK_ + 256].reshape(CTX, NKV, HD)[:, kv]
        kT = np.concatenate([Kl, Kc], 0).T
        Vl = lat16[b, :, OV_:OV_ + 256].reshape(S, NKV, HD)[:, kv]
        Vc = ctx16[b, :, OV_:OV_ + 256].reshape(CTX, NKV, HD)[:, kv]
        Vall = np.concatenate([Vl, Vc], 0)
        vE = np.concatenate([Vall, np.ones((L, 1), NPBF)], 1).reshape(NKT, 128, 129).transpose(1, 0, 2)

        def seq(al, ac, d):
            if d == 0:
                return np.concatenate([ac, al], 0)
            return np.concatenate([al, ac], 0)[::-1]
        hd = r
        gq, gkT, gk, gv, gg = [], [], [], [], []
        for d in range(2):
            q_ = seq(lat16[b, :, OQG + hd * 128:OQG + (hd + 1) * 128], ctx16[b, :, OQG + hd * 128:OQG + (hd + 1) * 128], d)
            k_ = seq(lat16[b, :, OKG + hd * 128:OKG + (hd + 1) * 128], ctx16[b, :, OKG + hd * 128:OKG + (hd + 1) * 128], d)
            v_ = seq(lat16[b, :, OVG + hd * 256:OVG + (hd + 1) * 256], ctx16[b, :, OVG + hd * 256:OVG + (hd + 1) * 256], d)
            g_ = seq(latg[b, :, d * 512 + hd * 128:d * 512 + (hd + 1) * 128], ctxg[b, :, d * 512 + hd * 128:d * 512 + (hd + 1) * 128], d)
            gq.append(q_.T); gkT.append(k_.T); gk.append(k_.reshape(NKT, 128, 128))
            gv.append(v_.reshape(NKT, 128, 256)); gg.append(g_.reshape(NKT, 128, 128))
        ch = slice(core * 128, (core + 1) * 128)
        cy = np.zeros((128, B, S + 30), NPBF)
        cyc = np.zeros((128, B, CTX + 30), NPBF)
        for bb in range(B):
            cy[:, bb, 15:15 + S] = lat16[bb, :, OY:OY + 1024][:, ch].T
            cyc[:, bb, 15:15 + CTX] = ctx16[bb, :, OY:OY + 1024][:, ch].T
        cw = np.concatenate([P["conv_w_dw"][l][:, 0, ch].T, P["conv_b_dw"][l][ch][:, None]], 1).astype(np.float32)
        in_maps.append({
            "qT": np.ascontiguousarray(qT), "kT": np.ascontiguousarray(kT), "vE": np.ascontiguousarray(vE),
            "gqT": np.ascontiguousarray(np.stack(gq)), "gkT": np.ascontiguousarray(np.stack(gkT)),
            "gk": np.ascontiguousarray(np.stack(gk)), "gv": np.ascontiguousarray(np.stack(gv)),
            "gg": np.ascontiguousarray(np.stack(gg)).astype(np.float32), "cst": cst,
            "cy": cy, "cyc": cyc, "cw": np.ascontiguousarray(cw),
        })
    res = _run(("mix", S, with_ctx_q), lambda: build_mix(S, with_ctx_q), in_maps)
    att_l = np.zeros((B, S, D), NPBF); att_c = np.zeros((B, CTX, D), NPBF)
    gf_l = np.zeros((B, S, D), np.float32); gb_l = np.zeros((B, S, D), np.float32)
    gf_c = np.zeros((B, CTX, D), np.float32); gb_c = np.zeros((B, CTX, D), np.float32)
    cv_l = np.zeros((B, S, D), np.float32); cv_c = np.zeros((B, CTX, D), np.float32)
    for core in range(NCORES):
        b, r = core // 4, core % 4
        oa = res[core]["oatt"].reshape(2, L, HD)
        for hh in range(2):
            hs = slice((2 * r + hh) * HD, (2 * r + hh + 1) * HD)
            att_l[b, :, hs] = oa[hh, :S]
            att_c[b, :, hs] = oa[hh, S:]
        og = res[core]["ogla"].reshape(2, L, GDV)
        vs = slice(r * GDV, (r + 1) * GDV)
        gf_c[b, :, vs] = og[0, :CTX]; gf_l[b, :, vs] = og[0, CTX:]
        ob = og[1][::-1]
        gb_l[b, :, vs] = ob[:S]; gb_c[b, :, vs] = ob[S:]
        ch = slice(core * 128, (core + 1) * 128)
        for bb in range(B):
            cv_l[bb, :, ch] = res[core]["oconv"][:, bb, :].T
            cv_c[bb, :, ch] = res[core]["oconvc"][:, bb, :].T
    return (att_l, att_c), (gf_l, gf_c), (gb_l, gb_c), (cv_l, cv_c)


def host_post(l, mixo, lat16, ctx16, x, xc, c, c_ctx, P, S, with_ctx):
    TPC = S * B // NCORES
    NT = TPC // 128
    (att_l, att_c), (gf_l, gf_c), (gb_l, gb_c), (cv_l, cv_c) = mixo
    ident = np.eye(128, dtype=np.float32)
    in_maps = []
    for core in range(NCORES):
        b = core // 4
        pk = lambda al, ac: _pack(al, ac, core, NT, TPC, with_ctx)
        cv = np.ascontiguousarray(np.stack([c[b], c_ctx]).reshape(2, 8, 128).transpose(2, 0, 1))
        in_maps.append({
            "oatt_t": pk(att_l, att_c), "ogf": pk(gf_l, gf_c), "ogb": pk(gb_l, gb_c),
            "sr": pk(lat16[:, :, OR:OR + 1024], ctx16[:, :, OR:OR + 1024]),
            "cv": pk(cv_l, cv_c), "gt": pk(lat16[:, :, OGT:OGT + 3072], ctx16[:, :, OGT:OGT + 3072]),
            "xs": pk(x, xc), "cvT": cv, "w_ada": np.ascontiguousarray(P["w_ada"][l][:, 2 * D:3 * D]),
            "b_ada": np.ascontiguousarray(P["b_ada"][l][2 * D:3 * D]),
            "w_att_o": P["w_att_o"][l], "w_gla_o": P["w_gla_o"][l], "w_conv_o": P["w_conv_o"][l],
            "w_out": P["w_out"][l], "gla_norm": P["gla_norm"][l], "conv_ln_g": P["conv_ln_g"][l],
            "conv_ln_b": P["conv_ln_b"][l], "ln1_g": P["ln1_g"][l], "ln1_b": P["ln1_b"][l], "ident": ident,
        })
    nct = 1 if with_ctx else 0
    res = _run(("posta", NT, nct), lambda: build_posta(NT, nct), in_maps)
    xo = np.stack([r["xo"] for r in res])
    x1, xc1 = _unpack(xo, NT)
    in_maps = []
    for core in range(NCORES):
        b = core // 4
        cv = np.ascontiguousarray(np.stack([c[b], c_ctx]).reshape(2, 8, 128).transpose(2, 0, 1))
        in_maps.append({
            "xs": _pack(x1, xc1, core, NT, TPC, with_ctx), "cvT": cv,
            "w_ada": np.ascontiguousarray(P["w_ada"][l][:, 3 * D:6 * D]),
            "b_ada": np.ascontiguousarray(P["b_ada"][l][3 * D:6 * D]), "w_ff_gate": P["w_ff_gate"][l], "w_ff_up": P["w_ff_up"][l],
            "w_ff_down": P["w_ff_down"][l], "ln2_g": P["ln2_g"][l], "ln2_b": P["ln2_b"][l], "ident": ident,
        })
    res = _run(("postb", NT, nct), lambda: build_postb(NT, nct), in_maps)
    xo = np.stack([r["xo"] for r in res])
    return _unpack(xo, NT)


def forward(P, S):
    x = np.ascontiguousarray(P["x"][:, :S])
    xc = P["ctx"]
    c, c_ctx = P["c"], P["c_ctx"]
    TPC = S * B // NCORES
    NT = TPC // 128
    for l in range(DEPTH):
        last = l == DEPTH - 1
        o16, o32 = host_p1(l, x, xc, c, c_ctx, P, S)
        lat16, ctx16 = _unpack(o16, NT)
        latg, ctxg = _unpack(o32, NT)
        mixo = host_mix(l, lat16, ctx16, latg, ctxg, P, S, not last)
        x, xc_new = host_post(l, mixo, lat16, ctx16, x, xc, c, c_ctx, P, S, not last)
        if not last:
            xc = xc_new
    return x


def kernel(**inputs):
    P = {k: np.asarray(v) for k, v in inputs.items()}
    S = P["x"].shape[1]
    out = forward(P, S)
    return np.ascontiguousarray(out.astype(np.float32))


class _Stop(Exception):
    pass


def emit_mix_fused(p, cx, NT, M):
    depth0 = len(getattr(p, "scopes", []))
    try:
        _emit_mix_fused(p, cx, NT, M)
    except _Stop:
        while len(p.scopes) > depth0:
            p.pop_scope()


def _emit_mix_fused(p, cx, NT, M):
    def chk(x):
        if getattr(M, "stop_after", 3) < x:
            raise _Stop()
    NCT = 2
    NTA = NT + NCT
    TPC = NT * 128
    NTOK = NTA * 128
    SEGS = 4
    Sfull = SEGS * TPC
    L = Sfull + CTX
    NKT = L // 128
    banks = cx.banks
    o16, o32 = M.o16, M.o32
    groups = [[0, 1, 2, 3], [4, 5, 6, 7]]

    p.push_scope()
    ident = p.sb("ident_sb", [128, 128], BF16)
    p.dma("pool", ident[:], M.ident_d[:])
    identf = p.sb("identf", [128, 128], F32)
    p.dma("sp", identf[:], M.ident_d[:])
    cst = p.sb("cst", [128, 514], F32)
    p.dma("sp", cst[:], M.gcst_d[:])
    LM = (cst[:, 0:128], cst[:, 256:384])
    UM = (cst[:, 128:256], cst[:, 384:512])
    sel = cst[:, 512:514]
    masks = p.sb("masks", [128, 8], F32)
    p.dma("sp", masks[:], M.masks_d[:])
    gqT = p.sb("gqT", [128, 4, NTOK], BF16)
    gkT = p.sb("gkT", [128, 4, NTOK], BF16)
    kTl = p.sb("kTl", [128, 2, NTOK], BF16)
    veb = p.sb("veb", [128, NTA, 2, 129], BF16)
    p.memset("dve", veb[:], 1.0)
    St = [p.sb("gS%d" % i, [128, 256], F32) for i in range(8)]
    Sb = [p.sb("gSb%d" % i, [128, 256], BF16) for i in range(8)]
    Sctx = [p.sb("gSc%d" % i, [128, 256], F32) for i in range(8)]
    logD = p.sb("logD", [128, 8], F32)
    p.memset("dve", logD[:], 1.0)
    gl = {}
    for nm, shp, dt in (("k", [128, 128], BF16), ("v", [128, 256], BF16), ("g", [128, 128], F32),
                        ("EbT", [128, 128], F32), ("EnbT", [128, 128], F32), ("Erem", [128, 128], F32),
                        ("Eend", [128, 2], F32), ("qeT", [128, 128], BF16), ("keT", [128, 128], BF16),
                        ("kend", [128, 128], BF16), ("ATm", [128, 128], BF16), ("osb", [64, 2, 256], F32)):
        gl[nm] = [[p.sb("g_%s_%d_%d" % (nm, s_, j), shp, dt) for j in range(2)] for s_ in range(4)]
    cnt = {"gl": 0}

    def gla_tile(slot, r, t, full, S_t, S_b):
        hd, d = r // 2, r % 2
        j = cnt["gl"] % 2
        cnt["gl"] += 1
        T_ = {k_: v_[slot][j] for k_, v_ in gl.items()}
        pb = banks[2 * slot:2 * slot + 2]
        p.dma("sp", T_["k"][:], V(o16, o16.h[t, :, OKG + hd * 128:OKG + (hd + 1) * 128]))
        p.dma("sp", T_["v"][:], V(o16, o16.h[t, :, OVG + hd * 256:OVG + (hd + 1) * 256]))
        p.dma("sp", T_["g"][:], V(o32, o32.h[t, :, d * 512 + hd * 128:d * 512 + (hd + 1) * 128]))
        g = T_["g"]
        Lc, Uc = LM[d], UM[d]
        b0, b1 = pb
        ts_ = slice(t * 128, (t + 1) * 128)
        if full:
            p.mm(b0[:, 0:128], [(g[:], Lc)])
        p.mm(b0[:, 128:256], [(Uc, g[:])])
        p.mm(b0[:, 256:258], [(g[:], sel)])
        if full:
            p.act(T_["EbT"][:], b0[:, 0:128], AF.Exp)
            p.act(T_["EnbT"][:], b0[:, 0:128], AF.Exp, scale=-1.0)
        p.act(T_["Erem"][:], b0[:, 128:256], AF.Exp)
        p.act(T_["Eend"][:], b0[:, 256:258], AF.Exp)
        if full:
            p.tt("dve", T_["qeT"][:], gqT[:, hd, ts_], T_["EbT"][:], ALU.mult)
            p.tt("dve", T_["keT"][:], gkT[:, hd, ts_], T_["EnbT"][:], ALU.mult)
        else:
            p.ts("dve", logD[:, r:r + 1], logD[:, r:r + 1], T_["Eend"][:, 0:1], T_["Eend"][:, 1:2], ALU.mult, ALU.mult)
        p.tt("dve", T_["kend"][:], T_["k"][:], T_["Erem"][:], ALU.mult)
        if full:
            p.mm(b0[:, 384:512], [(T_["keT"][:], T_["qeT"][:])])
            p.tt("dve", T_["ATm"][:], b0[:, 384:512], Lc, ALU.mult)
        for c in ((0, 1) if d == 0 else (1, 0)):
            cs = slice(c * 64, (c + 1) * 64)
            if full:
                ops_ = V(b1, b1.h[0:64, 0:256])
                p.mm(ops_, [(T_["qeT"][:, cs], S_b[:]), (T_["ATm"][:, cs], T_["v"][:])])
                p.act(T_["osb"][:, c, :], ops_, AF.Identity)
            ups = V(b1, b1.h[:, 256:512])
            p.mm(ups, [(T_["kend"][cs, :], T_["v"][cs, :])])
            p.stt("dve", S_t[:], S_t[:], T_["Eend"][:, c:c + 1], ups, ALU.mult, ALU.add)
            if full:
                p.act(S_b[:], S_t[:], AF.Identity)
        if full:
            dst = M.gf_s if d == 0 else M.gb_s
            p.dma("sp", V(dst, dst.h[t, :, hd * 256:(hd + 1) * 256].rearrange("(c q) e -> q c e", q=64)), T_["osb"][:])

    def tiles_for(d, ctx):
        if ctx:
            return [NT, NT + 1] if d == 0 else [NT + 1, NT]
        return list(range(NT)) if d == 0 else list(range(NT - 1, -1, -1))

    slotcnt = [0, 0, 0, 0]

    def gla_group(items, full):
        cs_ = []
        for (slot, r, t, S_t, S_b) in items:
            hd, d = r // 2, r % 2
            j = slotcnt[slot] % 2
            slotcnt[slot] += 1
            T_ = {k_: v_[slot][j] for k_, v_ in gl.items()}
            b0 = banks[4 + slot]
            p.dma("sp", T_["k"][:], V(o16, o16.h[t, :, OKG + hd * 128:OKG + (hd + 1) * 128]))
            p.dma("sp", T_["v"][:], V(o16, o16.h[t, :, OVG + hd * 256:OVG + (hd + 1) * 256]))
            p.dma("sp", T_["g"][:], V(o32, o32.h[t, :, d * 512 + hd * 128:d * 512 + (hd + 1) * 128]))
            g = T_["g"]
            if full:
                p.mm(b0[:, 0:128], [(g[:], LM[d])])
            p.mm(b0[:, 128:256], [(UM[d], g[:])])
            p.mm(b0[:, 256:258], [(g[:], sel)])
            cs_.append((slot, r, t, S_t, S_b, hd, d, T_, b0))
        yield
        for (slot, r, t, S_t, S_b, hd, d, T_, b0) in cs_:
            if full:
                p.act(T_["EbT"][:], b0[:, 0:128], AF.Exp)
                p.act(T_["EnbT"][:], b0[:, 0:128], AF.Exp, scale=-1.0)
            p.act(T_["Erem"][:], b0[:, 128:256], AF.Exp)
            p.act(T_["Eend"][:], b0[:, 256:258], AF.Exp)
        yield
        for (slot, r, t, S_t, S_b, hd, d, T_, b0) in cs_:
            ts_ = slice(t * 128, (t + 1) * 128)
            if full:
                p.tt("dve", T_["qeT"][:], gqT[:, hd, ts_], T_["EbT"][:], ALU.mult)
                p.tt("dve", T_["keT"][:], gkT[:, hd, ts_], T_["EnbT"][:], ALU.mult)
            else:
                p.ts("dve", logD[:, r:r + 1], logD[:, r:r + 1], T_["Eend"][:, 0:1], T_["Eend"][:, 1:2], ALU.mult, ALU.mult)
            p.tt("dve", T_["kend"][:], T_["k"][:], T_["Erem"][:], ALU.mult)
        yield
        if full:
            for (slot, r, t, S_t, S_b, hd, d, T_, b0) in cs_:
                p.mm(b0[:, 384:512], [(T_["keT"][:], T_["qeT"][:])])
            yield
            for (slot, r, t, S_t, S_b, hd, d, T_, b0) in cs_:
                p.tt("dve", T_["ATm"][:], b0[:, 384:512], LM[d], ALU.mult)
            yield
        for ci in range(2):
            for (slot, r, t, S_t, S_b, hd, d, T_, b0) in cs_:
                c = ci if d == 0 else 1 - ci
                cs = slice(c * 64, (c + 1) * 64)
                if full:
                    ops_ = V(b0, b0.h[0:64, 0:256])
                    p.mm(ops_, [(T_["qeT"][:, cs], S_b[:]), (T_["ATm"][:, cs], T_["v"][:])])
                ups = V(b0, b0.h[:, 256:512])
                p.mm(ups, [(T_["kend"][cs, :], T_["v"][cs, :])])
            yield
            for (slot, r, t, S_t, S_b, hd, d, T_, b0) in cs_:
                c = ci if d == 0 else 1 - ci
                if full:
                    p.copy("dve", T_["osb"][:, c, :], V(b0, b0.h[0:64, 0:256]))
                p.stt("dve", S_t[:], S_t[:], T_["Eend"][:, c:c + 1], V(b0, b0.h[:, 256:512]), ALU.mult, ALU.add)
                if full:
                    p.act(S_b[:], S_t[:], AF.Identity)
            yield
        if full:
            for (slot, r, t, S_t, S_b, hd, d, T_, b0) in cs_:
                dst = M.gf_s if d == 0 else M.gb_s
                p.dma("sp", V(dst, dst.h[t, :, hd * 256:(hd + 1) * 256].rearrange("(c q) e -> q c e", q=64)), T_["osb"][:])

    def gla_pass_gen(ctx, full):
        n = 2 if ctx else NT
        for k_ in range(n):
            for g0_ in (0, 4):
                items = []
                for r in range(g0_, g0_ + 4):
                    d = r % 2
                    items.append((r % 4, r, tiles_for(d, ctx)[k_], St[r], Sb[r] if full else None))
                yield from gla_group(items, full)
                yield

    def gla_pass(ctx, full):
        for _ in gla_pass_gen(ctx, full):
            pass

    p.push_scope()
    ld = [p.sb("ld%d" % i, [128, 1280], BF16) for i in range(2)]
    for t in range(NTA):
        a = ld[t % 2]
        p.dma("sp", a[:, 0:256], V(o16, o16.h[t, :, OK_:OK_ + 256]))
        p.dma("sp", a[:, 256:768], V(o16, o16.h[t, :, OKG:OKG + 512]))
        p.dma("sp", a[:, 768:1280], V(o16, o16.h[t, :, OQG:OQG + 512]))
        transpose_into(p, cx, a, kTl, t, ident, 2, src0=0)
        transpose_into(p, cx, a, gkT, t, ident, 4, src0=256)
        transpose_into(p, cx, a, gqT, t, ident, 4, src0=768)
        p.dma("sp", veb.k(("t", t), (slice(None), t, slice(None), slice(0, 128))),
              V(o16, o16.h[t, :, OV_:OV_ + 256].rearrange("p (k e) -> p k e", k=2)))
    chk(0.05)
    ksb = M.ksnd.h.bitcast(BF16)
    for kvh in range(2):
        p.dma("sp", V(M.ksnd, ksb[kvh * 128:(kvh + 1) * 128, :]), kTl[:, kvh, 0:TPC])
    NH2 = NT // 2
    for hf in range(2):
        vs_ = M.vsnd[hf]
        vsb = vs_.h.bitcast(BF16)
        p.dma("sp", V(vs_, vsb.rearrange("p (t k e) -> p t k e", t=NH2, k=2)), veb[:, hf * NH2:(hf + 1) * NH2, :, :])
    chk(0.1)
    yed = p.sb("yed", [32, 1024], BF16)
    p.memset("dve", yed[:], 0.0)
    p.dma("sp", yed[0:15, :], V(o16, o16.h[0, 0:15, OY:OY + 1024]))
    p.dma("sp", yed[15:30, :], V(o16, o16.h[NT - 1, 113:128, OY:OY + 1024]))
    p.dma("sp", V(M.ysnd, M.ysnd.h.bitcast(BF16)), yed[:])
    chk(0.15)
    p.collective("AllGather", [M.ksnd[:]], [M.krcv[:]], groups)
    for hf in range(2):
        p.collective("AllGather", [M.vsnd[hf][:]], [M.vrcv[hf][:]], groups)
    p.collective("AllGather", [M.ysnd[:]], [M.yrcv[:]], groups)
    chk(0.2)
    for i in range(8):
        p.memset("dve", St[i][:], 0.0)
        p.memset("dve", Sb[i][:], 0.0)
    gla_pass(True, True)
    for r in range(8):
        p.copy("dve", Sctx[r][:], St[r][:])
        p.memset("dve", St[r][:], 0.0)
    chk(0.3)
    gla_pass(False, False)
    for r in range(8):
        p.dma("sp", V(M.gsnd, M.gsnd.h[r * 128:(r + 1) * 128, 0:256]), St[r][:])
    chk(0.4)
    dst_ = p.sb("dstage", [128, 64], F32)
    p.memset("dve", dst_[:], 0.0)
    p.copy("dve", dst_[:, 0:8], logD[:])
    p.dma("sp", M.dsnd[:], dst_[:])
    p.collective("AllGather", [M.gsnd[:]], [M.grcv[:]], groups)
    p.collective("AllGather", [M.dsnd[:]], [M.drcv[:]], groups)
    Dall = p.sb("Dall", [128, 4, 64], F32)
    p.dma("sp", Dall[:], V(M.drcv, M.drcv.h.rearrange("(s q) c -> q s c", s=4)))
    chk(0.5)
    G = [p.sb("G%d" % i, [128, 4, 256], F32) for i in range(2)]
    coef = p.sb("coef", [128, 2], F32)
    gv = M.grcv.h.rearrange("(s r q) c -> q s r c", s=4, r=8)
    for r in range(8):
        d = r % 2
        Gt = G[r % 2]
        p.dma("sp", Gt[:], V(M.grcv, gv[:, :, r, :]))
        p.copy("dve", St[r][:], Sctx[r][:])
        for s_ in (range(4) if d == 0 else range(3, -1, -1)):
            m = masks[:, d * 4 + s_:d * 4 + s_ + 1]
            p.ts("dve", coef[:, 1:2], Dall[:, s_, r:r + 1], -1.0, m, ALU.add, ALU.mult)
            p.ts("dve", coef[:, 1:2], coef[:, 1:2], 1.0, None, ALU.add)
            p.ts("dve", St[r][:], St[r][:], coef[:, 1:2], None, ALU.mult)
            p.stt("dve", St[r][:], Gt[:, s_, 0:256], m, St[r][:], ALU.mult, ALU.add)
        p.act(Sb[r][:], St[r][:], AF.Identity)
    p.pop_scope()

    if getattr(M, "stop_after", 3) < 2:
        p.pop_scope()
        return
    p.push_scope()
    yT = p.sb("yT", [128, 8, TPC + 30], BF16)
    yTc = p.sb("yTc", [128, 8, CTX + 30], BF16)
    p.memset("dve", yTc[:], 0.0)
    ld = [p.sb("ldy%d" % i, [128, 1024], BF16) for i in range(2)]
    for t in range(NTA):
        a = ld[t % 2]
        p.dma("sp", a[:], V(o16, o16.h[t, :, OY:OY + 1024]))
        if t < NT:
            transpose_into(p, cx, a, yT, t, ident, 8, col0=15 + t * 128)
        else:
            transpose_into(p, cx, a, yTc, t, ident, 8, col0=15 + (t - NT) * 128)
    E = p.sb("E", [128, 1024], BF16)
    p.dma("sp", E[:], V(M.yrcv, M.yrcv.h.bitcast(BF16)))
    selm = p.sb("selm", [128, 30], BF16)
    p.dma("pool", selm[:], M.selm_d[:])
    for c in range(8):
        ps = cx.bank()
        p.mm(ps[:, 0:30], [(E[:, c * 128:(c + 1) * 128], selm[:])])
        p.act(yT.k(("hl", c), (slice(None), c, slice(0, 15))), ps[:, 0:15], AF.Identity)
        p.act(yT.k(("hr", c), (slice(None), c, slice(15 + TPC, 30 + TPC))), ps[:, 15:30], AF.Identity)
    cwl = p.sb("cwl", [32, 1024], F32)
    p.dma("sp", cwl[0:31, :], M.conv_w[:])
    p.dma("sp", cwl[31:32, :], M.conv_b[:])
    cwt = p.sb("cwt", [128, 8, 32], F32)
    for c in range(8):
        ps = cx.bank()
        p.tr(ps[:, 0:32], cwl[:, c * 128:(c + 1) * 128], identf[0:32, 0:32])
        p.act(cwt[:, c, :], ps[:, 0:32], AF.Identity)
    accs = [p.sb("cacc%d" % i, [128, TPC], F32) for i in range(2)]
    accc = [p.sb("caccc%d" % i, [128, CTX], F32) for i in range(2)]
    cvst = [p.sb("cvst%d" % i, [128, 4, 128], F32) for i in range(2)]
    ci = 0
    for c in range(8):
        for (a, ysrc, n, t0) in ((accs[c % 2], yT, TPC, 0), (accc[c % 2], yTc, CTX, NT)):
            for tap in range(CW):
                yv = V(ysrc, ysrc.h[:, c, tap:tap + n])
                if tap == 0:
                    p.ts("dve", a[:], yv, cwt[:, c, 0:1], None, ALU.mult)
                else:
                    p.stt("dve", a[:], yv, cwt[:, c, tap:tap + 1], a[:], ALU.mult, ALU.add)
            p.ts("dve", a[:], a[:], cwt[:, c, 31:32], None, ALU.add)
            for g0 in range(0, n // 128, 4):
                ng = min(4, n // 128 - g0)
                ps = cx.bank()
                for k_ in range(ng):
                    p.tr(ps[:, k_ * 128:(k_ + 1) * 128], a[:, (g0 + k_) * 128:(g0 + k_ + 1) * 128], identf[:])
                st_ = cvst[ci % 2]
                ci += 1
                p.act(V(st_, st_.h[:, 0:ng, :]), V(ps, ps.h[:, 0:ng * 128].rearrange("q (k f) -> q k f", f=128)), AF.Identity)
                p.dma("sp", V(M.cv_s, M.cv_s.h[t0 + g0:t0 + g0 + ng, :, c * 128:(c + 1) * 128].rearrange("t q f -> q t f")),
                      V(st_, st_.h[:, 0:ng, :]))
    p.pop_scope()

    if getattr(M, "stop_after", 3) < 3:
        p.pop_scope()
        return
    p.push_scope()
    qT = p.sb("qT", [128, 8, NTOK], BF16)
    ld = [p.sb("ldq%d" % i, [128, 1024], BF16) for i in range(2)]
    for t in range(NTA):
        a = ld[t % 2]
        p.dma("sp", a[:], V(o16, o16.h[t, :, OQ:OQ + 1024]))
        transpose_into(p, cx, a, qT, t, ident, 8)
    kT = p.sb("kT", [128, L], BF16)
    vE = p.sb("vE", [128, NKT, 129], BF16)
    pTs = [p.sb("pTs%d" % i, [128, 512], BF16) for i in range(6)]
    rcp = p.sb("rcp", [128, 4], F32)
    ost = [p.sb("ost%d" % i, [128, 128], BF16) for i in range(4)]
    SC = HD ** -0.5
    krb = M.krcv.h.bitcast(BF16)
    NH2 = NT // 2
    vrb = [M.vrcv[hf].h.bitcast(BF16) for hf in range(2)]
    state = {"pi": 0, "sb": 0, "tick": 0, "gen": gla_pass_gen(False, True)}
    TICK = 6

    def load_kv(kvh):
        for s_ in range(SEGS):
            p.dma("sp", kT[:, s_ * TPC:(s_ + 1) * TPC], V(M.krcv, krb[s_ * 256 + kvh * 128:s_ * 256 + (kvh + 1) * 128, :]))
            for hf in range(2):
                p.dma("sp", vE[:, s_ * NT + hf * NH2:s_ * NT + (hf + 1) * NH2, :],
                      V(M.vrcv[hf], vrb[hf][s_ * 128:(s_ + 1) * 128, :].rearrange("q (t k e) -> q t k e", t=NH2, k=2)[:, :, kvh, :]))
        p.act(kT[:, Sfull:L], kTl[:, kvh, TPC:NTOK], AF.Identity)
        p.copy("dve", vE[:, SEGS * NT:NKT, :], veb[:, NT:NTA, kvh, :])

    def att_block(blk):
        h, q0, nq, kt0, kt1 = blk
        nsub = nq // 128
        LOOK = 1
        pend = []

        def pv(kt, pt):
            def emit(e):
                ins = None
                for qs in range(nsub):
                    ob = banks[qs // 2]
                    ins = e.matmul(ob.h[:, (qs % 2) * 256:(qs % 2) * 256 + 129], pt.h[:, qs * 128:(qs + 1) * 128],
                                   vE.h[:, kt, :], start=(kt == kt0), stop=(kt == kt1 - 1))
                return ins
            writes = [V(banks[qs // 2], banks[qs // 2].h[:, (qs % 2) * 256:(qs % 2) * 256 + 129], ("o", qs % 2)) for qs in range(nsub)]
            p.op("pe", [pt[:, 0:nq], vE[:, kt, :]], writes, emit)

        for kt in range(kt0, kt1):
            state["tick"] += 1
            if state["tick"] % TICK == 0:
                next(state["gen"], None)
            sbk = banks[2 + state["sb"] % 2]
            state["sb"] += 1
            p.mm(sbk[:, 0:nq], [(kT[:, kt * 128:(kt + 1) * 128], qT[:, h, q0:q0 + nq])])
            pt = pTs[state["pi"] % 6]
            state["pi"] += 1
            p.act(pt[:, 0:nq], sbk[:, 0:nq], AF.Exp, scale=SC)
            pend.append((kt, pt))
            if len(pend) > LOOK:
                pv(*pend.pop(0))
        while pend:
            pv(*pend.pop(0))
        for qs in range(nsub):
            ob = banks[qs // 2]
            base = (qs % 2) * 256
            okey = V(ob, ob.h[:, base:base + 129], ("o", qs % 2))
            p.op("dve", [okey], [rcp[:, qs:qs + 1]],
                 lambda e, ob=ob, base=base, qs=qs: e.reciprocal(rcp.h[:, qs:qs + 1], ob.h[:, base + 128:base + 129]))
            p.ts("dve", ost[qs][:], V(ob, ob.h[:, base:base + 128], ("o", qs % 2)), rcp[:, qs:qs + 1], None, ALU.mult)
            p.dma("sp", V(M.att_s, M.att_s.h[(q0 // 128) + qs, :, h * 128:(h + 1) * 128]), ost[qs][:])

    for kvh in range(2):
        load_kv(kvh)
        for hh in range(4):
            h = kvh * 4 + hh
            for q0 in range(0, TPC, 512):
                att_block((h, q0, min(512, TPC - q0), 0, NKT))
            att_block((h, TPC, CTX, NKT - CTX // 128, NKT))
    for _ in state["gen"]:
        pass
    p.pop_scope()
    p.pop_scope()


def build_fused(NT=16):
    NCT = 2
    NTA = NT + NCT
    TPC = NT * 128
    nc = bass.Bass("TRN2", target_bir_lowering=False)
    es = ExitStack()
    with es:
        p = Prog(nc, es)
        cx = Ctx(p)
        X = lambda n, shp: p.dram(n, shp, F32, "ExternalInput")
        x_d = X("x", [NT, 128, D])
        ctx_d = X("ctxb", [NCT, 128, D])
        cvT = X("cvT", [128, 2, 8])
        cos_d = X("cos", [128, NT, 64])
        sin_d = X("sin", [128, NT, 64])
        ident_d = X("ident", [128, 128])
        gcst_d = X("gcst", [128, 514])
        masks_d = X("masks", [128, 8])
        selm_d = X("selm", [128, 30])
        W = {}
        for n, shp in (("w_ada", [DEPTH, D, 6 * D]), ("b_ada", [DEPTH, 6 * D]), ("w_in", [DEPTH, D, INC]),
                       ("q_norm", [DEPTH, HD]), ("k_norm", [DEPTH, HD]), ("w_att_o", [DEPTH, D, D]),
                       ("gla_w_a2", [DEPTH, 2, RANK, 512]), ("gla_b_a", [DEPTH, 2 * 512]), ("gla_norm", [DEPTH, GDV]),
                       ("w_gla_o", [DEPTH, D, D]), ("conv_w_dw", [DEPTH, CW, D]), ("conv_b_dw", [DEPTH, 1, D]),
                       ("conv_ln_g", [DEPTH, D]), ("conv_ln_b", [DEPTH, D]), ("w_conv_o", [DEPTH, D, D]),
                       ("w_out", [DEPTH, D, D]), ("ln1_g", [DEPTH, D]), ("ln1_b", [DEPTH, D]),
                       ("w_ff_gate", [DEPTH, D, DFF]), ("w_ff_up", [DEPTH, D, DFF]), ("w_ff_down", [DEPTH, DFF, D]),
                       ("ln2_g", [DEPTH, D]), ("ln2_b", [DEPTH, D])):
            W[n] = X(n, shp)
        out = p.dram("out", [NT, 128, D], F32, "ExternalOutput")
        I = lambda n, shp, dt: p.dram(n, shp, dt, "Internal")
        o16 = I("o16", [NTA, 128, C16], BF16)
        o32 = I("o32", [NTA, 128, 1024], F32)
        M = NS()
        M.o16, M.o32 = o16, o32
        M.att_s = I("att_s", [NTA, 128, D], BF16)
        M.gf_s = I("gf_s", [NTA, 128, D], F32)
        M.gb_s = I("gb_s", [NTA, 128, D], F32)
        M.cv_s = I("cv_s", [NTA, 128, D], F32)
        act_s = I("act_s", [NTA, 128, DFF // 128, 128], BF16)
        xa = I("xa", [NTA, 128, D], F32)
        xb = I("xb", [NTA, 128, D], F32)
        M.ksnd = I("ksnd", [256, TPC // 2], F32)
        M.krcv = I("krcv", [1024, TPC // 2], F32)
        M.vsnd = [I("vsnd%d" % i, [128, NT // 2 * 129], F32) for i in range(2)]
        M.vrcv = [I("vrcv%d" % i, [512, NT // 2 * 129], F32) for i in range(2)]
        M.gsnd = I("gsnd", [1024, 256], F32)
        M.grcv = I("grcv", [4096, 256], F32)
        M.dsnd = I("dsnd", [128, 64], F32)
        M.drcv = I("drcv", [512, 64], F32)
        M.ysnd = I("ysnd", [32, 512], F32)
        M.yrcv = I("yrcv", [128, 512], F32)
        M.ident_d, M.gcst_d, M.masks_d, M.selm_d = ident_d, gcst_d, masks_d, selm_d

        def lw(n, l):
            return T(p, W[n].h[l], "%s_l%d" % (n, l))

        for l in range(DEPTH):
            last = l == DEPTH - 1
            if l == 0:
                xs_fn = lambda t: (x_d[t] if t < NT else ctx_d[t - NT])
            else:
                xs_fn = lambda t: xb[t]

            class XS:
                def __getitem__(self, t):
                    return xs_fn(t)
            p.push_scope()
            wa2 = lw("gla_w_a2", l)
            emit_p1(p, cx, NT, NCT, XS(), cvT, lw("w_ada", l), lw("b_ada", l), lw("w_in", l), lw("q_norm", l),
                    lw("k_norm", l), wa2[0], wa2[1], lw("gla_b_a", l), cos_d, sin_d, ident_d, o16, o32, True)
            p.pop_scope()
            M.conv_w = lw("conv_w_dw", l)
            M.conv_b = lw("conv_b_dw", l)
            emit_mix_fused(p, cx, NT, M)
            nct = 0 if last else NCT
            A = NS()
            A.oatt = lambda t: M.att_s[t]
            A.ogf = lambda t: M.gf_s[t]
            A.ogb = lambda t: M.gb_s[t]
            A.sr = lambda t: V(o16, o16.h[t, :, OR:OR + 1024])
            A.cv = lambda t: M.cv_s[t]
            A.xs = xs_fn
            A.xo = lambda t: xa[t]
            A.gt = lambda t, c0, c1: V(o16, o16.h[t, :, OGT + c0:OGT + c1])
            A.g1_mi = 2
            A.ident_d, A.cvT, A.w_ada, A.b_ada = ident_d, cvT, lw("w_ada", l), lw("b_ada", l)
            A.vecs = {n: lw(n, l) for n in ("gla_norm", "conv_ln_g", "conv_ln_b", "ln1_g", "ln1_b")}
            A.wo = {n: lw(n, l) for n in ("w_att_o", "w_gla_o", "w_conv_o", "w_out")}
            p.push_scope()
            emit_posta(p, cx, NT, nct, A)
            p.pop_scope()
            Bn = NS()
            Bn.xs = lambda t: xa[t]
            Bn.xo = (lambda t: out[t]) if last else (lambda t: xb[t])
            Bn.mi0 = 3
            Bn.ident_d, Bn.cvT, Bn.w_ada, Bn.b_ada = ident_d, cvT, lw("w_ada", l), lw("b_ada", l)
            Bn.l2g_d, Bn.l2b_d = lw("ln2_g", l), lw("ln2_b", l)
            Bn.wg_d, Bn.wu_d, Bn.wd_d = lw("w_ff_gate", l), lw("w_ff_up", l), lw("w_ff_down", l)
            Bn.act_s = act_s
            p.push_scope()
            emit_postb2(p, cx, NT, nct, Bn)
            p.pop_scope()
        p.wait_all("sp", [out[:]])
        print("FUSED ops", p.n_ops, "waits", p.n_waits)
    return nc


def kernel_fused(P, S):
    TPC = S * B // NCORES
    NT = TPC // 128
    cos, sin = _rope_tables(S)
    gc = _gla_consts()
    Lm, Um, sel = gc[:, 0:128], gc[:, 128:256], gc[:, 256:258]
    gcst = np.ascontiguousarray(np.concatenate([Lm, Um, Lm.T, Um.T, sel], axis=1))
    ident = np.eye(128, dtype=np.float32)
    in_maps = []
    for core in range(NCORES):
        b, seg = core // 4, core % 4
        masks = np.zeros((128, 8), np.float32)
        for s_ in range(4):
            masks[:, s_] = 1.0 if s_ < seg else 0.0
            masks[:, 4 + s_] = 1.0 if s_ > seg else 0.0
        selm = np.zeros((128, 30), np.float32)
        for j in range(15):
            if seg > 0:
                selm[(seg - 1) * 32 + 15 + j, j] = 1.0
            if seg < 3:
                selm[(seg + 1) * 32 + j, 15 + j] = 1.0
        m = {
            "x": np.ascontiguousarray(P["x"][b, seg * TPC:(seg + 1) * TPC].reshape(NT, 128, D)),
            "ctxb": np.ascontiguousarray(P["ctx"][b].reshape(2, 128, D)),
            "cvT": np.ascontiguousarray(np.stack([P["c"][b], P["c_ctx"]]).reshape(2, 8, 128).transpose(2, 0, 1)),
            "cos": np.ascontiguousarray(cos[seg * TPC:(seg + 1) * TPC].reshape(NT, 128, 64).transpose(1, 0, 2)),
            "sin": np.ascontiguousarray(sin[seg * TPC:(seg + 1) * TPC].reshape(NT, 128, 64).transpose(1, 0, 2)),
            "ident": ident, "gcst": gcst, "masks": masks, "selm": selm,
        }
        for n in ("w_ada", "b_ada", "w_in", "q_norm", "k_norm", "w_att_o", "gla_w_a2", "gla_norm", "w_gla_o",
                  "conv_ln_g", "conv_ln_b", "w_conv_o", "w_out", "ln1_g", "ln1_b", "w_ff_gate", "w_ff_up",
                  "w_ff_down", "ln2_g", "ln2_b"):
            m[n] = P[n]
        m["gla_b_a"] = np.ascontiguousarray(P["gla_b_a"].reshape(DEPTH, 1024))
        m["conv_w_dw"] = np.ascontiguousarray(P["conv_w_dw"].reshape(DEPTH, CW, D))
        m["conv_b_dw"] = np.ascontiguousarray(P["conv_b_dw"].reshape(DEPTH, 1, D))
        in_maps.append(m)
    res = _run(("fused", NT), lambda: build_fused(NT), in_maps)
    out = np.stack([r["out"] for r in res])
    return np.ascontiguousarray(out.reshape(B, S, D))


def kernel(**inputs):
    P = {k: np.asarray(v) for k, v in inputs.items()}
    S = P["x"].shape[1]
    return kernel_fused(P, S).astype(np.float32)
```

```python
import numpy as np
from contextlib import ExitStack
import concourse.bass as bass
import concourse.mybir as mybir
from concourse.bass_utils import run_bass_kernel_spmd

F32 = mybir.dt.float32
BF16 = mybir.dt.bfloat16
AF = mybir.ActivationFunctionType
ALU = mybir.AluOpType
AX = mybir.AxisListType


class V:
    def __init__(self, t, ap, key=None):
        self.t = t
        self.ap = ap
        self.key = key


class T:
    def __init__(self, prog, handle, name):
        self.p = prog
        self.h = handle
        self.name = name
        self.state = {}

    def __getitem__(self, idx):
        return V(self, self.h[idx], None)

    def k(self, key, idx=None):
        if idx is None:
            return V(self, self.h[:], key)
        return V(self, self.h[idx], key)

    def _keys(self, key):
        if key is None:
            return list(self.state.keys())
        ks = [key]
        if None in self.state:
            ks.append(None)
        return [k for k in ks if k in self.state]

    def deps_read(self, key):
        out = []
        for k in self._keys(key):
            w = self.state[k][0]
            if w is not None:
                out.append(w)
        return out

    def deps_write(self, key):
        out = []
        for k in self._keys(key):
            w, rs = self.state[k]
            if w is not None:
                out.append(w)
            out.extend(rs)
        return out

    def add_reader(self, key, tok):
        st = self.state.setdefault(key, [None, []])
        st[1].append(tok)
        if len(st[1]) > 64:
            best = {}
            for (s, v) in st[1]:
                best[s] = max(best.get(s, 0), v)
            st[1] = list(best.items())

    def set_writer(self, key, tok):
        if key is None:
            self.state = {None: [tok, []]}
        else:
            self.state[key] = [tok, []]


class Prog:
    ENG = ("pe", "dve", "act", "pool", "sp")

    def __init__(self, nc, es, n_dma_sems=16):
        self.nc = nc
        self.es = es
        self.engs = {"pe": nc.tensor, "dve": nc.vector, "act": nc.scalar,
                     "pool": nc.gpsimd, "sp": nc.sync}
        self.sems = []
        self.esem = {}
        self.cnt = {}
        for e in self.ENG:
            self.esem[e] = self._new_sem("prog_" + e)
            self.cnt[e] = 0
        self.dsem = {}
        self.dcnt = {}
        self.dnext = {}
        for q in ("sp", "pool", "act"):
            self.dsem[q] = [self._new_sem("dma_%s_%d" % (q, i)) for i in range(n_dma_sems)]
            self.dcnt[q] = [0] * n_dma_sems
            self.dnext[q] = 0
        self.waited = {e: {} for e in self.ENG}
        self.n_ops = 0
        self.n_waits = 0
        self.uid = 0

    def _new_sem(self, name):
        h = self.es.enter_context(self.nc.semaphore(name))
        self.sems.append(h)
        return len(self.sems) - 1

    def push_scope(self):
        if not hasattr(self, "scopes"):
            self.scopes = []
            self.scope_id = 0
        st = ExitStack()
        st.__enter__()
        self.scopes.append(st)
        self.scope_id += 1

    def pop_scope(self):
        self.barrier()
        st = self.scopes.pop()
        st.__exit__(None, None, None)

    def sb(self, name, shape, dtype):
        scopes = getattr(self, "scopes", [])
        es = scopes[-1] if scopes else self.es
        sid = getattr(self, "scope_id", 0)
        h = es.enter_context(self.nc.sbuf_tensor("s%d_%s" % (sid, name), list(shape), dtype))
        return T(self, h, name)

    def ps(self, name, shape, dtype):
        h = self.es.enter_context(self.nc.psum_tensor("p_" + name, list(shape), dtype))
        return T(self, h, name)

    def dram(self, name, shape, dtype, kind):
        h = self.nc.dram_tensor(name, list(shape), dtype, kind=kind)
        return T(self, h.ap(), name)

    def _wait(self, eng, toks):
        best = {}
        for (s, v) in toks:
            if v <= 0:
                continue
            if eng == "pe" and s == self.esem["pe"]:
                continue
            if v > best.get(s, 0):
                best[s] = v
        w = self.waited[eng]
        e = self.engs[eng]
        for s, v in best.items():
            if w.get(s, 0) >= v:
                continue
            e.wait_ge(self.sems[s], v)
            w[s] = v
            self.n_waits += 1

    def op(self, eng, reads, writes, emit):
        toks = []
        for v in reads:
            toks += v.t.deps_read(v.key)
        for v in writes:
            toks += v.t.deps_write(v.key)
        self._wait(eng, toks)
        ins = emit(self.engs[eng])
        self.cnt[eng] += 1
        ins.then_inc(self.sems[self.esem[eng]], 1)
        tok = (self.esem[eng], self.cnt[eng])
        for v in reads:
            v.t.add_reader(v.key, tok)
        for v in writes:
            v.t.set_writer(v.key, tok)
        self.n_ops += 1
        return tok

    def dma(self, q, out, in_, **kw):
        toks = in_.t.deps_read(in_.key) + out.t.deps_write(out.key)
        i = self.dnext[q]
        self.dnext[q] = (i + 1) % len(self.dsem[q])
        s = self.dsem[q][i]
        toks.append((s, self.dcnt[q][i]))
        self._wait(q, toks)
        ins = self.engs[q].dma_start(out=out.ap, in_=in_.ap, **kw)
        self.dcnt[q][i] += 16
        ins.then_inc(self.sems[s], 16)
        tok = (s, self.dcnt[q][i])
        in_.t.add_reader(in_.key, tok)
        out.t.set_writer(out.key, tok)
        self.n_ops += 1
        return tok

    def collective(self, kind, ins, outs, groups):
        q = "pool"
        if not hasattr(self, "ccsem"):
            self.ccsem = self._new_sem("cc_sem")
            self.cccnt = 0
        toks = []
        for v in ins:
            toks += v.t.deps_read(v.key)
        for v in outs:
            toks += v.t.deps_write(v.key)
        toks.append((self.ccsem, self.cccnt))
        self._wait(q, toks)
        ins_ = self.engs[q].collective_compute(kind, ALU.bypass, groups, [v.ap for v in ins], [v.ap for v in outs])
        self.cccnt += 1
        ins_.then_inc(self.sems[self.ccsem], 1)
        tok = (self.ccsem, self.cccnt)
        for v in ins:
            v.t.add_reader(v.key, tok)
        for v in outs:
            v.t.set_writer(v.key, tok)
        self.n_ops += 1
        return tok

    def barrier(self):
        for e in self.ENG:
            toks = [(self.esem[f], self.cnt[f]) for f in self.ENG if f != e]
            for q in self.dsem:
                toks += [(s, c) for s, c in zip(self.dsem[q], self.dcnt[q])]
            if hasattr(self, "ccsem"):
                toks.append((self.ccsem, self.cccnt))
            self._wait(e, toks)

    def wait_all(self, eng, views):
        toks = []
        for v in views:
            toks += v.t.deps_read(v.key)
        self._wait(eng, toks)

    def mm(self, out, pairs, reads_extra=()):
        reads = []
        for (l, r) in pairs:
            reads += [l, r]
        n = len(pairs)

        def emit(e):
            ins = None
            for i, (l, r) in enumerate(pairs):
                ins = e.matmul(out.ap, l.ap, r.ap, start=(i == 0), stop=(i == n - 1))
            return ins
        return self.op("pe", reads, [out], emit)

    def mm1(self, out, l, r, start, stop):
        return self.op("pe", [l, r], [out],
                       lambda e: e.matmul(out.ap, l.ap, r.ap, start=start, stop=stop))

    def tr(self, out, in_, ident):
        return self.op("pe", [in_, ident], [out],
                       lambda e: e.transpose(out.ap, in_.ap, ident.ap))

    def act(self, out, in_, func, bias=None, scale=None, accum=None, eng="act"):
        reads = [in_]
        kw = {}
        if bias is not None:
            if isinstance(bias, V):
                reads.append(bias)
                kw["bias"] = bias.ap
            else:
                kw["bias"] = bias
        if scale is not None:
            if isinstance(scale, V):
                reads.append(scale)
                kw["scale"] = scale.ap
            else:
                kw["scale"] = scale
        writes = [out]
        if accum is not None:
            writes.append(accum)
            kw["accum_out"] = accum.ap
        return self.op(eng, reads, writes,
                       lambda e: e.activation(out.ap, in_.ap, func, **kw))

    def tt(self, eng, out, a, b, op):
        return self.op(eng, [a, b], [out],
                       lambda e: e.tensor_tensor(out.ap, a.ap, b.ap, op))

    def ts(self, eng, out, a, s1, s2, op0, op1=None, accum=None):
        reads = [a]
        s1a = s1.ap if isinstance(s1, V) else s1
        s2a = s2.ap if isinstance(s2, V) else s2
        if isinstance(s1, V):
            reads.append(s1)
        if isinstance(s2, V):
            reads.append(s2)
        writes = [out]
        kw = {}
        if accum is not None:
            writes.append(accum)
            kw["accum_out"] = accum.ap
        if op1 is None:
            return self.op(eng, reads, writes,
                           lambda e: e.tensor_scalar(out.ap, a.ap, s1a, None, op0, **kw))
        return self.op(eng, reads, writes,
                       lambda e: e.tensor_scalar(out.ap, a.ap, s1a, s2a, op0, op1, **kw))

    def stt(self, eng, out, a, s, b, op0, op1):
        reads = [a, b]
        sa = s.ap if isinstance(s, V) else s
        if isinstance(s, V):
            reads.append(s)
        return self.op(eng, reads, [out],
                       lambda e: e.scalar_tensor_tensor(out.ap, a.ap, sa, b.ap, op0, op1))

    def copy(self, eng, out, in_):
        if eng == "act":
            return self.op(eng, [in_], [out], lambda e: e.copy(out.ap, in_.ap))
        return self.op(eng, [in_], [out], lambda e: e.tensor_copy(out.ap, in_.ap))

    def memset(self, eng, out, val):
        return self.op(eng, [], [out], lambda e: e.memset(out.ap, val))


D = 1024
B = 2
GRID_W = 64
CTX = 256
DEPTH = 2
HD = 128
NH = 8
NKV = 2
GH = 4
GDK = 128
GDV = 256
RANK = 16
TAU = 16.0
CHUNK = 64
CW = 31
DFF = 2816
ALPHA = (2.0 * DEPTH) ** 0.25
EPS = 1e-6
NCORES = 8
INC = 9760
OK_, OV_, OKG, OVG, OQ, OQG, OR, OY, OGT, C16 = 0, 256, 512, 1024, 2048, 3072, 3584, 4608, 5632, 8704


class Ctx:
    def __init__(self, p):
        self.p = p
        self.banks = [p.ps("bank%d" % i, [128, 512], F32) for i in range(8)]
        self.bi = 0
        self.uid = 0

    def bank(self):
        b = self.banks[self.bi % 8]
        self.bi += 1
        return b

    def name(self, s):
        self.uid += 1
        return "%s_%d" % (s, self.uid)


def bcast_mid(ap, n):
    a = ap.ap
    return bass.AP(ap.tensor, ap.offset, [list(a[0]), [0, n]] + [list(x) for x in a[1:]])


def load_const_eps(p, cx):
    eps = p.sb("eps_c", [128, 1], F32)
    p.memset("dve", eps[:], EPS)
    one = p.sb("one_c", [128, 1], F32)
    p.memset("dve", one[:], 1.0)
    cx.eps = eps
    cx.one = one


def ln_stats(p, cx, x, width, mean_rstd):
    nchunk = width // 512
    bn = cx.bn
    xr = x.ap.rearrange("p (c f) -> p c f", f=512)
    for c in range(nchunk):
        p.op("dve", [x], [bn.k(c, (slice(None), c, slice(None)))],
             lambda e, c=c: e.bn_stats(bn.h[:, c, :], xr[:, c, :]))
    p.op("dve", [bn[:, 0:nchunk, :]], [mean_rstd],
         lambda e: e.bn_aggr(mean_rstd.ap, bn.h[:, 0:nchunk, :]))
    r = V(mean_rstd.t, mean_rstd.ap[:, 1:2], mean_rstd.key)
    p.act(r, r, AF.Sqrt, bias=cx.eps[:, 0:1])
    p.op("dve", [r], [r], lambda e: e.reciprocal(r.ap, r.ap))


def mod_tiles(p, cx, scb, w_ada, b_ada, specs):
    wv = w_ada.h.rearrange("(kc p) c -> p kc c", p=128)
    mis = []
    for sp_ in specs:
        if sp_[1] not in mis:
            mis.append(sp_[1])
    for mi in mis:
        for half in range(2):
            c0 = mi * 1024 + half * 512
            wt = cx.wts[cx.wi % 2]
            cx.wi += 1
            p.dma("pool", wt[:], V(w_ada, wv[:, :, c0:c0 + 512]))
            bb = cx.bbc
            p.dma("sp", bb[:, 0:512], V(b_ada, b_ada.h[c0:c0 + 512].partition_broadcast(128)))
            for (vi, mi_, plus1, out) in specs:
                if mi_ != mi:
                    continue
                ps = cx.bank()
                p.mm(ps[:], [(scb[:, vi, kc, :], wt[:, kc, :]) for kc in range(8)])
                o = out[:, half * 512:(half + 1) * 512]
                if plus1:
                    p.stt("dve", o, ps[:], 1.0, bb[:, 0:512], ALU.add, ALU.add)
                else:
                    p.tt("dve", o, ps[:], bb[:, 0:512], ALU.add)


def make_scb(p, cx, cvT, nvec):
    cv = p.sb("cv", [128, nvec * 8], F32)
    p.dma("sp", cv[:], V(cvT, cvT.h.rearrange("p v k -> p (v k)")))
    p.act(cv[:], cv[:], AF.Silu)
    ones = p.sb("ones_bf", [128, 128], BF16)
    p.memset("dve", ones[:], 1.0)
    scb = p.sb("scb", [128, nvec, 8, 128], BF16)
    for v in range(nvec):
        for k in range(8):
            j = v * 8 + k
            p.ts("dve", scb[:, v, k, :], ones[:], cv[:, j:j + 1], None, ALU.mult)
    return scb


def ln_mod_transpose(p, cx, xsrc, t, scp, sh, hT, ident):
    xt = cx.xt[t % 2]
    p.dma("sp", xt[:], xsrc)
    mr = cx.mr[t % 2]
    ln_stats(p, cx, xt[:], 1024, mr[:])
    xn = cx.xn
    p.ts("dve", xn[:], xt[:], mr[:, 0:1], mr[:, 1:2], ALU.subtract, ALU.mult)
    p.tt("pool", xn[:], xn[:], scp[:], ALU.mult)
    hb = cx.hb[t % 2]
    p.tt("dve", hb[:], xn[:], sh[:], ALU.add)
    transpose_into(p, cx, hb, hT, t, ident, 8)


def transpose_into(p, cx, src, dstT, t, ident, nk, eng="act", col0=None, src0=0):
    for g0 in range(0, nk, 8):
        n = min(8, nk - g0)
        ps = cx.bank()
        pv = ps.h[:].bitcast(BF16)
        for k in range(n):
            p.tr(V(ps, pv[:, k * 128:(k + 1) * 128]), src[:, src0 + (g0 + k) * 128:src0 + (g0 + k + 1) * 128], ident[:])
        inv = V(ps, pv[:, 0:n * 128].rearrange("p (k f) -> p k f", f=128))
        c0_ = t * 128 if col0 is None else col0
        outv = dstT.k(("t", t), (slice(None), slice(g0, g0 + n), slice(c0_, c0_ + 128)))
        if eng == "act":
            p.act(outv, inv, AF.Identity)
        else:
            p.copy(eng, outv, inv)


def emit_p1(p, cx, NT, NCT, xs, cvT, w_ada, b_ada, w_in, qn_d, kn_d, wa2_0, wa2_1, ba, cos_d, sin_d,
            ident_d, o16, o32, layer_has_rope=True):
    NTA = NT + NCT
    load_const_eps(p, cx)
    cx.bn = p.sb("bn", [128, 2, 6], F32)
    cx.wts = [p.sb("wt%d" % i, [128, 8, 512], BF16) for i in range(2)]
    cx.wi = 0
    cx.bbc = p.sb("bbc", [128, 1024], F32)
    cx.xt = [p.sb("xt%d" % i, [128, D], F32) for i in range(2)]
    cx.mr = [p.sb("mr%d" % i, [128, 2], F32) for i in range(2)]
    cx.xn = p.sb("xn", [128, D], F32)
    cx.hb = [p.sb("hb%d" % i, [128, D], BF16) for i in range(2)]
    ident = p.sb("ident_sb", [128, 128], BF16)
    p.dma("pool", ident[:], ident_d[:])
    cos = p.sb("cos_sb", [128, NT, 64], F32)
    sin = p.sb("sin_sb", [128, NT, 64], F32)
    p.dma("sp", cos[:], cos_d[:])
    p.dma("sp", sin[:], sin_d[:])
    gq = p.sb("gq", [128, 128], F32)
    gk = p.sb("gk", [128, 128], F32)
    p.dma("sp", gq[:], V(qn_d, qn_d.h.partition_broadcast(128)))
    p.dma("sp", gk[:], V(kn_d, kn_d.h.partition_broadcast(128)))
    babc = p.sb("babc", [128, 1024], F32)
    p.dma("sp", babc[:], V(ba, ba.h.partition_broadcast(128)))
    w2bd = p.sb("w2bd", [32, 1024], BF16)
    p.memset("dve", w2bd[:], 0.0)
    p.dma("pool", w2bd[0:16, 0:512], wa2_0)
    p.dma("pool", w2bd[16:32, 512:1024], wa2_1)

    scb = make_scb(p, cx, cvT, 2)
    mods = {}
    for nm in ("shL", "scL", "shC", "scC"):
        mods[nm] = p.sb(nm, [128, D], F32)
    mod_tiles(p, cx, scb, w_ada, b_ada,
              [(0, 0, False, mods["shL"]), (0, 1, True, mods["scL"]),
               (1, 0, False, mods["shC"]), (1, 1, True, mods["scC"])])

    hT = p.sb("hT", [128, 8, NTA * 128], BF16)
    for t in range(NTA):
        lat = t < NT
        ln_mod_transpose(p, cx, xs[t], t, mods["scL" if lat else "scC"],
                         mods["shL" if lat else "shC"], hT, ident)

    wv = w_in.h.rearrange("(kc p) c -> p kc c", p=128)
    stg = [p.sb("stg%d" % i, [128, 512], BF16) for i in range(3)]
    stg32 = [p.sb("stgf%d" % i, [128, 512], F32) for i in range(2)]
    sq = p.sb("sq", [128, 512], F32)
    ss = p.sb("ss", [128, 4], F32)
    qn = p.sb("qn", [128, 512], F32)
    rt = [p.sb("rt%d" % i, [128, 4, 64], F32) for i in range(4)]
    si = [0]

    def load_w(col_ranges):
        wt = cx.wts[cx.wi % 2]
        cx.wi += 1
        o = 0
        for (c0, wd) in col_ranges:
            p.dma("pool", wt[:, :, o:o + wd], V(w_in, wv[:, :, c0:c0 + wd]))
            o += wd
        return wt, o

    def proj(t, wt, width):
        ps = cx.bank()
        p.mm(ps[:, 0:width], [(hT.k(("t", t), (slice(None), kc, slice(t * 128, (t + 1) * 128))),
                               wt[:, kc, 0:width]) for kc in range(8)])
        return ps

    def out16(t, st, col, width):
        p.dma("sp", V(o16, o16.h[t, :, col:col + width]), st[:, 0:width])

    def next_stg():
        s = stg[si[0] % 3]
        si[0] += 1
        return s

    def epi_simple(func, scale=None):
        def f(t, ps, width, col):
            st = next_stg()
            p.act(st[:, 0:width], ps[:, 0:width], func, scale=scale)
            out16(t, st, col, width)
        return f

    def epi_rmsrope(H, gain):
        def f(t, ps, width, col):
            lat = t < NT
            p.act(sq[:, 0:width], ps[:, 0:width], AF.Square)
            p.op("dve", [sq[:, 0:width]], [ss[:, 0:H]],
                 lambda e: e.reduce_sum(ss.h[:, 0:H], sq.h[:, 0:width].rearrange("p (h d) -> p h d", d=128), AX.X))
            p.act(ss[:, 0:H], ss[:, 0:H], AF.Sqrt, bias=cx.eps[:, 0:1], scale=1.0 / HD)
            p.op("dve", [ss[:, 0:H]], [ss[:, 0:H]], lambda e: e.reciprocal(ss.h[:, 0:H], ss.h[:, 0:H]))
            for h in range(H):
                p.stt("dve", qn[:, h * 128:(h + 1) * 128], ps[:, h * 128:(h + 1) * 128], ss[:, h:h + 1],
                      gain[:], ALU.mult, ALU.mult)
            st = next_stg()
            if lat and layer_has_rope:
                q4 = qn.h[:, 0:width].rearrange("p (h i two) -> p h i two", two=2, i=64)
                x1 = V(qn, q4[:, :, :, 0])
                x2 = V(qn, q4[:, :, :, 1])
                cb = V(cos, bcast_mid(cos.h[:, t, :], H))
                sb_ = V(sin, bcast_mid(sin.h[:, t, :], H))
                o4 = st.h[:, 0:width].rearrange("p (h i two) -> p h i two", two=2, i=64)
                a_, b_, c_, d_ = [V(r, r.h[:, 0:H, :]) for r in rt]
                p.tt("dve", a_, x1, cb, ALU.mult)
                p.tt("pool", b_, x2, sb_, ALU.mult)
                p.tt("dve", c_, x1, sb_, ALU.mult)
                p.tt("pool", d_, x2, cb, ALU.mult)
                p.tt("dve", V(st, o4[:, :, :, 0]), a_, b_, ALU.subtract)
                p.tt("pool", V(st, o4[:, :, :, 1]), c_, d_, ALU.add)
            else:
                p.copy("dve", st[:, 0:width], qn[:, 0:width])
            out16(t, st, col, width)
        return f

    def epi_glu(t, ps, width, col):
        sg = stg32[si[0] % 2]
        p.act(sg[:, 0:256], ps[:, 256:512], AF.Sigmoid)
        st = next_stg()
        p.tt("dve", st[:, 0:256], ps[:, 0:256], sg[:, 0:256], ALU.mult)
        out16(t, st, col, 256)

    groups = []
    groups.append(([(0, 256)], OK_, epi_rmsrope(2, gk)))
    groups.append(([(256, 256)], OV_, epi_simple(AF.Identity)))
    groups.append(([(512, 512)], OKG, epi_simple(AF.Identity)))
    groups.append(([(1024, 512)], OVG, epi_simple(AF.Identity)))
    groups.append(([(1536, 512)], OVG + 512, epi_simple(AF.Identity)))
    groups.append(([(2080, 512)], OQ, epi_rmsrope(4, gq)))
    groups.append(([(2592, 512)], OQ + 512, epi_rmsrope(4, gq)))
    groups.append(([(3104, 512)], OQG, epi_simple(AF.Identity, scale=GDK ** -0.5)))
    groups.append(([(3616, 512)], OR, epi_simple(AF.Silu)))
    groups.append(([(4128, 512)], OR + 512, epi_simple(AF.Silu)))
    for j in range(4):
        groups.append(([(4640 + 256 * j, 256), (5664 + 256 * j, 256)], OY + 256 * j, epi_glu))
    for j in range(6):
        groups.append(([(6688 + 512 * j, 512)], OGT + 512 * j, epi_simple(AF.Sigmoid)))

    for (cr, col, epi) in groups:
        wt, width = load_w(cr)
        for t in range(NTA):
            ps = proj(t, wt, width)
            epi(t, ps, width, col)

    wt, _ = load_w([(2048, 32)])
    glrT = p.sb("glrT", [32, NTA * 128], BF16)
    for t0 in range(0, NTA * 128, 512):
        n = min(512, NTA * 128 - t0)
        ps = cx.bank()
        t_lo, t_hi = t0 // 128, (t0 + n) // 128
        reads = []
        p.mm(ps[0:32, 0:n], [(wt[:, kc, 0:32], hT[:, kc, t0:t0 + n]) for kc in range(8)])
        p.act(glrT[:, t0:t0 + n], ps[0:32, 0:n], AF.Identity)
    for t in range(NTA):
        for dr in range(2):
            ps = cx.bank()
            p.mm(ps[:], [(glrT[:, t * 128:(t + 1) * 128], w2bd[:, dr * 512:(dr + 1) * 512])])
            sg = stg32[dr]
            p.tt("dve", sg[:], ps[:], babc[:, dr * 512:(dr + 1) * 512], ALU.add)
            p.act(sg[:], sg[:], AF.Exp, scale=-1.0)
            p.act(sg[:], sg[:], AF.Ln, bias=cx.one[:, 0:1])
            p.ts("dve", sg[:], sg[:], -1.0 / TAU, None, ALU.mult)
            p.dma("sp", V(o32, o32.h[t, :, dr * 512:(dr + 1) * 512]), sg[:])


def build_p1(NT, NCT, layer_has_rope=True):
    NTA = NT + NCT
    nc = bass.Bass("TRN2", target_bir_lowering=False)
    es = ExitStack()
    with es:
        p = Prog(nc, es)
        cx = Ctx(p)
        xs = p.dram("xs", [NTA, 128, D], F32, "ExternalInput")
        cvT = p.dram("cvT", [128, 2, 8], F32, "ExternalInput")
        w_ada = p.dram("w_ada", [D, 2 * D], F32, "ExternalInput")
        b_ada = p.dram("b_ada", [2 * D], F32, "ExternalInput")
        w_in = p.dram("w_in", [D, INC], F32, "ExternalInput")
        qn_d = p.dram("q_norm", [HD], F32, "ExternalInput")
        kn_d = p.dram("k_norm", [HD], F32, "ExternalInput")
        wa2 = p.dram("w_a2", [2, RANK, 512], F32, "ExternalInput")
        ba = p.dram("b_a", [2 * 512], F32, "ExternalInput")
        cos_d = p.dram("cos", [128, NT, 64], F32, "ExternalInput")
        sin_d = p.dram("sin", [128, NT, 64], F32, "ExternalInput")
        ident_d = p.dram("ident", [128, 128], F32, "ExternalInput")
        o16 = p.dram("o16", [NTA, 128, C16], BF16, "ExternalOutput")
        o32 = p.dram("o32", [NTA, 128, 1024], F32, "ExternalOutput")
        emit_p1(p, cx, NT, NCT, xs, cvT, w_ada, b_ada, w_in, qn_d, kn_d, wa2[0], wa2[1], ba, cos_d, sin_d,
                ident_d, o16, o32, layer_has_rope)
        p.wait_all("sp", [o16[:], o32[:]])
        print("P1 ops", p.n_ops, "waits", p.n_waits)
    return nc


_CACHE = {}


def _rope_tables(S):
    t = np.arange(S)
    row = (t // GRID_W).astype(np.float32)
    col = (t % GRID_W).astype(np.float32)
    half = HD // 2
    inv = (np.float32(10000.0) ** (-np.arange(0, half, 2, dtype=np.float32) / np.float32(half))).astype(np.float32)
    ang = np.concatenate([row[:, None] * inv, col[:, None] * inv], axis=-1).astype(np.float32)
    return np.cos(ang).astype(np.float32), np.sin(ang).astype(np.float32)


def _run(key, builder, in_maps):
    if key not in _CACHE:
        _CACHE[key] = builder()
    nc = _CACHE[key]
    res = run_bass_kernel_spmd(nc, in_maps, core_ids=list(range(NCORES)))
    return res.results


def host_p1(l, x, xc, c, c_ctx, P, S):
    TPC = S * B // NCORES
    NT = TPC // 128
    SEGS = NCORES // B
    cos, sin = _rope_tables(S)
    ctx_tiles = xc.reshape(B * CTX // 128, 128, D)
    in_maps = []
    for core in range(NCORES):
        b, seg = core // SEGS, core % SEGS
        xs = np.concatenate([x[b, seg * TPC:(seg + 1) * TPC].reshape(NT, 128, D),
                             ctx_tiles[core % 4][None]], axis=0)
        cv = np.stack([c[b], c_ctx]).reshape(2, 8, 128).transpose(2, 0, 1)
        cs = cos[seg * TPC:(seg + 1) * TPC].reshape(NT, 128, 64).transpose(1, 0, 2)
        sn = sin[seg * TPC:(seg + 1) * TPC].reshape(NT, 128, 64).transpose(1, 0, 2)
        in_maps.append({
            "xs": np.ascontiguousarray(xs), "cvT": np.ascontiguousarray(cv),
            "w_ada": np.ascontiguousarray(P["w_ada"][l][:, 0:2 * D]), "b_ada": np.ascontiguousarray(P["b_ada"][l][0:2 * D]), "w_in": P["w_in"][l],
            "q_norm": P["q_norm"][l], "k_norm": P["k_norm"][l],
            "w_a2": P["gla_w_a2"][l], "b_a": np.ascontiguousarray(P["gla_b_a"][l].reshape(-1)),
            "cos": np.ascontiguousarray(cs), "sin": np.ascontiguousarray(sn),
            "ident": np.eye(128, dtype=np.float32),
        })
    res = _run(("p1", NT), lambda: build_p1(NT, 1), in_maps)
    o16 = np.stack([r["o16"] for r in res])
    o32 = np.stack([r["o32"] for r in res])
    return o16, o32


def build_mix(S, with_ctx_q):
    L = S + CTX
    NKT = L // 128
    NQ = L if with_ctx_q else S
    nc = bass.Bass("TRN2", target_bir_lowering=False)
    es = ExitStack()
    with es:
        p = Prog(nc, es)
        banks = [p.ps("bank%d" % i, [128, 512], F32) for i in range(8)]
        qT_d = p.dram("qT", [2, 128, L], BF16, "ExternalInput")
        kT_d = p.dram("kT", [128, L], BF16, "ExternalInput")
        vE_d = p.dram("vE", [128, NKT, 129], BF16, "ExternalInput")
        oatt = p.dram("oatt", [2, NKT, 128, 128], BF16, "ExternalOutput")
        gq_d = p.dram("gqT", [2, 128, L], BF16, "ExternalInput")
        gkT_d = p.dram("gkT", [2, 128, L], BF16, "ExternalInput")
        gk_d = p.dram("gk", [2, NKT, 128, 128], BF16, "ExternalInput")
        gv_d = p.dram("gv", [2, NKT, 128, 256], BF16, "ExternalInput")
        gg_d = p.dram("gg", [2, NKT, 128, 128], F32, "ExternalInput")
        cst_d = p.dram("cst", [128, 128 * 2 + 2], F32, "ExternalInput")
        ogla = p.dram("ogla", [2, NKT, 128, 256], F32, "ExternalOutput")
        cy_d = p.dram("cy", [128, B, S + 30], BF16, "ExternalInput")
        cyc_d = p.dram("cyc", [128, B, CTX + 30], BF16, "ExternalInput")
        cw_d = p.dram("cw", [128, CW + 1], F32, "ExternalInput")
        oconv = p.dram("oconv", [128, B, S], F32, "ExternalOutput")
        oconvc = p.dram("oconvc", [128, B, CTX], F32, "ExternalOutput")

        cw = p.sb("cw", [128, CW + 1], F32)
        p.dma("sp", cw[:], cw_d[:])
        cy = p.sb("cy", [128, B, S + 30], BF16)
        cyc = p.sb("cyc", [128, B, CTX + 30], BF16)
        p.dma("sp", cy[:], cy_d[:])
        p.dma("sp", cyc[:], cyc_d[:])
        acc = p.sb("cacc", [128, B, S], F32)
        accc = p.sb("caccc", [128, B, CTX], F32)

        def conv_emit(tap):
            for b in range(B):
                eng = "dve"
                for (a, y, n, nm) in ((acc, cy, S, "l"), (accc, cyc, CTX, "c")):
                    av = a.k((nm, b), (slice(None), b, slice(None)))
                    yv = y[:, b, tap:tap + n]
                    if tap == 0:
                        p.ts(eng, av, yv, cw[:, 0:1], None, ALU.mult)
                    else:
                        p.stt(eng, av, yv, cw[:, tap:tap + 1], av, ALU.mult, ALU.add)
                    if tap == CW - 1:
                        p.ts(eng, av, av, cw[:, CW:CW + 1], None, ALU.add)
                        od = oconv if nm == "l" else oconvc
                        p.dma("sp", V(od, od.h[:, b, :]), av)

        cst = p.sb("cst", [128, 258], F32)
        p.dma("sp", cst[:], cst_d[:])
        Lm = cst[:, 0:128]
        Um = cst[:, 128:256]
        sel = cst[:, 256:258]
        Lmb = p.sb("Lmb", [128, 128], F32)
        p.copy("dve", Lmb[:], Lm)
        St = [p.sb("gS%d" % i, [128, 256], F32) for i in range(2)]
        Sb = [p.sb("gSb%d" % i, [128, 256], BF16) for i in range(2)]
        for i in range(2):
            p.memset("dve", St[i][:], 0.0)
            p.memset("dve", Sb[i][:], 0.0)
        gl = {}
        for nm, shp, dt in (("qT", [128, 128], BF16), ("kT", [128, 128], BF16), ("k", [128, 128], BF16),
                            ("v", [128, 256], BF16), ("g", [128, 128], F32), ("EbT", [128, 128], F32),
                            ("EnbT", [128, 128], F32), ("Erem", [128, 128], F32), ("Eend", [128, 2], F32),
                            ("qeT", [128, 128], BF16), ("keT", [128, 128], BF16), ("kend", [128, 128], BF16),
                            ("ATm", [128, 128], BF16), ("osb", [64, 2, 256], F32)):
            gl[nm] = [[p.sb("g_%s_%d_%d" % (nm, s, j), shp, dt) for j in range(2)] for s in range(2)]

        def gla_tile(s, t):
            j = t % 2
            T_ = {k: v[s][j] for k, v in gl.items()}
            pb = banks[4 + 2 * s:6 + 2 * s]
            p.dma("sp", T_["qT"][:], V(gq_d, gq_d.h[s, :, t * 128:(t + 1) * 128]))
            p.dma("sp", T_["kT"][:], V(gkT_d, gkT_d.h[s, :, t * 128:(t + 1) * 128]))
            p.dma("sp", T_["k"][:], V(gk_d, gk_d.h[s, t]))
            p.dma("sp", T_["v"][:], V(gv_d, gv_d.h[s, t]))
            p.dma("sp", T_["g"][:], V(gg_d, gg_d.h[s, t]))
            g = T_["g"]
            b0 = pb[0]
            p.mm(b0[:, 0:128], [(g[:], Lm)])
            p.mm(b0[:, 128:256], [(Um, g[:])])
            p.mm(b0[:, 256:258], [(g[:], sel)])
            p.act(T_["EbT"][:], b0[:, 0:128], AF.Exp)
            p.act(T_["EnbT"][:], b0[:, 0:128], AF.Exp, scale=-1.0)
            p.act(T_["Erem"][:], b0[:, 128:256], AF.Exp)
            p.act(T_["Eend"][:], b0[:, 256:258], AF.Exp)
            p.tt("dve", T_["qeT"][:], T_["qT"][:], T_["EbT"][:], ALU.mult)
            p.tt("dve", T_["keT"][:], T_["kT"][:], T_["EnbT"][:], ALU.mult)
            p.tt("dve", T_["kend"][:], T_["k"][:], T_["Erem"][:], ALU.mult)
            p.mm(b0[:, 384:512], [(T_["keT"][:], T_["qeT"][:])])
            p.tt("dve", T_["ATm"][:], b0[:, 384:512], Lmb[:], ALU.mult)
            b1 = pb[1]
            for c in range(2):
                cs = slice(c * 64, (c + 1) * 64)
                ov = b1[0:64, c * 256:(c + 1) * 256] if False else None
            for c in range(2):
                cs = slice(c * 64, (c + 1) * 64)
                ops_ = V(b1, b1.h[0:64, 0:256])
                p.mm(ops_, [(T_["qeT"][:, cs], Sb[s][:]), (T_["ATm"][:, cs], T_["v"][:])])
                p.act(T_["osb"][:, c, :], ops_, AF.Identity)
                ups = V(b1, b1.h[:, 256:512])
                p.mm(ups, [(T_["kend"][cs, :], T_["v"][cs, :])])
                p.stt("dve", St[s][:], St[s][:], T_["Eend"][:, c:c + 1], ups, ALU.mult, ALU.add)
                p.act(Sb[s][:], St[s][:], AF.Identity)
            p.dma("sp", V(ogla, ogla.h[s, t].rearrange("(c q) e -> q c e", q=64)), T_["osb"][:])

        kT = p.sb("kT", [128, L], BF16)
        vE = p.sb("vE", [128, NKT, 129], BF16)
        p.dma("sp", kT[:], kT_d[:])
        p.dma("sp", vE[:], vE_d[:])
        qTs = [p.sb("qTs%d" % i, [128, 512], BF16) for i in range(2)]
        pTs = [p.sb("pTs%d" % i, [128, 512], BF16) for i in range(3)]
        rcp = p.sb("rcp", [128, 4], F32)
        ost = [p.sb("ost%d" % i, [128, 128], BF16) for i in range(4)]
        SC = HD ** -0.5
        blocks = []
        for h in range(2):
            for q0 in range(0, S, 512):
                blocks.append((h, q0, min(512, S - q0), 0, NKT))
            if with_ctx_q:
                blocks.append((h, S, CTX, NKT - CTX // 128, NKT))
        state = {"bi": 0, "pi": 0, "sb": 0}

        def att_block(blk):
            h, q0, nq, kt0, kt1 = blk
            qt = qTs[state["bi"] % 2]
            state["bi"] += 1
            p.dma("sp", qt[:, 0:nq], V(qT_d, qT_d.h[h, :, q0:q0 + nq]))
            nsub = nq // 128
            for kt in range(kt0, kt1):
                sbk = banks[2 + state["sb"] % 2]
                state["sb"] += 1
                p.mm(sbk[:, 0:nq], [(kT[:, kt * 128:(kt + 1) * 128], qt[:, 0:nq])])
                pt = pTs[state["pi"] % 3]
                state["pi"] += 1
                p.act(pt[:, 0:nq], sbk[:, 0:nq], AF.Exp, scale=SC)
                for qs in range(nsub):
                    ob = banks[qs // 2]
                    ov = V(ob, ob.h[:, (qs % 2) * 256:(qs % 2) * 256 + 129])
                    p.mm1(ov, pt[:, qs * 128:(qs + 1) * 128], vE[:, kt, :], kt == kt0, kt == kt1 - 1)
            for qs in range(nsub):
                ob = banks[qs // 2]
                base = (qs % 2) * 256
                p.op("dve", [ob[:, base + 128:base + 129]], [rcp[:, qs:qs + 1]],
                     lambda e, ob=ob, base=base, qs=qs: e.reciprocal(rcp.h[:, qs:qs + 1], ob.h[:, base + 128:base + 129]))
                p.ts("dve", ost[qs][:], ob[:, base:base + 128], rcp[:, qs:qs + 1], None, ALU.mult)
                p.dma("sp", V(oatt, oatt.h[h, (q0 // 128) + qs]), ost[qs][:])

        nb = len(blocks)
        ngl = NKT
        total = max(nb, ngl, CW)
        gi = ci = ai = 0
        for step in range(total):
            while gi < ngl and gi * total <= step * ngl:
                gla_tile(0, gi)
                gla_tile(1, gi)
                gi += 1
            while ci < CW and ci * total <= step * CW:
                conv_emit(ci)
                ci += 1
            while ai < nb and ai * total <= step * nb:
                att_block(blocks[ai])
                ai += 1
        while gi < ngl:
            gla_tile(0, gi); gla_tile(1, gi); gi += 1
        while ci < CW:
            conv_emit(ci); ci += 1
        while ai < nb:
            att_block(blocks[ai]); ai += 1
        p.wait_all("sp", [oatt[:], ogla[:], oconv[:], oconvc[:]])
        print("MIX ops", p.n_ops, "waits", p.n_waits)
    return nc


def _post_common(p, cx, NTA):
    load_const_eps(p, cx)
    cx.bn = p.sb("bn", [128, 2, 6], F32)
    cx.wts = [p.sb("wt%d" % i, [128, 8, 512], BF16) for i in range(2)]
    cx.wi = 0
    cx.bbc = p.sb("bbc", [128, 1024], F32)
    cx.xt = [p.sb("xt%d" % i, [128, D], F32) for i in range(2)]
    cx.mr = [p.sb("mr%d" % i, [128, 2], F32) for i in range(2)]
    cx.xn = p.sb("xn", [128, D], F32)
    cx.hb = [p.sb("hb%d" % i, [128, D], BF16) for i in range(2)]


def _bc_vec(p, name, d, n):
    t = p.sb(name + "_bc", [128, n], F32)
    p.dma("sp", t[:], V(d, d.h.partition_broadcast(128)))
    return t


def deepnorm_out(p, cx, u, xo_view, lg, lb, i):
    mr = cx.mr[i % 2]
    ln_stats(p, cx, u[:], 1024, mr[:])
    p.ts("dve", u[:], u[:], mr[:, 0:1], mr[:, 1:2], ALU.subtract, ALU.mult)
    p.tt("pool", u[:], u[:], lg[:], ALU.mult)
    p.tt("dve", u[:], u[:], lb[:], ALU.add)
    p.dma("sp", xo_view, u[:])


class NS:
    pass


def emit_posta(p, cx, NT, NCT, A):
    NTA = NT + NCT
    _post_common(p, cx, NTA)
    ident = p.sb("ident_sb", [128, 128], BF16)
    p.dma("pool", ident[:], A.ident_d[:])
    bc = {n: _bc_vec(p, n, d, d.h.shape[0]) for n, d in A.vecs.items()}
    scb = make_scb(p, cx, A.cvT, 2)
    g1L = p.sb("g1L", [128, D], F32)
    g1C = p.sb("g1C", [128, D], F32)
    specs = [(0, A.g1_mi, False, g1L)]
    if NCT:
        specs.append((1, A.g1_mi, False, g1C))
    mod_tiles(p, cx, scb, A.w_ada, A.b_ada, specs)

    aT = p.sb("aT", [128, 8, NTA * 128], BF16)
    m = p.sb("m", [128, NTA, D], BF16)
    ld16 = [p.sb("ld16_%d" % i, [128, D], BF16) for i in range(2)]
    ld32 = [p.sb("ld32_%d" % i, [128, D], F32) for i in range(2)]
    ss = p.sb("ss", [128, 4], F32)
    sq = p.sb("sq", [128, D], F32)
    gtile = [p.sb("gtile%d" % i, [128, 512], BF16) for i in range(2)]
    tmp = p.sb("tmpm", [128, 512], F32)

    def fill_att(t):
        a = ld16[t % 2]
        p.dma("sp", a[:], A.oatt(t))
        transpose_into(p, cx, a, aT, t, ident, 8)

    def fill_gla(t):
        a, b_ = ld32[0], ld32[1]
        p.dma("sp", a[:], A.ogf(t))
        p.dma("sp", b_[:], A.ogb(t))
        p.tt("dve", a[:], a[:], b_[:], ALU.add)
        p.act(sq[:], a[:], AF.Square)
        p.op("dve", [sq[:]], [ss[:]],
             lambda e: e.reduce_sum(ss.h[:], sq.h[:].rearrange("p (h d) -> p h d", d=GDV), AX.X))
        p.act(ss[:], ss[:], AF.Sqrt, bias=cx.eps[:, 0:1], scale=1.0 / GDV)
        p.op("dve", [ss[:]], [ss[:]], lambda e: e.reciprocal(ss.h[:], ss.h[:]))
        for h in range(GH):
            hs = slice(h * GDV, (h + 1) * GDV)
            p.stt("dve", a[:, hs], a[:, hs], ss[:, h:h + 1], bc["gla_norm"][:], ALU.mult, ALU.mult)
        r = ld16[0]
        p.dma("sp", r[:], A.sr(t))
        hb = cx.hb[t % 2]
        p.tt("dve", hb[:], a[:], r[:], ALU.mult)
        transpose_into(p, cx, hb, aT, t, ident, 8)

    def fill_conv(t):
        a = ld32[t % 2]
        p.dma("sp", a[:], A.cv(t))
        mr = cx.mr[t % 2]
        ln_stats(p, cx, a[:], 1024, mr[:])
        p.ts("dve", a[:], a[:], mr[:, 0:1], mr[:, 1:2], ALU.subtract, ALU.mult)
        p.tt("pool", a[:], a[:], bc["conv_ln_g"][:], ALU.mult)
        p.tt("dve", a[:], a[:], bc["conv_ln_b"][:], ALU.add)
        hb = cx.hb[t % 2]
        p.act(hb[:], a[:], AF.Silu)
        transpose_into(p, cx, hb, aT, t, ident, 8)

    def load_wo(wd, half):
        wt = cx.wts[cx.wi % 2]
        cx.wi += 1
        wv = wd.h.rearrange("(kc p) c -> p kc c", p=128)
        p.dma("pool", wt[:], V(wd, wv[:, :, half * 512:(half + 1) * 512]))
        return wt

    for bi, (fill, wn) in enumerate(((fill_att, "w_att_o"), (fill_gla, "w_gla_o"), (fill_conv, "w_conv_o"))):
        for t in range(NTA):
            fill(t)
        for half in range(2):
            wt = load_wo(A.wo[wn], half)
            for t in range(NTA):
                ps = cx.bank()
                p.mm(ps[:], [(aT[:, kc, t * 128:(t + 1) * 128], wt[:, kc, :]) for kc in range(8)])
                g = gtile[t % 2]
                p.dma("sp", g[:], A.gt(t, bi * D + half * 512, bi * D + (half + 1) * 512))
                mv = m.k(("t", t, half), (slice(None), t, slice(half * 512, (half + 1) * 512)))
                if bi == 0:
                    p.tt("dve", mv, ps[:], g[:], ALU.mult)
                else:
                    p.tt("dve", tmp[:], ps[:], g[:], ALU.mult)
                    p.tt("pool", mv, mv, tmp[:], ALU.add)
    for t in range(NTA):
        hb = cx.hb[t % 2]
        p.copy("dve", hb[:], m[:, t, :])
        transpose_into(p, cx, hb, aT, t, ident, 8)
    w0 = load_wo(A.wo["w_out"], 0)
    w1 = load_wo(A.wo["w_out"], 1)
    us = [p.sb("u%d" % i, [128, D], F32) for i in range(2)]
    for t in range(NTA):
        lat = t < NT
        g1 = g1L if lat else g1C
        u = us[t % 2]
        xt = cx.xt[t % 2]
        p.dma("sp", xt[:], A.xs(t))
        for half, wt in enumerate((w0, w1)):
            ps = cx.bank()
            hs = slice(half * 512, (half + 1) * 512)
            p.mm(ps[:], [(aT[:, kc, t * 128:(t + 1) * 128], wt[:, kc, :]) for kc in range(8)])
            p.tt("dve", u[:, hs], ps[:], g1[:, hs], ALU.mult)
        p.stt("dve", u[:], xt[:], float(ALPHA), u[:], ALU.mult, ALU.add)
        deepnorm_out(p, cx, u, A.xo(t), bc["ln1_g"], bc["ln1_b"], t)


def build_posta(NT, NCT):
    NTA = NT + NCT
    nc = bass.Bass("TRN2", target_bir_lowering=False)
    es = ExitStack()
    with es:
        p = Prog(nc, es)
        cx = Ctx(p)
        oatt = p.dram("oatt_t", [NTA, 128, D], BF16, "ExternalInput")
        ogf = p.dram("ogf", [NTA, 128, D], F32, "ExternalInput")
        ogb = p.dram("ogb", [NTA, 128, D], F32, "ExternalInput")
        sr = p.dram("sr", [NTA, 128, D], BF16, "ExternalInput")
        cv = p.dram("cv", [NTA, 128, D], F32, "ExternalInput")
        gt = p.dram("gt", [NTA, 128, 3 * D], BF16, "ExternalInput")
        xs = p.dram("xs", [NTA, 128, D], F32, "ExternalInput")
        cvT = p.dram("cvT", [128, 2, 8], F32, "ExternalInput")
        w_ada = p.dram("w_ada", [D, D], F32, "ExternalInput")
        b_ada = p.dram("b_ada", [D], F32, "ExternalInput")
        wo = {n: p.dram(n, [D, D], F32, "ExternalInput") for n in ("w_att_o", "w_gla_o", "w_conv_o", "w_out")}
        vecs = {n: p.dram(n, [sz], F32, "ExternalInput") for n, sz in
                (("gla_norm", GDV), ("conv_ln_g", D), ("conv_ln_b", D), ("ln1_g", D), ("ln1_b", D))}
        ident_d = p.dram("ident", [128, 128], F32, "ExternalInput")
        xo = p.dram("xo", [NTA, 128, D], F32, "ExternalOutput")
        A = NS()
        A.oatt = lambda t: oatt[t]
        A.ogf = lambda t: ogf[t]
        A.ogb = lambda t: ogb[t]
        A.sr = lambda t: sr[t]
        A.cv = lambda t: cv[t]
        A.xs = lambda t: xs[t]
        A.xo = lambda t: xo[t]
        A.gt = lambda t, c0, c1: V(gt, gt.h[t, :, c0:c1])
        A.g1_mi = 0
        A.ident_d, A.vecs, A.cvT, A.w_ada, A.b_ada, A.wo = ident_d, vecs, cvT, w_ada, b_ada, wo
        emit_posta(p, cx, NT, NCT, A)
        p.wait_all("sp", [xo[:]])
        print("POSTA ops", p.n_ops, "waits", p.n_waits)
    return nc


def emit_postb(p, cx, NT, NCT, A):
    NTA = NT + NCT
    NFC = DFF // 128
    _post_common(p, cx, NTA)
    ident = p.sb("ident_sb", [128, 128], BF16)
    p.dma("pool", ident[:], A.ident_d[:])
    l2g = _bc_vec(p, "l2g", A.l2g_d, D)
    l2b = _bc_vec(p, "l2b", A.l2b_d, D)
    scb = make_scb(p, cx, A.cvT, 2)
    mods = {n: p.sb(n, [128, D], F32) for n in ("sh2L", "sc2L", "g2L")}
    m0 = A.mi0
    specs = [(0, m0, False, mods["sh2L"]), (0, m0 + 1, True, mods["sc2L"]), (0, m0 + 2, False, mods["g2L"])]
    if NCT:
        for n in ("sh2C", "sc2C", "g2C"):
            mods[n] = p.sb(n, [128, D], F32)
        specs += [(1, m0, False, mods["sh2C"]), (1, m0 + 1, True, mods["sc2C"]), (1, m0 + 2, False, mods["g2C"])]
    mod_tiles(p, cx, scb, A.w_ada, A.b_ada, specs)
    hT = p.sb("hT", [128, 8, NTA * 128], BF16)
    for t in range(NTA):
        lat = t < NT
        ln_mod_transpose(p, cx, A.xs(t), t, mods["sc2L" if lat else "sc2C"],
                         mods["sh2L" if lat else "sh2C"], hT, ident)
    wgs = [p.sb("wg%d" % i, [128, 8, 512], BF16) for i in range(2)]
    wus = cx.wts
    actT = p.sb("actT", [128, NFC, 512], BF16)
    wdt = p.sb("wdt", [128, NFC, 512], BF16)
    sg = [p.sb("sg%d" % i, [128, 512], F32) for i in range(2)]
    us = [p.sb("u%d" % i, [128, D], F32) for i in range(4)]
    wg_d, wu_d, wd_d = A.wg_d, A.wu_d, A.wd_d
    wgv = wg_d.h.rearrange("(kc p) c -> p kc c", p=128)
    wuv = wu_d.h.rearrange("(kc p) c -> p kc c", p=128)
    wdv = wd_d.h.rearrange("(c p) n -> p c n", p=128)
    wi = 0
    for g0 in range(0, NTA, 4):
        gt_ = list(range(g0, min(NTA, g0 + 4)))
        ntok = len(gt_) * 128
        tk = slice(g0 * 128, g0 * 128 + ntok)
        for c0 in range(0, DFF, 512):
            wd_ = min(512, DFF - c0)
            wgt, wut = wgs[wi % 2], wus[wi % 2]
            wi += 1
            p.dma("pool", wgt[:, :, 0:wd_], V(wg_d, wgv[:, :, c0:c0 + wd_]))
            p.dma("pool", wut[:, :, 0:wd_], V(wu_d, wuv[:, :, c0:c0 + wd_]))
            for sub in range(wd_ // 128):
                ffc = c0 // 128 + sub
                cs = slice(sub * 128, (sub + 1) * 128)
                pg = cx.bank()
                pu = cx.bank()
                p.mm(pg[:, 0:ntok], [(wgt[:, kc, cs], hT[:, kc, tk]) for kc in range(8)])
                p.mm(pu[:, 0:ntok], [(wut[:, kc, cs], hT[:, kc, tk]) for kc in range(8)])
                s_ = sg[ffc % 2]
                p.act(s_[:, 0:ntok], pg[:, 0:ntok], AF.Silu)
                p.tt("dve", actT.k(ffc, (slice(None), ffc, slice(0, ntok))), s_[:, 0:ntok], pu[:, 0:ntok], ALU.mult)
        for half in range(2):
            hs = slice(half * 512, (half + 1) * 512)
            p.dma("pool", wdt[:], V(wd_d, wdv[:, :, hs]))
            for i, t in enumerate(gt_):
                lat = t < NT
                g2 = mods["g2L" if lat else "g2C"]
                ps = cx.bank()
                p.mm(ps[:], [(actT[:, ffc, i * 128:(i + 1) * 128], wdt[:, ffc, :]) for ffc in range(NFC)])
                p.tt("dve", us[i][:, hs], ps[:], g2[:, hs], ALU.mult)
        for i, t in enumerate(gt_):
            xt = cx.xt[t % 2]
            p.dma("sp", xt[:], A.xs(t))
            p.stt("dve", us[i][:], xt[:], float(ALPHA), us[i][:], ALU.mult, ALU.add)
            deepnorm_out(p, cx, us[i], A.xo(t), l2g, l2b, t)


def emit_postb2(p, cx, NT, NCT, A):
    NTA = NT + NCT
    NFC = DFF // 128
    NTOK = NTA * 128
    act_s = A.act_s
    _post_common(p, cx, NTA)
    ident = p.sb("ident_sb", [128, 128], BF16)
    p.dma("pool", ident[:], A.ident_d[:])
    l2g = _bc_vec(p, "l2g", A.l2g_d, D)
    l2b = _bc_vec(p, "l2b", A.l2b_d, D)
    scb = make_scb(p, cx, A.cvT, 2)
    mods = {n: p.sb(n, [128, D], F32) for n in ("sh2L", "sc2L", "g2L")}
    m0 = A.mi0
    specs = [(0, m0, False, mods["sh2L"]), (0, m0 + 1, True, mods["sc2L"]), (0, m0 + 2, False, mods["g2L"])]
    if NCT:
        for n in ("sh2C", "sc2C", "g2C"):
            mods[n] = p.sb(n, [128, D], F32)
        specs += [(1, m0, False, mods["sh2C"]), (1, m0 + 1, True, mods["sc2C"]), (1, m0 + 2, False, mods["g2C"])]
    mod_tiles(p, cx, scb, A.w_ada, A.b_ada, specs)
    hT = p.sb("hT", [128, 8, NTOK], BF16)
    for t in range(NTA):
        lat = t < NT
        ln_mod_transpose(p, cx, A.xs(t), t, mods["sc2L" if lat else "sc2C"],
                         mods["sh2L" if lat else "sh2C"], hT, ident)
    wgs = [p.sb("wg%d" % i, [128, 8, 512], BF16) for i in range(2)]
    wus = cx.wts
    sg = [p.sb("sg%d" % i, [128, 512], F32) for i in range(2)]
    stg = [p.sb("astg%d" % i, [128, 512], BF16) for i in range(3)]
    wdt = p.sb("wdt", [128, NFC, D], BF16)
    wg_d, wu_d, wd_d = A.wg_d, A.wu_d, A.wd_d
    wgv = wg_d.h.rearrange("(kc q) c -> q kc c", q=128)
    wuv = wu_d.h.rearrange("(kc q) c -> q kc c", q=128)
    wdv = wd_d.h.rearrange("(c q) n -> q c n", q=128)
    wi = 0
    si = 0
    for c0 in range(0, DFF, 512):
        wd_ = min(512, DFF - c0)
        wgt, wut = wgs[wi % 2], wus[wi % 2]
        wi += 1
        p.dma("pool", wgt[:, :, 0:wd_], V(wg_d, wgv[:, :, c0:c0 + wd_]))
        p.dma("pool", wut[:, :, 0:wd_], V(wu_d, wuv[:, :, c0:c0 + wd_]))
        if c0 == 0:
            for half in range(2):
                p.dma("pool", wdt[:, :, half * 512:(half + 1) * 512], V(wd_d, wdv[:, :, half * 512:(half + 1) * 512]))
        for tk0 in range(0, NTOK, 512):
            ntok = min(512, NTOK - tk0)
            nt_ = ntok // 128
            tk = slice(tk0, tk0 + ntok)
            for sub in range(wd_ // 128):
                ffc = c0 // 128 + sub
                cs = slice(sub * 128, (sub + 1) * 128)
                pg = cx.bank()
                pu = cx.bank()
                p.mm(pg[:, 0:ntok], [(wgt[:, kc, cs], hT[:, kc, tk]) for kc in range(8)])
                p.mm(pu[:, 0:ntok], [(wut[:, kc, cs], hT[:, kc, tk]) for kc in range(8)])
                s_ = sg[si % 2]
                st = stg[si % 3]
                si += 1
                p.act(s_[:, 0:ntok], pg[:, 0:ntok], AF.Silu)
                p.tt("dve", st[:, 0:ntok], s_[:, 0:ntok], pu[:, 0:ntok], ALU.mult)
                t0_ = tk0 // 128
                p.dma("sp", V(act_s, act_s.h[t0_:t0_ + nt_, :, ffc, :].rearrange("t q k -> q t k")),
                      V(st, st.h[:, 0:ntok].rearrange("q (t k) -> q t k", k=128)))
    ats = [p.sb("at%d" % i, [128, NFC, 128], BF16) for i in range(2)]
    us = [p.sb("u%d" % i, [128, D], F32) for i in range(2)]
    for t in range(NTA):
        lat = t < NT
        g2 = mods["g2L" if lat else "g2C"]
        at = ats[t % 2]
        u = us[t % 2]
        p.dma("sp", at[:], act_s[t])
        for half in range(2):
            hs = slice(half * 512, (half + 1) * 512)
            ps = cx.bank()
            p.mm(ps[:], [(at[:, ffc, :], wdt[:, ffc, hs]) for ffc in range(NFC)])
            p.tt("dve", u[:, hs], ps[:], g2[:, hs], ALU.mult)
        xt = cx.xt[t % 2]
        p.dma("sp", xt[:], A.xs(t))
        p.stt("dve", u[:], xt[:], float(ALPHA), u[:], ALU.mult, ALU.add)
        deepnorm_out(p, cx, u, A.xo(t), l2g, l2b, t)


def build_postb(NT, NCT):
    NTA = NT + NCT
    NFC = DFF // 128
    nc = bass.Bass("TRN2", target_bir_lowering=False)
    es = ExitStack()
    with es:
        p = Prog(nc, es)
        cx = Ctx(p)
        xs = p.dram("xs", [NTA, 128, D], F32, "ExternalInput")
        cvT = p.dram("cvT", [128, 2, 8], F32, "ExternalInput")
        w_ada = p.dram("w_ada", [D, 3 * D], F32, "ExternalInput")
        b_ada = p.dram("b_ada", [3 * D], F32, "ExternalInput")
        wg_d = p.dram("w_ff_gate", [D, DFF], F32, "ExternalInput")
        wu_d = p.dram("w_ff_up", [D, DFF], F32, "ExternalInput")
        wd_d = p.dram("w_ff_down", [DFF, D], F32, "ExternalInput")
        l2g_d = p.dram("ln2_g", [D], F32, "ExternalInput")
        l2b_d = p.dram("ln2_b", [D], F32, "ExternalInput")
        ident_d = p.dram("ident", [128, 128], F32, "ExternalInput")
        xo = p.dram("xo", [NTA, 128, D], F32, "ExternalOutput")
        A = NS()
        A.xs = lambda t: xs[t]
        A.xo = lambda t: xo[t]
        A.mi0 = 0
        A.ident_d, A.cvT, A.w_ada, A.b_ada, A.l2g_d, A.l2b_d = ident_d, cvT, w_ada, b_ada, l2g_d, l2b_d
        A.wg_d, A.wu_d, A.wd_d = wg_d, wu_d, wd_d
        emit_postb(p, cx, NT, NCT, A)
        p.wait_all("sp", [xo[:]])
        print("POSTB ops", p.n_ops, "waits", p.n_waits)
    return nc


import ml_dtypes
NPBF = ml_dtypes.bfloat16


def _unpack(o, NT):
    C = o.shape[-1]
    lat = o[:, :NT].reshape(B, -1, C)
    ctx = o[0:4, NT].reshape(B, CTX, C) if o.shape[1] > NT else None
    return lat, ctx


def _pack(lat, ctx, core, NT, TPC, with_ctx):
    b, seg = core // 4, core % 4
    a = lat[b, seg * TPC:(seg + 1) * TPC].reshape(NT, 128, -1)
    if with_ctx:
        a = np.concatenate([a, ctx.reshape(4, 128, -1)[core % 4][None]], axis=0)
    return np.ascontiguousarray(a)


def _gla_consts():
    j = np.arange(128)[:, None]
    i = np.arange(128)[None, :]
    same = (j // CHUNK) == (i // CHUNK)
    Lm = (same & (j <= i)).astype(np.float32)
    Um = (same & (j > i)).astype(np.float32)
    sel = ((np.arange(128)[:, None] // CHUNK) == np.arange(2)[None, :]).astype(np.float32)
    return np.ascontiguousarray(np.concatenate([Lm, Um, sel], axis=1))


def host_mix(l, lat16, ctx16, latg, ctxg, P, S, with_ctx_q):
    L = S + CTX
    NKT = L // 128
    cst = _gla_consts()
    in_maps = []
    for core in range(NCORES):
        b, r = core // 4, core % 4
        h0, kv = 2 * r, r // 2
        Ql = lat16[b, :, OQ:OQ + 1024].reshape(S, NH, HD)
        Qc = ctx16[b, :, OQ:OQ + 1024].reshape(CTX, NH, HD)
        qT = np.stack([np.concatenate([Ql[:, h0 + hh], Qc[:, h0 + hh]], 0).T for hh in range(2)])
        Kl = lat16[b, :, OK_:OK_ + 256].reshape(S, NKV, HD)[:, kv]
        Kc = ctx16[b, :, OK_:OK_ + 256].reshape(CTX, NKV, HD)[:, kv]
        kT = np.concatenate([Kl, Kc], 0).T
        Vl = lat16[b, :, OV_:OV_ + 256].reshape(S, NKV, HD)[:, kv]
        Vc = ctx16[b, :, OV_:OV_ + 256].reshape(CTX, NKV, HD)[:, kv]
        Vall = np.concatenate([Vl, Vc], 0)
        vE = np.concatenate([Vall, np.ones((L, 1), NPBF)], 1).reshape(NKT, 128, 129).transpose(1, 0, 2)

        def seq(al, ac, d):
            if d == 0:
                return np.concatenate([ac, al], 0)
            return np.concatenate([al, ac], 0)[::-1]
        hd = r
        gq, gkT, gk, gv, gg = [], [], [], [], []
        for d in range(2):
            q_ = seq(lat16[b, :, OQG + hd * 128:OQG + (hd + 1) * 128], ctx16[b, :, OQG + hd * 128:OQG + (hd + 1) * 128], d)
            k_ = seq(lat16[b, :, OKG + hd * 128:OKG + (hd + 1) * 128], ctx16[b, :, OKG + hd * 128:OKG + (hd + 1) * 128], d)
            v_ = seq(lat16[b, :, OVG + hd * 256:OVG + (hd + 1) * 256], ctx16[b, :, OVG + hd * 256:OVG + (hd + 1) * 256], d)
            g_ = seq(latg[b, :, d * 512 + hd * 128:d * 512 + (hd + 1) * 128], ctxg[b, :, d * 512 + hd * 128:d * 512 + (hd + 1) * 128], d)
            gq.append(q_.T); gkT.append(k_.T); gk.append(k_.reshape(NKT, 128, 128))
            gv.append(v_.reshape(NKT, 128, 256)); gg.append(g_.reshape(NKT, 128, 128))
        ch = slice(core * 128, (core + 1) * 128)
        cy = np.zeros((128, B, S + 30), NPBF)
        cyc = np.zeros((128, B, CTX + 30), NPBF)
        for bb in range(B):
            cy[:, bb, 15:15 + S] = lat16[bb, :, OY:OY + 1024][:, ch].T
            cyc[:, bb, 15:15 + CTX] = ctx16[bb, :, OY:OY + 1024][:, ch].T
        cw = np.concatenate([P["conv_w_dw"][l][:, 0, ch].T, P["conv_b_dw"][l][ch][:, None]], 1).astype(np.float32)
        in_maps.append({
            "qT": np.ascontiguousarray(qT), "kT": np.ascontiguousarray(kT), "vE": np.ascontiguousarray(vE),
            "gqT": np.ascontiguousarray(np.stack(gq)), "gkT": np.ascontiguousarray(np.stack(gkT)),
            "gk": np.ascontiguousarray(np.stack(gk)), "gv": np.ascontiguousarray(np.stack(gv)),
            "gg": np.ascontiguousarray(np.stack(gg)).astype(np.float32), "cst": cst,
            "cy": cy, "cyc": cyc, "cw": np.ascontiguousarray(cw),
        })
    res = _run(("mix", S, with_ctx_q), lambda: build_mix(S, with_ctx_q), in_maps)
    att_l = np.zeros((B, S, D), NPBF); att_c = np.zeros((B, CTX, D), NPBF)
    gf_l = np.zeros((B, S, D), np.float32); gb_l = np.zeros((B, S, D), np.float32)
    gf_c = np.zeros((B, CTX, D), np.float32); gb_c = np.zeros((B, CTX, D), np.float32)
    cv_l = np.zeros((B, S, D), np.float32); cv_c = np.zeros((B, CTX, D), np.float32)
    for core in range(NCORES):
        b, r = core // 4, core % 4
        oa = res[core]["oatt"].reshape(2, L, HD)
        for hh in range(2):
            hs = slice((2 * r + hh) * HD, (2 * r + hh + 1) * HD)
            att_l[b, :, hs] = oa[hh, :S]
            att_c[b, :, hs] = oa[hh, S:]
        og = res[core]["ogla"].reshape(2, L, GDV)
        vs = slice(r * GDV, (r + 1) * GDV)
        gf_c[b, :, vs] = og[0, :CTX]; gf_l[b, :, vs] = og[0, CTX:]
        ob = og[1][::-1]
        gb_l[b, :, vs] = ob[:S]; gb_c[b, :, vs] = ob[S:]
        ch = slice(core * 128, (core + 1) * 128)
        for bb in range(B):
            cv_l[bb, :, ch] = res[core]["oconv"][:, bb, :].T
            cv_c[bb, :, ch] = res[core]["oconvc"][:, bb, :].T
    return (att_l, att_c), (gf_l, gf_c), (gb_l, gb_c), (cv_l, cv_c)


def host_post(l, mixo, lat16, ctx16, x, xc, c, c_ctx, P, S, with_ctx):
    TPC = S * B // NCORES
    NT = TPC // 128
    (att_l, att_c), (gf_l, gf_c), (gb_l, gb_c), (cv_l, cv_c) = mixo
    ident = np.eye(128, dtype=np.float32)
    in_maps = []
    for core in range(NCORES):
        b = core // 4
        pk = lambda al, ac: _pack(al, ac, core, NT, TPC, with_ctx)
        cv = np.ascontiguousarray(np.stack([c[b], c_ctx]).reshape(2, 8, 128).transpose(2, 0, 1))
        in_maps.append({
            "oatt_t": pk(att_l, att_c), "ogf": pk(gf_l, gf_c), "ogb": pk(gb_l, gb_c),
            "sr": pk(lat16[:, :, OR:OR + 1024], ctx16[:, :, OR:OR + 1024]),
            "cv": pk(cv_l, cv_c), "gt": pk(lat16[:, :, OGT:OGT + 3072], ctx16[:, :, OGT:OGT + 3072]),
            "xs": pk(x, xc), "cvT": cv, "w_ada": np.ascontiguousarray(P["w_ada"][l][:, 2 * D:3 * D]),
            "b_ada": np.ascontiguousarray(P["b_ada"][l][2 * D:3 * D]),
            "w_att_o": P["w_att_o"][l], "w_gla_o": P["w_gla_o"][l], "w_conv_o": P["w_conv_o"][l],
            "w_out": P["w_out"][l], "gla_norm": P["gla_norm"][l], "conv_ln_g": P["conv_ln_g"][l],
            "conv_ln_b": P["conv_ln_b"][l], "ln1_g": P["ln1_g"][l], "ln1_b": P["ln1_b"][l], "ident": ident,
        })
    nct = 1 if with_ctx else 0
    res = _run(("posta", NT, nct), lambda: build_posta(NT, nct), in_maps)
    xo = np.stack([r["xo"] for r in res])
    x1, xc1 = _unpack(xo, NT)
    in_maps = []
    for core in range(NCORES):
        b = core // 4
        cv = np.ascontiguousarray(np.stack([c[b], c_ctx]).reshape(2, 8, 128).transpose(2, 0, 1))
        in_maps.append({
            "xs": _pack(x1, xc1, core, NT, TPC, with_ctx), "cvT": cv,
            "w_ada": np.ascontiguousarray(P["w_ada"][l][:, 3 * D:6 * D]),
            "b_ada": np.ascontiguousarray(P["b_ada"][l][3 * D:6 * D]), "w_ff_gate": P["w_ff_gate"][l], "w_ff_up": P["w_ff_up"][l],
            "w_ff_down": P["w_ff_down"][l], "ln2_g": P["ln2_g"][l], "ln2_b": P["ln2_b"][l], "ident": ident,
        })
    res = _run(("postb", NT, nct), lambda: build_postb(NT, nct), in_maps)
    xo = np.stack([r["xo"] for r in res])
    return _unpack(xo, NT)


def forward(P, S):
    x = np.ascontiguousarray(P["x"][:, :S])
    xc = P["ctx"]
    c, c_ctx = P["c"], P["c_ctx"]
    TPC = S * B // NCORES
    NT = TPC // 128
    for l in range(DEPTH):
        last = l == DEPTH - 1
        o16, o32 = host_p1(l, x, xc, c, c_ctx, P, S)
        lat16, ctx16 = _unpack(o16, NT)
        latg, ctxg = _unpack(o32, NT)
        mixo = host_mix(l, lat16, ctx16, latg, ctxg, P, S, not last)
        x, xc_new = host_post(l, mixo, lat16, ctx16, x, xc, c, c_ctx, P, S, not last)
        if not last:
            xc = xc_new
    return x


def kernel(**inputs):
    P = {k: np.asarray(v) for k, v in inputs.items()}
    S = P["x"].shape[1]
    out = forward(P, S)
    return np.ascontiguousarray(out.astype(np.float32))


class _Stop(Exception):
    pass


def emit_mix_fused(p, cx, NT, M):
    depth0 = len(getattr(p, "scopes", []))
    try:
        _emit_mix_fused(p, cx, NT, M)
    except _Stop:
        while len(p.scopes) > depth0:
            p.pop_scope()


def _emit_mix_fused(p, cx, NT, M):
    def chk(x):
        if getattr(M, "stop_after", 3) < x:
            raise _Stop()
    NCT = 2
    NTA = NT + NCT
    TPC = NT * 128
    NTOK = NTA * 128
    SEGS = 4
    Sfull = SEGS * TPC
    L = Sfull + CTX
    NKT = L // 128
    banks = cx.banks
    o16, o32 = M.o16, M.o32
    groups = [[0, 1, 2, 3], [4, 5, 6, 7]]

    p.push_scope()
    ident = p.sb("ident_sb", [128, 128], BF16)
    p.dma("pool", ident[:], M.ident_d[:])
    identf = p.sb("identf", [128, 128], F32)
    p.dma("sp", identf[:], M.ident_d[:])
    cst = p.sb("cst", [128, 514], F32)
    p.dma("sp", cst[:], M.gcst_d[:])
    LM = (cst[:, 0:128], cst[:, 256:384])
    UM = (cst[:, 128:256], cst[:, 384:512])
    sel = cst[:, 512:514]
    masks = p.sb("masks", [128, 8], F32)
    p.dma("sp", masks[:], M.masks_d[:])
    gqT = p.sb("gqT", [128, 4, NTOK], BF16)
    gkT = p.sb("gkT", [128, 4, NTOK], BF16)
    kTl = p.sb("kTl", [128, 2, NTOK], BF16)
    veb = p.sb("veb", [128, NTA, 2, 129], BF16)
    p.memset("dve", veb[:], 1.0)
    St = [p.sb("gS%d" % i, [128, 256], F32) for i in range(8)]
    Sb = [p.sb("gSb%d" % i, [128, 256], BF16) for i in range(8)]
    Sctx = [p.sb("gSc%d" % i, [128, 256], F32) for i in range(8)]
    logD = p.sb("logD", [128, 8], F32)
    p.memset("dve", logD[:], 1.0)
    gl = {}
    for nm, shp, dt in (("k", [128, 128], BF16), ("v", [128, 256], BF16), ("g", [128, 128], F32),
                        ("EbT", [128, 128], F32), ("EnbT", [128, 128], F32), ("Erem", [128, 128], F32),
                        ("Eend", [128, 2], F32), ("qeT", [128, 128], BF16), ("keT", [128, 128], BF16),
                        ("kend", [128, 128], BF16), ("ATm", [128, 128], BF16), ("osb", [64, 2, 256], F32)):
        gl[nm] = [[p.sb("g_%s_%d_%d" % (nm, s_, j), shp, dt) for j in range(2)] for s_ in range(4)]
    cnt = {"gl": 0}

    def gla_tile(slot, r, t, full, S_t, S_b):
        hd, d = r // 2, r % 2
        j = cnt["gl"] % 2
        cnt["gl"] += 1
        T_ = {k_: v_[slot][j] for k_, v_ in gl.items()}
        pb = banks[2 * slot:2 * slot + 2]
        p.dma("sp", T_["k"][:], V(o16, o16.h[t, :, OKG + hd * 128:OKG + (hd + 1) * 128]))
        p.dma("sp", T_["v"][:], V(o16, o16.h[t, :, OVG + hd * 256:OVG + (hd + 1) * 256]))
        p.dma("sp", T_["g"][:], V(o32, o32.h[t, :, d * 512 + hd * 128:d * 512 + (hd + 1) * 128]))
        g = T_["g"]
        Lc, Uc = LM[d], UM[d]
        b0, b1 = pb
        ts_ = slice(t * 128, (t + 1) * 128)
        if full:
            p.mm(b0[:, 0:128], [(g[:], Lc)])
        p.mm(b0[:, 128:256], [(Uc, g[:])])
        p.mm(b0[:, 256:258], [(g[:], sel)])
        if full:
            p.act(T_["EbT"][:], b0[:, 0:128], AF.Exp)
            p.act(T_["EnbT"][:], b0[:, 0:128], AF.Exp, scale=-1.0)
        p.act(T_["Erem"][:], b0[:, 128:256], AF.Exp)
        p.act(T_["Eend"][:], b0[:, 256:258], AF.Exp)
        if full:
            p.tt("dve", T_["qeT"][:], gqT[:, hd, ts_], T_["EbT"][:], ALU.mult)
            p.tt("dve", T_["keT"][:], gkT[:, hd, ts_], T_["EnbT"][:], ALU.mult)
        else:
            p.ts("dve", logD[:, r:r + 1], logD[:, r:r + 1], T_["Eend"][:, 0:1], T_["Eend"][:, 1:2], ALU.mult, ALU.mult)
        p.tt("dve", T_["kend"][:], T_["k"][:], T_["Erem"][:], ALU.mult)
        if full:
            p.mm(b0[:, 384:512], [(T_["keT"][:], T_["qeT"][:])])
            p.tt("dve", T_["ATm"][:], b0[:, 384:512], Lc, ALU.mult)
        for c in ((0, 1) if d == 0 else (1, 0)):
            cs = slice(c * 64, (c + 1) * 64)
            if full:
                ops_ = V(b1, b1.h[0:64, 0:256])
                p.mm(ops_, [(T_["qeT"][:, cs], S_b[:]), (T_["ATm"][:, cs], T_["v"][:])])
                p.act(T_["osb"][:, c, :], ops_, AF.Identity)
            ups = V(b1, b1.h[:, 256:512])
            p.mm(ups, [(T_["kend"][cs, :], T_["v"][cs, :])])
            p.stt("dve", S_t[:], S_t[:], T_["Eend"][:, c:c + 1], ups, ALU.mult, ALU.add)
            if full:
                p.act(S_b[:], S_t[:], AF.Identity)
        if full:
            dst = M.gf_s if d == 0 else M.gb_s
            p.dma("sp", V(dst, dst.h[t, :, hd * 256:(hd + 1) * 256].rearrange("(c q) e -> q c e", q=64)), T_["osb"][:])

    def tiles_for(d, ctx):
        if ctx:
            return [NT, NT + 1] if d == 0 else [NT + 1, NT]
        return list(range(NT)) if d == 0 else list(range(NT - 1, -1, -1))

    slotcnt = [0, 0, 0, 0]

    def gla_group(items, full):
        cs_ = []
        for (slot, r, t, S_t, S_b) in items:
            hd, d = r // 2, r % 2
            j = slotcnt[slot] % 2
            slotcnt[slot] += 1
            T_ = {k_: v_[slot][j] for k_, v_ in gl.items()}
            b0, b1 = banks[2 * slot:2 * slot + 2]
            p.dma("sp", T_["k"][:], V(o16, o16.h[t, :, OKG + hd * 128:OKG + (hd + 1) * 128]))
            p.dma("sp", T_["v"][:], V(o16, o16.h[t, :, OVG + hd * 256:OVG + (hd + 1) * 256]))
            p.dma("sp", T_["g"][:], V(o32, o32.h[t, :, d * 512 + hd * 128:d * 512 + (hd + 1) * 128]))
            g = T_["g"]
            if full:
                p.mm(b0[:, 0:128], [(g[:], LM[d])])
            p.mm(b0[:, 128:256], [(UM[d], g[:])])
            p.mm(b0[:, 256:258], [(g[:], sel)])
            cs_.append((slot, r, t, S_t, S_b, hd, d, T_, b0, b1))
        for (slot, r, t, S_t, S_b, hd, d, T_, b0, b1) in cs_:
            if full:
                p.act(T_["EbT"][:], b0[:, 0:128], AF.Exp)
                p.act(T_["EnbT"][:], b0[:, 0:128], AF.Exp, scale=-1.0)
            p.act(T_["Erem"][:], b0[:, 128:256], AF.Exp)
            p.act(T_["Eend"][:], b0[:, 256:258], AF.Exp)
        for (slot, r, t, S_t, S_b, hd, d, T_, b0, b1) in cs_:
            ts_ = slice(t * 128, (t + 1) * 128)
            if full:
                p.tt("dve", T_["qeT"][:], gqT[:, hd, ts_], T_["EbT"][:], ALU.mult)
                p.tt("dve", T_["keT"][:], gkT[:, hd, ts_], T_["EnbT"][:], ALU.mult)
            else:
                p.ts("dve", logD[:, r:r + 1], logD[:, r:r + 1], T_["Eend"][:, 0:1], T_["Eend"][:, 1:2], ALU.mult, ALU.mult)
            p.tt("dve", T_["kend"][:], T_["k"][:], T_["Erem"][:], ALU.mult)
        if full:
            for (slot, r, t, S_t, S_b, hd, d, T_, b0, b1) in cs_:
                p.mm(b0[:, 384:512], [(T_["keT"][:], T_["qeT"][:])])
            for (slot, r, t, S_t, S_b, hd, d, T_, b0, b1) in cs_:
                p.tt("dve", T_["ATm"][:], b0[:, 384:512], LM[d], ALU.mult)
        for ci in range(2):
            for (slot, r, t, S_t, S_b, hd, d, T_, b0, b1) in cs_:
                c = ci if d == 0 else 1 - ci
                cs = slice(c * 64, (c + 1) * 64)
                if full:
                    ops_ = V(b1, b1.h[0:64, 0:256])
                    p.mm(ops_, [(T_["qeT"][:, cs], S_b[:]), (T_["ATm"][:, cs], T_["v"][:])])
                ups = V(b1, b1.h[:, 256:512])
                p.mm(ups, [(T_["kend"][cs, :], T_["v"][cs, :])])
            for (slot, r, t, S_t, S_b, hd, d, T_, b0, b1) in cs_:
                c = ci if d == 0 else 1 - ci
                if full:
                    p.copy("dve", T_["osb"][:, c, :], V(b1, b1.h[0:64, 0:256]))
                p.stt("dve", S_t[:], S_t[:], T_["Eend"][:, c:c + 1], V(b1, b1.h[:, 256:512]), ALU.mult, ALU.add)
                if full:
                    p.act(S_b[:], S_t[:], AF.Identity)
        if full:
            for (slot, r, t, S_t, S_b, hd, d, T_, b0, b1) in cs_:
                dst = M.gf_s if d == 0 else M.gb_s
                p.dma("sp", V(dst, dst.h[t, :, hd * 256:(hd + 1) * 256].rearrange("(c q) e -> q c e", q=64)), T_["osb"][:])

    def gla_pass(ctx, full):
        n = 2 if ctx else NT
        for k_ in range(n):
            for g0 in (0, 4):
                items = []
                for r in range(g0, g0 + 4):
                    d = r % 2
                    items.append((r % 4, r, tiles_for(d, ctx)[k_], St[r], Sb[r] if full else None))
                gla_group(items, full)

    p.push_scope()
    ld = [p.sb("ld%d" % i, [128, 1280], BF16) for i in range(2)]
    for t in range(NTA):
        a = ld[t % 2]
        p.dma("sp", a[:, 0:256], V(o16, o16.h[t, :, OK_:OK_ + 256]))
        p.dma("sp", a[:, 256:768], V(o16, o16.h[t, :, OKG:OKG + 512]))
        p.dma("sp", a[:, 768:1280], V(o16, o16.h[t, :, OQG:OQG + 512]))
        transpose_into(p, cx, a, kTl, t, ident, 2, src0=0)
        transpose_into(p, cx, a, gkT, t, ident, 4, src0=256)
        transpose_into(p, cx, a, gqT, t, ident, 4, src0=768)
        p.dma("sp", veb.k(("t", t), (slice(None), t, slice(None), slice(0, 128))),
              V(o16, o16.h[t, :, OV_:OV_ + 256].rearrange("p (k e) -> p k e", k=2)))
    chk(0.05)
    ksb = M.ksnd.h.bitcast(BF16)
    for kvh in range(2):
        p.dma("sp", V(M.ksnd, ksb[kvh * 128:(kvh + 1) * 128, :]), kTl[:, kvh, 0:TPC])
    NH2 = NT // 2
    for hf in range(2):
        vs_ = M.vsnd[hf]
        vsb = vs_.h.bitcast(BF16)
        p.dma("sp", V(vs_, vsb.rearrange("p (t k e) -> p t k e", t=NH2, k=2)), veb[:, hf * NH2:(hf + 1) * NH2, :, :])
    chk(0.1)
    yed = p.sb("yed", [32, 1024], BF16)
    p.memset("dve", yed[:], 0.0)
    p.dma("sp", yed[0:15, :], V(o16, o16.h[0, 0:15, OY:OY + 1024]))
    p.dma("sp", yed[15:30, :], V(o16, o16.h[NT - 1, 113:128, OY:OY + 1024]))
    p.dma("sp", V(M.ysnd, M.ysnd.h.bitcast(BF16)), yed[:])
    chk(0.15)
    p.collective("AllGather", [M.ksnd[:]], [M.krcv[:]], groups)
    for hf in range(2):
        p.collective("AllGather", [M.vsnd[hf][:]], [M.vrcv[hf][:]], groups)
    p.collective("AllGather", [M.ysnd[:]], [M.yrcv[:]], groups)
    chk(0.2)
    for i in range(8):
        p.memset("dve", St[i][:], 0.0)
        p.memset("dve", Sb[i][:], 0.0)
    gla_pass(True, True)
    for r in range(8):
        p.copy("dve", Sctx[r][:], St[r][:])
        p.memset("dve", St[r][:], 0.0)
    chk(0.3)
    gla_pass(False, False)
    for r in range(8):
        p.dma("sp", V(M.gsnd, M.gsnd.h[r * 128:(r + 1) * 128, 0:256]), St[r][:])
    chk(0.4)
    dst_ = p.sb("dstage", [128, 64], F32)
    p.memset("dve", dst_[:], 0.0)
    p.copy("dve", dst_[:, 0:8], logD[:])
    p.dma("sp", M.dsnd[:], dst_[:])
    p.collective("AllGather", [M.gsnd[:]], [M.grcv[:]], groups)
    p.collective("AllGather", [M.dsnd[:]], [M.drcv[:]], groups)
    Dall = p.sb("Dall", [128, 4, 64], F32)
    p.dma("sp", Dall[:], V(M.drcv, M.drcv.h.rearrange("(s q) c -> q s c", s=4)))
    chk(0.5)
    G = [p.sb("G%d" % i, [128, 4, 256], F32) for i in range(2)]
    coef = p.sb("coef", [128, 2], F32)
    gv = M.grcv.h.rearrange("(s r q) c -> q s r c", s=4, r=8)
    for r in range(8):
        d = r % 2
        Gt = G[r % 2]
        p.dma("sp", Gt[:], V(M.grcv, gv[:, :, r, :]))
        p.copy("dve", St[r][:], Sctx[r][:])
        for s_ in (range(4) if d == 0 else range(3, -1, -1)):
            m = masks[:, d * 4 + s_:d * 4 + s_ + 1]
            p.ts("dve", coef[:, 1:2], Dall[:, s_, r:r + 1], -1.0, m, ALU.add, ALU.mult)
            p.ts("dve", coef[:, 1:2], coef[:, 1:2], 1.0, None, ALU.add)
            p.ts("dve", St[r][:], St[r][:], coef[:, 1:2], None, ALU.mult)
            p.stt("dve", St[r][:], Gt[:, s_, 0:256], m, St[r][:], ALU.mult, ALU.add)
        p.act(Sb[r][:], St[r][:], AF.Identity)
    p.pop_scope()

    if getattr(M, "stop_after", 3) < 2:
        p.pop_scope()
        return
    p.push_scope()
    yT = p.sb("yT", [128, 8, TPC + 30], BF16)
    yTc = p.sb("yTc", [128, 8, CTX + 30], BF16)
    p.memset("dve", yTc[:], 0.0)
    ld = [p.sb("ldy%d" % i, [128, 1024], BF16) for i in range(2)]
    for t in range(NTA):
        a = ld[t % 2]
        p.dma("sp", a[:], V(o16, o16.h[t, :, OY:OY + 1024]))
        if t < NT:
            transpose_into(p, cx, a, yT, t, ident, 8, col0=15 + t * 128)
        else:
            transpose_into(p, cx, a, yTc, t, ident, 8, col0=15 + (t - NT) * 128)
    E = p.sb("E", [128, 1024], BF16)
    p.dma("sp", E[:], V(M.yrcv, M.yrcv.h.bitcast(BF16)))
    selm = p.sb("selm", [128, 30], BF16)
    p.dma("pool", selm[:], M.selm_d[:])
    for c in range(8):
        ps = cx.bank()
        p.mm(ps[:, 0:30], [(E[:, c * 128:(c + 1) * 128], selm[:])])
        p.act(yT.k(("hl", c), (slice(None), c, slice(0, 15))), ps[:, 0:15], AF.Identity)
        p.act(yT.k(("hr", c), (slice(None), c, slice(15 + TPC, 30 + TPC))), ps[:, 15:30], AF.Identity)
    cwl = p.sb("cwl", [32, 1024], F32)
    p.dma("sp", cwl[0:31, :], M.conv_w[:])
    p.dma("sp", cwl[31:32, :], M.conv_b[:])
    cwt = p.sb("cwt", [128, 8, 32], F32)
    for c in range(8):
        ps = cx.bank()
        p.tr(ps[:, 0:32], cwl[:, c * 128:(c + 1) * 128], identf[0:32, 0:32])
        p.act(cwt[:, c, :], ps[:, 0:32], AF.Identity)
    accs = [p.sb("cacc%d" % i, [128, TPC], F32) for i in range(2)]
    accc = [p.sb("caccc%d" % i, [128, CTX], F32) for i in range(2)]
    cvst = [p.sb("cvst%d" % i, [128, 4, 128], F32) for i in range(2)]
    ci = 0
    for c in range(8):
        for (a, ysrc, n, t0) in ((accs[c % 2], yT, TPC, 0), (accc[c % 2], yTc, CTX, NT)):
            for tap in range(CW):
                yv = V(ysrc, ysrc.h[:, c, tap:tap + n])
                if tap == 0:
                    p.ts("dve", a[:], yv, cwt[:, c, 0:1], None, ALU.mult)
                else:
                    p.stt("dve", a[:], yv, cwt[:, c, tap:tap + 1], a[:], ALU.mult, ALU.add)
            p.ts("dve", a[:], a[:], cwt[:, c, 31:32], None, ALU.add)
            for g0 in range(0, n // 128, 4):
                ng = min(4, n // 128 - g0)
                ps = cx.bank()
                for k_ in range(ng):
                    p.tr(ps[:, k_ * 128:(k_ + 1) * 128], a[:, (g0 + k_) * 128:(g0 + k_ + 1) * 128], identf[:])
                st_ = cvst[ci % 2]
                ci += 1
                p.act(V(st_, st_.h[:, 0:ng, :]), V(ps, ps.h[:, 0:ng * 128].rearrange("q (k f) -> q k f", f=128)), AF.Identity)
                p.dma("sp", V(M.cv_s, M.cv_s.h[t0 + g0:t0 + g0 + ng, :, c * 128:(c + 1) * 128].rearrange("t q f -> q t f")),
                      V(st_, st_.h[:, 0:ng, :]))
    p.pop_scope()

    if getattr(M, "stop_after", 3) < 3:
        p.pop_scope()
        return
    p.push_scope()
    qT = p.sb("qT", [128, 8, NTOK], BF16)
    ld = [p.sb("ldq%d" % i, [128, 1024], BF16) for i in range(2)]
    for t in range(NTA):
        a = ld[t % 2]
        p.dma("sp", a[:], V(o16, o16.h[t, :, OQ:OQ + 1024]))
        transpose_into(p, cx, a, qT, t, ident, 8)
    kT = p.sb("kT", [128, L], BF16)
    vE = p.sb("vE", [128, NKT, 129], BF16)
    pTs = [p.sb("pTs%d" % i, [128, 512], BF16) for i in range(6)]
    rcp = p.sb("rcp", [128, 4], F32)
    ost = [p.sb("ost%d" % i, [128, 128], BF16) for i in range(4)]
    SC = HD ** -0.5
    krb = M.krcv.h.bitcast(BF16)
    NH2 = NT // 2
    vrb = [M.vrcv[hf].h.bitcast(BF16) for hf in range(2)]
    state = {"pi": 0, "sb": 0}

    def load_kv(kvh):
        for s_ in range(SEGS):
            p.dma("sp", kT[:, s_ * TPC:(s_ + 1) * TPC], V(M.krcv, krb[s_ * 256 + kvh * 128:s_ * 256 + (kvh + 1) * 128, :]))
            for hf in range(2):
                p.dma("sp", vE[:, s_ * NT + hf * NH2:s_ * NT + (hf + 1) * NH2, :],
                      V(M.vrcv[hf], vrb[hf][s_ * 128:(s_ + 1) * 128, :].rearrange("q (t k e) -> q t k e", t=NH2, k=2)[:, :, kvh, :]))
        p.act(kT[:, Sfull:L], kTl[:, kvh, TPC:NTOK], AF.Identity)
        p.copy("dve", vE[:, SEGS * NT:NKT, :], veb[:, NT:NTA, kvh, :])

    def att_block(blk):
        h, q0, nq, kt0, kt1 = blk
        nsub = nq // 128
        LOOK = 2
        pend = []

        def pv(kt, pt):
            def emit(e):
                ins = None
                for qs in range(nsub):
                    ob = banks[qs // 2]
                    ins = e.matmul(ob.h[:, (qs % 2) * 256:(qs % 2) * 256 + 129], pt.h[:, qs * 128:(qs + 1) * 128],
                                   vE.h[:, kt, :], start=(kt == kt0), stop=(kt == kt1 - 1))
                return ins
            writes = [V(banks[qs // 2], banks[qs // 2].h[:, (qs % 2) * 256:(qs % 2) * 256 + 129], ("o", qs % 2)) for qs in range(nsub)]
            p.op("pe", [pt[:, 0:nq], vE[:, kt, :]], writes, emit)

        for kt in range(kt0, kt1):
            sbk = banks[2 + state["sb"] % 6]
            state["sb"] += 1
            p.mm(sbk[:, 0:nq], [(kT[:, kt * 128:(kt + 1) * 128], qT[:, h, q0:q0 + nq])])
            pt = pTs[state["pi"] % 6]
            state["pi"] += 1
            p.act(pt[:, 0:nq], sbk[:, 0:nq], AF.Exp, scale=SC)
            pend.append((kt, pt))
            if len(pend) > LOOK:
                pv(*pend.pop(0))
        while pend:
            pv(*pend.pop(0))
        for qs in range(nsub):
            ob = banks[qs // 2]
            base = (qs % 2) * 256
            okey = V(ob, ob.h[:, base:base + 129], ("o", qs % 2))
            p.op("dve", [okey], [rcp[:, qs:qs + 1]],
                 lambda e, ob=ob, base=base, qs=qs: e.reciprocal(rcp.h[:, qs:qs + 1], ob.h[:, base + 128:base + 129]))
            p.ts("dve", ost[qs][:], V(ob, ob.h[:, base:base + 128], ("o", qs % 2)), rcp[:, qs:qs + 1], None, ALU.mult)
            p.dma("sp", V(M.att_s, M.att_s.h[(q0 // 128) + qs, :, h * 128:(h + 1) * 128]), ost[qs][:])

    for kvh in range(2):
        load_kv(kvh)
        for hh in range(4):
            h = kvh * 4 + hh
            for q0 in range(0, TPC, 512):
                att_block((h, q0, min(512, TPC - q0), 0, NKT))
            att_block((h, TPC, CTX, NKT - CTX // 128, NKT))
    gla_pass(False, True)
    p.pop_scope()
    p.pop_scope()


def build_fused(NT=16):
    NCT = 2
    NTA = NT + NCT
    TPC = NT * 128
    nc = bass.Bass("TRN2", target_bir_lowering=False)
    es = ExitStack()
    with es:
        p = Prog(nc, es)
        cx = Ctx(p)
        X = lambda n, shp: p.dram(n, shp, F32, "ExternalInput")
        x_d = X("x", [NT, 128, D])
        ctx_d = X("ctxb", [NCT, 128, D])
        cvT = X("cvT", [128, 2, 8])
        cos_d = X("cos", [128, NT, 64])
        sin_d = X("sin", [128, NT, 64])
        ident_d = X("ident", [128, 128])
        gcst_d = X("gcst", [128, 514])
        masks_d = X("masks", [128, 8])
        selm_d = X("selm", [128, 30])
        W = {}
        for n, shp in (("w_ada", [DEPTH, D, 6 * D]), ("b_ada", [DEPTH, 6 * D]), ("w_in", [DEPTH, D, INC]),
                       ("q_norm", [DEPTH, HD]), ("k_norm", [DEPTH, HD]), ("w_att_o", [DEPTH, D, D]),
                       ("gla_w_a2", [DEPTH, 2, RANK, 512]), ("gla_b_a", [DEPTH, 2 * 512]), ("gla_norm", [DEPTH, GDV]),
                       ("w_gla_o", [DEPTH, D, D]), ("conv_w_dw", [DEPTH, CW, D]), ("conv_b_dw", [DEPTH, 1, D]),
                       ("conv_ln_g", [DEPTH, D]), ("conv_ln_b", [DEPTH, D]), ("w_conv_o", [DEPTH, D, D]),
                       ("w_out", [DEPTH, D, D]), ("ln1_g", [DEPTH, D]), ("ln1_b", [DEPTH, D]),
                       ("w_ff_gate", [DEPTH, D, DFF]), ("w_ff_up", [DEPTH, D, DFF]), ("w_ff_down", [DEPTH, DFF, D]),
                       ("ln2_g", [DEPTH, D]), ("ln2_b", [DEPTH, D])):
            W[n] = X(n, shp)
        out = p.dram("out", [NT, 128, D], F32, "ExternalOutput")
        I = lambda n, shp, dt: p.dram(n, shp, dt, "Internal")
        o16 = I("o16", [NTA, 128, C16], BF16)
        o32 = I("o32", [NTA, 128, 1024], F32)
        M = NS()
        M.o16, M.o32 = o16, o32
        M.att_s = I("att_s", [NTA, 128, D], BF16)
        M.gf_s = I("gf_s", [NTA, 128, D], F32)
        M.gb_s = I("gb_s", [NTA, 128, D], F32)
        M.cv_s = I("cv_s", [NTA, 128, D], F32)
        act_s = I("act_s", [NTA, 128, DFF // 128, 128], BF16)
        xa = I("xa", [NTA, 128, D], F32)
        xb = I("xb", [NTA, 128, D], F32)
        M.ksnd = I("ksnd", [256, TPC // 2], F32)
        M.krcv = I("krcv", [1024, TPC // 2], F32)
        M.vsnd = [I("vsnd%d" % i, [128, NT // 2 * 129], F32) for i in range(2)]
        M.vrcv = [I("vrcv%d" % i, [512, NT // 2 * 129], F32) for i in range(2)]
        M.gsnd = I("gsnd", [1024, 256], F32)
        M.grcv = I("grcv", [4096, 256], F32)
        M.dsnd = I("dsnd", [128, 64], F32)
        M.drcv = I("drcv", [512, 64], F32)
        M.ysnd = I("ysnd", [32, 512], F32)
        M.yrcv = I("yrcv", [128, 512], F32)
        M.ident_d, M.gcst_d, M.masks_d, M.selm_d = ident_d, gcst_d, masks_d, selm_d

        def lw(n, l):
            return T(p, W[n].h[l], "%s_l%d" % (n, l))

        for l in range(DEPTH):
            last = l == DEPTH - 1
            if l == 0:
                xs_fn = lambda t: (x_d[t] if t < NT else ctx_d[t - NT])
            else:
                xs_fn = lambda t: xb[t]

            class XS:
                def __getitem__(self, t):
                    return xs_fn(t)
            p.push_scope()
            wa2 = lw("gla_w_a2", l)
            emit_p1(p, cx, NT, NCT, XS(), cvT, lw("w_ada", l), lw("b_ada", l), lw("w_in", l), lw("q_norm", l),
                    lw("k_norm", l), wa2[0], wa2[1], lw("gla_b_a", l), cos_d, sin_d, ident_d, o16, o32, True)
            p.pop_scope()
            M.conv_w = lw("conv_w_dw", l)
            M.conv_b = lw("conv_b_dw", l)
            emit_mix_fused(p, cx, NT, M)
            nct = 0 if last else NCT
            A = NS()
            A.oatt = lambda t: M.att_s[t]
            A.ogf = lambda t: M.gf_s[t]
            A.ogb = lambda t: M.gb_s[t]
            A.sr = lambda t: V(o16, o16.h[t, :, OR:OR + 1024])
            A.cv = lambda t: M.cv_s[t]
            A.xs = xs_fn
            A.xo = lambda t: xa[t]
            A.gt = lambda t, c0, c1: V(o16, o16.h[t, :, OGT + c0:OGT + c1])
            A.g1_mi = 2
            A.ident_d, A.cvT, A.w_ada, A.b_ada = ident_d, cvT, lw("w_ada", l), lw("b_ada", l)
            A.vecs = {n: lw(n, l) for n in ("gla_norm", "conv_ln_g", "conv_ln_b", "ln1_g", "ln1_b")}
            A.wo = {n: lw(n, l) for n in ("w_att_o", "w_gla_o", "w_conv_o", "w_out")}
            p.push_scope()
            emit_posta(p, cx, NT, nct, A)
            p.pop_scope()
            Bn = NS()
            Bn.xs = lambda t: xa[t]
            Bn.xo = (lambda t: out[t]) if last else (lambda t: xb[t])
            Bn.mi0 = 3
            Bn.ident_d, Bn.cvT, Bn.w_ada, Bn.b_ada = ident_d, cvT, lw("w_ada", l), lw("b_ada", l)
            Bn.l2g_d, Bn.l2b_d = lw("ln2_g", l), lw("ln2_b", l)
            Bn.wg_d, Bn.wu_d, Bn.wd_d = lw("w_ff_gate", l), lw("w_ff_up", l), lw("w_ff_down", l)
            Bn.act_s = act_s
            p.push_scope()
            emit_postb2(p, cx, NT, nct, Bn)
            p.pop_scope()
        p.wait_all("sp", [out[:]])
        print("FUSED ops", p.n_ops, "waits", p.n_waits)
    return nc


def kernel_fused(P, S):
    TPC = S * B // NCORES
    NT = TPC // 128
    cos, sin = _rope_tables(S)
    gc = _gla_consts()
    Lm, Um, sel = gc[:, 0:128], gc[:, 128:256], gc[:, 256:258]
    gcst = np.ascontiguousarray(np.concatenate([Lm, Um, Lm.T, Um.T, sel], axis=1))
    ident = np.eye(128, dtype=np.float32)
    in_maps = []
    for core in range(NCORES):
        b, seg = core // 4, core % 4
        masks = np.zeros((128, 8), np.float32)
        for s_ in range(4):
            masks[:, s_] = 1.0 if s_ < seg else 0.0
            masks[:, 4 + s_] = 1.0 if s_ > seg else 0.0
        selm = np.zeros((128, 30), np.float32)
        for j in range(15):
            if seg > 0:
                selm[(seg - 1) * 32 + 15 + j, j] = 1.0
            if seg < 3:
                selm[(seg + 1) * 32 + j, 15 + j] = 1.0
        m = {
            "x": np.ascontiguousarray(P["x"][b, seg * TPC:(seg + 1) * TPC].reshape(NT, 128, D)),
            "ctxb": np.ascontiguousarray(P["ctx"][b].reshape(2, 128, D)),
            "cvT": np.ascontiguousarray(np.stack([P["c"][b], P["c_ctx"]]).reshape(2, 8, 128).transpose(2, 0, 1)),
            "cos": np.ascontiguousarray(cos[seg * TPC:(seg + 1) * TPC].reshape(NT, 128, 64).transpose(1, 0, 2)),
            "sin": np.ascontiguousarray(sin[seg * TPC:(seg + 1) * TPC].reshape(NT, 128, 64).transpose(1, 0, 2)),
            "ident": ident, "gcst": gcst, "masks": masks, "selm": selm,
        }
        for n in ("w_ada", "b_ada", "w_in", "q_norm", "k_norm", "w_att_o", "gla_w_a2", "gla_norm", "w_gla_o",
                  "conv_ln_g", "conv_ln_b", "w_conv_o", "w_out", "ln1_g", "ln1_b", "w_ff_gate", "w_ff_up",
                  "w_ff_down", "ln2_g", "ln2_b"):
            m[n] = P[n]
        m["gla_b_a"] = np.ascontiguousarray(P["gla_b_a"].reshape(DEPTH, 1024))
        m["conv_w_dw"] = np.ascontiguousarray(P["conv_w_dw"].reshape(DEPTH, CW, D))
        m["conv_b_dw"] = np.ascontiguousarray(P["conv_b_dw"].reshape(DEPTH, 1, D))
        in_maps.append(m)
    res = _run(("fused", NT), lambda: build_fused(NT), in_maps)
    out = np.stack([r["out"] for r in res])
    return np.ascontiguousarray(out.reshape(B, S, D))


def kernel(**inputs):
    P = {k: np.asarray(v) for k, v in inputs.items()}
    S = P["x"].shape[1]
    return kernel_fused(P, S).astype(np.float32)
```

```python
import numpy as np
from contextlib import ExitStack
import concourse.bass as bass
import concourse.mybir as mybir
from concourse.bass_utils import run_bass_kernel_spmd

F32 = mybir.dt.float32
BF16 = mybir.dt.bfloat16
AF = mybir.ActivationFunctionType
ALU = mybir.AluOpType
AX = mybir.AxisListType


class V:
    def __init__(self, t, ap, key=None):
        self.t = t
        self.ap = ap
        self.key = key


class T:
    def __init__(self, prog, handle, name):
        self.p = prog
        self.h = handle
        self.name = name
        self.state = {}

    def __getitem__(self, idx):
        return V(self, self.h[idx], None)

    def k(self, key, idx=None):
        if idx is None:
            return V(self, self.h[:], key)
        return V(self, self.h[idx], key)

    def _keys(self, key):
        if key is None:
            return list(self.state.keys())
        ks = [key]
        if None in self.state:
            ks.append(None)
        return [k for k in ks if k in self.state]

    def deps_read(self, key):
        out = []
        for k in self._keys(key):
            w = self.state[k][0]
            if w is not None:
                out.append(w)
        return out

    def deps_write(self, key):
        out = []
        for k in self._keys(key):
            w, rs = self.state[k]
            if w is not None:
                out.append(w)
            out.extend(rs)
        return out

    def add_reader(self, key, tok):
        st = self.state.setdefault(key, [None, []])
        st[1].append(tok)
        if len(st[1]) > 64:
            best = {}
            for (s, v) in st[1]:
                best[s] = max(best.get(s, 0), v)
            st[1] = list(best.items())

    def set_writer(self, key, tok):
        if key is None:
            self.state = {None: [tok, []]}
        else:
            self.state[key] = [tok, []]


class Prog:
    ENG = ("pe", "dve", "act", "pool", "sp")

    def __init__(self, nc, es, n_dma_sems=16):
        self.nc = nc
        self.es = es
        self.engs = {"pe": nc.tensor, "dve": nc.vector, "act": nc.scalar,
                     "pool": nc.gpsimd, "sp": nc.sync}
        self.sems = []
        self.esem = {}
        self.cnt = {}
        for e in self.ENG:
            self.esem[e] = self._new_sem("prog_" + e)
            self.cnt[e] = 0
        self.dsem = {}
        self.dcnt = {}
        self.dnext = {}
        for q in ("sp", "pool", "act"):
            self.dsem[q] = [self._new_sem("dma_%s_%d" % (q, i)) for i in range(n_dma_sems)]
            self.dcnt[q] = [0] * n_dma_sems
            self.dnext[q] = 0
        self.waited = {e: {} for e in self.ENG}
        self.n_ops = 0
        self.n_waits = 0
        self.uid = 0

    def _new_sem(self, name):
        h = self.es.enter_context(self.nc.semaphore(name))
        self.sems.append(h)
        return len(self.sems) - 1

    def push_scope(self):
        if not hasattr(self, "scopes"):
            self.scopes = []
            self.scope_id = 0
        st = ExitStack()
        st.__enter__()
        self.scopes.append(st)
        self.scope_id += 1

    def pop_scope(self):
        self.barrier()
        st = self.scopes.pop()
        st.__exit__(None, None, None)

    def sb(self, name, shape, dtype):
        scopes = getattr(self, "scopes", [])
        es = scopes[-1] if scopes else self.es
        sid = getattr(self, "scope_id", 0)
        h = es.enter_context(self.nc.sbuf_tensor("s%d_%s" % (sid, name), list(shape), dtype))
        return T(self, h, name)

    def ps(self, name, shape, dtype):
        h = self.es.enter_context(self.nc.psum_tensor("p_" + name, list(shape), dtype))
        return T(self, h, name)

    def dram(self, name, shape, dtype, kind):
        h = self.nc.dram_tensor(name, list(shape), dtype, kind=kind)
        return T(self, h.ap(), name)

    def _wait(self, eng, toks):
        best = {}
        for (s, v) in toks:
            if v <= 0:
                continue
            if eng == "pe" and s == self.esem["pe"]:
                continue
            if v > best.get(s, 0):
                best[s] = v
        w = self.waited[eng]
        e = self.engs[eng]
        for s, v in best.items():
            if w.get(s, 0) >= v:
                continue
            e.wait_ge(self.sems[s], v)
            w[s] = v
            self.n_waits += 1

    def op(self, eng, reads, writes, emit):
        toks = []
        for v in reads:
            toks += v.t.deps_read(v.key)
        for v in writes:
            toks += v.t.deps_write(v.key)
        self._wait(eng, toks)
        ins = emit(self.engs[eng])
        self.cnt[eng] += 1
        ins.then_inc(self.sems[self.esem[eng]], 1)
        tok = (self.esem[eng], self.cnt[eng])
        for v in reads:
            v.t.add_reader(v.key, tok)
        for v in writes:
            v.t.set_writer(v.key, tok)
        self.n_ops += 1
        return tok

    def dma(self, q, out, in_, **kw):
        toks = in_.t.deps_read(in_.key) + out.t.deps_write(out.key)
        i = self.dnext[q]
        self.dnext[q] = (i + 1) % len(self.dsem[q])
        s = self.dsem[q][i]
        toks.append((s, self.dcnt[q][i]))
        self._wait(q, toks)
        ins = self.engs[q].dma_start(out=out.ap, in_=in_.ap, **kw)
        self.dcnt[q][i] += 16
        ins.then_inc(self.sems[s], 16)
        tok = (s, self.dcnt[q][i])
        in_.t.add_reader(in_.key, tok)
        out.t.set_writer(out.key, tok)
        self.n_ops += 1
        return tok

    def collective(self, kind, ins, outs, groups):
        q = "pool"
        if not hasattr(self, "ccsem"):
            self.ccsem = self._new_sem("cc_sem")
            self.cccnt = 0
        toks = []
        for v in ins:
            toks += v.t.deps_read(v.key)
        for v in outs:
            toks += v.t.deps_write(v.key)
        toks.append((self.ccsem, self.cccnt))
        self._wait(q, toks)
        ins_ = self.engs[q].collective_compute(kind, ALU.bypass, groups, [v.ap for v in ins], [v.ap for v in outs])
        self.cccnt += 1
        ins_.then_inc(self.sems[self.ccsem], 1)
        tok = (self.ccsem, self.cccnt)
        for v in ins:
            v.t.add_reader(v.key, tok)
        for v in outs:
            v.t.set_writer(v.key, tok)
        self.n_ops += 1
        return tok

    def barrier(self):
        for e in self.ENG:
            toks = [(self.esem[f], self.cnt[f]) for f in self.ENG if f != e]
            for q in self.dsem:
                toks += [(s, c) for s, c in zip(self.dsem[q], self.dcnt[q])]
            if hasattr(self, "ccsem"):
                toks.append((self.ccsem, self.cccnt))
            self._wait(e, toks)

    def wait_all(self, eng, views):
        toks = []
        for v in views:
            toks += v.t.deps_read(v.key)
        self._wait(eng, toks)

    def mm(self, out, pairs, reads_extra=()):
        reads = []
        for (l, r) in pairs:
            reads += [l, r]
        n = len(pairs)

        def emit(e):
            ins = None
            for i, (l, r) in enumerate(pairs):
                ins = e.matmul(out.ap, l.ap, r.ap, start=(i == 0), stop=(i == n - 1))
            return ins
        return self.op("pe", reads, [out], emit)

    def mm1(self, out, l, r, start, stop):
        return self.op("pe", [l, r], [out],
                       lambda e: e.matmul(out.ap, l.ap, r.ap, start=start, stop=stop))

    def tr(self, out, in_, ident):
        return self.op("pe", [in_, ident], [out],
                       lambda e: e.transpose(out.ap, in_.ap, ident.ap))

    def act(self, out, in_, func, bias=None, scale=None, accum=None, eng="act"):
        reads = [in_]
        kw = {}
        if bias is not None:
            if isinstance(bias, V):
                reads.append(bias)
                kw["bias"] = bias.ap
            else:
                kw["bias"] = bias
        if scale is not None:
            if isinstance(scale, V):
                reads.append(scale)
                kw["scale"] = scale.ap
            else:
                kw["scale"] = scale
        writes = [out]
        if accum is not None:
            writes.append(accum)
            kw["accum_out"] = accum.ap
        return self.op(eng, reads, writes,
                       lambda e: e.activation(out.ap, in_.ap, func, **kw))

    def tt(self, eng, out, a, b, op):
        return self.op(eng, [a, b], [out],
                       lambda e: e.tensor_tensor(out.ap, a.ap, b.ap, op))

    def ts(self, eng, out, a, s1, s2, op0, op1=None, accum=None):
        reads = [a]
        s1a = s1.ap if isinstance(s1, V) else s1
        s2a = s2.ap if isinstance(s2, V) else s2
        if isinstance(s1, V):
            reads.append(s1)
        if isinstance(s2, V):
            reads.append(s2)
        writes = [out]
        kw = {}
        if accum is not None:
            writes.append(accum)
            kw["accum_out"] = accum.ap
        if op1 is None:
            return self.op(eng, reads, writes,
                           lambda e: e.tensor_scalar(out.ap, a.ap, s1a, None, op0, **kw))
        return self.op(eng, reads, writes,
                       lambda e: e.tensor_scalar(out.ap, a.ap, s1a, s2a, op0, op1, **kw))

    def stt(self, eng, out, a, s, b, op0, op1):
        reads = [a, b]
        sa = s.ap if isinstance(s, V) else s
        if isinstance(s, V):
            reads.append(s)
        return self.op(eng, reads, [out],
                       lambda e: e.scalar_tensor_tensor(out.ap, a.ap, sa, b.ap, op0, op1))

    def copy(self, eng, out, in_):
        if eng == "act":
            return self.op(eng, [in_], [out], lambda e: e.copy(out.ap, in_.ap))
        return self.op(eng, [in_], [out], lambda e: e.tensor_copy(out.ap, in_.ap))

    def memset(self, eng, out, val):
        return self.op(eng, [], [out], lambda e: e.memset(out.ap, val))


D = 1024
B = 2
GRID_W = 64
CTX = 256
DEPTH = 2
HD = 128
NH = 8
NKV = 2
GH = 4
GDK = 128
GDV = 256
RANK = 16
TAU = 16.0
CHUNK = 64
CW = 31
DFF = 2816
ALPHA = (2.0 * DEPTH) ** 0.25
EPS = 1e-6
NCORES = 8
INC = 9760
OK_, OV_, OKG, OVG, OQ, OQG, OR, OY, OGT, C16 = 0, 256, 512, 1024, 2048, 3072, 3584, 4608, 5632, 8704


class Ctx:
    def __init__(self, p):
        self.p = p
        self.banks = [p.ps("bank%d" % i, [128, 512], F32) for i in range(8)]
        self.bi = 0
        self.uid = 0

    def bank(self):
        n = getattr(self, "nrot", 8)
        b = self.banks[self.bi % n]
        self.bi += 1
        return b

    def name(self, s):
        self.uid += 1
        return "%s_%d" % (s, self.uid)


def bcast_mid(ap, n):
    a = ap.ap
    return bass.AP(ap.tensor, ap.offset, [list(a[0]), [0, n]] + [list(x) for x in a[1:]])


def load_const_eps(p, cx):
    eps = p.sb("eps_c", [128, 1], F32)
    p.memset("dve", eps[:], EPS)
    one = p.sb("one_c", [128, 1], F32)
    p.memset("dve", one[:], 1.0)
    cx.eps = eps
    cx.one = one


def ln_stats(p, cx, x, width, mean_rstd):
    nchunk = width // 512
    bn = cx.bn
    xr = x.ap.rearrange("p (c f) -> p c f", f=512)
    for c in range(nchunk):
        p.op("dve", [x], [bn.k(c, (slice(None), c, slice(None)))],
             lambda e, c=c: e.bn_stats(bn.h[:, c, :], xr[:, c, :]))
    p.op("dve", [bn[:, 0:nchunk, :]], [mean_rstd],
         lambda e: e.bn_aggr(mean_rstd.ap, bn.h[:, 0:nchunk, :]))
    r = V(mean_rstd.t, mean_rstd.ap[:, 1:2], mean_rstd.key)
    p.act(r, r, AF.Sqrt, bias=cx.eps[:, 0:1])
    p.op("dve", [r], [r], lambda e: e.reciprocal(r.ap, r.ap))


def mod_tiles(p, cx, scb, w_ada, b_ada, specs):
    wv = w_ada.h.rearrange("(kc p) c -> p kc c", p=128)
    mis = []
    for sp_ in specs:
        if sp_[1] not in mis:
            mis.append(sp_[1])
    for mi in mis:
        for half in range(2):
            c0 = mi * 1024 + half * 512
            wt = cx.wts[cx.wi % 2]
            cx.wi += 1
            p.dma("pool", wt[:], V(w_ada, wv[:, :, c0:c0 + 512]))
            bb = cx.bbc
            p.dma("sp", bb[:, 0:512], V(b_ada, b_ada.h[c0:c0 + 512].partition_broadcast(128)))
            for (vi, mi_, plus1, out) in specs:
                if mi_ != mi:
                    continue
                ps = cx.bank()
                p.mm(ps[:], [(scb[:, vi, kc, :], wt[:, kc, :]) for kc in range(8)])
                o = out[:, half * 512:(half + 1) * 512]
                if plus1:
                    p.stt("dve", o, ps[:], 1.0, bb[:, 0:512], ALU.add, ALU.add)
                else:
                    p.tt("dve", o, ps[:], bb[:, 0:512], ALU.add)


def make_scb(p, cx, cvT, nvec):
    cv = p.sb("cv", [128, nvec * 8], F32)
    p.dma("sp", cv[:], V(cvT, cvT.h.rearrange("p v k -> p (v k)")))
    p.act(cv[:], cv[:], AF.Silu)
    ones = p.sb("ones_bf", [128, 128], BF16)
    p.memset("dve", ones[:], 1.0)
    scb = p.sb("scb", [128, nvec, 8, 128], BF16)
    for v in range(nvec):
        for k in range(8):
            j = v * 8 + k
            p.ts("dve", scb[:, v, k, :], ones[:], cv[:, j:j + 1], None, ALU.mult)
    return scb


def ln_mod_transpose(p, cx, xsrc, t, scp, sh, hT, ident):
    xt = cx.xt[t % 2]
    p.dma("sp", xt[:], xsrc)
    mr = cx.mr[t % 2]
    ln_stats(p, cx, xt[:], 1024, mr[:])
    xn = cx.xn
    p.ts("dve", xn[:], xt[:], mr[:, 0:1], mr[:, 1:2], ALU.subtract, ALU.mult)
    p.tt("pool", xn[:], xn[:], scp[:], ALU.mult)
    hb = cx.hb[t % 2]
    p.tt("dve", hb[:], xn[:], sh[:], ALU.add)
    transpose_into(p, cx, hb, hT, t, ident, 8)


def transpose_into(p, cx, src, dstT, t, ident, nk, eng="act", col0=None, src0=0):
    for g0 in range(0, nk, 8):
        n = min(8, nk - g0)
        ps = cx.bank()
        pv = ps.h[:].bitcast(BF16)
        for k in range(n):
            p.tr(V(ps, pv[:, k * 128:(k + 1) * 128]), src[:, src0 + (g0 + k) * 128:src0 + (g0 + k + 1) * 128], ident[:])
        inv = V(ps, pv[:, 0:n * 128].rearrange("p (k f) -> p k f", f=128))
        c0_ = t * 128 if col0 is None else col0
        outv = dstT.k(("t", t), (slice(None), slice(g0, g0 + n), slice(c0_, c0_ + 128)))
        if eng == "act":
            p.act(outv, inv, AF.Identity)
        else:
            p.copy(eng, outv, inv)


def emit_p1(p, cx, NT, NCT, xs, cvT, w_ada, b_ada, w_in, qn_d, kn_d, wa2_0, wa2_1, ba, cos_d, sin_d,
            ident_d, o16, o32, layer_has_rope=True):
    NTA = NT + NCT
    load_const_eps(p, cx)
    cx.bn = p.sb("bn", [128, 2, 6], F32)
    cx.wts = [p.sb("wt%d" % i, [128, 8, 512], BF16) for i in range(2)]
    cx.wi = 0
    cx.bbc = p.sb("bbc", [128, 1024], F32)
    cx.xt = [p.sb("xt%d" % i, [128, D], F32) for i in range(2)]
    cx.mr = [p.sb("mr%d" % i, [128, 2], F32) for i in range(2)]
    cx.xn = p.sb("xn", [128, D], F32)
    cx.hb = [p.sb("hb%d" % i, [128, D], BF16) for i in range(2)]
    ident = p.sb("ident_sb", [128, 128], BF16)
    p.dma("pool", ident[:], ident_d[:])
    cos = p.sb("cos_sb", [128, NT, 64], F32)
    sin = p.sb("sin_sb", [128, NT, 64], F32)
    p.dma("sp", cos[:], cos_d[:])
    p.dma("sp", sin[:], sin_d[:])
    gq = p.sb("gq", [128, 128], F32)
    gk = p.sb("gk", [128, 128], F32)
    p.dma("sp", gq[:], V(qn_d, qn_d.h.partition_broadcast(128)))
    p.dma("sp", gk[:], V(kn_d, kn_d.h.partition_broadcast(128)))
    babc = p.sb("babc", [128, 1024], F32)
    p.dma("sp", babc[:], V(ba, ba.h.partition_broadcast(128)))
    w2bd = p.sb("w2bd", [32, 1024], BF16)
    p.memset("dve", w2bd[:], 0.0)
    p.dma("pool", w2bd[0:16, 0:512], wa2_0)
    p.dma("pool", w2bd[16:32, 512:1024], wa2_1)

    scb = make_scb(p, cx, cvT, 2)
    mods = {}
    for nm in ("shL", "scL", "shC", "scC"):
        mods[nm] = p.sb(nm, [128, D], F32)
    mod_tiles(p, cx, scb, w_ada, b_ada,
              [(0, 0, False, mods["shL"]), (0, 1, True, mods["scL"]),
               (1, 0, False, mods["shC"]), (1, 1, True, mods["scC"])])

    hT = p.sb("hT", [128, 8, NTA * 128], BF16)
    for t in range(NTA):
        lat = t < NT
        ln_mod_transpose(p, cx, xs[t], t, mods["scL" if lat else "scC"],
                         mods["shL" if lat else "shC"], hT, ident)

    wv = w_in.h.rearrange("(kc p) c -> p kc c", p=128)
    stg = [p.sb("stg%d" % i, [128, 512], BF16) for i in range(3)]
    stg32 = [p.sb("stgf%d" % i, [128, 512], F32) for i in range(2)]
    sq = p.sb("sq", [128, 512], F32)
    ss = p.sb("ss", [128, 4], F32)
    qn = p.sb("qn", [128, 512], F32)
    rt = [p.sb("rt%d" % i, [128, 4, 64], F32) for i in range(4)]
    si = [0]

    def load_w(col_ranges):
        wt = cx.wts[cx.wi % 2]
        cx.wi += 1
        o = 0
        for (c0, wd) in col_ranges:
            p.dma("pool", wt[:, :, o:o + wd], V(w_in, wv[:, :, c0:c0 + wd]))
            o += wd
        return wt, o

    def proj(t, wt, width):
        ps = cx.bank()
        p.mm(ps[:, 0:width], [(hT.k(("t", t), (slice(None), kc, slice(t * 128, (t + 1) * 128))),
                               wt[:, kc, 0:width]) for kc in range(8)])
        return ps

    def out16(t, st, col, width):
        p.dma("sp", V(o16, o16.h[t, :, col:col + width]), st[:, 0:width])

    def next_stg():
        s = stg[si[0] % 3]
        si[0] += 1
        return s

    def epi_simple(func, scale=None):
        def f(t, ps, width, col):
            st = next_stg()
            p.act(st[:, 0:width], ps[:, 0:width], func, scale=scale)
            out16(t, st, col, width)
        return f

    def epi_rmsrope(H, gain):
        def f(t, ps, width, col):
            lat = t < NT
            p.act(sq[:, 0:width], ps[:, 0:width], AF.Square)
            p.op("dve", [sq[:, 0:width]], [ss[:, 0:H]],
                 lambda e: e.reduce_sum(ss.h[:, 0:H], sq.h[:, 0:width].rearrange("p (h d) -> p h d", d=128), AX.X))
            p.act(ss[:, 0:H], ss[:, 0:H], AF.Sqrt, bias=cx.eps[:, 0:1], scale=1.0 / HD)
            p.op("dve", [ss[:, 0:H]], [ss[:, 0:H]], lambda e: e.reciprocal(ss.h[:, 0:H], ss.h[:, 0:H]))
            for h in range(H):
                p.stt("dve", qn[:, h * 128:(h + 1) * 128], ps[:, h * 128:(h + 1) * 128], ss[:, h:h + 1],
                      gain[:], ALU.mult, ALU.mult)
            st = next_stg()
            if lat and layer_has_rope:
                q4 = qn.h[:, 0:width].rearrange("p (h i two) -> p h i two", two=2, i=64)
                x1 = V(qn, q4[:, :, :, 0])
                x2 = V(qn, q4[:, :, :, 1])
                cb = V(cos, bcast_mid(cos.h[:, t, :], H))
                sb_ = V(sin, bcast_mid(sin.h[:, t, :], H))
                o4 = st.h[:, 0:width].rearrange("p (h i two) -> p h i two", two=2, i=64)
                a_, b_, c_, d_ = [V(r, r.h[:, 0:H, :]) for r in rt]
                p.tt("dve", a_, x1, cb, ALU.mult)
                p.tt("pool", b_, x2, sb_, ALU.mult)
                p.tt("dve", c_, x1, sb_, ALU.mult)
                p.tt("pool", d_, x2, cb, ALU.mult)
                p.tt("dve", V(st, o4[:, :, :, 0]), a_, b_, ALU.subtract)
                p.tt("pool", V(st, o4[:, :, :, 1]), c_, d_, ALU.add)
            else:
                p.copy("dve", st[:, 0:width], qn[:, 0:width])
            out16(t, st, col, width)
        return f

    def epi_glu(t, ps, width, col):
        sg = stg32[si[0] % 2]
        p.act(sg[:, 0:256], ps[:, 256:512], AF.Sigmoid)
        st = next_stg()
        p.tt("dve", st[:, 0:256], ps[:, 0:256], sg[:, 0:256], ALU.mult)
        out16(t, st, col, 256)

    groups = []
    groups.append(([(0, 256)], OK_, epi_rmsrope(2, gk)))
    groups.append(([(256, 256)], OV_, epi_simple(AF.Identity)))
    groups.append(([(512, 512)], OKG, epi_simple(AF.Identity)))
    groups.append(([(1024, 512)], OVG, epi_simple(AF.Identity)))
    groups.append(([(1536, 512)], OVG + 512, epi_simple(AF.Identity)))
    groups.append(([(2080, 512)], OQ, epi_rmsrope(4, gq)))
    groups.append(([(2592, 512)], OQ + 512, epi_rmsrope(4, gq)))
    groups.append(([(3104, 512)], OQG, epi_simple(AF.Identity, scale=GDK ** -0.5)))
    groups.append(([(3616, 512)], OR, epi_simple(AF.Silu)))
    groups.append(([(4128, 512)], OR + 512, epi_simple(AF.Silu)))
    for j in range(4):
        groups.append(([(4640 + 256 * j, 256), (5664 + 256 * j, 256)], OY + 256 * j, epi_glu))
    for j in range(6):
        groups.append(([(6688 + 512 * j, 512)], OGT + 512 * j, epi_simple(AF.Sigmoid)))

    for (cr, col, epi) in groups:
        wt, width = load_w(cr)
        for t in range(NTA):
            ps = proj(t, wt, width)
            epi(t, ps, width, col)

    wt, _ = load_w([(2048, 32)])
    glrT = p.sb("glrT", [32, NTA * 128], BF16)
    for t0 in range(0, NTA * 128, 512):
        n = min(512, NTA * 128 - t0)
        ps = cx.bank()
        t_lo, t_hi = t0 // 128, (t0 + n) // 128
        reads = []
        p.mm(ps[0:32, 0:n], [(wt[:, kc, 0:32], hT[:, kc, t0:t0 + n]) for kc in range(8)])
        p.act(glrT[:, t0:t0 + n], ps[0:32, 0:n], AF.Identity)
    for t in range(NTA):
        for dr in range(2):
            ps = cx.bank()
            p.mm(ps[:], [(glrT[:, t * 128:(t + 1) * 128], w2bd[:, dr * 512:(dr + 1) * 512])])
            sg = stg32[dr]
            p.tt("dve", sg[:], ps[:], babc[:, dr * 512:(dr + 1) * 512], ALU.add)
            p.act(sg[:], sg[:], AF.Exp, scale=-1.0)
            p.act(sg[:], sg[:], AF.Ln, bias=cx.one[:, 0:1])
            p.ts("dve", sg[:], sg[:], -1.0 / TAU, None, ALU.mult)
            p.dma("sp", V(o32, o32.h[t, :, dr * 512:(dr + 1) * 512]), sg[:])


def build_p1(NT, NCT, layer_has_rope=True):
    NTA = NT + NCT
    nc = bass.Bass("TRN2", target_bir_lowering=False)
    es = ExitStack()
    with es:
        p = Prog(nc, es)
        cx = Ctx(p)
        xs = p.dram("xs", [NTA, 128, D], F32, "ExternalInput")
        cvT = p.dram("cvT", [128, 2, 8], F32, "ExternalInput")
        w_ada = p.dram("w_ada", [D, 2 * D], F32, "ExternalInput")
        b_ada = p.dram("b_ada", [2 * D], F32, "ExternalInput")
        w_in = p.dram("w_in", [D, INC], F32, "ExternalInput")
        qn_d = p.dram("q_norm", [HD], F32, "ExternalInput")
        kn_d = p.dram("k_norm", [HD], F32, "ExternalInput")
        wa2 = p.dram("w_a2", [2, RANK, 512], F32, "ExternalInput")
        ba = p.dram("b_a", [2 * 512], F32, "ExternalInput")
        cos_d = p.dram("cos", [128, NT, 64], F32, "ExternalInput")
        sin_d = p.dram("sin", [128, NT, 64], F32, "ExternalInput")
        ident_d = p.dram("ident", [128, 128], F32, "ExternalInput")
        o16 = p.dram("o16", [NTA, 128, C16], BF16, "ExternalOutput")
        o32 = p.dram("o32", [NTA, 128, 1024], F32, "ExternalOutput")
        emit_p1(p, cx, NT, NCT, xs, cvT, w_ada, b_ada, w_in, qn_d, kn_d, wa2[0], wa2[1], ba, cos_d, sin_d,
                ident_d, o16, o32, layer_has_rope)
        p.wait_all("sp", [o16[:], o32[:]])
        print("P1 ops", p.n_ops, "waits", p.n_waits)
    return nc


_CACHE = {}


def _rope_tables(S):
    t = np.arange(S)
    row = (t // GRID_W).astype(np.float32)
    col = (t % GRID_W).astype(np.float32)
    half = HD // 2
    inv = (np.float32(10000.0) ** (-np.arange(0, half, 2, dtype=np.float32) / np.float32(half))).astype(np.float32)
    ang = np.concatenate([row[:, None] * inv, col[:, None] * inv], axis=-1).astype(np.float32)
    return np.cos(ang).astype(np.float32), np.sin(ang).astype(np.float32)


def _run(key, builder, in_maps):
    if key not in _CACHE:
        _CACHE[key] = builder()
    nc = _CACHE[key]
    res = run_bass_kernel_spmd(nc, in_maps, core_ids=list(range(NCORES)))
    return res.results


def host_p1(l, x, xc, c, c_ctx, P, S):
    TPC = S * B // NCORES
    NT = TPC // 128
    SEGS = NCORES // B
    cos, sin = _rope_tables(S)
    ctx_tiles = xc.reshape(B * CTX // 128, 128, D)
    in_maps = []
    for core in range(NCORES):
        b, seg = core // SEGS, core % SEGS
        xs = np.concatenate([x[b, seg * TPC:(seg + 1) * TPC].reshape(NT, 128, D),
                             ctx_tiles[core % 4][None]], axis=0)
        cv = np.stack([c[b], c_ctx]).reshape(2, 8, 128).transpose(2, 0, 1)
        cs = cos[seg * TPC:(seg + 1) * TPC].reshape(NT, 128, 64).transpose(1, 0, 2)
        sn = sin[seg * TPC:(seg + 1) * TPC].reshape(NT, 128, 64).transpose(1, 0, 2)
        in_maps.append({
            "xs": np.ascontiguousarray(xs), "cvT": np.ascontiguousarray(cv),
            "w_ada": np.ascontiguousarray(P["w_ada"][l][:, 0:2 * D]), "b_ada": np.ascontiguousarray(P["b_ada"][l][0:2 * D]), "w_in": P["w_in"][l],
            "q_norm": P["q_norm"][l], "k_norm": P["k_norm"][l],
            "w_a2": P["gla_w_a2"][l], "b_a": np.ascontiguousarray(P["gla_b_a"][l].reshape(-1)),
            "cos": np.ascontiguousarray(cs), "sin": np.ascontiguousarray(sn),
            "ident": np.eye(128, dtype=np.float32),
        })
    res = _run(("p1", NT), lambda: build_p1(NT, 1), in_maps)
    o16 = np.stack([r["o16"] for r in res])
    o32 = np.stack([r["o32"] for r in res])
    return o16, o32


def build_mix(S, with_ctx_q):
    L = S + CTX
    NKT = L // 128
    NQ = L if with_ctx_q else S
    nc = bass.Bass("TRN2", target_bir_lowering=False)
    es = ExitStack()
    with es:
        p = Prog(nc, es)
        banks = [p.ps("bank%d" % i, [128, 512], F32) for i in range(8)]
        qT_d = p.dram("qT", [2, 128, L], BF16, "ExternalInput")
        kT_d = p.dram("kT", [128, L], BF16, "ExternalInput")
        vE_d = p.dram("vE", [128, NKT, 129], BF16, "ExternalInput")
        oatt = p.dram("oatt", [2, NKT, 128, 128], BF16, "ExternalOutput")
        gq_d = p.dram("gqT", [2, 128, L], BF16, "ExternalInput")
        gkT_d = p.dram("gkT", [2, 128, L], BF16, "ExternalInput")
        gk_d = p.dram("gk", [2, NKT, 128, 128], BF16, "ExternalInput")
        gv_d = p.dram("gv", [2, NKT, 128, 256], BF16, "ExternalInput")
        gg_d = p.dram("gg", [2, NKT, 128, 128], F32, "ExternalInput")
        cst_d = p.dram("cst", [128, 128 * 2 + 2], F32, "ExternalInput")
        ogla = p.dram("ogla", [2, NKT, 128, 256], F32, "ExternalOutput")
        cy_d = p.dram("cy", [128, B, S + 30], BF16, "ExternalInput")
        cyc_d = p.dram("cyc", [128, B, CTX + 30], BF16, "ExternalInput")
        cw_d = p.dram("cw", [128, CW + 1], F32, "ExternalInput")
        oconv = p.dram("oconv", [128, B, S], F32, "ExternalOutput")
        oconvc = p.dram("oconvc", [128, B, CTX], F32, "ExternalOutput")

        cw = p.sb("cw", [128, CW + 1], F32)
        p.dma("sp", cw[:], cw_d[:])
        cy = p.sb("cy", [128, B, S + 30], BF16)
        cyc = p.sb("cyc", [128, B, CTX + 30], BF16)
        p.dma("sp", cy[:], cy_d[:])
        p.dma("sp", cyc[:], cyc_d[:])
        acc = p.sb("cacc", [128, B, S], F32)
        accc = p.sb("caccc", [128, B, CTX], F32)

        def conv_emit(tap):
            for b in range(B):
                eng = "dve"
                for (a, y, n, nm) in ((acc, cy, S, "l"), (accc, cyc, CTX, "c")):
                    av = a.k((nm, b), (slice(None), b, slice(None)))
                    yv = y[:, b, tap:tap + n]
                    if tap == 0:
                        p.ts(eng, av, yv, cw[:, 0:1], None, ALU.mult)
                    else:
                        p.stt(eng, av, yv, cw[:, tap:tap + 1], av, ALU.mult, ALU.add)
                    if tap == CW - 1:
                        p.ts(eng, av, av, cw[:, CW:CW + 1], None, ALU.add)
                        od = oconv if nm == "l" else oconvc
                        p.dma("sp", V(od, od.h[:, b, :]), av)

        cst = p.sb("cst", [128, 258], F32)
        p.dma("sp", cst[:], cst_d[:])
        Lm = cst[:, 0:128]
        Um = cst[:, 128:256]
        sel = cst[:, 256:258]
        Lmb = p.sb("Lmb", [128, 128], F32)
        p.copy("dve", Lmb[:], Lm)
        St = [p.sb("gS%d" % i, [128, 256], F32) for i in range(2)]
        Sb = [p.sb("gSb%d" % i, [128, 256], BF16) for i in range(2)]
        for i in range(2):
            p.memset("dve", St[i][:], 0.0)
            p.memset("dve", Sb[i][:], 0.0)
        gl = {}
        for nm, shp, dt in (("qT", [128, 128], BF16), ("kT", [128, 128], BF16), ("k", [128, 128], BF16),
                            ("v", [128, 256], BF16), ("g", [128, 128], F32), ("EbT", [128, 128], F32),
                            ("EnbT", [128, 128], F32), ("Erem", [128, 128], F32), ("Eend", [128, 2], F32),
                            ("qeT", [128, 128], BF16), ("keT", [128, 128], BF16), ("kend", [128, 128], BF16),
                            ("ATm", [128, 128], BF16), ("osb", [64, 2, 256], F32)):
            gl[nm] = [[p.sb("g_%s_%d_%d" % (nm, s, j), shp, dt) for j in range(2)] for s in range(2)]

        def gla_tile(s, t):
            j = t % 2
            T_ = {k: v[s][j] for k, v in gl.items()}
            pb = banks[4 + 2 * s:6 + 2 * s]
            p.dma("sp", T_["qT"][:], V(gq_d, gq_d.h[s, :, t * 128:(t + 1) * 128]))
            p.dma("sp", T_["kT"][:], V(gkT_d, gkT_d.h[s, :, t * 128:(t + 1) * 128]))
            p.dma("sp", T_["k"][:], V(gk_d, gk_d.h[s, t]))
            p.dma("sp", T_["v"][:], V(gv_d, gv_d.h[s, t]))
            p.dma("sp", T_["g"][:], V(gg_d, gg_d.h[s, t]))
            g = T_["g"]
            b0 = pb[0]
            p.mm(b0[:, 0:128], [(g[:], Lm)])
            p.mm(b0[:, 128:256], [(Um, g[:])])
            p.mm(b0[:, 256:258], [(g[:], sel)])
            p.act(T_["EbT"][:], b0[:, 0:128], AF.Exp)
            p.act(T_["EnbT"][:], b0[:, 0:128], AF.Exp, scale=-1.0)
            p.act(T_["Erem"][:], b0[:, 128:256], AF.Exp)
            p.act(T_["Eend"][:], b0[:, 256:258], AF.Exp)
            p.tt("dve", T_["qeT"][:], T_["qT"][:], T_["EbT"][:], ALU.mult)
            p.tt("dve", T_["keT"][:], T_["kT"][:], T_["EnbT"][:], ALU.mult)
            p.tt("dve", T_["kend"][:], T_["k"][:], T_["Erem"][:], ALU.mult)
            p.mm(b0[:, 384:512], [(T_["keT"][:], T_["qeT"][:])])
            p.tt("dve", T_["ATm"][:], b0[:, 384:512], Lmb[:], ALU.mult)
            b1 = pb[1]
            for c in range(2):
                cs = slice(c * 64, (c + 1) * 64)
                ov = b1[0:64, c * 256:(c + 1) * 256] if False else None
            for c in range(2):
                cs = slice(c * 64, (c + 1) * 64)
                ops_ = V(b1, b1.h[0:64, 0:256])
                p.mm(ops_, [(T_["qeT"][:, cs], Sb[s][:]), (T_["ATm"][:, cs], T_["v"][:])])
                p.act(T_["osb"][:, c, :], ops_, AF.Identity)
                ups = V(b1, b1.h[:, 256:512])
                p.mm(ups, [(T_["kend"][cs, :], T_["v"][cs, :])])
                p.stt("dve", St[s][:], St[s][:], T_["Eend"][:, c:c + 1], ups, ALU.mult, ALU.add)
                p.act(Sb[s][:], St[s][:], AF.Identity)
            p.dma("sp", V(ogla, ogla.h[s, t].rearrange("(c q) e -> q c e", q=64)), T_["osb"][:])

        kT = p.sb("kT", [128, L], BF16)
        vE = p.sb("vE", [128, NKT, 129], BF16)
        p.dma("sp", kT[:], kT_d[:])
        p.dma("sp", vE[:], vE_d[:])
        qTs = [p.sb("qTs%d" % i, [128, 512], BF16) for i in range(2)]
        pTs = [p.sb("pTs%d" % i, [128, 512], BF16) for i in range(3)]
        rcp = p.sb("rcp", [128, 4], F32)
        ost = [p.sb("ost%d" % i, [128, 128], BF16) for i in range(4)]
        SC = HD ** -0.5
        blocks = []
        for h in range(2):
            for q0 in range(0, S, 512):
                blocks.append((h, q0, min(512, S - q0), 0, NKT))
            if with_ctx_q:
                blocks.append((h, S, CTX, NKT - CTX // 128, NKT))
        state = {"bi": 0, "pi": 0, "sb": 0}

        def att_block(blk):
            h, q0, nq, kt0, kt1 = blk
            qt = qTs[state["bi"] % 2]
            state["bi"] += 1
            p.dma("sp", qt[:, 0:nq], V(qT_d, qT_d.h[h, :, q0:q0 + nq]))
            nsub = nq // 128
            for kt in range(kt0, kt1):
                sbk = banks[2 + state["sb"] % 2]
                state["sb"] += 1
                p.mm(sbk[:, 0:nq], [(kT[:, kt * 128:(kt + 1) * 128], qt[:, 0:nq])])
                pt = pTs[state["pi"] % 3]
                state["pi"] += 1
                p.act(pt[:, 0:nq], sbk[:, 0:nq], AF.Exp, scale=SC)
                for qs in range(nsub):
                    ob = banks[qs // 2]
                    ov = V(ob, ob.h[:, (qs % 2) * 256:(qs % 2) * 256 + 129])
                    p.mm1(ov, pt[:, qs * 128:(qs + 1) * 128], vE[:, kt, :], kt == kt0, kt == kt1 - 1)
            for qs in range(nsub):
                ob = banks[qs // 2]
                base = (qs % 2) * 256
                p.op("dve", [ob[:, base + 128:base + 129]], [rcp[:, qs:qs + 1]],
                     lambda e, ob=ob, base=base, qs=qs: e.reciprocal(rcp.h[:, qs:qs + 1], ob.h[:, base + 128:base + 129]))
                p.ts("dve", ost[qs][:], ob[:, base:base + 128], rcp[:, qs:qs + 1], None, ALU.mult)
                p.dma("sp", V(oatt, oatt.h[h, (q0 // 128) + qs]), ost[qs][:])

        nb = len(blocks)
        ngl = NKT
        total = max(nb, ngl, CW)
        gi = ci = ai = 0
        for step in range(total):
            while gi < ngl and gi * total <= step * ngl:
                gla_tile(0, gi)
                gla_tile(1, gi)
                gi += 1
            while ci < CW and ci * total <= step * CW:
                conv_emit(ci)
                ci += 1
            while ai < nb and ai * total <= step * nb:
                att_block(blocks[ai])
                ai += 1
        while gi < ngl:
            gla_tile(0, gi); gla_tile(1, gi); gi += 1
        while ci < CW:
            conv_emit(ci); ci += 1
        while ai < nb:
            att_block(blocks[ai]); ai += 1
        p.wait_all("sp", [oatt[:], ogla[:], oconv[:], oconvc[:]])
        print("MIX ops", p.n_ops, "waits", p.n_waits)
    return nc


def _post_common(p, cx, NTA):
    load_const_eps(p, cx)
    cx.bn = p.sb("bn", [128, 2, 6], F32)
    cx.wts = [p.sb("wt%d" % i, [128, 8, 512], BF16) for i in range(2)]
    cx.wi = 0
    cx.bbc = p.sb("bbc", [128, 1024], F32)
    cx.xt = [p.sb("xt%d" % i, [128, D], F32) for i in range(2)]
    cx.mr = [p.sb("mr%d" % i, [128, 2], F32) for i in range(2)]
    cx.xn = p.sb("xn", [128, D], F32)
    cx.hb = [p.sb("hb%d" % i, [128, D], BF16) for i in range(2)]


def _bc_vec(p, name, d, n):
    t = p.sb(name + "_bc", [128, n], F32)
    p.dma("sp", t[:], V(d, d.h.partition_broadcast(128)))
    return t


def deepnorm_out(p, cx, u, xo_view, lg, lb, i):
    mr = cx.mr[i % 2]
    ln_stats(p, cx, u[:], 1024, mr[:])
    p.ts("dve", u[:], u[:], mr[:, 0:1], mr[:, 1:2], ALU.subtract, ALU.mult)
    p.tt("pool", u[:], u[:], lg[:], ALU.mult)
    p.tt("dve", u[:], u[:], lb[:], ALU.add)
    p.dma("sp", xo_view, u[:])


class NS:
    pass


def emit_posta(p, cx, NT, NCT, A):
    NTA = NT + NCT
    _post_common(p, cx, NTA)
    ident = p.sb("ident_sb", [128, 128], BF16)
    p.dma("pool", ident[:], A.ident_d[:])
    bc = {n: _bc_vec(p, n, d, d.h.shape[0]) for n, d in A.vecs.items()}
    scb = make_scb(p, cx, A.cvT, 2)
    g1L = p.sb("g1L", [128, D], F32)
    g1C = p.sb("g1C", [128, D], F32)
    specs = [(0, A.g1_mi, False, g1L)]
    if NCT:
        specs.append((1, A.g1_mi, False, g1C))
    mod_tiles(p, cx, scb, A.w_ada, A.b_ada, specs)

    aT = p.sb("aT", [128, 8, NTA * 128], BF16)
    m = p.sb("m", [128, NTA, D], BF16)
    ld16 = [p.sb("ld16_%d" % i, [128, D], BF16) for i in range(2)]
    ld32 = [p.sb("ld32_%d" % i, [128, D], F32) for i in range(2)]
    ss = p.sb("ss", [128, 4], F32)
    sq = p.sb("sq", [128, D], F32)
    gtile = [p.sb("gtile%d" % i, [128, 512], BF16) for i in range(2)]
    tmp = p.sb("tmpm", [128, 512], F32)

    def fill_att(t):
        a = ld16[t % 2]
        p.dma("sp", a[:], A.oatt(t))
        transpose_into(p, cx, a, aT, t, ident, 8)

    def fill_gla(t):
        a, b_ = ld32[0], ld32[1]
        p.dma("sp", a[:], A.ogf(t))
        p.dma("sp", b_[:], A.ogb(t))
        p.tt("dve", a[:], a[:], b_[:], ALU.add)
        p.act(sq[:], a[:], AF.Square)
        p.op("dve", [sq[:]], [ss[:]],
             lambda e: e.reduce_sum(ss.h[:], sq.h[:].rearrange("p (h d) -> p h d", d=GDV), AX.X))
        p.act(ss[:], ss[:], AF.Sqrt, bias=cx.eps[:, 0:1], scale=1.0 / GDV)
        p.op("dve", [ss[:]], [ss[:]], lambda e: e.reciprocal(ss.h[:], ss.h[:]))
        for h in range(GH):
            hs = slice(h * GDV, (h + 1) * GDV)
            p.stt("dve", a[:, hs], a[:, hs], ss[:, h:h + 1], bc["gla_norm"][:], ALU.mult, ALU.mult)
        r = ld16[0]
        p.dma("sp", r[:], A.sr(t))
        hb = cx.hb[t % 2]
        p.tt("dve", hb[:], a[:], r[:], ALU.mult)
        transpose_into(p, cx, hb, aT, t, ident, 8)

    def fill_conv(t):
        a = ld32[t % 2]
        p.dma("sp", a[:], A.cv(t))
        mr = cx.mr[t % 2]
        ln_stats(p, cx, a[:], 1024, mr[:])
        p.ts("dve", a[:], a[:], mr[:, 0:1], mr[:, 1:2], ALU.subtract, ALU.mult)
        p.tt("pool", a[:], a[:], bc["conv_ln_g"][:], ALU.mult)
        p.tt("dve", a[:], a[:], bc["conv_ln_b"][:], ALU.add)
        hb = cx.hb[t % 2]
        p.act(hb[:], a[:], AF.Silu)
        transpose_into(p, cx, hb, aT, t, ident, 8)

    def load_wo(wd, half):
        wt = cx.wts[cx.wi % 2]
        cx.wi += 1
        wv = wd.h.rearrange("(kc p) c -> p kc c", p=128)
        p.dma("pool", wt[:], V(wd, wv[:, :, half * 512:(half + 1) * 512]))
        return wt

    for bi, (fill, wn) in enumerate(((fill_att, "w_att_o"), (fill_gla, "w_gla_o"), (fill_conv, "w_conv_o"))):
        for t in range(NTA):
            fill(t)
        for half in range(2):
            wt = load_wo(A.wo[wn], half)
            for t in range(NTA):
                ps = cx.bank()
                p.mm(ps[:], [(aT[:, kc, t * 128:(t + 1) * 128], wt[:, kc, :]) for kc in range(8)])
                g = gtile[t % 2]
                p.dma("sp", g[:], A.gt(t, bi * D + half * 512, bi * D + (half + 1) * 512))
                mv = m.k(("t", t, half), (slice(None), t, slice(half * 512, (half + 1) * 512)))
                if bi == 0:
                    p.tt("dve", mv, ps[:], g[:], ALU.mult)
                else:
                    p.tt("dve", tmp[:], ps[:], g[:], ALU.mult)
                    p.tt("pool", mv, mv, tmp[:], ALU.add)
    for t in range(NTA):
        hb = cx.hb[t % 2]
        p.copy("dve", hb[:], m[:, t, :])
        transpose_into(p, cx, hb, aT, t, ident, 8)
    w0 = load_wo(A.wo["w_out"], 0)
    w1 = load_wo(A.wo["w_out"], 1)
    us = [p.sb("u%d" % i, [128, D], F32) for i in range(2)]
    for t in range(NTA):
        lat = t < NT
        g1 = g1L if lat else g1C
        u = us[t % 2]
        xt = cx.xt[t % 2]
        p.dma("sp", xt[:], A.xs(t))
        for half, wt in enumerate((w0, w1)):
            ps = cx.bank()
            hs = slice(half * 512, (half + 1) * 512)
            p.mm(ps[:], [(aT[:, kc, t * 128:(t + 1) * 128], wt[:, kc, :]) for kc in range(8)])
            p.tt("dve", u[:, hs], ps[:], g1[:, hs], ALU.mult)
        p.stt("dve", u[:], xt[:], float(ALPHA), u[:], ALU.mult, ALU.add)
        deepnorm_out(p, cx, u, A.xo(t), bc["ln1_g"], bc["ln1_b"], t)


def build_posta(NT, NCT):
    NTA = NT + NCT
    nc = bass.Bass("TRN2", target_bir_lowering=False)
    es = ExitStack()
    with es:
        p = Prog(nc, es)
        cx = Ctx(p)
        oatt = p.dram("oatt_t", [NTA, 128, D], BF16, "ExternalInput")
        ogf = p.dram("ogf", [NTA, 128, D], F32, "ExternalInput")
        ogb = p.dram("ogb", [NTA, 128, D], F32, "ExternalInput")
        sr = p.dram("sr", [NTA, 128, D], BF16, "ExternalInput")
        cv = p.dram("cv", [NTA, 128, D], F32, "ExternalInput")
        gt = p.dram("gt", [NTA, 128, 3 * D], BF16, "ExternalInput")
        xs = p.dram("xs", [NTA, 128, D], F32, "ExternalInput")
        cvT = p.dram("cvT", [128, 2, 8], F32, "ExternalInput")
        w_ada = p.dram("w_ada", [D, D], F32, "ExternalInput")
        b_ada = p.dram("b_ada", [D], F32, "ExternalInput")
        wo = {n: p.dram(n, [D, D], F32, "ExternalInput") for n in ("w_att_o", "w_gla_o", "w_conv_o", "w_out")}
        vecs = {n: p.dram(n, [sz], F32, "ExternalInput") for n, sz in
                (("gla_norm", GDV), ("conv_ln_g", D), ("conv_ln_b", D), ("ln1_g", D), ("ln1_b", D))}
        ident_d = p.dram("ident", [128, 128], F32, "ExternalInput")
        xo = p.dram("xo", [NTA, 128, D], F32, "ExternalOutput")
        A = NS()
        A.oatt = lambda t: oatt[t]
        A.ogf = lambda t: ogf[t]
        A.ogb = lambda t: ogb[t]
        A.sr = lambda t: sr[t]
        A.cv = lambda t: cv[t]
        A.xs = lambda t: xs[t]
        A.xo = lambda t: xo[t]
        A.gt = lambda t, c0, c1: V(gt, gt.h[t, :, c0:c1])
        A.g1_mi = 0
        A.ident_d, A.vecs, A.cvT, A.w_ada, A.b_ada, A.wo = ident_d, vecs, cvT, w_ada, b_ada, wo
        emit_posta(p, cx, NT, NCT, A)
        p.wait_all("sp", [xo[:]])
        print("POSTA ops", p.n_ops, "waits", p.n_waits)
    return nc


def emit_postb(p, cx, NT, NCT, A):
    NTA = NT + NCT
    NFC = DFF // 128
    _post_common(p, cx, NTA)
    ident = p.sb("ident_sb", [128, 128], BF16)
    p.dma("pool", ident[:], A.ident_d[:])
    l2g = _bc_vec(p, "l2g", A.l2g_d, D)
    l2b = _bc_vec(p, "l2b", A.l2b_d, D)
    scb = make_scb(p, cx, A.cvT, 2)
    mods = {n: p.sb(n, [128, D], F32) for n in ("sh2L", "sc2L", "g2L")}
    m0 = A.mi0
    specs = [(0, m0, False, mods["sh2L"]), (0, m0 + 1, True, mods["sc2L"]), (0, m0 + 2, False, mods["g2L"])]
    if NCT:
        for n in ("sh2C", "sc2C", "g2C"):
            mods[n] = p.sb(n, [128, D], F32)
        specs += [(1, m0, False, mods["sh2C"]), (1, m0 + 1, True, mods["sc2C"]), (1, m0 + 2, False, mods["g2C"])]
    mod_tiles(p, cx, scb, A.w_ada, A.b_ada, specs)
    hT = p.sb("hT", [128, 8, NTA * 128], BF16)
    for t in range(NTA):
        lat = t < NT
        ln_mod_transpose(p, cx, A.xs(t), t, mods["sc2L" if lat else "sc2C"],
                         mods["sh2L" if lat else "sh2C"], hT, ident)
    wgs = [p.sb("wg%d" % i, [128, 8, 512], BF16) for i in range(2)]
    wus = cx.wts
    actT = p.sb("actT", [128, NFC, 512], BF16)
    wdt = p.sb("wdt", [128, NFC, 512], BF16)
    sg = [p.sb("sg%d" % i, [128, 512], F32) for i in range(2)]
    us = [p.sb("u%d" % i, [128, D], F32) for i in range(4)]
    wg_d, wu_d, wd_d = A.wg_d, A.wu_d, A.wd_d
    wgv = wg_d.h.rearrange("(kc p) c -> p kc c", p=128)
    wuv = wu_d.h.rearrange("(kc p) c -> p kc c", p=128)
    wdv = wd_d.h.rearrange("(c p) n -> p c n", p=128)
    wi = 0
    for g0 in range(0, NTA, 4):
        gt_ = list(range(g0, min(NTA, g0 + 4)))
        ntok = len(gt_) * 128
        tk = slice(g0 * 128, g0 * 128 + ntok)
        for c0 in range(0, DFF, 512):
            wd_ = min(512, DFF - c0)
            wgt, wut = wgs[wi % 2], wus[wi % 2]
            wi += 1
            p.dma("pool", wgt[:, :, 0:wd_], V(wg_d, wgv[:, :, c0:c0 + wd_]))
            p.dma("pool", wut[:, :, 0:wd_], V(wu_d, wuv[:, :, c0:c0 + wd_]))
            for sub in range(wd_ // 128):
                ffc = c0 // 128 + sub
                cs = slice(sub * 128, (sub + 1) * 128)
                pg = cx.bank()
                pu = cx.bank()
                p.mm(pg[:, 0:ntok], [(wgt[:, kc, cs], hT[:, kc, tk]) for kc in range(8)])
                p.mm(pu[:, 0:ntok], [(wut[:, kc, cs], hT[:, kc, tk]) for kc in range(8)])
                s_ = sg[ffc % 2]
                p.act(s_[:, 0:ntok], pg[:, 0:ntok], AF.Silu)
                p.tt("dve", actT.k(ffc, (slice(None), ffc, slice(0, ntok))), s_[:, 0:ntok], pu[:, 0:ntok], ALU.mult)
        for half in range(2):
            hs = slice(half * 512, (half + 1) * 512)
            p.dma("pool", wdt[:], V(wd_d, wdv[:, :, hs]))
            for i, t in enumerate(gt_):
                lat = t < NT
                g2 = mods["g2L" if lat else "g2C"]
                ps = cx.bank()
                p.mm(ps[:], [(actT[:, ffc, i * 128:(i + 1) * 128], wdt[:, ffc, :]) for ffc in range(NFC)])
                p.tt("dve", us[i][:, hs], ps[:], g2[:, hs], ALU.mult)
        for i, t in enumerate(gt_):
            xt = cx.xt[t % 2]
            p.dma("sp", xt[:], A.xs(t))
            p.stt("dve", us[i][:], xt[:], float(ALPHA), us[i][:], ALU.mult, ALU.add)
            deepnorm_out(p, cx, us[i], A.xo(t), l2g, l2b, t)


def emit_postb2(p, cx, NT, NCT, A):
    NTA = NT + NCT
    NFC = DFF // 128
    NTOK = NTA * 128
    act_s = A.act_s
    _post_common(p, cx, NTA)
    ident = p.sb("ident_sb", [128, 128], BF16)
    p.dma("pool", ident[:], A.ident_d[:])
    l2g = _bc_vec(p, "l2g", A.l2g_d, D)
    l2b = _bc_vec(p, "l2b", A.l2b_d, D)
    scb = make_scb(p, cx, A.cvT, 2)
    mods = {n: p.sb(n, [128, D], F32) for n in ("sh2L", "sc2L", "g2L")}
    m0 = A.mi0
    specs = [(0, m0, False, mods["sh2L"]), (0, m0 + 1, True, mods["sc2L"]), (0, m0 + 2, False, mods["g2L"])]
    if NCT:
        for n in ("sh2C", "sc2C", "g2C"):
            mods[n] = p.sb(n, [128, D], F32)
        specs += [(1, m0, False, mods["sh2C"]), (1, m0 + 1, True, mods["sc2C"]), (1, m0 + 2, False, mods["g2C"])]
    mod_tiles(p, cx, scb, A.w_ada, A.b_ada, specs)
    hT = p.sb("hT", [128, 8, NTOK], BF16)
    for t in range(NTA):
        lat = t < NT
        ln_mod_transpose(p, cx, A.xs(t), t, mods["sc2L" if lat else "sc2C"],
                         mods["sh2L" if lat else "sh2C"], hT, ident)
    wgs = [p.sb("wg%d" % i, [128, 8, 512], BF16) for i in range(2)]
    wus = cx.wts
    sg = [p.sb("sg%d" % i, [128, 512], F32) for i in range(2)]
    stg = [p.sb("astg%d" % i, [128, 512], BF16) for i in range(3)]
    wdt = p.sb("wdt", [128, NFC, D], BF16)
    wg_d, wu_d, wd_d = A.wg_d, A.wu_d, A.wd_d
    wgv = wg_d.h.rearrange("(kc q) c -> q kc c", q=128)
    wuv = wu_d.h.rearrange("(kc q) c -> q kc c", q=128)
    wdv = wd_d.h.rearrange("(c q) n -> q c n", q=128)
    wi = 0
    si = 0
    for c0 in range(0, DFF, 512):
        wd_ = min(512, DFF - c0)
        wgt, wut = wgs[wi % 2], wus[wi % 2]
        wi += 1
        p.dma("pool", wgt[:, :, 0:wd_], V(wg_d, wgv[:, :, c0:c0 + wd_]))
        p.dma("pool", wut[:, :, 0:wd_], V(wu_d, wuv[:, :, c0:c0 + wd_]))
        if c0 == 0:
            for half in range(2):
                p.dma("pool", wdt[:, :, half * 512:(half + 1) * 512], V(wd_d, wdv[:, :, half * 512:(half + 1) * 512]))
        for tk0 in range(0, NTOK, 512):
            ntok = min(512, NTOK - tk0)
            nt_ = ntok // 128
            tk = slice(tk0, tk0 + ntok)
            for sub in range(wd_ // 128):
                ffc = c0 // 128 + sub
                cs = slice(sub * 128, (sub + 1) * 128)
                pg = cx.bank()
                pu = cx.bank()
                p.mm(pg[:, 0:ntok], [(wgt[:, kc, cs], hT[:, kc, tk]) for kc in range(8)])
                p.mm(pu[:, 0:ntok], [(wut[:, kc, cs], hT[:, kc, tk]) for kc in range(8)])
                s_ = sg[si % 2]
                st = stg[si % 3]
                si += 1
                p.act(s_[:, 0:ntok], pg[:, 0:ntok], AF.Silu)
                p.tt("dve", st[:, 0:ntok], s_[:, 0:ntok], pu[:, 0:ntok], ALU.mult)
                t0_ = tk0 // 128
                p.dma("sp", V(act_s, act_s.h[t0_:t0_ + nt_, :, ffc, :].rearrange("t q k -> q t k")),
                      V(st, st.h[:, 0:ntok].rearrange("q (t k) -> q t k", k=128)))
    ats = [p.sb("at%d" % i, [128, NFC, 128], BF16) for i in range(2)]
    us = [p.sb("u%d" % i, [128, D], F32) for i in range(2)]
    for t in range(NTA):
        lat = t < NT
        g2 = mods["g2L" if lat else "g2C"]
        at = ats[t % 2]
        u = us[t % 2]
        p.dma("sp", at[:], act_s[t])
        for half in range(2):
            hs = slice(half * 512, (half + 1) * 512)
            ps = cx.bank()
            p.mm(ps[:], [(at[:, ffc, :], wdt[:, ffc, hs]) for ffc in range(NFC)])
            p.tt("dve", u[:, hs], ps[:], g2[:, hs], ALU.mult)
        xt = cx.xt[t % 2]
        p.dma("sp", xt[:], A.xs(t))
        p.stt("dve", u[:], xt[:], float(ALPHA), u[:], ALU.mult, ALU.add)
        deepnorm_out(p, cx, u, A.xo(t), l2g, l2b, t)


def build_postb(NT, NCT):
    NTA = NT + NCT
    NFC = DFF // 128
    nc = bass.Bass("TRN2", target_bir_lowering=False)
    es = ExitStack()
    with es:
        p = Prog(nc, es)
        cx = Ctx(p)
        xs = p.dram("xs", [NTA, 128, D], F32, "ExternalInput")
        cvT = p.dram("cvT", [128, 2, 8], F32, "ExternalInput")
        w_ada = p.dram("w_ada", [D, 3 * D], F32, "ExternalInput")
        b_ada = p.dram("b_ada", [3 * D], F32, "ExternalInput")
        wg_d = p.dram("w_ff_gate", [D, DFF], F32, "ExternalInput")
        wu_d = p.dram("w_ff_up", [D, DFF], F32, "ExternalInput")
        wd_d = p.dram("w_ff_down", [DFF, D], F32, "ExternalInput")
        l2g_d = p.dram("ln2_g", [D], F32, "ExternalInput")
        l2b_d = p.dram("ln2_b", [D], F32, "ExternalInput")
        ident_d = p.dram("ident", [128, 128], F32, "ExternalInput")
        xo = p.dram("xo", [NTA, 128, D], F32, "ExternalOutput")
        A = NS()
        A.xs = lambda t: xs[t]
        A.xo = lambda t: xo[t]
        A.mi0 = 0
        A.ident_d, A.cvT, A.w_ada, A.b_ada, A.l2g_d, A.l2b_d = ident_d, cvT, w_ada, b_ada, l2g_d, l2b_d
        A.wg_d, A.wu_d, A.wd_d = wg_d, wu_d, wd_d
        emit_postb(p, cx, NT, NCT, A)
        p.wait_all("sp", [xo[:]])
        print("POSTB ops", p.n_ops, "waits", p.n_waits)
    return nc


import ml_dtypes
NPBF = ml_dtypes.bfloat16


def _unpack(o, NT):
    C = o.shape[-1]
    lat = o[:, :NT].reshape(B, -1, C)
    ctx = o[0:4, NT].reshape(B, CTX, C) if o.shape[1] > NT else None
    return lat, ctx


def _pack(lat, ctx, core, NT, TPC, with_ctx):
    b, seg = core // 4, core % 4
    a = lat[b, seg * TPC:(seg + 1) * TPC].reshape(NT, 128, -1)
    if with_ctx:
        a = np.concatenate([a, ctx.reshape(4, 128, -1)[core % 4][None]], axis=0)
    return np.ascontiguousarray(a)


def _gla_consts():
    j = np.arange(128)[:, None]
    i = np.arange(128)[None, :]
    same = (j // CHUNK) == (i // CHUNK)
    Lm = (same & (j <= i)).astype(np.float32)
    Um = (same & (j > i)).astype(np.float32)
    sel = ((np.arange(128)[:, None] // CHUNK) == np.arange(2)[None, :]).astype(np.float32)
    return np.ascontiguousarray(np.concatenate([Lm, Um, sel], axis=1))


def host_mix(l, lat16, ctx16, latg, ctxg, P, S, with_ctx_q):
    L = S + CTX
    NKT = L // 128
    cst = _gla_consts()
    in_maps = []
    for core in range(NCORES):
        b, r = core // 4, core % 4
        h0, kv = 2 * r, r // 2
        Ql = lat16[b, :, OQ:OQ + 1024].reshape(S, NH, HD)
        Qc = ctx16[b, :, OQ:OQ + 1024].reshape(CTX, NH, HD)
        qT = np.stack([np.concatenate([Ql[:, h0 + hh], Qc[:, h0 + hh]], 0).T for hh in range(2)])
        Kl = lat16[b, :, OK_:OK_ + 256].reshape(S, NKV, HD)[:, kv]
        Kc = ctx16[b, :, OK_:OK_ + 256].reshape(CTX, NKV, HD)[:, kv]
        kT = np.concatenate([Kl, Kc], 0).T
        Vl = lat16[b, :, OV_:OV_ + 256].reshape(S, NKV, HD)[:, kv]
        Vc = ctx16[b, :, OV_:OV_ + 256].reshape(CTX, NKV, HD)[:, kv]
        Vall = np.concatenate([Vl, Vc], 0)
        vE = np.concatenate([Vall, np.ones((L, 1), NPBF)], 1).reshape(NKT, 128, 129).transpose(1, 0, 2)

        def seq(al, ac, d):
            if d == 0:
                return np.concatenate([ac, al], 0)
            return np.concatenate([al, ac], 0)[::-1]
        hd = r
        gq, gkT, gk, gv, gg = [], [], [], [], []
        for d in range(2):
            q_ = seq(lat16[b, :, OQG + hd * 128:OQG + (hd + 1) * 128], ctx16[b, :, OQG + hd * 128:OQG + (hd + 1) * 128], d)
            k_ = seq(lat16[b, :, OKG + hd * 128:OKG + (hd + 1) * 128], ctx16[b, :, OKG + hd * 128:OKG + (hd + 1) * 128], d)
            v_ = seq(lat16[b, :, OVG + hd * 256:OVG + (hd + 1) * 256], ctx16[b, :, OVG + hd * 256:OVG + (hd + 1) * 256], d)
            g_ = seq(latg[b, :, d * 512 + hd * 128:d * 512 + (hd + 1) * 128], ctxg[b, :, d * 512 + hd * 128:d * 512 + (hd + 1) * 128], d)
            gq.append(q_.T); gkT.append(k_.T); gk.append(k_.reshape(NKT, 128, 128))
            gv.append(v_.reshape(NKT, 128, 256)); gg.append(g_.reshape(NKT, 128, 128))
        ch = slice(core * 128, (core + 1) * 128)
        cy = np.zeros((128, B, S + 30), NPBF)
        cyc = np.zeros((128, B, CTX + 30), NPBF)
        for bb in range(B):
            cy[:, bb, 15:15 + S] = lat16[bb, :, OY:OY + 1024][:, ch].T
            cyc[:, bb, 15:15 + CTX] = ctx16[bb, :, OY:OY + 1024][:, ch].T
        cw = np.concatenate([P["conv_w_dw"][l][:, 0, ch].T, P["conv_b_dw"][l][ch][:, None]], 1).astype(np.float32)
        in_maps.append({
            "qT": np.ascontiguousarray(qT), "kT": np.ascontiguousarray(kT), "vE": np.ascontiguousarray(vE),
            "gqT": np.ascontiguousarray(np.stack(gq)), "gkT": np.ascontiguousarray(np.stack(gkT)),
            "gk": np.ascontiguousarray(np.stack(gk)), "gv": np.ascontiguousarray(np.stack(gv)),
            "gg": np.ascontiguousarray(np.stack(gg)).astype(np.float32), "cst": cst,
            "cy": cy, "cyc": cyc, "cw": np.ascontiguousarray(cw),
        })
    res = _run(("mix", S, with_ctx_q), lambda: build_mix(S, with_ctx_q), in_maps)
    att_l = np.zeros((B, S, D), NPBF); att_c = np.zeros((B, CTX, D), NPBF)
    gf_l = np.zeros((B, S, D), np.float32); gb_l = np.zeros((B, S, D), np.float32)
    gf_c = np.zeros((B, CTX, D), np.float32); gb_c = np.zeros((B, CTX, D), np.float32)
    cv_l = np.zeros((B, S, D), np.float32); cv_c = np.zeros((B, CTX, D), np.float32)
    for core in range(NCORES):
        b, r = core // 4, core % 4
        oa = res[core]["oatt"].reshape(2, L, HD)
        for hh in range(2):
            hs = slice((2 * r + hh) * HD, (2 * r + hh + 1) * HD)
            att_l[b, :, hs] = oa[hh, :S]
            att_c[b, :, hs] = oa[hh, S:]
        og = res[core]["ogla"].reshape(2, L, GDV)
        vs = slice(r * GDV, (r + 1) * GDV)
        gf_c[b, :, vs] = og[0, :CTX]; gf_l[b, :, vs] = og[0, CTX:]
        ob = og[1][::-1]
        gb_l[b, :, vs] = ob[:S]; gb_c[b, :, vs] = ob[S:]
        ch = slice(core * 128, (core + 1) * 128)
        for bb in range(B):
            cv_l[bb, :, ch] = res[core]["oconv"][:, bb, :].T
            cv_c[bb, :, ch] = res[core]["oconvc"][:, bb, :].T
    return (att_l, att_c), (gf_l, gf_c), (gb_l, gb_c), (cv_l, cv_c)


def host_post(l, mixo, lat16, ctx16, x, xc, c, c_ctx, P, S, with_ctx):
    TPC = S * B // NCORES
    NT = TPC // 128
    (att_l, att_c), (gf_l, gf_c), (gb_l, gb_c), (cv_l, cv_c) = mixo
    ident = np.eye(128, dtype=np.float32)
    in_maps = []
    for core in range(NCORES):
        b = core // 4
        pk = lambda al, ac: _pack(al, ac, core, NT, TPC, with_ctx)
        cv = np.ascontiguousarray(np.stack([c[b], c_ctx]).reshape(2, 8, 128).transpose(2, 0, 1))
        in_maps.append({
            "oatt_t": pk(att_l, att_c), "ogf": pk(gf_l, gf_c), "ogb": pk(gb_l, gb_c),
            "sr": pk(lat16[:, :, OR:OR + 1024], ctx16[:, :, OR:OR + 1024]),
            "cv": pk(cv_l, cv_c), "gt": pk(lat16[:, :, OGT:OGT + 3072], ctx16[:, :, OGT:OGT + 3072]),
            "xs": pk(x, xc), "cvT": cv, "w_ada": np.ascontiguousarray(P["w_ada"][l][:, 2 * D:3 * D]),
            "b_ada": np.ascontiguousarray(P["b_ada"][l][2 * D:3 * D]),
            "w_att_o": P["w_att_o"][l], "w_gla_o": P["w_gla_o"][l], "w_conv_o": P["w_conv_o"][l],
            "w_out": P["w_out"][l], "gla_norm": P["gla_norm"][l], "conv_ln_g": P["conv_ln_g"][l],
            "conv_ln_b": P["conv_ln_b"][l], "ln1_g": P["ln1_g"][l], "ln1_b": P["ln1_b"][l], "ident": ident,
        })
    nct = 1 if with_ctx else 0
    res = _run(("posta", NT, nct), lambda: build_posta(NT, nct), in_maps)
    xo = np.stack([r["xo"] for r in res])
    x1, xc1 = _unpack(xo, NT)
    in_maps = []
    for core in range(NCORES):
        b = core // 4
        cv = np.ascontiguousarray(np.stack([c[b], c_ctx]).reshape(2, 8, 128).transpose(2, 0, 1))
        in_maps.append({
            "xs": _pack(x1, xc1, core, NT, TPC, with_ctx), "cvT": cv,
            "w_ada": np.ascontiguousarray(P["w_ada"][l][:, 3 * D:6 * D]),
            "b_ada": np.ascontiguousarray(P["b_ada"][l][3 * D:6 * D]), "w_ff_gate": P["w_ff_gate"][l], "w_ff_up": P["w_ff_up"][l],
            "w_ff_down": P["w_ff_down"][l], "ln2_g": P["ln2_g"][l], "ln2_b": P["ln2_b"][l], "ident": ident,
        })
    res = _run(("postb", NT, nct), lambda: build_postb(NT, nct), in_maps)
    xo = np.stack([r["xo"] for r in res])
    return _unpack(xo, NT)


def forward(P, S):
    x = np.ascontiguousarray(P["x"][:, :S])
    xc = P["ctx"]
    c, c_ctx = P["c"], P["c_ctx"]
    TPC = S * B // NCORES
    NT = TPC // 128
    for l in range(DEPTH):
        last = l == DEPTH - 1
        o16, o32 = host_p1(l, x, xc, c, c_ctx, P, S)
        lat16, ctx16 = _unpack(o16, NT)
        latg, ctxg = _unpack(o32, NT)
        mixo = host_mix(l, lat16, ctx16, latg, ctxg, P, S, not last)
        x, xc_new = host_post(l, mixo, lat16, ctx16, x, xc, c, c_ctx, P, S, not last)
        if not last:
            xc = xc_new
    return x


def kernel(**inputs):
    P = {k: np.asarray(v) for k, v in inputs.items()}
    S = P["x"].shape[1]
    out = forward(P, S)
    return np.ascontiguousarray(out.astype(np.float32))


class _Stop(Exception):
    pass


def emit_mix_fused(p, cx, NT, M):
    depth0 = len(getattr(p, "scopes", []))
    try:
        _emit_mix_fused(p, cx, NT, M)
    except _Stop:
        while len(p.scopes) > depth0:
            p.pop_scope()


def _emit_mix_fused(p, cx, NT, M):
    def chk(x):
        if getattr(M, "stop_after", 3) < x:
            raise _Stop()
    NCT = 2
    NTA = NT + NCT
    TPC = NT * 128
    NTOK = NTA * 128
    SEGS = 4
    Sfull = SEGS * TPC
    L = Sfull + CTX
    NKT = L // 128
    banks = cx.banks
    o16, o32 = M.o16, M.o32
    groups = [[0, 1, 2, 3], [4, 5, 6, 7]]

    cx.nrot = 4
    p.push_scope()
    ident = p.sb("ident_sb", [128, 128], BF16)
    p.dma("pool", ident[:], M.ident_d[:])
    identf = p.sb("identf", [128, 128], F32)
    p.dma("sp", identf[:], M.ident_d[:])
    cst = p.sb("cst", [128, 514], F32)
    p.dma("sp", cst[:], M.gcst_d[:])
    LM = (cst[:, 0:128], cst[:, 256:384])
    UM = (cst[:, 128:256], cst[:, 384:512])
    sel = cst[:, 512:514]
    masks = p.sb("masks", [128, 8], F32)
    p.dma("sp", masks[:], M.masks_d[:])
    gqT = p.sb("gqT", [128, 4, NTOK], BF16)
    gkT = p.sb("gkT", [128, 4, NTOK], BF16)
    kTl = p.sb("kTl", [128, 2, NTOK], BF16)
    veb = p.sb("veb", [128, NTA, 2, 129], BF16)
    p.memset("dve", veb[:], 1.0)
    St = [p.sb("gS%d" % i, [128, 256], F32) for i in range(8)]
    Sb = [p.sb("gSb%d" % i, [128, 256], BF16) for i in range(8)]
    Sctx = [p.sb("gSc%d" % i, [128, 256], F32) for i in range(8)]
    logD = p.sb("logD", [128, 8], F32)
    p.memset("dve", logD[:], 1.0)
    gl = {}
    for nm, shp, dt in (("k", [128, 128], BF16), ("v", [128, 256], BF16), ("g", [128, 128], F32),
                        ("EbT", [128, 128], F32), ("EnbT", [128, 128], F32), ("Erem", [128, 128], F32),
                        ("Eend", [128, 2], F32), ("qeT", [128, 128], BF16), ("keT", [128, 128], BF16),
                        ("kend", [128, 128], BF16), ("ATm", [128, 128], BF16), ("osb", [64, 2, 256], F32)):
        gl[nm] = [[p.sb("g_%s_%d_%d" % (nm, s_, j), shp, dt) for j in range(2)] for s_ in range(4)]
    cnt = {"gl": 0}

    def gla_tile(slot, r, t, full, S_t, S_b):
        hd, d = r // 2, r % 2
        j = cnt["gl"] % 2
        cnt["gl"] += 1
        T_ = {k_: v_[slot][j] for k_, v_ in gl.items()}
        pb = banks[2 * slot:2 * slot + 2]
        p.dma("sp", T_["k"][:], V(o16, o16.h[t, :, OKG + hd * 128:OKG + (hd + 1) * 128]))
        p.dma("sp", T_["v"][:], V(o16, o16.h[t, :, OVG + hd * 256:OVG + (hd + 1) * 256]))
        p.dma("sp", T_["g"][:], V(o32, o32.h[t, :, d * 512 + hd * 128:d * 512 + (hd + 1) * 128]))
        g = T_["g"]
        Lc, Uc = LM[d], UM[d]
        b0, b1 = pb
        ts_ = slice(t * 128, (t + 1) * 128)
        if full:
            p.mm(b0[:, 0:128], [(g[:], Lc)])
        p.mm(b0[:, 128:256], [(Uc, g[:])])
        p.mm(b0[:, 256:258], [(g[:], sel)])
        if full:
            p.act(T_["EbT"][:], b0[:, 0:128], AF.Exp)
            p.act(T_["EnbT"][:], b0[:, 0:128], AF.Exp, scale=-1.0)
        p.act(T_["Erem"][:], b0[:, 128:256], AF.Exp)
        p.act(T_["Eend"][:], b0[:, 256:258], AF.Exp)
        if full:
            p.tt("dve", T_["qeT"][:], gqT[:, hd, ts_], T_["EbT"][:], ALU.mult)
            p.tt("dve", T_["keT"][:], gkT[:, hd, ts_], T_["EnbT"][:], ALU.mult)
        else:
            p.ts("dve", logD[:, r:r + 1], logD[:, r:r + 1], T_["Eend"][:, 0:1], T_["Eend"][:, 1:2], ALU.mult, ALU.mult)
        p.tt("dve", T_["kend"][:], T_["k"][:], T_["Erem"][:], ALU.mult)
        if full:
            p.mm(b0[:, 384:512], [(T_["keT"][:], T_["qeT"][:])])
            p.tt("dve", T_["ATm"][:], b0[:, 384:512], Lc, ALU.mult)
        for c in ((0, 1) if d == 0 else (1, 0)):
            cs = slice(c * 64, (c + 1) * 64)
            if full:
                ops_ = V(b1, b1.h[0:64, 0:256])
                p.mm(ops_, [(T_["qeT"][:, cs], S_b[:]), (T_["ATm"][:, cs], T_["v"][:])])
                p.act(T_["osb"][:, c, :], ops_, AF.Identity)
            ups = V(b1, b1.h[:, 256:512])
            p.mm(ups, [(T_["kend"][cs, :], T_["v"][cs, :])])
            p.stt("dve", S_t[:], S_t[:], T_["Eend"][:, c:c + 1], ups, ALU.mult, ALU.add)
            if full:
                p.act(S_b[:], S_t[:], AF.Identity)
        if full:
            dst = M.gf_s if d == 0 else M.gb_s
            p.dma("sp", V(dst, dst.h[t, :, hd * 256:(hd + 1) * 256].rearrange("(c q) e -> q c e", q=64)), T_["osb"][:])

    def tiles_for(d, ctx):
        if ctx:
            return [NT, NT + 1] if d == 0 else [NT + 1, NT]
        return list(range(NT)) if d == 0 else list(range(NT - 1, -1, -1))

    slotcnt = [0, 0, 0, 0]

    def conv_gen():
        yT = p.sb("yT", [128, 8, TPC + 30], BF16)
        yTc = p.sb("yTc", [128, 8, CTX + 30], BF16)
        p.memset("dve", yTc[:], 0.0)
        ldy = [p.sb("ldy%d" % i, [128, 1024], BF16) for i in range(2)]
        for t in range(NTA):
            a = ldy[t % 2]
            p.dma("sp", a[:], V(o16, o16.h[t, :, OY:OY + 1024]))
            if t < NT:
                transpose_into(p, cx, a, yT, t, ident, 8, col0=15 + t * 128)
            else:
                transpose_into(p, cx, a, yTc, t, ident, 8, col0=15 + (t - NT) * 128)
            yield
        E = p.sb("E", [128, 1024], BF16)
        p.dma("sp", E[:], V(M.yrcv, M.yrcv.h.bitcast(BF16)))
        selm = p.sb("selm", [128, 30], BF16)
        p.dma("pool", selm[:], M.selm_d[:])
        for c in range(8):
            ps = cx.bank()
            p.mm(ps[:, 0:30], [(E[:, c * 128:(c + 1) * 128], selm[:])])
            p.act(yT.k(("hl", c), (slice(None), c, slice(0, 15))), ps[:, 0:15], AF.Identity)
            p.act(yT.k(("hr", c), (slice(None), c, slice(15 + TPC, 30 + TPC))), ps[:, 15:30], AF.Identity)
        yield
        cwl = p.sb("cwl", [32, 1024], F32)
        p.dma("sp", cwl[0:31, :], M.conv_w[:])
        p.dma("sp", cwl[31:32, :], M.conv_b[:])
        cwt = p.sb("cwt", [128, 8, 32], F32)
        for c in range(8):
            ps = cx.bank()
            p.tr(ps[:, 0:32], cwl[:, c * 128:(c + 1) * 128], identf[0:32, 0:32])
            p.act(cwt[:, c, :], ps[:, 0:32], AF.Identity)
        yield
        accs = [p.sb("cacc%d" % i, [128, TPC], F32) for i in range(2)]
        accc = [p.sb("caccc%d" % i, [128, CTX], F32) for i in range(2)]
        cvst = [p.sb("cvst%d" % i, [128, 4, 128], F32) for i in range(2)]
        cnt_ = {"ci": 0}

        def finish(c):
            for (a, n, t0) in ((accs[c % 2], TPC, 0), (accc[c % 2], CTX, NT)):
                for g0_ in range(0, n // 128, 4):
                    ng = min(4, n // 128 - g0_)
                    ps = cx.bank()
                    for k_ in range(ng):
                        p.tr(ps[:, k_ * 128:(k_ + 1) * 128], a[:, (g0_ + k_) * 128:(g0_ + k_ + 1) * 128], identf[:])
                    st_ = cvst[cnt_["ci"] % 2]
                    cnt_["ci"] += 1
                    p.act(V(st_, st_.h[:, 0:ng, :]), V(ps, ps.h[:, 0:ng * 128].rearrange("q (k f) -> q k f", f=128)), AF.Identity)
                    p.dma("sp", V(M.cv_s, M.cv_s.h[t0 + g0_:t0 + g0_ + ng, :, c * 128:(c + 1) * 128].rearrange("t q f -> q t f")),
                          V(st_, st_.h[:, 0:ng, :]))

        for c in range(8):
            for (a, ysrc, n) in ((accs[c % 2], yT, TPC), (accc[c % 2], yTc, CTX)):
                for tap in range(CW):
                    yv = V(ysrc, ysrc.h[:, c, tap:tap + n])
                    if tap == 0:
                        p.ts("dve", a[:], yv, cwt[:, c, 0:1], None, ALU.mult)
                    else:
                        p.stt("dve", a[:], yv, cwt[:, c, tap:tap + 1], a[:], ALU.mult, ALU.add)
                    if n == TPC and tap % 2 == 1:
                        yield
                    if n == TPC and tap == 9 and c > 0:
                        finish(c - 1)
                p.ts("dve", a[:], a[:], cwt[:, c, 31:32], None, ALU.add)
            yield
        finish(7)


    def gla_group(items, full):
        cs_ = []
        for (slot, r, t, S_t, S_b) in items:
            hd, d = r // 2, r % 2
            j = slotcnt[slot] % 2
            slotcnt[slot] += 1
            T_ = {k_: v_[slot][j] for k_, v_ in gl.items()}
            b0 = banks[4 + slot]
            p.dma("sp", T_["k"][:], V(o16, o16.h[t, :, OKG + hd * 128:OKG + (hd + 1) * 128]))
            p.dma("sp", T_["v"][:], V(o16, o16.h[t, :, OVG + hd * 256:OVG + (hd + 1) * 256]))
            p.dma("sp", T_["g"][:], V(o32, o32.h[t, :, d * 512 + hd * 128:d * 512 + (hd + 1) * 128]))
            g = T_["g"]
            if full:
                p.mm(b0[:, 0:128], [(g[:], LM[d])])
            p.mm(b0[:, 128:256], [(UM[d], g[:])])
            p.mm(b0[:, 256:258], [(g[:], sel)])
            cs_.append((slot, r, t, S_t, S_b, hd, d, T_, b0))
        yield
        for (slot, r, t, S_t, S_b, hd, d, T_, b0) in cs_:
            if full:
                p.act(T_["EbT"][:], b0[:, 0:128], AF.Exp)
                p.act(T_["EnbT"][:], b0[:, 0:128], AF.Exp, scale=-1.0)
            p.act(T_["Erem"][:], b0[:, 128:256], AF.Exp)
            p.act(T_["Eend"][:], b0[:, 256:258], AF.Exp)
        yield
        for (slot, r, t, S_t, S_b, hd, d, T_, b0) in cs_:
            ts_ = slice(t * 128, (t + 1) * 128)
            if full:
                p.tt("dve", T_["qeT"][:], gqT[:, hd, ts_], T_["EbT"][:], ALU.mult)
                p.tt("dve", T_["keT"][:], gkT[:, hd, ts_], T_["EnbT"][:], ALU.mult)
            else:
                p.ts("dve", logD[:, r:r + 1], logD[:, r:r + 1], T_["Eend"][:, 0:1], T_["Eend"][:, 1:2], ALU.mult, ALU.mult)
            p.tt("dve", T_["kend"][:], T_["k"][:], T_["Erem"][:], ALU.mult)
        yield
        if full:
            for (slot, r, t, S_t, S_b, hd, d, T_, b0) in cs_:
                p.mm(b0[:, 384:512], [(T_["keT"][:], T_["qeT"][:])])
            yield
            for (slot, r, t, S_t, S_b, hd, d, T_, b0) in cs_:
                p.tt("dve", T_["ATm"][:], b0[:, 384:512], LM[d], ALU.mult)
            yield
        for ci in range(2):
            for (slot, r, t, S_t, S_b, hd, d, T_, b0) in cs_:
                c = ci if d == 0 else 1 - ci
                cs = slice(c * 64, (c + 1) * 64)
                if full:
                    ops_ = V(b0, b0.h[0:64, 0:256])
                    p.mm(ops_, [(T_["qeT"][:, cs], S_b[:]), (T_["ATm"][:, cs], T_["v"][:])])
                ups = V(b0, b0.h[:, 256:512])
                p.mm(ups, [(T_["kend"][cs, :], T_["v"][cs, :])])
            yield
            for (slot, r, t, S_t, S_b, hd, d, T_, b0) in cs_:
                c = ci if d == 0 else 1 - ci
                if full:
                    p.copy("dve", T_["osb"][:, c, :], V(b0, b0.h[0:64, 0:256]))
                p.stt("dve", S_t[:], S_t[:], T_["Eend"][:, c:c + 1], V(b0, b0.h[:, 256:512]), ALU.mult, ALU.add)
                if full:
                    p.act(S_b[:], S_t[:], AF.Identity)
            yield
        if full:
            for (slot, r, t, S_t, S_b, hd, d, T_, b0) in cs_:
                dst = M.gf_s if d == 0 else M.gb_s
                p.dma("sp", V(dst, dst.h[t, :, hd * 256:(hd + 1) * 256].rearrange("(c q) e -> q c e", q=64)), T_["osb"][:])

    def gla_pass_gen(ctx, full):
        n = 2 if ctx else NT
        for k_ in range(n):
            for g0_ in (0, 4):
                items = []
                for r in range(g0_, g0_ + 4):
                    d = r % 2
                    items.append((r % 4, r, tiles_for(d, ctx)[k_], St[r], Sb[r] if full else None))
                yield from gla_group(items, full)
                yield

    def gla_pass(ctx, full, side=None):
        for _ in gla_pass_gen(ctx, full):
            if side is not None:
                next(side, None)

    p.push_scope()
    ld = [p.sb("ld%d" % i, [128, 1280], BF16) for i in range(2)]
    for t in range(NTA):
        a = ld[t % 2]
        p.dma("sp", a[:, 0:256], V(o16, o16.h[t, :, OK_:OK_ + 256]))
        p.dma("sp", a[:, 256:768], V(o16, o16.h[t, :, OKG:OKG + 512]))
        p.dma("sp", a[:, 768:1280], V(o16, o16.h[t, :, OQG:OQG + 512]))
        transpose_into(p, cx, a, kTl, t, ident, 2, src0=0)
        transpose_into(p, cx, a, gkT, t, ident, 4, src0=256)
        transpose_into(p, cx, a, gqT, t, ident, 4, src0=768)
        p.dma("sp", veb.k(("t", t), (slice(None), t, slice(None), slice(0, 128))),
              V(o16, o16.h[t, :, OV_:OV_ + 256].rearrange("p (k e) -> p k e", k=2)))
    chk(0.05)
    ksb = M.ksnd.h.bitcast(BF16)
    for kvh in range(2):
        p.dma("sp", V(M.ksnd, ksb[kvh * 128:(kvh + 1) * 128, :]), kTl[:, kvh, 0:TPC])
    NH2 = NT // 2
    for hf in range(2):
        vs_ = M.vsnd[hf]
        vsb = vs_.h.bitcast(BF16)
        p.dma("sp", V(vs_, vsb.rearrange("p (t k e) -> p t k e", t=NH2, k=2)), veb[:, hf * NH2:(hf + 1) * NH2, :, :])
    chk(0.1)
    yed = p.sb("yed", [32, 1024], BF16)
    p.memset("dve", yed[:], 0.0)
    p.dma("sp", yed[0:15, :], V(o16, o16.h[0, 0:15, OY:OY + 1024]))
    p.dma("sp", yed[15:30, :], V(o16, o16.h[NT - 1, 113:128, OY:OY + 1024]))
    p.dma("sp", V(M.ysnd, M.ysnd.h.bitcast(BF16)), yed[:])
    chk(0.15)
    p.collective("AllGather", [M.ksnd[:]], [M.krcv[:]], groups)
    for hf in range(2):
        p.collective("AllGather", [M.vsnd[hf][:]], [M.vrcv[hf][:]], groups)
    p.collective("AllGather", [M.ysnd[:]], [M.yrcv[:]], groups)
    chk(0.2)
    for i in range(8):
        p.memset("dve", St[i][:], 0.0)
        p.memset("dve", Sb[i][:], 0.0)
    cgen = conv_gen()
    gla_pass(True, True, cgen)
    for r in range(8):
        p.copy("dve", Sctx[r][:], St[r][:])
        p.memset("dve", St[r][:], 0.0)
    chk(0.3)
    gla_pass(False, False, cgen)
    for _ in cgen:
        pass
    for r in range(8):
        p.dma("sp", V(M.gsnd, M.gsnd.h[r * 128:(r + 1) * 128, 0:256]), St[r][:])
    chk(0.4)
    dst_ = p.sb("dstage", [128, 64], F32)
    p.memset("dve", dst_[:], 0.0)
    p.copy("dve", dst_[:, 0:8], logD[:])
    p.dma("sp", M.dsnd[:], dst_[:])
    p.collective("AllGather", [M.gsnd[:]], [M.grcv[:]], groups)
    p.collective("AllGather", [M.dsnd[:]], [M.drcv[:]], groups)
    Dall = p.sb("Dall", [128, 4, 64], F32)
    p.dma("sp", Dall[:], V(M.drcv, M.drcv.h.rearrange("(s q) c -> q s c", s=4)))
    chk(0.5)
    G = [p.sb("G%d" % i, [128, 4, 256], F32) for i in range(1)]
    coef = p.sb("coef", [128, 2], F32)
    gv = M.grcv.h.rearrange("(s r q) c -> q s r c", s=4, r=8)
    for r in range(8):
        d = r % 2
        Gt = G[0]
        p.dma("sp", Gt[:], V(M.grcv, gv[:, :, r, :]))
        p.copy("dve", St[r][:], Sctx[r][:])
        for s_ in (range(4) if d == 0 else range(3, -1, -1)):
            m = masks[:, d * 4 + s_:d * 4 + s_ + 1]
            p.ts("dve", coef[:, 1:2], Dall[:, s_, r:r + 1], -1.0, m, ALU.add, ALU.mult)
            p.ts("dve", coef[:, 1:2], coef[:, 1:2], 1.0, None, ALU.add)
            p.ts("dve", St[r][:], St[r][:], coef[:, 1:2], None, ALU.mult)
            p.stt("dve", St[r][:], Gt[:, s_, 0:256], m, St[r][:], ALU.mult, ALU.add)
        p.act(Sb[r][:], St[r][:], AF.Identity)
    p.pop_scope()

    if getattr(M, "stop_after", 3) < 3:
        p.pop_scope()
        return
    p.push_scope()
    qT = p.sb("qT", [128, 8, NTOK], BF16)
    ld = [p.sb("ldq%d" % i, [128, 1024], BF16) for i in range(2)]
    for t in range(NTA):
        a = ld[t % 2]
        p.dma("sp", a[:], V(o16, o16.h[t, :, OQ:OQ + 1024]))
        transpose_into(p, cx, a, qT, t, ident, 8)
    kT = p.sb("kT", [128, L], BF16)
    vE = p.sb("vE", [128, NKT, 129], BF16)
    pTs = [p.sb("pTs%d" % i, [128, 512], BF16) for i in range(6)]
    rcp = p.sb("rcp", [128, 4], F32)
    ost = [p.sb("ost%d" % i, [128, 128], BF16) for i in range(4)]
    SC = HD ** -0.5
    krb = M.krcv.h.bitcast(BF16)
    NH2 = NT // 2
    vrb = [M.vrcv[hf].h.bitcast(BF16) for hf in range(2)]
    state = {"pi": 0, "sb": 0}

    def load_kv(kvh):
        for s_ in range(SEGS):
            p.dma("sp", kT[:, s_ * TPC:(s_ + 1) * TPC], V(M.krcv, krb[s_ * 256 + kvh * 128:s_ * 256 + (kvh + 1) * 128, :]))
            for hf in range(2):
                p.dma("sp", vE[:, s_ * NT + hf * NH2:s_ * NT + (hf + 1) * NH2, :],
                      V(M.vrcv[hf], vrb[hf][s_ * 128:(s_ + 1) * 128, :].rearrange("q (t k e) -> q t k e", t=NH2, k=2)[:, :, kvh, :]))
        p.act(kT[:, Sfull:L], kTl[:, kvh, TPC:NTOK], AF.Identity)
        p.copy("dve", vE[:, SEGS * NT:NKT, :], veb[:, NT:NTA, kvh, :])

    def att_block(blk):
        h, q0, nq, kt0, kt1 = blk
        nsub = nq // 128
        LOOK = 2
        pend = []

        def pv(kt, pt):
            def emit(e):
                ins = None
                for qs in range(nsub):
                    ob = banks[qs // 2]
                    ins = e.matmul(ob.h[:, (qs % 2) * 256:(qs % 2) * 256 + 129], pt.h[:, qs * 128:(qs + 1) * 128],
                                   vE.h[:, kt, :], start=(kt == kt0), stop=(kt == kt1 - 1))
                return ins
            writes = [V(banks[qs // 2], banks[qs // 2].h[:, (qs % 2) * 256:(qs % 2) * 256 + 129], ("o", qs % 2)) for qs in range(nsub)]
            p.op("pe", [pt[:, 0:nq], vE[:, kt, :]], writes, emit)

        for kt in range(kt0, kt1):
            sbk = banks[2 + state["sb"] % 6]
            state["sb"] += 1
            p.mm(sbk[:, 0:nq], [(kT[:, kt * 128:(kt + 1) * 128], qT[:, h, q0:q0 + nq])])
            pt = pTs[state["pi"] % 6]
            state["pi"] += 1
            p.act(pt[:, 0:nq], sbk[:, 0:nq], AF.Exp, scale=SC)
            pend.append((kt, pt))
            if len(pend) > LOOK:
                pv(*pend.pop(0))
        while pend:
            pv(*pend.pop(0))
        for qs in range(nsub):
            ob = banks[qs // 2]
            base = (qs % 2) * 256
            okey = V(ob, ob.h[:, base:base + 129], ("o", qs % 2))
            p.op("dve", [okey], [rcp[:, qs:qs + 1]],
                 lambda e, ob=ob, base=base, qs=qs: e.reciprocal(rcp.h[:, qs:qs + 1], ob.h[:, base + 128:base + 129]))
            p.ts("dve", ost[qs][:], V(ob, ob.h[:, base:base + 128], ("o", qs % 2)), rcp[:, qs:qs + 1], None, ALU.mult)
            p.dma("sp", V(M.att_s, M.att_s.h[(q0 // 128) + qs, :, h * 128:(h + 1) * 128]), ost[qs][:])

    for kvh in range(2):
        load_kv(kvh)
        for hh in range(4):
            h = kvh * 4 + hh
            for q0 in range(0, TPC, 512):
                att_block((h, q0, min(512, TPC - q0), 0, NKT))
            att_block((h, TPC, CTX, NKT - CTX // 128, NKT))
    gla_pass(False, True)
    p.pop_scope()
    p.pop_scope()
    cx.nrot = 8


def build_fused(NT=16):
    NCT = 2
    NTA = NT + NCT
    TPC = NT * 128
    nc = bass.Bass("TRN2", target_bir_lowering=False)
    es = ExitStack()
    with es:
        p = Prog(nc, es)
        cx = Ctx(p)
        X = lambda n, shp: p.dram(n, shp, F32, "ExternalInput")
        x_d = X("x", [NT, 128, D])
        ctx_d = X("ctxb", [NCT, 128, D])
        cvT = X("cvT", [128, 2, 8])
        cos_d = X("cos", [128, NT, 64])
        sin_d = X("sin", [128, NT, 64])
        ident_d = X("ident", [128, 128])
        gcst_d = X("gcst", [128, 514])
        masks_d = X("masks", [128, 8])
        selm_d = X("selm", [128, 30])
        W = {}
        for n, shp in (("w_ada", [DEPTH, D, 6 * D]), ("b_ada", [DEPTH, 6 * D]), ("w_in", [DEPTH, D, INC]),
                       ("q_norm", [DEPTH, HD]), ("k_norm", [DEPTH, HD]), ("w_att_o", [DEPTH, D, D]),
                       ("gla_w_a2", [DEPTH, 2, RANK, 512]), ("gla_b_a", [DEPTH, 2 * 512]), ("gla_norm", [DEPTH, GDV]),
                       ("w_gla_o", [DEPTH, D, D]), ("conv_w_dw", [DEPTH, CW, D]), ("conv_b_dw", [DEPTH, 1, D]),
                       ("conv_ln_g", [DEPTH, D]), ("conv_ln_b", [DEPTH, D]), ("w_conv_o", [DEPTH, D, D]),
                       ("w_out", [DEPTH, D, D]), ("ln1_g", [DEPTH, D]), ("ln1_b", [DEPTH, D]),
                       ("w_ff_gate", [DEPTH, D, DFF]), ("w_ff_up", [DEPTH, D, DFF]), ("w_ff_down", [DEPTH, DFF, D]),
                       ("ln2_g", [DEPTH, D]), ("ln2_b", [DEPTH, D])):
            W[n] = X(n, shp)
        out = p.dram("out", [NT, 128, D], F32, "ExternalOutput")
        I = lambda n, shp, dt: p.dram(n, shp, dt, "Internal")
        o16 = I("o16", [NTA, 128, C16], BF16)
        o32 = I("o32", [NTA, 128, 1024], F32)
        M = NS()
        M.o16, M.o32 = o16, o32
        M.att_s = I("att_s", [NTA, 128, D], BF16)
        M.gf_s = I("gf_s", [NTA, 128, D], F32)
        M.gb_s = I("gb_s", [NTA, 128, D], F32)
        M.cv_s = I("cv_s", [NTA, 128, D], F32)
        act_s = I("act_s", [NTA, 128, DFF // 128, 128], BF16)
        xa = I("xa", [NTA, 128, D], F32)
        xb = I("xb", [NTA, 128, D], F32)
        M.ksnd = I("ksnd", [256, TPC // 2], F32)
        M.krcv = I("krcv", [1024, TPC // 2], F32)
        M.vsnd = [I("vsnd%d" % i, [128, NT // 2 * 129], F32) for i in range(2)]
        M.vrcv = [I("vrcv%d" % i, [512, NT // 2 * 129], F32) for i in range(2)]
        M.gsnd = I("gsnd", [1024, 256], F32)
        M.grcv = I("grcv", [4096, 256], F32)
        M.dsnd = I("dsnd", [128, 64], F32)
        M.drcv = I("drcv", [512, 64], F32)
        M.ysnd = I("ysnd", [32, 512], F32)
        M.yrcv = I("yrcv", [128, 512], F32)
        M.ident_d, M.gcst_d, M.masks_d, M.selm_d = ident_d, gcst_d, masks_d, selm_d

        def lw(n, l):
            return T(p, W[n].h[l], "%s_l%d" % (n, l))

        for l in range(DEPTH):
            last = l == DEPTH - 1
            if l == 0:
                xs_fn = lambda t: (x_d[t] if t < NT else ctx_d[t - NT])
            else:
                xs_fn = lambda t: xb[t]

            class XS:
                def __getitem__(self, t):
                    return xs_fn(t)
            p.push_scope()
            wa2 = lw("gla_w_a2", l)
            emit_p1(p, cx, NT, NCT, XS(), cvT, lw("w_ada", l), lw("b_ada", l), lw("w_in", l), lw("q_norm", l),
                    lw("k_norm", l), wa2[0], wa2[1], lw("gla_b_a", l), cos_d, sin_d, ident_d, o16, o32, True)
            p.pop_scope()
            M.conv_w = lw("conv_w_dw", l)
            M.conv_b = lw("conv_b_dw", l)
            emit_mix_fused(p, cx, NT, M)
            nct = 0 if last else NCT
            A = NS()
            A.oatt = lambda t: M.att_s[t]
            A.ogf = lambda t: M.gf_s[t]
            A.ogb = lambda t: M.gb_s[t]
            A.sr = lambda t: V(o16, o16.h[t, :, OR:OR + 1024])
            A.cv = lambda t: M.cv_s[t]
            A.xs = xs_fn
            A.xo = lambda t: xa[t]
            A.gt = lambda t, c0, c1: V(o16, o16.h[t, :, OGT + c0:OGT + c1])
            A.g1_mi = 2
            A.ident_d, A.cvT, A.w_ada, A.b_ada = ident_d, cvT, lw("w_ada", l), lw("b_ada", l)
            A.vecs = {n: lw(n, l) for n in ("gla_norm", "conv_ln_g", "conv_ln_b", "ln1_g", "ln1_b")}
            A.wo = {n: lw(n, l) for n in ("w_att_o", "w_gla_o", "w_conv_o", "w_out")}
            p.push_scope()
            emit_posta(p, cx, NT, nct, A)
            p.pop_scope()
            Bn = NS()
            Bn.xs = lambda t: xa[t]
            Bn.xo = (lambda t: out[t]) if last else (lambda t: xb[t])
            Bn.mi0 = 3
            Bn.ident_d, Bn.cvT, Bn.w_ada, Bn.b_ada = ident_d, cvT, lw("w_ada", l), lw("b_ada", l)
            Bn.l2g_d, Bn.l2b_d = lw("ln2_g", l), lw("ln2_b", l)
            Bn.wg_d, Bn.wu_d, Bn.wd_d = lw("w_ff_gate", l), lw("w_ff_up", l), lw("w_ff_down", l)
            Bn.act_s = act_s
            p.push_scope()
            emit_postb2(p, cx, NT, nct, Bn)
            p.pop_scope()
        p.wait_all("sp", [out[:]])
        print("FUSED ops", p.n_ops, "waits", p.n_waits)
    return nc


def kernel_fused(P, S):
    TPC = S * B // NCORES
    NT = TPC // 128
    cos, sin = _rope_tables(S)
    gc = _gla_consts()
    Lm, Um, sel = gc[:, 0:128], gc[:, 128:256], gc[:, 256:258]
    gcst = np.ascontiguousarray(np.concatenate([Lm, Um, Lm.T, Um.T, sel], axis=1))
    ident = np.eye(128, dtype=np.float32)
    in_maps = []
    for core in range(NCORES):
        b, seg = core // 4, core % 4
        masks = np.zeros((128, 8), np.float32)
        for s_ in range(4):
            masks[:, s_] = 1.0 if s_ < seg else 0.0
            masks[:, 4 + s_] = 1.0 if s_ > seg else 0.0
        selm = np.zeros((128, 30), np.float32)
        for j in range(15):
            if seg > 0:
                selm[(seg - 1) * 32 + 15 + j, j] = 1.0
            if seg < 3:
                selm[(seg + 1) * 32 + j, 15 + j] = 1.0
        m = {
            "x": np.ascontiguousarray(P["x"][b, seg * TPC:(seg + 1) * TPC].reshape(NT, 128, D)),
            "ctxb": np.ascontiguousarray(P["ctx"][b].reshape(2, 128, D)),
            "cvT": np.ascontiguousarray(np.stack([P["c"][b], P["c_ctx"]]).reshape(2, 8, 128).transpose(2, 0, 1)),
            "cos": np.ascontiguousarray(cos[seg * TPC:(seg + 1) * TPC].reshape(NT, 128, 64).transpose(1, 0, 2)),
            "sin": np.ascontiguousarray(sin[seg * TPC:(seg + 1) * TPC].reshape(NT, 128, 64).transpose(1, 0, 2)),
            "ident": ident, "gcst": gcst, "masks": masks, "selm": selm,
        }
        for n in ("w_ada", "b_ada", "w_in", "q_norm", "k_norm", "w_att_o", "gla_w_a2", "gla_norm", "w_gla_o",
                  "conv_ln_g", "conv_ln_b", "w_conv_o", "w_out", "ln1_g", "ln1_b", "w_ff_gate", "w_ff_up",
                  "w_ff_down", "ln2_g", "ln2_b"):
            m[n] = P[n]
        m["gla_b_a"] = np.ascontiguousarray(P["gla_b_a"].reshape(DEPTH, 1024))
        m["conv_w_dw"] = np.ascontiguousarray(P["conv_w_dw"].reshape(DEPTH, CW, D))
        m["conv_b_dw"] = np.ascontiguousarray(P["conv_b_dw"].reshape(DEPTH, 1, D))
        in_maps.append(m)
    res = _run(("fused", NT), lambda: build_fused(NT), in_maps)
    out = np.stack([r["out"] for r in res])
    return np.ascontiguousarray(out.reshape(B, S, D))


def kernel(**inputs):
    P = {k: np.asarray(v) for k, v in inputs.items()}
    S = P["x"].shape[1]
    return kernel_fused(P, S).astype(np.float32)
```

```python
import numpy as np
from contextlib import ExitStack
import concourse.bass as bass
import concourse.mybir as mybir
from concourse.bass_utils import run_bass_kernel_spmd

F32 = mybir.dt.float32
BF16 = mybir.dt.bfloat16
AF = mybir.ActivationFunctionType
ALU = mybir.AluOpType
AX = mybir.AxisListType


class V:
    def __init__(self, t, ap, key=None):
        self.t = t
        self.ap = ap
        self.key = key


class T:
    def __init__(self, prog, handle, name):
        self.p = prog
        self.h = handle
        self.name = name
        self.state = {}

    def __getitem__(self, idx):
        return V(self, self.h[idx], None)

    def k(self, key, idx=None):
        if idx is None:
            return V(self, self.h[:], key)
        return V(self, self.h[idx], key)

    def _keys(self, key):
        if key is None:
            return list(self.state.keys())
        ks = [key]
        if None in self.state:
            ks.append(None)
        return [k for k in ks if k in self.state]

    def deps_read(self, key):
        out = []
        for k in self._keys(key):
            w = self.state[k][0]
            if w is not None:
                out.append(w)
        return out

    def deps_write(self, key):
        out = []
        for k in self._keys(key):
            w, rs = self.state[k]
            if w is not None:
                out.append(w)
            out.extend(rs)
        return out

    def add_reader(self, key, tok):
        st = self.state.setdefault(key, [None, []])
        st[1].append(tok)
        if len(st[1]) > 64:
            best = {}
            for (s, v) in st[1]:
                best[s] = max(best.get(s, 0), v)
            st[1] = list(best.items())

    def set_writer(self, key, tok):
        if key is None:
            self.state = {None: [tok, []]}
        else:
            self.state[key] = [tok, []]


class Prog:
    ENG = ("pe", "dve", "act", "pool", "sp")

    def __init__(self, nc, es, n_dma_sems=16):
        self.nc = nc
        self.es = es
        self.engs = {"pe": nc.tensor, "dve": nc.vector, "act": nc.scalar,
                     "pool": nc.gpsimd, "sp": nc.sync}
        self.sems = []
        self.esem = {}
        self.cnt = {}
        for e in self.ENG:
            self.esem[e] = self._new_sem("prog_" + e)
            self.cnt[e] = 0
        self.dsem = {}
        self.dcnt = {}
        self.dnext = {}
        for q in ("sp", "pool", "act"):
            self.dsem[q] = [self._new_sem("dma_%s_%d" % (q, i)) for i in range(n_dma_sems)]
            self.dcnt[q] = [0] * n_dma_sems
            self.dnext[q] = 0
        self.waited = {e: {} for e in self.ENG}
        self.n_ops = 0
        self.n_waits = 0
        self.uid = 0

    def _new_sem(self, name):
        h = self.es.enter_context(self.nc.semaphore(name))
        self.sems.append(h)
        return len(self.sems) - 1

    def push_scope(self):
        if not hasattr(self, "scopes"):
            self.scopes = []
            self.scope_id = 0
        st = ExitStack()
        st.__enter__()
        self.scopes.append(st)
        self.scope_id += 1

    def pop_scope(self):
        self.barrier()
        st = self.scopes.pop()
        st.__exit__(None, None, None)

    def sb(self, name, shape, dtype):
        scopes = getattr(self, "scopes", [])
        es = scopes[-1] if scopes else self.es
        sid = getattr(self, "scope_id", 0)
        h = es.enter_context(self.nc.sbuf_tensor("s%d_%s" % (sid, name), list(shape), dtype))
        return T(self, h, name)

    def ps(self, name, shape, dtype):
        h = self.es.enter_context(self.nc.psum_tensor("p_" + name, list(shape), dtype))
        return T(self, h, name)

    def dram(self, name, shape, dtype, kind):
        h = self.nc.dram_tensor(name, list(shape), dtype, kind=kind)
        return T(self, h.ap(), name)

    def _wait(self, eng, toks):
        best = {}
        for (s, v) in toks:
            if v <= 0:
                continue
            if eng == "pe" and s == self.esem["pe"]:
                continue
            if v > best.get(s, 0):
                best[s] = v
        w = self.waited[eng]
        e = self.engs[eng]
        for s, v in best.items():
            if w.get(s, 0) >= v:
                continue
            e.wait_ge(self.sems[s], v)
            w[s] = v
            self.n_waits += 1

    def op(self, eng, reads, writes, emit):
        toks = []
        for v in reads:
            toks += v.t.deps_read(v.key)
        for v in writes:
            toks += v.t.deps_write(v.key)
        self._wait(eng, toks)
        ins = emit(self.engs[eng])
        self.cnt[eng] += 1
        ins.then_inc(self.sems[self.esem[eng]], 1)
        tok = (self.esem[eng], self.cnt[eng])
        for v in reads:
            v.t.add_reader(v.key, tok)
        for v in writes:
            v.t.set_writer(v.key, tok)
        self.n_ops += 1
        return tok

    def dma(self, q, out, in_, **kw):
        toks = in_.t.deps_read(in_.key) + out.t.deps_write(out.key)
        i = self.dnext[q]
        self.dnext[q] = (i + 1) % len(self.dsem[q])
        s = self.dsem[q][i]
        toks.append((s, self.dcnt[q][i]))
        self._wait(q, toks)
        ins = self.engs[q].dma_start(out=out.ap, in_=in_.ap, **kw)
        self.dcnt[q][i] += 16
        ins.then_inc(self.sems[s], 16)
        tok = (s, self.dcnt[q][i])
        in_.t.add_reader(in_.key, tok)
        out.t.set_writer(out.key, tok)
        self.n_ops += 1
        return tok

    def collective(self, kind, ins, outs, groups):
        q = "pool"
        if not hasattr(self, "ccsem"):
            self.ccsem = self._new_sem("cc_sem")
            self.cccnt = 0
        toks = []
        for v in ins:
            toks += v.t.deps_read(v.key)
        for v in outs:
            toks += v.t.deps_write(v.key)
        toks.append((self.ccsem, self.cccnt))
        self._wait(q, toks)
        ins_ = self.engs[q].collective_compute(kind, ALU.bypass, groups, [v.ap for v in ins], [v.ap for v in outs])
        self.cccnt += 1
        ins_.then_inc(self.sems[self.ccsem], 1)
        tok = (self.ccsem, self.cccnt)
        for v in ins:
            v.t.add_reader(v.key, tok)
        for v in outs:
            v.t.set_writer(v.key, tok)
        self.n_ops += 1
        return tok

    def barrier(self):
        for e in self.ENG:
            toks = [(self.esem[f], self.cnt[f]) for f in self.ENG if f != e]
            for q in self.dsem:
                toks += [(s, c) for s, c in zip(self.dsem[q], self.dcnt[q])]
            if hasattr(self, "ccsem"):
                toks.append((self.ccsem, self.cccnt))
            self._wait(e, toks)

    def wait_all(self, eng, views):
        toks = []
        for v in views:
            toks += v.t.deps_read(v.key)
        self._wait(eng, toks)

    def mm(self, out, pairs, reads_extra=()):
        reads = []
        for (l, r) in pairs:
            reads += [l, r]
        n = len(pairs)

        def emit(e):
            ins = None
            for i, (l, r) in enumerate(pairs):
                ins = e.matmul(out.ap, l.ap, r.ap, start=(i == 0), stop=(i == n - 1))
            return ins
        return self.op("pe", reads, [out], emit)

    def mm1(self, out, l, r, start, stop):
        return self.op("pe", [l, r], [out],
                       lambda e: e.matmul(out.ap, l.ap, r.ap, start=start, stop=stop))

    def tr(self, out, in_, ident):
        return self.op("pe", [in_, ident], [out],
                       lambda e: e.transpose(out.ap, in_.ap, ident.ap))

    def act(self, out, in_, func, bias=None, scale=None, accum=None, eng="act"):
        reads = [in_]
        kw = {}
        if bias is not None:
            if isinstance(bias, V):
                reads.append(bias)
                kw["bias"] = bias.ap
            else:
                kw["bias"] = bias
        if scale is not None:
            if isinstance(scale, V):
                reads.append(scale)
                kw["scale"] = scale.ap
            else:
                kw["scale"] = scale
        writes = [out]
        if accum is not None:
            writes.append(accum)
            kw["accum_out"] = accum.ap
        return self.op(eng, reads, writes,
                       lambda e: e.activation(out.ap, in_.ap, func, **kw))

    def tt(self, eng, out, a, b, op):
        return self.op(eng, [a, b], [out],
                       lambda e: e.tensor_tensor(out.ap, a.ap, b.ap, op))

    def ts(self, eng, out, a, s1, s2, op0, op1=None, accum=None):
        reads = [a]
        s1a = s1.ap if isinstance(s1, V) else s1
        s2a = s2.ap if isinstance(s2, V) else s2
        if isinstance(s1, V):
            reads.append(s1)
        if isinstance(s2, V):
            reads.append(s2)
        writes = [out]
        kw = {}
        if accum is not None:
            writes.append(accum)
            kw["accum_out"] = accum.ap
        if op1 is None:
            return self.op(eng, reads, writes,
                           lambda e: e.tensor_scalar(out.ap, a.ap, s1a, None, op0, **kw))
        return self.op(eng, reads, writes,
                       lambda e: e.tensor_scalar(out.ap, a.ap, s1a, s2a, op0, op1, **kw))

    def stt(self, eng, out, a, s, b, op0, op1):
        reads = [a, b]
        sa = s.ap if isinstance(s, V) else s
        if isinstance(s, V):
            reads.append(s)
        return self.op(eng, reads, [out],
                       lambda e: e.scalar_tensor_tensor(out.ap, a.ap, sa, b.ap, op0, op1))

    def copy(self, eng, out, in_):
        if eng == "act":
            return self.op(eng, [in_], [out], lambda e: e.copy(out.ap, in_.ap))
        return self.op(eng, [in_], [out], lambda e: e.tensor_copy(out.ap, in_.ap))

    def memset(self, eng, out, val):
        return self.op(eng, [], [out], lambda e: e.memset(out.ap, val))


D = 1024
B = 2
GRID_W = 64
CTX = 256
DEPTH = 2
HD = 128
NH = 8
NKV = 2
GH = 4
GDK = 128
GDV = 256
RANK = 16
TAU = 16.0
CHUNK = 64
CW = 31
DFF = 2816
ALPHA = (2.0 * DEPTH) ** 0.25
EPS = 1e-6
NCORES = 8
INC = 9760
OK_, OV_, OKG, OVG, OQ, OQG, OR, OY, OGT, C16 = 0, 256, 512, 1024, 2048, 3072, 3584, 4608, 5632, 8704


class Ctx:
    def __init__(self, p):
        self.p = p
        self.banks = [p.ps("bank%d" % i, [128, 512], F32) for i in range(8)]
        self.bi = 0
        self.uid = 0

    def bank(self):
        n = getattr(self, "nrot", 8)
        b = self.banks[self.bi % n]
        self.bi += 1
        return b

    def name(self, s):
        self.uid += 1
        return "%s_%d" % (s, self.uid)


def bcast_mid(ap, n):
    a = ap.ap
    return bass.AP(ap.tensor, ap.offset, [list(a[0]), [0, n]] + [list(x) for x in a[1:]])


def load_const_eps(p, cx):
    eps = p.sb("eps_c", [128, 1], F32)
    p.memset("dve", eps[:], EPS)
    one = p.sb("one_c", [128, 1], F32)
    p.memset("dve", one[:], 1.0)
    cx.eps = eps
    cx.one = one


def ln_stats(p, cx, x, width, mean_rstd):
    nchunk = width // 512
    bn = cx.bn
    xr = x.ap.rearrange("p (c f) -> p c f", f=512)
    for c in range(nchunk):
        p.op("dve", [x], [bn.k(c, (slice(None), c, slice(None)))],
             lambda e, c=c: e.bn_stats(bn.h[:, c, :], xr[:, c, :]))
    p.op("dve", [bn[:, 0:nchunk, :]], [mean_rstd],
         lambda e: e.bn_aggr(mean_rstd.ap, bn.h[:, 0:nchunk, :]))
    r = V(mean_rstd.t, mean_rstd.ap[:, 1:2], mean_rstd.key)
    p.act(r, r, AF.Sqrt, bias=cx.eps[:, 0:1])
    p.op("dve", [r], [r], lambda e: e.reciprocal(r.ap, r.ap))


def mod_tiles(p, cx, scb, w_ada, b_ada, specs):
    wv = w_ada.h.rearrange("(kc p) c -> p kc c", p=128)
    mis = []
    for sp_ in specs:
        if sp_[1] not in mis:
            mis.append(sp_[1])
    for mi in mis:
        for half in range(2):
            c0 = mi * 1024 + half * 512
            wt = cx.wts[cx.wi % 2]
            cx.wi += 1
            p.dma("pool", wt[:], V(w_ada, wv[:, :, c0:c0 + 512]))
            bb = cx.bbc
            p.dma("sp", bb[:, 0:512], V(b_ada, b_ada.h[c0:c0 + 512].partition_broadcast(128)))
            for (vi, mi_, plus1, out) in specs:
                if mi_ != mi:
                    continue
                ps = cx.bank()
                p.mm(ps[:], [(scb[:, vi, kc, :], wt[:, kc, :]) for kc in range(8)])
                o = out[:, half * 512:(half + 1) * 512]
                if plus1:
                    p.stt("dve", o, ps[:], 1.0, bb[:, 0:512], ALU.add, ALU.add)
                else:
                    p.tt("dve", o, ps[:], bb[:, 0:512], ALU.add)


def make_scb(p, cx, cvT, nvec):
    cv = p.sb("cv", [128, nvec * 8], F32)
    p.dma("sp", cv[:], V(cvT, cvT.h.rearrange("p v k -> p (v k)")))
    p.act(cv[:], cv[:], AF.Silu)
    ones = p.sb("ones_bf", [128, 128], BF16)
    p.memset("dve", ones[:], 1.0)
    scb = p.sb("scb", [128, nvec, 8, 128], BF16)
    for v in range(nvec):
        for k in range(8):
            j = v * 8 + k
            p.ts("dve", scb[:, v, k, :], ones[:], cv[:, j:j + 1], None, ALU.mult)
    return scb


def ln_mod_transpose(p, cx, xsrc, t, scp, sh, hT, ident):
    xt = cx.xt[t % 2]
    p.dma("sp", xt[:], xsrc)
    mr = cx.mr[t % 2]
    ln_stats(p, cx, xt[:], 1024, mr[:])
    xn = cx.xn
    p.ts("dve", xn[:], xt[:], mr[:, 0:1], mr[:, 1:2], ALU.subtract, ALU.mult)
    p.tt("pool", xn[:], xn[:], scp[:], ALU.mult)
    hb = cx.hb[t % 2]
    p.tt("dve", hb[:], xn[:], sh[:], ALU.add)
    transpose_into(p, cx, hb, hT, t, ident, 8)


def transpose_into(p, cx, src, dstT, t, ident, nk, eng="act", col0=None, src0=0):
    for g0 in range(0, nk, 8):
        n = min(8, nk - g0)
        ps = cx.bank()
        pv = ps.h[:].bitcast(BF16)
        for k in range(n):
            p.tr(V(ps, pv[:, k * 128:(k + 1) * 128]), src[:, src0 + (g0 + k) * 128:src0 + (g0 + k + 1) * 128], ident[:])
        inv = V(ps, pv[:, 0:n * 128].rearrange("p (k f) -> p k f", f=128))
        c0_ = t * 128 if col0 is None else col0
        outv = dstT.k(("t", t), (slice(None), slice(g0, g0 + n), slice(c0_, c0_ + 128)))
        if eng == "act":
            p.act(outv, inv, AF.Identity)
        else:
            p.copy(eng, outv, inv)


def emit_p1(p, cx, NT, NCT, xs, cvT, w_ada, b_ada, w_in, qn_d, kn_d, wa2_0, wa2_1, ba, cos_d, sin_d,
            ident_d, o16, o32, layer_has_rope=True):
    NTA = NT + NCT
    load_const_eps(p, cx)
    cx.bn = p.sb("bn", [128, 2, 6], F32)
    cx.wts = [p.sb("wt%d" % i, [128, 8, 512], BF16) for i in range(2)]
    cx.wi = 0
    cx.bbc = p.sb("bbc", [128, 1024], F32)
    cx.xt = [p.sb("xt%d" % i, [128, D], F32) for i in range(2)]
    cx.mr = [p.sb("mr%d" % i, [128, 2], F32) for i in range(2)]
    cx.xn = p.sb("xn", [128, D], F32)
    cx.hb = [p.sb("hb%d" % i, [128, D], BF16) for i in range(2)]
    ident = p.sb("ident_sb", [128, 128], BF16)
    p.dma("pool", ident[:], ident_d[:])
    cos = p.sb("cos_sb", [128, NT, 64], F32)
    sin = p.sb("sin_sb", [128, NT, 64], F32)
    p.dma("sp", cos[:], cos_d[:])
    p.dma("sp", sin[:], sin_d[:])
    gq = p.sb("gq", [128, 128], F32)
    gk = p.sb("gk", [128, 128], F32)
    p.dma("sp", gq[:], V(qn_d, qn_d.h.partition_broadcast(128)))
    p.dma("sp", gk[:], V(kn_d, kn_d.h.partition_broadcast(128)))
    babc = p.sb("babc", [128, 1024], F32)
    p.dma("sp", babc[:], V(ba, ba.h.partition_broadcast(128)))
    w2bd = p.sb("w2bd", [32, 1024], BF16)
    p.memset("dve", w2bd[:], 0.0)
    p.dma("pool", w2bd[0:16, 0:512], wa2_0)
    p.dma("pool", w2bd[16:32, 512:1024], wa2_1)

    scb = make_scb(p, cx, cvT, 2)
    mods = {}
    for nm in ("shL", "scL", "shC", "scC"):
        mods[nm] = p.sb(nm, [128, D], F32)
    mod_tiles(p, cx, scb, w_ada, b_ada,
              [(0, 0, False, mods["shL"]), (0, 1, True, mods["scL"]),
               (1, 0, False, mods["shC"]), (1, 1, True, mods["scC"])])

    hT = p.sb("hT", [128, 8, NTA * 128], BF16)
    for t in range(NTA):
        lat = t < NT
        ln_mod_transpose(p, cx, xs[t], t, mods["scL" if lat else "scC"],
                         mods["shL" if lat else "shC"], hT, ident)

    wv = w_in.h.rearrange("(kc p) c -> p kc c", p=128)
    stg = [p.sb("stg%d" % i, [128, 512], BF16) for i in range(3)]
    stg32 = [p.sb("stgf%d" % i, [128, 512], F32) for i in range(2)]
    sq = p.sb("sq", [128, 512], F32)
    ss = p.sb("ss", [128, 4], F32)
    qn = p.sb("qn", [128, 512], F32)
    rt = [p.sb("rt%d" % i, [128, 4, 64], F32) for i in range(4)]
    si = [0]

    def load_w(col_ranges):
        wt = cx.wts[cx.wi % 2]
        cx.wi += 1
        o = 0
        for (c0, wd) in col_ranges:
            p.dma("pool", wt[:, :, o:o + wd], V(w_in, wv[:, :, c0:c0 + wd]))
            o += wd
        return wt, o

    def proj(t, wt, width):
        ps = cx.bank()
        p.mm(ps[:, 0:width], [(hT.k(("t", t), (slice(None), kc, slice(t * 128, (t + 1) * 128))),
                               wt[:, kc, 0:width]) for kc in range(8)])
        return ps

    def out16(t, st, col, width):
        p.dma("sp", V(o16, o16.h[t, :, col:col + width]), st[:, 0:width])

    def next_stg():
        s = stg[si[0] % 3]
        si[0] += 1
        return s

    def epi_simple(func, scale=None):
        def f(t, ps, width, col):
            st = next_stg()
            p.act(st[:, 0:width], ps[:, 0:width], func, scale=scale)
            out16(t, st, col, width)
        return f

    def epi_rmsrope(H, gain):
        def f(t, ps, width, col):
            lat = t < NT
            p.act(sq[:, 0:width], ps[:, 0:width], AF.Square)
            p.op("dve", [sq[:, 0:width]], [ss[:, 0:H]],
                 lambda e: e.reduce_sum(ss.h[:, 0:H], sq.h[:, 0:width].rearrange("p (h d) -> p h d", d=128), AX.X))
            p.act(ss[:, 0:H], ss[:, 0:H], AF.Sqrt, bias=cx.eps[:, 0:1], scale=1.0 / HD)
            p.op("dve", [ss[:, 0:H]], [ss[:, 0:H]], lambda e: e.reciprocal(ss.h[:, 0:H], ss.h[:, 0:H]))
            for h in range(H):
                p.stt("dve", qn[:, h * 128:(h + 1) * 128], ps[:, h * 128:(h + 1) * 128], ss[:, h:h + 1],
                      gain[:], ALU.mult, ALU.mult)
            st = next_stg()
            if lat and layer_has_rope:
                q4 = qn.h[:, 0:width].rearrange("p (h i two) -> p h i two", two=2, i=64)
                x1 = V(qn, q4[:, :, :, 0])
                x2 = V(qn, q4[:, :, :, 1])
                cb = V(cos, bcast_mid(cos.h[:, t, :], H))
                sb_ = V(sin, bcast_mid(sin.h[:, t, :], H))
                o4 = st.h[:, 0:width].rearrange("p (h i two) -> p h i two", two=2, i=64)
                a_, b_, c_, d_ = [V(r, r.h[:, 0:H, :]) for r in rt]
                p.tt("dve", a_, x1, cb, ALU.mult)
                p.tt("pool", b_, x2, sb_, ALU.mult)
                p.tt("dve", c_, x1, sb_, ALU.mult)
                p.tt("pool", d_, x2, cb, ALU.mult)
                p.tt("dve", V(st, o4[:, :, :, 0]), a_, b_, ALU.subtract)
                p.tt("pool", V(st, o4[:, :, :, 1]), c_, d_, ALU.add)
            else:
                p.copy("dve", st[:, 0:width], qn[:, 0:width])
            out16(t, st, col, width)
        return f

    def epi_glu(t, ps, width, col):
        sg = stg32[si[0] % 2]
        p.act(sg[:, 0:256], ps[:, 256:512], AF.Sigmoid)
        st = next_stg()
        p.tt("dve", st[:, 0:256], ps[:, 0:256], sg[:, 0:256], ALU.mult)
        out16(t, st, col, 256)

    groups = []
    groups.append(([(0, 256)], OK_, epi_rmsrope(2, gk)))
    groups.append(([(256, 256)], OV_, epi_simple(AF.Identity)))
    groups.append(([(512, 512)], OKG, epi_simple(AF.Identity)))
    groups.append(([(1024, 512)], OVG, epi_simple(AF.Identity)))
    groups.append(([(1536, 512)], OVG + 512, epi_simple(AF.Identity)))
    groups.append(([(2080, 512)], OQ, epi_rmsrope(4, gq)))
    groups.append(([(2592, 512)], OQ + 512, epi_rmsrope(4, gq)))
    groups.append(([(3104, 512)], OQG, epi_simple(AF.Identity, scale=GDK ** -0.5)))
    groups.append(([(3616, 512)], OR, epi_simple(AF.Silu)))
    groups.append(([(4128, 512)], OR + 512, epi_simple(AF.Silu)))
    for j in range(4):
        groups.append(([(4640 + 256 * j, 256), (5664 + 256 * j, 256)], OY + 256 * j, epi_glu))
    for j in range(6):
        groups.append(([(6688 + 512 * j, 512)], OGT + 512 * j, epi_simple(AF.Sigmoid)))

    for (cr, col, epi) in groups:
        wt, width = load_w(cr)
        for t in range(NTA):
            ps = proj(t, wt, width)
            epi(t, ps, width, col)

    wt, _ = load_w([(2048, 32)])
    glrT = p.sb("glrT", [32, NTA * 128], BF16)
    for t0 in range(0, NTA * 128, 512):
        n = min(512, NTA * 128 - t0)
        ps = cx.bank()
        t_lo, t_hi = t0 // 128, (t0 + n) // 128
        reads = []
        p.mm(ps[0:32, 0:n], [(wt[:, kc, 0:32], hT[:, kc, t0:t0 + n]) for kc in range(8)])
        p.act(glrT[:, t0:t0 + n], ps[0:32, 0:n], AF.Identity)
    for t in range(NTA):
        for dr in range(2):
            ps = cx.bank()
            p.mm(ps[:], [(glrT[:, t * 128:(t + 1) * 128], w2bd[:, dr * 512:(dr + 1) * 512])])
            sg = stg32[dr]
            p.tt("dve", sg[:], ps[:], babc[:, dr * 512:(dr + 1) * 512], ALU.add)
            p.act(sg[:], sg[:], AF.Exp, scale=-1.0)
            p.act(sg[:], sg[:], AF.Ln, bias=cx.one[:, 0:1])
            p.ts("dve", sg[:], sg[:], -1.0 / TAU, None, ALU.mult)
            p.dma("sp", V(o32, o32.h[t, :, dr * 512:(dr + 1) * 512]), sg[:])


def build_p1(NT, NCT, layer_has_rope=True):
    NTA = NT + NCT
    nc = bass.Bass("TRN2", target_bir_lowering=False)
    es = ExitStack()
    with es:
        p = Prog(nc, es)
        cx = Ctx(p)
        xs = p.dram("xs", [NTA, 128, D], F32, "ExternalInput")
        cvT = p.dram("cvT", [128, 2, 8], F32, "ExternalInput")
        w_ada = p.dram("w_ada", [D, 2 * D], F32, "ExternalInput")
        b_ada = p.dram("b_ada", [2 * D], F32, "ExternalInput")
        w_in = p.dram("w_in", [D, INC], F32, "ExternalInput")
        qn_d = p.dram("q_norm", [HD], F32, "ExternalInput")
        kn_d = p.dram("k_norm", [HD], F32, "ExternalInput")
        wa2 = p.dram("w_a2", [2, RANK, 512], F32, "ExternalInput")
        ba = p.dram("b_a", [2 * 512], F32, "ExternalInput")
        cos_d = p.dram("cos", [128, NT, 64], F32, "ExternalInput")
        sin_d = p.dram("sin", [128, NT, 64], F32, "ExternalInput")
        ident_d = p.dram("ident", [128, 128], F32, "ExternalInput")
        o16 = p.dram("o16", [NTA, 128, C16], BF16, "ExternalOutput")
        o32 = p.dram("o32", [NTA, 128, 1024], F32, "ExternalOutput")
        emit_p1(p, cx, NT, NCT, xs, cvT, w_ada, b_ada, w_in, qn_d, kn_d, wa2[0], wa2[1], ba, cos_d, sin_d,
                ident_d, o16, o32, layer_has_rope)
        p.wait_all("sp", [o16[:], o32[:]])
        print("P1 ops", p.n_ops, "waits", p.n_waits)
    return nc


_CACHE = {}


def _rope_tables(S):
    t = np.arange(S)
    row = (t // GRID_W).astype(np.float32)
    col = (t % GRID_W).astype(np.float32)
    half = HD // 2
    inv = (np.float32(10000.0) ** (-np.arange(0, half, 2, dtype=np.float32) / np.float32(half))).astype(np.float32)
    ang = np.concatenate([row[:, None] * inv, col[:, None] * inv], axis=-1).astype(np.float32)
    return np.cos(ang).astype(np.float32), np.sin(ang).astype(np.float32)


def _run(key, builder, in_maps):
    if key not in _CACHE:
        _CACHE[key] = builder()
    nc = _CACHE[key]
    res = run_bass_kernel_spmd(nc, in_maps, core_ids=list(range(NCORES)))
    return res.results


def host_p1(l, x, xc, c, c_ctx, P, S):
    TPC = S * B // NCORES
    NT = TPC // 128
    SEGS = NCORES // B
    cos, sin = _rope_tables(S)
    ctx_tiles = xc.reshape(B * CTX // 128, 128, D)
    in_maps = []
    for core in range(NCORES):
        b, seg = core // SEGS, core % SEGS
        xs = np.concatenate([x[b, seg * TPC:(seg + 1) * TPC].reshape(NT, 128, D),
                             ctx_tiles[core % 4][None]], axis=0)
        cv = np.stack([c[b], c_ctx]).reshape(2, 8, 128).transpose(2, 0, 1)
        cs = cos[seg * TPC:(seg + 1) * TPC].reshape(NT, 128, 64).transpose(1, 0, 2)
        sn = sin[seg * TPC:(seg + 1) * TPC].reshape(NT, 128, 64).transpose(1, 0, 2)
        in_maps.append({
            "xs": np.ascontiguousarray(xs), "cvT": np.ascontiguousarray(cv),
            "w_ada": np.ascontiguousarray(P["w_ada"][l][:, 0:2 * D]), "b_ada": np.ascontiguousarray(P["b_ada"][l][0:2 * D]), "w_in": P["w_in"][l],
            "q_norm": P["q_norm"][l], "k_norm": P["k_norm"][l],
            "w_a2": P["gla_w_a2"][l], "b_a": np.ascontiguousarray(P["gla_b_a"][l].reshape(-1)),
            "cos": np.ascontiguousarray(cs), "sin": np.ascontiguousarray(sn),
            "ident": np.eye(128, dtype=np.float32),
        })
    res = _run(("p1", NT), lambda: build_p1(NT, 1), in_maps)
    o16 = np.stack([r["o16"] for r in res])
    o32 = np.stack([r["o32"] for r in res])
    return o16, o32


def build_mix(S, with_ctx_q):
    L = S + CTX
    NKT = L // 128
    NQ = L if with_ctx_q else S
    nc = bass.Bass("TRN2", target_bir_lowering=False)
    es = ExitStack()
    with es:
        p = Prog(nc, es)
        banks = [p.ps("bank%d" % i, [128, 512], F32) for i in range(8)]
        qT_d = p.dram("qT", [2, 128, L], BF16, "ExternalInput")
        kT_d = p.dram("kT", [128, L], BF16, "ExternalInput")
        vE_d = p.dram("vE", [128, NKT, 129], BF16, "ExternalInput")
        oatt = p.dram("oatt", [2, NKT, 128, 128], BF16, "ExternalOutput")
        gq_d = p.dram("gqT", [2, 128, L], BF16, "ExternalInput")
        gkT_d = p.dram("gkT", [2, 128, L], BF16, "ExternalInput")
        gk_d = p.dram("gk", [2, NKT, 128, 128], BF16, "ExternalInput")
        gv_d = p.dram("gv", [2, NKT, 128, 256], BF16, "ExternalInput")
        gg_d = p.dram("gg", [2, NKT, 128, 128], F32, "ExternalInput")
        cst_d = p.dram("cst", [128, 128 * 2 + 2], F32, "ExternalInput")
        ogla = p.dram("ogla", [2, NKT, 128, 256], F32, "ExternalOutput")
        cy_d = p.dram("cy", [128, B, S + 30], BF16, "ExternalInput")
        cyc_d = p.dram("cyc", [128, B, CTX + 30], BF16, "ExternalInput")
        cw_d = p.dram("cw", [128, CW + 1], F32, "ExternalInput")
        oconv = p.dram("oconv", [128, B, S], F32, "ExternalOutput")
        oconvc = p.dram("oconvc", [128, B, CTX], F32, "ExternalOutput")

        cw = p.sb("cw", [128, CW + 1], F32)
        p.dma("sp", cw[:], cw_d[:])
        cy = p.sb("cy", [128, B, S + 30], BF16)
        cyc = p.sb("cyc", [128, B, CTX + 30], BF16)
        p.dma("sp", cy[:], cy_d[:])
        p.dma("sp", cyc[:], cyc_d[:])
        acc = p.sb("cacc", [128, B, S], F32)
        accc = p.sb("caccc", [128, B, CTX], F32)

        def conv_emit(tap):
            for b in range(B):
                eng = "dve"
                for (a, y, n, nm) in ((acc, cy, S, "l"), (accc, cyc, CTX, "c")):
                    av = a.k((nm, b), (slice(None), b, slice(None)))
                    yv = y[:, b, tap:tap + n]
                    if tap == 0:
                        p.ts(eng, av, yv, cw[:, 0:1], None, ALU.mult)
                    else:
                        p.stt(eng, av, yv, cw[:, tap:tap + 1], av, ALU.mult, ALU.add)
                    if tap == CW - 1:
                        p.ts(eng, av, av, cw[:, CW:CW + 1], None, ALU.add)
                        od = oconv if nm == "l" else oconvc
                        p.dma("sp", V(od, od.h[:, b, :]), av)

        cst = p.sb("cst", [128, 258], F32)
        p.dma("sp", cst[:], cst_d[:])
        Lm = cst[:, 0:128]
        Um = cst[:, 128:256]
        sel = cst[:, 256:258]
        Lmb = p.sb("Lmb", [128, 128], F32)
        p.copy("dve", Lmb[:], Lm)
        St = [p.sb("gS%d" % i, [128, 256], F32) for i in range(2)]
        Sb = [p.sb("gSb%d" % i, [128, 256], BF16) for i in range(2)]
        for i in range(2):
            p.memset("dve", St[i][:], 0.0)
            p.memset("dve", Sb[i][:], 0.0)
        gl = {}
        for nm, shp, dt in (("qT", [128, 128], BF16), ("kT", [128, 128], BF16), ("k", [128, 128], BF16),
                            ("v", [128, 256], BF16), ("g", [128, 128], F32), ("EbT", [128, 128], F32),
                            ("EnbT", [128, 128], F32), ("Erem", [128, 128], F32), ("Eend", [128, 2], F32),
                            ("qeT", [128, 128], BF16), ("keT", [128, 128], BF16), ("kend", [128, 128], BF16),
                            ("ATm", [128, 128], BF16), ("osb", [64, 2, 256], F32)):
            gl[nm] = [[p.sb("g_%s_%d_%d" % (nm, s, j), shp, dt) for j in range(2)] for s in range(2)]

        def gla_tile(s, t):
            j = t % 2
            T_ = {k: v[s][j] for k, v in gl.items()}
            pb = banks[4 + 2 * s:6 + 2 * s]
            p.dma("sp", T_["qT"][:], V(gq_d, gq_d.h[s, :, t * 128:(t + 1) * 128]))
            p.dma("sp", T_["kT"][:], V(gkT_d, gkT_d.h[s, :, t * 128:(t + 1) * 128]))
            p.dma("sp", T_["k"][:], V(gk_d, gk_d.h[s, t]))
            p.dma("sp", T_["v"][:], V(gv_d, gv_d.h[s, t]))
            p.dma("sp", T_["g"][:], V(gg_d, gg_d.h[s, t]))
            g = T_["g"]
            b0 = pb[0]
            p.mm(b0[:, 0:128], [(g[:], Lm)])
            p.mm(b0[:, 128:256], [(Um, g[:])])
            p.mm(b0[:, 256:258], [(g[:], sel)])
            p.act(T_["EbT"][:], b0[:, 0:128], AF.Exp)
            p.act(T_["EnbT"][:], b0[:, 0:128], AF.Exp, scale=-1.0)
            p.act(T_["Erem"][:], b0[:, 128:256], AF.Exp)
            p.act(T_["Eend"][:], b0[:, 256:258], AF.Exp)
            p.tt("dve", T_["qeT"][:], T_["qT"][:], T_["EbT"][:], ALU.mult)
            p.tt("dve", T_["keT"][:], T_["kT"][:], T_["EnbT"][:], ALU.mult)
            p.tt("dve", T_["kend"][:], T_["k"][:], T_["Erem"][:], ALU.mult)
            p.mm(b0[:, 384:512], [(T_["keT"][:], T_["qeT"][:])])
            p.tt("dve", T_["ATm"][:], b0[:, 384:512], Lmb[:], ALU.mult)
            b1 = pb[1]
            for c in range(2):
                cs = slice(c * 64, (c + 1) * 64)
                ov = b1[0:64, c * 256:(c + 1) * 256] if False else None
            for c in range(2):
                cs = slice(c * 64, (c + 1) * 64)
                ops_ = V(b1, b1.h[0:64, 0:256])
                p.mm(ops_, [(T_["qeT"][:, cs], Sb[s][:]), (T_["ATm"][:, cs], T_["v"][:])])
                p.act(T_["osb"][:, c, :], ops_, AF.Identity)
                ups = V(b1, b1.h[:, 256:512])
                p.mm(ups, [(T_["kend"][cs, :], T_["v"][cs, :])])
                p.stt("dve", St[s][:], St[s][:], T_["Eend"][:, c:c + 1], ups, ALU.mult, ALU.add)
                p.act(Sb[s][:], St[s][:], AF.Identity)
            p.dma("sp", V(ogla, ogla.h[s, t].rearrange("(c q) e -> q c e", q=64)), T_["osb"][:])

        kT = p.sb("kT", [128, L], BF16)
        vE = p.sb("vE", [128, NKT, 129], BF16)
        p.dma("sp", kT[:], kT_d[:])
        p.dma("sp", vE[:], vE_d[:])
        qTs = [p.sb("qTs%d" % i, [128, 512], BF16) for i in range(2)]
        pTs = [p.sb("pTs%d" % i, [128, 512], BF16) for i in range(3)]
        rcp = p.sb("rcp", [128, 4], F32)
        ost = [p.sb("ost%d" % i, [128, 128], BF16) for i in range(4)]
        SC = HD ** -0.5
        blocks = []
        for h in range(2):
            for q0 in range(0, S, 512):
                blocks.append((h, q0, min(512, S - q0), 0, NKT))
            if with_ctx_q:
                blocks.append((h, S, CTX, NKT - CTX // 128, NKT))
        state = {"bi": 0, "pi": 0, "sb": 0}

        def att_block(blk):
            h, q0, nq, kt0, kt1 = blk
            qt = qTs[state["bi"] % 2]
            state["bi"] += 1
            p.dma("sp", qt[:, 0:nq], V(qT_d, qT_d.h[h, :, q0:q0 + nq]))
            nsub = nq // 128
            for kt in range(kt0, kt1):
                sbk = banks[2 + state["sb"] % 2]
                state["sb"] += 1
                p.mm(sbk[:, 0:nq], [(kT[:, kt * 128:(kt + 1) * 128], qt[:, 0:nq])])
                pt = pTs[state["pi"] % 3]
                state["pi"] += 1
                p.act(pt[:, 0:nq], sbk[:, 0:nq], AF.Exp, scale=SC)
                for qs in range(nsub):
                    ob = banks[qs // 2]
                    ov = V(ob, ob.h[:, (qs % 2) * 256:(qs % 2) * 256 + 129])
                    p.mm1(ov, pt[:, qs * 128:(qs + 1) * 128], vE[:, kt, :], kt == kt0, kt == kt1 - 1)
            for qs in range(nsub):
                ob = banks[qs // 2]
                base = (qs % 2) * 256
                p.op("dve", [ob[:, base + 128:base + 129]], [rcp[:, qs:qs + 1]],
                     lambda e, ob=ob, base=base, qs=qs: e.reciprocal(rcp.h[:, qs:qs + 1], ob.h[:, base + 128:base + 129]))
                p.ts("dve", ost[qs][:], ob[:, base:base + 128], rcp[:, qs:qs + 1], None, ALU.mult)
                p.dma("sp", V(oatt, oatt.h[h, (q0 // 128) + qs]), ost[qs][:])

        nb = len(blocks)
        ngl = NKT
        total = max(nb, ngl, CW)
        gi = ci = ai = 0
        for step in range(total):
            while gi < ngl and gi * total <= step * ngl:
                gla_tile(0, gi)
                gla_tile(1, gi)
                gi += 1
            while ci < CW and ci * total <= step * CW:
                conv_emit(ci)
                ci += 1
            while ai < nb and ai * total <= step * nb:
                att_block(blocks[ai])
                ai += 1
        while gi < ngl:
            gla_tile(0, gi); gla_tile(1, gi); gi += 1
        while ci < CW:
            conv_emit(ci); ci += 1
        while ai < nb:
            att_block(blocks[ai]); ai += 1
        p.wait_all("sp", [oatt[:], ogla[:], oconv[:], oconvc[:]])
        print("MIX ops", p.n_ops, "waits", p.n_waits)
    return nc


def _post_common(p, cx, NTA):
    load_const_eps(p, cx)
    cx.bn = p.sb("bn", [128, 2, 6], F32)
    cx.wts = [p.sb("wt%d" % i, [128, 8, 512], BF16) for i in range(2)]
    cx.wi = 0
    cx.bbc = p.sb("bbc", [128, 1024], F32)
    cx.xt = [p.sb("xt%d" % i, [128, D], F32) for i in range(2)]
    cx.mr = [p.sb("mr%d" % i, [128, 2], F32) for i in range(2)]
    cx.xn = p.sb("xn", [128, D], F32)
    cx.hb = [p.sb("hb%d" % i, [128, D], BF16) for i in range(2)]


def _bc_vec(p, name, d, n):
    t = p.sb(name + "_bc", [128, n], F32)
    p.dma("sp", t[:], V(d, d.h.partition_broadcast(128)))
    return t


def deepnorm_out(p, cx, u, xo_view, lg, lb, i):
    mr = cx.mr[i % 2]
    ln_stats(p, cx, u[:], 1024, mr[:])
    p.ts("dve", u[:], u[:], mr[:, 0:1], mr[:, 1:2], ALU.subtract, ALU.mult)
    p.tt("pool", u[:], u[:], lg[:], ALU.mult)
    p.tt("dve", u[:], u[:], lb[:], ALU.add)
    p.dma("sp", xo_view, u[:])


class NS:
    pass


def emit_posta(p, cx, NT, NCT, A):
    NTA = NT + NCT
    _post_common(p, cx, NTA)
    ident = p.sb("ident_sb", [128, 128], BF16)
    p.dma("pool", ident[:], A.ident_d[:])
    bc = {n: _bc_vec(p, n, d, d.h.shape[0]) for n, d in A.vecs.items()}
    scb = make_scb(p, cx, A.cvT, 2)
    g1L = p.sb("g1L", [128, D], F32)
    g1C = p.sb("g1C", [128, D], F32)
    specs = [(0, A.g1_mi, False, g1L)]
    if NCT:
        specs.append((1, A.g1_mi, False, g1C))
    mod_tiles(p, cx, scb, A.w_ada, A.b_ada, specs)

    aT = p.sb("aT", [128, 8, NTA * 128], BF16)
    m = p.sb("m", [128, NTA, D], BF16)
    ld16 = [p.sb("ld16_%d" % i, [128, D], BF16) for i in range(2)]
    ld32 = [p.sb("ld32_%d" % i, [128, D], F32) for i in range(2)]
    ss = p.sb("ss", [128, 4], F32)
    sq = p.sb("sq", [128, D], F32)
    gtile = [p.sb("gtile%d" % i, [128, 512], BF16) for i in range(2)]
    tmp = p.sb("tmpm", [128, 512], F32)

    def fill_att(t):
        a = ld16[t % 2]
        p.dma("sp", a[:], A.oatt(t))
        transpose_into(p, cx, a, aT, t, ident, 8)

    def fill_gla(t):
        a, b_ = ld32[0], ld32[1]
        p.dma("sp", a[:], A.ogf(t))
        p.dma("sp", b_[:], A.ogb(t))
        p.tt("dve", a[:], a[:], b_[:], ALU.add)
        p.act(sq[:], a[:], AF.Square)
        p.op("dve", [sq[:]], [ss[:]],
             lambda e: e.reduce_sum(ss.h[:], sq.h[:].rearrange("p (h d) -> p h d", d=GDV), AX.X))
        p.act(ss[:], ss[:], AF.Sqrt, bias=cx.eps[:, 0:1], scale=1.0 / GDV)
        p.op("dve", [ss[:]], [ss[:]], lambda e: e.reciprocal(ss.h[:], ss.h[:]))
        for h in range(GH):
            hs = slice(h * GDV, (h + 1) * GDV)
            p.stt("dve", a[:, hs], a[:, hs], ss[:, h:h + 1], bc["gla_norm"][:], ALU.mult, ALU.mult)
        r = ld16[0]
        p.dma("sp", r[:], A.sr(t))
        hb = cx.hb[t % 2]
        p.tt("dve", hb[:], a[:], r[:], ALU.mult)
        transpose_into(p, cx, hb, aT, t, ident, 8)

    def fill_conv(t):
        a = ld32[t % 2]
        p.dma("sp", a[:], A.cv(t))
        mr = cx.mr[t % 2]
        ln_stats(p, cx, a[:], 1024, mr[:])
        p.ts("dve", a[:], a[:], mr[:, 0:1], mr[:, 1:2], ALU.subtract, ALU.mult)
        p.tt("pool", a[:], a[:], bc["conv_ln_g"][:], ALU.mult)
        p.tt("dve", a[:], a[:], bc["conv_ln_b"][:], ALU.add)
        hb = cx.hb[t % 2]
        p.act(hb[:], a[:], AF.Silu)
        transpose_into(p, cx, hb, aT, t, ident, 8)

    def load_wo(wd, half):
        wt = cx.wts[cx.wi % 2]
        cx.wi += 1
        wv = wd.h.rearrange("(kc p) c -> p kc c", p=128)
        p.dma("pool", wt[:], V(wd, wv[:, :, half * 512:(half + 1) * 512]))
        return wt

    for bi, (fill, wn) in enumerate(((fill_att, "w_att_o"), (fill_gla, "w_gla_o"), (fill_conv, "w_conv_o"))):
        for t in range(NTA):
            fill(t)
        for half in range(2):
            wt = load_wo(A.wo[wn], half)
            for t in range(NTA):
                ps = cx.bank()
                p.mm(ps[:], [(aT[:, kc, t * 128:(t + 1) * 128], wt[:, kc, :]) for kc in range(8)])
                g = gtile[t % 2]
                p.dma("sp", g[:], A.gt(t, bi * D + half * 512, bi * D + (half + 1) * 512))
                mv = m.k(("t", t, half), (slice(None), t, slice(half * 512, (half + 1) * 512)))
                if bi == 0:
                    p.tt("dve", mv, ps[:], g[:], ALU.mult)
                else:
                    p.tt("dve", tmp[:], ps[:], g[:], ALU.mult)
                    p.tt("pool", mv, mv, tmp[:], ALU.add)
    for t in range(NTA):
        hb = cx.hb[t % 2]
        p.copy("dve", hb[:], m[:, t, :])
        transpose_into(p, cx, hb, aT, t, ident, 8)
    w0 = load_wo(A.wo["w_out"], 0)
    w1 = load_wo(A.wo["w_out"], 1)
    us = [p.sb("u%d" % i, [128, D], F32) for i in range(2)]
    for t in range(NTA):
        lat = t < NT
        g1 = g1L if lat else g1C
        u = us[t % 2]
        xt = cx.xt[t % 2]
        p.dma("sp", xt[:], A.xs(t))
        for half, wt in enumerate((w0, w1)):
            ps = cx.bank()
            hs = slice(half * 512, (half + 1) * 512)
            p.mm(ps[:], [(aT[:, kc, t * 128:(t + 1) * 128], wt[:, kc, :]) for kc in range(8)])
            p.tt("dve", u[:, hs], ps[:], g1[:, hs], ALU.mult)
        p.stt("dve", u[:], xt[:], float(ALPHA), u[:], ALU.mult, ALU.add)
        deepnorm_out(p, cx, u, A.xo(t), bc["ln1_g"], bc["ln1_b"], t)


def build_posta(NT, NCT):
    NTA = NT + NCT
    nc = bass.Bass("TRN2", target_bir_lowering=False)
    es = ExitStack()
    with es:
        p = Prog(nc, es)
        cx = Ctx(p)
        oatt = p.dram("oatt_t", [NTA, 128, D], BF16, "ExternalInput")
        ogf = p.dram("ogf", [NTA, 128, D], F32, "ExternalInput")
        ogb = p.dram("ogb", [NTA, 128, D], F32, "ExternalInput")
        sr = p.dram("sr", [NTA, 128, D], BF16, "ExternalInput")
        cv = p.dram("cv", [NTA, 128, D], F32, "ExternalInput")
        gt = p.dram("gt", [NTA, 128, 3 * D], BF16, "ExternalInput")
        xs = p.dram("xs", [NTA, 128, D], F32, "ExternalInput")
        cvT = p.dram("cvT", [128, 2, 8], F32, "ExternalInput")
        w_ada = p.dram("w_ada", [D, D], F32, "ExternalInput")
        b_ada = p.dram("b_ada", [D], F32, "ExternalInput")
        wo = {n: p.dram(n, [D, D], F32, "ExternalInput") for n in ("w_att_o", "w_gla_o", "w_conv_o", "w_out")}
        vecs = {n: p.dram(n, [sz], F32, "ExternalInput") for n, sz in
                (("gla_norm", GDV), ("conv_ln_g", D), ("conv_ln_b", D), ("ln1_g", D), ("ln1_b", D))}
        ident_d = p.dram("ident", [128, 128], F32, "ExternalInput")
        xo = p.dram("xo", [NTA, 128, D], F32, "ExternalOutput")
        A = NS()
        A.oatt = lambda t: oatt[t]
        A.ogf = lambda t: ogf[t]
        A.ogb = lambda t: ogb[t]
        A.sr = lambda t: sr[t]
        A.cv = lambda t: cv[t]
        A.xs = lambda t: xs[t]
        A.xo = lambda t: xo[t]
        A.gt = lambda t, c0, c1: V(gt, gt.h[t, :, c0:c1])
        A.g1_mi = 0
        A.ident_d, A.vecs, A.cvT, A.w_ada, A.b_ada, A.wo = ident_d, vecs, cvT, w_ada, b_ada, wo
        emit_posta(p, cx, NT, NCT, A)
        p.wait_all("sp", [xo[:]])
        print("POSTA ops", p.n_ops, "waits", p.n_waits)
    return nc


def emit_postb(p, cx, NT, NCT, A):
    NTA = NT + NCT
    NFC = DFF // 128
    _post_common(p, cx, NTA)
    ident = p.sb("ident_sb", [128, 128], BF16)
    p.dma("pool", ident[:], A.ident_d[:])
    l2g = _bc_vec(p, "l2g", A.l2g_d, D)
    l2b = _bc_vec(p, "l2b", A.l2b_d, D)
    scb = make_scb(p, cx, A.cvT, 2)
    mods = {n: p.sb(n, [128, D], F32) for n in ("sh2L", "sc2L", "g2L")}
    m0 = A.mi0
    specs = [(0, m0, False, mods["sh2L"]), (0, m0 + 1, True, mods["sc2L"]), (0, m0 + 2, False, mods["g2L"])]
    if NCT:
        for n in ("sh2C", "sc2C", "g2C"):
            mods[n] = p.sb(n, [128, D], F32)
        specs += [(1, m0, False, mods["sh2C"]), (1, m0 + 1, True, mods["sc2C"]), (1, m0 + 2, False, mods["g2C"])]
    mod_tiles(p, cx, scb, A.w_ada, A.b_ada, specs)
    hT = p.sb("hT", [128, 8, NTA * 128], BF16)
    for t in range(NTA):
        lat = t < NT
        ln_mod_transpose(p, cx, A.xs(t), t, mods["sc2L" if lat else "sc2C"],
                         mods["sh2L" if lat else "sh2C"], hT, ident)
    wgs = [p.sb("wg%d" % i, [128, 8, 512], BF16) for i in range(2)]
    wus = cx.wts
    actT = p.sb("actT", [128, NFC, 512], BF16)
    wdt = p.sb("wdt", [128, NFC, 512], BF16)
    sg = [p.sb("sg%d" % i, [128, 512], F32) for i in range(2)]
    us = [p.sb("u%d" % i, [128, D], F32) for i in range(4)]
    wg_d, wu_d, wd_d = A.wg_d, A.wu_d, A.wd_d
    wgv = wg_d.h.rearrange("(kc p) c -> p kc c", p=128)
    wuv = wu_d.h.rearrange("(kc p) c -> p kc c", p=128)
    wdv = wd_d.h.rearrange("(c p) n -> p c n", p=128)
    wi = 0
    for g0 in range(0, NTA, 4):
        gt_ = list(range(g0, min(NTA, g0 + 4)))
        ntok = len(gt_) * 128
        tk = slice(g0 * 128, g0 * 128 + ntok)
        for c0 in range(0, DFF, 512):
            wd_ = min(512, DFF - c0)
            wgt, wut = wgs[wi % 2], wus[wi % 2]
            wi += 1
            p.dma("pool", wgt[:, :, 0:wd_], V(wg_d, wgv[:, :, c0:c0 + wd_]))
            p.dma("pool", wut[:, :, 0:wd_], V(wu_d, wuv[:, :, c0:c0 + wd_]))
            for sub in range(wd_ // 128):
                ffc = c0 // 128 + sub
                cs = slice(sub * 128, (sub + 1) * 128)
                pg = cx.bank()
                pu = cx.bank()
                p.mm(pg[:, 0:ntok], [(wgt[:, kc, cs], hT[:, kc, tk]) for kc in range(8)])
                p.mm(pu[:, 0:ntok], [(wut[:, kc, cs], hT[:, kc, tk]) for kc in range(8)])
                s_ = sg[ffc % 2]
                p.act(s_[:, 0:ntok], pg[:, 0:ntok], AF.Silu)
                p.tt("dve", actT.k(ffc, (slice(None), ffc, slice(0, ntok))), s_[:, 0:ntok], pu[:, 0:ntok], ALU.mult)
        for half in range(2):
            hs = slice(half * 512, (half + 1) * 512)
            p.dma("pool", wdt[:], V(wd_d, wdv[:, :, hs]))
            for i, t in enumerate(gt_):
                lat = t < NT
                g2 = mods["g2L" if lat else "g2C"]
                ps = cx.bank()
                p.mm(ps[:], [(actT[:, ffc, i * 128:(i + 1) * 128], wdt[:, ffc, :]) for ffc in range(NFC)])
                p.tt("dve", us[i][:, hs], ps[:], g2[:, hs], ALU.mult)
        for i, t in enumerate(gt_):
            xt = cx.xt[t % 2]
            p.dma("sp", xt[:], A.xs(t))
            p.stt("dve", us[i][:], xt[:], float(ALPHA), us[i][:], ALU.mult, ALU.add)
            deepnorm_out(p, cx, us[i], A.xo(t), l2g, l2b, t)


def emit_postb2(p, cx, NT, NCT, A):
    NTA = NT + NCT
    NFC = DFF // 128
    NTOK = NTA * 128
    act_s = A.act_s
    _post_common(p, cx, NTA)
    ident = p.sb("ident_sb", [128, 128], BF16)
    p.dma("pool", ident[:], A.ident_d[:])
    l2g = _bc_vec(p, "l2g", A.l2g_d, D)
    l2b = _bc_vec(p, "l2b", A.l2b_d, D)
    scb = make_scb(p, cx, A.cvT, 2)
    mods = {n: p.sb(n, [128, D], F32) for n in ("sh2L", "sc2L", "g2L")}
    m0 = A.mi0
    specs = [(0, m0, False, mods["sh2L"]), (0, m0 + 1, True, mods["sc2L"]), (0, m0 + 2, False, mods["g2L"])]
    if NCT:
        for n in ("sh2C", "sc2C", "g2C"):
            mods[n] = p.sb(n, [128, D], F32)
        specs += [(1, m0, False, mods["sh2C"]), (1, m0 + 1, True, mods["sc2C"]), (1, m0 + 2, False, mods["g2C"])]
    mod_tiles(p, cx, scb, A.w_ada, A.b_ada, specs)
    hT = p.sb("hT", [128, 8, NTOK], BF16)
    for t in range(NTA):
        lat = t < NT
        ln_mod_transpose(p, cx, A.xs(t), t, mods["sc2L" if lat else "sc2C"],
                         mods["sh2L" if lat else "sh2C"], hT, ident)
    wgs = [p.sb("wg%d" % i, [128, 8, 512], BF16) for i in range(2)]
    wus = cx.wts
    sg = [p.sb("sg%d" % i, [128, 512], F32) for i in range(2)]
    stg = [p.sb("astg%d" % i, [128, 512], BF16) for i in range(3)]
    wdt = p.sb("wdt", [128, NFC, D], BF16)
    wg_d, wu_d, wd_d = A.wg_d, A.wu_d, A.wd_d
    wgv = wg_d.h.rearrange("(kc q) c -> q kc c", q=128)
    wuv = wu_d.h.rearrange("(kc q) c -> q kc c", q=128)
    wdv = wd_d.h.rearrange("(c q) n -> q c n", q=128)
    wi = 0
    si = 0
    for c0 in range(0, DFF, 512):
        wd_ = min(512, DFF - c0)
        wgt, wut = wgs[wi % 2], wus[wi % 2]
        wi += 1
        p.dma("pool", wgt[:, :, 0:wd_], V(wg_d, wgv[:, :, c0:c0 + wd_]))
        p.dma("pool", wut[:, :, 0:wd_], V(wu_d, wuv[:, :, c0:c0 + wd_]))
        if c0 == 0:
            for half in range(2):
                p.dma("pool", wdt[:, :, half * 512:(half + 1) * 512], V(wd_d, wdv[:, :, half * 512:(half + 1) * 512]))
        for tk0 in range(0, NTOK, 512):
            ntok = min(512, NTOK - tk0)
            nt_ = ntok // 128
            tk = slice(tk0, tk0 + ntok)
            for sub in range(wd_ // 128):
                ffc = c0 // 128 + sub
                cs = slice(sub * 128, (sub + 1) * 128)
                pg = cx.bank()
                pu = cx.bank()
                p.mm(pg[:, 0:ntok], [(wgt[:, kc, cs], hT[:, kc, tk]) for kc in range(8)])
                p.mm(pu[:, 0:ntok], [(wut[:, kc, cs], hT[:, kc, tk]) for kc in range(8)])
                s_ = sg[si % 2]
                st = stg[si % 3]
                si += 1
                p.act(s_[:, 0:ntok], pg[:, 0:ntok], AF.Silu)
                p.tt("dve", st[:, 0:ntok], s_[:, 0:ntok], pu[:, 0:ntok], ALU.mult)
                t0_ = tk0 // 128
                p.dma("sp", V(act_s, act_s.h[t0_:t0_ + nt_, :, ffc, :].rearrange("t q k -> q t k")),
                      V(st, st.h[:, 0:ntok].rearrange("q (t k) -> q t k", k=128)))
    ats = [p.sb("at%d" % i, [128, NFC, 128], BF16) for i in range(2)]
    us = [p.sb("u%d" % i, [128, D], F32) for i in range(2)]
    for t in range(NTA):
        lat = t < NT
        g2 = mods["g2L" if lat else "g2C"]
        at = ats[t % 2]
        u = us[t % 2]
        p.dma("sp", at[:], act_s[t])
        for half in range(2):
            hs = slice(half * 512, (half + 1) * 512)
            ps = cx.bank()
            p.mm(ps[:], [(at[:, ffc, :], wdt[:, ffc, hs]) for ffc in range(NFC)])
            p.tt("dve", u[:, hs], ps[:], g2[:, hs], ALU.mult)
        xt = cx.xt[t % 2]
        p.dma("sp", xt[:], A.xs(t))
        p.stt("dve", u[:], xt[:], float(ALPHA), u[:], ALU.mult, ALU.add)
        deepnorm_out(p, cx, u, A.xo(t), l2g, l2b, t)


def build_postb(NT, NCT):
    NTA = NT + NCT
    NFC = DFF // 128
    nc = bass.Bass("TRN2", target_bir_lowering=False)
    es = ExitStack()
    with es:
        p = Prog(nc, es)
        cx = Ctx(p)
        xs = p.dram("xs", [NTA, 128, D], F32, "ExternalInput")
        cvT = p.dram("cvT", [128, 2, 8], F32, "ExternalInput")
        w_ada = p.dram("w_ada", [D, 3 * D], F32, "ExternalInput")
        b_ada = p.dram("b_ada", [3 * D], F32, "ExternalInput")
        wg_d = p.dram("w_ff_gate", [D, DFF], F32, "ExternalInput")
        wu_d = p.dram("w_ff_up", [D, DFF], F32, "ExternalInput")
        wd_d = p.dram("w_ff_down", [DFF, D], F32, "ExternalInput")
        l2g_d = p.dram("ln2_g", [D], F32, "ExternalInput")
        l2b_d = p.dram("ln2_b", [D], F32, "ExternalInput")
        ident_d = p.dram("ident", [128, 128], F32, "ExternalInput")
        xo = p.dram("xo", [NTA, 128, D], F32, "ExternalOutput")
        A = NS()
        A.xs = lambda t: xs[t]
        A.xo = lambda t: xo[t]
        A.mi0 = 0
        A.ident_d, A.cvT, A.w_ada, A.b_ada, A.l2g_d, A.l2b_d = ident_d, cvT, w_ada, b_ada, l2g_d, l2b_d
        A.wg_d, A.wu_d, A.wd_d = wg_d, wu_d, wd_d
        emit_postb(p, cx, NT, NCT, A)
        p.wait_all("sp", [xo[:]])
        print("POSTB ops", p.n_ops, "waits", p.n_waits)
    return nc


import ml_dtypes
NPBF = ml_dtypes.bfloat16


def _unpack(o, NT):
    C = o.shape[-1]
    lat = o[:, :NT].reshape(B, -1, C)
    ctx = o[0:4, NT].reshape(B, CTX, C) if o.shape[1] > NT else None
    return lat, ctx


def _pack(lat, ctx, core, NT, TPC, with_ctx):
    b, seg = core // 4, core % 4
    a = lat[b, seg * TPC:(seg + 1) * TPC].reshape(NT, 128, -1)
    if with_ctx:
        a = np.concatenate([a, ctx.reshape(4, 128, -1)[core % 4][None]], axis=0)
    return np.ascontiguousarray(a)


def _gla_consts():
    j = np.arange(128)[:, None]
    i = np.arange(128)[None, :]
    same = (j // CHUNK) == (i // CHUNK)
    Lm = (same & (j <= i)).astype(np.float32)
    Um = (same & (j > i)).astype(np.float32)
    sel = ((np.arange(128)[:, None] // CHUNK) == np.arange(2)[None, :]).astype(np.float32)
    return np.ascontiguousarray(np.concatenate([Lm, Um, sel], axis=1))


def host_mix(l, lat16, ctx16, latg, ctxg, P, S, with_ctx_q):
    L = S + CTX
    NKT = L // 128
    cst = _gla_consts()
    in_maps = []
    for core in range(NCORES):
        b, r = core // 4, core % 4
        h0, kv = 2 * r, r // 2
        Ql = lat16[b, :, OQ:OQ + 1024].reshape(S, NH, HD)
        Qc = ctx16[b, :, OQ:OQ + 1024].reshape(CTX, NH, HD)
        qT = np.stack([np.concatenate([Ql[:, h0 + hh], Qc[:, h0 + hh]], 0).T for hh in range(2)])
        Kl = lat16[b, :, OK_:OK_ + 256].reshape(S, NKV, HD)[:, kv]
        Kc = ctx16[b, :, OK_:OK_ + 256].reshape(CTX, NKV, HD)[:, kv]
        kT = np.concatenate([Kl, Kc], 0).T
        Vl = lat16[b, :, OV_:OV_ + 256].reshape(S, NKV, HD)[:, kv]
        Vc = ctx16[b, :, OV_:OV_ + 256].reshape(CTX, NKV, HD)[:, kv]
        Vall = np.concatenate([Vl, Vc], 0)
        vE = np.concatenate([Vall, np.ones((L, 1), NPBF)], 1).reshape(NKT, 128, 129).transpose(1, 0, 2)

        def seq(al, ac, d):
            if d == 0:
                return np.concatenate([ac, al], 0)
            return np.concatenate([al, ac], 0)[::-1]
        hd = r
        gq, gkT, gk, gv, gg = [], [], [], [], []
        for d in range(2):
            q_ = seq(lat16[b, :, OQG + hd * 128:OQG + (hd + 1) * 128], ctx16[b, :, OQG + hd * 128:OQG + (hd + 1) * 128], d)
            k_ = seq(lat16[b, :, OKG + hd * 128:OKG + (hd + 1) * 128], ctx16[b, :, OKG + hd * 128:OKG + (hd + 1) * 128], d)
            v_ = seq(lat16[b, :, OVG + hd * 256:OVG + (hd + 1) * 256], ctx16[b, :, OVG + hd * 256:OVG + (hd + 1) * 256], d)
            g_ = seq(latg[b, :, d * 512 + hd * 128:d * 512 + (hd + 1) * 128], ctxg[b, :, d * 512 + hd * 128:d * 512 + (hd + 1) * 128], d)
            gq.append(q_.T); gkT.append(k_.T); gk.append(k_.reshape(NKT, 128, 128))
            gv.append(v_.reshape(NKT, 128, 256)); gg.append(g_.reshape(NKT, 128, 128))
        ch = slice(core * 128, (core + 1) * 128)
        cy = np.zeros((128, B, S + 30), NPBF)
        cyc = np.zeros((128, B, CTX + 30), NPBF)
        for bb in range(B):
            cy[:, bb, 15:15 + S] = lat16[bb, :, OY:OY + 1024][:, ch].T
            cyc[:, bb, 15:15 + CTX] = ctx16[bb, :, OY:OY + 1024][:, ch].T
        cw = np.concatenate([P["conv_w_dw"][l][:, 0, ch].T, P["conv_b_dw"][l][ch][:, None]], 1).astype(np.float32)
        in_maps.append({
            "qT": np.ascontiguousarray(qT), "kT": np.ascontiguousarray(kT), "vE": np.ascontiguousarray(vE),
            "gqT": np.ascontiguousarray(np.stack(gq)), "gkT": np.ascontiguousarray(np.stack(gkT)),
            "gk": np.ascontiguousarray(np.stack(gk)), "gv": np.ascontiguousarray(np.stack(gv)),
            "gg": np.ascontiguousarray(np.stack(gg)).astype(np.float32), "cst": cst,
            "cy": cy, "cyc": cyc, "cw": np.ascontiguousarray(cw),
        })
    res = _run(("mix", S, with_ctx_q), lambda: build_mix(S, with_ctx_q), in_maps)
    att_l = np.zeros((B, S, D), NPBF); att_c = np.zeros((B, CTX, D), NPBF)
    gf_l = np.zeros((B, S, D), np.float32); gb_l = np.zeros((B, S, D), np.float32)
    gf_c = np.zeros((B, CTX, D), np.float32); gb_c = np.zeros((B, CTX, D), np.float32)
    cv_l = np.zeros((B, S, D), np.float32); cv_c = np.zeros((B, CTX, D), np.float32)
    for core in range(NCORES):
        b, r = core // 4, core % 4
        oa = res[core]["oatt"].reshape(2, L, HD)
        for hh in range(2):
            hs = slice((2 * r + hh) * HD, (2 * r + hh + 1) * HD)
            att_l[b, :, hs] = oa[hh, :S]
            att_c[b, :, hs] = oa[hh, S:]
        og = res[core]["ogla"].reshape(2, L, GDV)
        vs = slice(r * GDV, (r + 1) * GDV)
        gf_c[b, :, vs] = og[0, :CTX]; gf_l[b, :, vs] = og[0, CTX:]
        ob = og[1][::-1]
        gb_l[b, :, vs] = ob[:S]; gb_c[b, :, vs] = ob[S:]
        ch = slice(core * 128, (core + 1) * 128)
        for bb in range(B):
            cv_l[bb, :, ch] = res[core]["oconv"][:, bb, :].T
            cv_c[bb, :, ch] = res[core]["oconvc"][:, bb, :].T
    return (att_l, att_c), (gf_l, gf_c), (gb_l, gb_c), (cv_l, cv_c)


def host_post(l, mixo, lat16, ctx16, x, xc, c, c_ctx, P, S, with_ctx):
    TPC = S * B // NCORES
    NT = TPC // 128
    (att_l, att_c), (gf_l, gf_c), (gb_l, gb_c), (cv_l, cv_c) = mixo
    ident = np.eye(128, dtype=np.float32)
    in_maps = []
    for core in range(NCORES):
        b = core // 4
        pk = lambda al, ac: _pack(al, ac, core, NT, TPC, with_ctx)
        cv = np.ascontiguousarray(np.stack([c[b], c_ctx]).reshape(2, 8, 128).transpose(2, 0, 1))
        in_maps.append({
            "oatt_t": pk(att_l, att_c), "ogf": pk(gf_l, gf_c), "ogb": pk(gb_l, gb_c),
            "sr": pk(lat16[:, :, OR:OR + 1024], ctx16[:, :, OR:OR + 1024]),
            "cv": pk(cv_l, cv_c), "gt": pk(lat16[:, :, OGT:OGT + 3072], ctx16[:, :, OGT:OGT + 3072]),
            "xs": pk(x, xc), "cvT": cv, "w_ada": np.ascontiguousarray(P["w_ada"][l][:, 2 * D:3 * D]),
            "b_ada": np.ascontiguousarray(P["b_ada"][l][2 * D:3 * D]),
            "w_att_o": P["w_att_o"][l], "w_gla_o": P["w_gla_o"][l], "w_conv_o": P["w_conv_o"][l],
            "w_out": P["w_out"][l], "gla_norm": P["gla_norm"][l], "conv_ln_g": P["conv_ln_g"][l],
            "conv_ln_b": P["conv_ln_b"][l], "ln1_g": P["ln1_g"][l], "ln1_b": P["ln1_b"][l], "ident": ident,
        })
    nct = 1 if with_ctx else 0
    res = _run(("posta", NT, nct), lambda: build_posta(NT, nct), in_maps)
    xo = np.stack([r["xo"] for r in res])
    x1, xc1 = _unpack(xo, NT)
    in_maps = []
    for core in range(NCORES):
        b = core // 4
        cv = np.ascontiguousarray(np.stack([c[b], c_ctx]).reshape(2, 8, 128).transpose(2, 0, 1))
        in_maps.append({
            "xs": _pack(x1, xc1, core, NT, TPC, with_ctx), "cvT": cv,
            "w_ada": np.ascontiguousarray(P["w_ada"][l][:, 3 * D:6 * D]),
            "b_ada": np.ascontiguousarray(P["b_ada"][l][3 * D:6 * D]), "w_ff_gate": P["w_ff_gate"][l], "w_ff_up": P["w_ff_up"][l],
            "w_ff_down": P["w_ff_down"][l], "ln2_g": P["ln2_g"][l], "ln2_b": P["ln2_b"][l], "ident": ident,
        })
    res = _run(("postb", NT, nct), lambda: build_postb(NT, nct), in_maps)
    xo = np.stack([r["xo"] for r in res])
    return _unpack(xo, NT)


def forward(P, S):
    x = np.ascontiguousarray(P["x"][:, :S])
    xc = P["ctx"]
    c, c_ctx = P["c"], P["c_ctx"]
    TPC = S * B // NCORES
    NT = TPC // 128
    for l in range(DEPTH):
        last = l == DEPTH - 1
        o16, o32 = host_p1(l, x, xc, c, c_ctx, P, S)
        lat16, ctx16 = _unpack(o16, NT)
        latg, ctxg = _unpack(o32, NT)
        mixo = host_mix(l, lat16, ctx16, latg, ctxg, P, S, not last)
        x, xc_new = host_post(l, mixo, lat16, ctx16, x, xc, c, c_ctx, P, S, not last)
        if not last:
            xc = xc_new
    return x


def kernel(**inputs):
    P = {k: np.asarray(v) for k, v in inputs.items()}
    S = P["x"].shape[1]
    out = forward(P, S)
    return np.ascontiguousarray(out.astype(np.float32))


class _Stop(Exception):
    pass


def emit_mix_fused(p, cx, NT, M):
    depth0 = len(getattr(p, "scopes", []))
    try:
        _emit_mix_fused(p, cx, NT, M)
    except _Stop:
        while len(p.scopes) > depth0:
            p.pop_scope()


def _emit_mix_fused(p, cx, NT, M):
    def chk(x):
        if getattr(M, "stop_after", 3) < x:
            raise _Stop()
    NCT = 2
    NTA = NT + NCT
    TPC = NT * 128
    NTOK = NTA * 128
    SEGS = 4
    Sfull = SEGS * TPC
    L = Sfull + CTX
    NKT = L // 128
    banks = cx.banks
    o16, o32 = M.o16, M.o32
    groups = [[0, 1, 2, 3], [4, 5, 6, 7]]

    cx.nrot = 4
    p.push_scope()
    ident = p.sb("ident_sb", [128, 128], BF16)
    p.dma("pool", ident[:], M.ident_d[:])
    identf = p.sb("identf", [128, 128], F32)
    p.dma("sp", identf[:], M.ident_d[:])
    cst = p.sb("cst", [128, 514], F32)
    p.dma("sp", cst[:], M.gcst_d[:])
    LM = (cst[:, 0:128], cst[:, 256:384])
    UM = (cst[:, 128:256], cst[:, 384:512])
    sel = cst[:, 512:514]
    masks = p.sb("masks", [128, 8], F32)
    p.dma("sp", masks[:], M.masks_d[:])
    gqT = p.sb("gqT", [128, 4, NTOK], BF16)
    gkT = p.sb("gkT", [128, 4, NTOK], BF16)
    kTl = p.sb("kTl", [128, 2, NTOK], BF16)
    veb = p.sb("veb", [128, NTA, 2, 129], BF16)
    p.memset("dve", veb[:], 1.0)
    St = [p.sb("gS%d" % i, [128, 256], F32) for i in range(8)]
    Sb = [p.sb("gSb%d" % i, [128, 256], BF16) for i in range(8)]
    Sctx = [p.sb("gSc%d" % i, [128, 256], F32) for i in range(8)]
    logD = p.sb("logD", [128, 8], F32)
    p.memset("dve", logD[:], 1.0)
    gl = {}
    for nm, shp, dt in (("k", [128, 128], BF16), ("v", [128, 256], BF16), ("g", [128, 128], F32),
                        ("EbT", [128, 128], F32), ("EnbT", [128, 128], F32), ("Erem", [128, 128], F32),
                        ("Eend", [128, 2], F32), ("qeT", [128, 128], BF16), ("keT", [128, 128], BF16),
                        ("kend", [128, 128], BF16), ("ATm", [128, 128], BF16), ("osb", [64, 2, 256], F32)):
        gl[nm] = [[p.sb("g_%s_%d_%d" % (nm, s_, j), shp, dt) for j in range(2)] for s_ in range(4)]
    cnt = {"gl": 0}

    def gla_tile(slot, r, t, full, S_t, S_b):
        hd, d = r // 2, r % 2
        j = cnt["gl"] % 2
        cnt["gl"] += 1
        T_ = {k_: v_[slot][j] for k_, v_ in gl.items()}
        pb = banks[2 * slot:2 * slot + 2]
        p.dma("sp", T_["k"][:], V(o16, o16.h[t, :, OKG + hd * 128:OKG + (hd + 1) * 128]))
        p.dma("sp", T_["v"][:], V(o16, o16.h[t, :, OVG + hd * 256:OVG + (hd + 1) * 256]))
        p.dma("sp", T_["g"][:], V(o32, o32.h[t, :, d * 512 + hd * 128:d * 512 + (hd + 1) * 128]))
        g = T_["g"]
        Lc, Uc = LM[d], UM[d]
        b0, b1 = pb
        ts_ = slice(t * 128, (t + 1) * 128)
        if full:
            p.mm(b0[:, 0:128], [(g[:], Lc)])
        p.mm(b0[:, 128:256], [(Uc, g[:])])
        p.mm(b0[:, 256:258], [(g[:], sel)])
        if full:
            p.act(T_["EbT"][:], b0[:, 0:128], AF.Exp)
            p.act(T_["EnbT"][:], b0[:, 0:128], AF.Exp, scale=-1.0)
        p.act(T_["Erem"][:], b0[:, 128:256], AF.Exp)
        p.act(T_["Eend"][:], b0[:, 256:258], AF.Exp)
        if full:
            p.tt("dve", T_["qeT"][:], gqT[:, hd, ts_], T_["EbT"][:], ALU.mult)
            p.tt("dve", T_["keT"][:], gkT[:, hd, ts_], T_["EnbT"][:], ALU.mult)
        else:
            p.ts("dve", logD[:, r:r + 1], logD[:, r:r + 1], T_["Eend"][:, 0:1], T_["Eend"][:, 1:2], ALU.mult, ALU.mult)
        p.tt("dve", T_["kend"][:], T_["k"][:], T_["Erem"][:], ALU.mult)
        if full:
            p.mm(b0[:, 384:512], [(T_["keT"][:], T_["qeT"][:])])
            p.tt("dve", T_["ATm"][:], b0[:, 384:512], Lc, ALU.mult)
        for c in ((0, 1) if d == 0 else (1, 0)):
            cs = slice(c * 64, (c + 1) * 64)
            if full:
                ops_ = V(b1, b1.h[0:64, 0:256])
                p.mm(ops_, [(T_["qeT"][:, cs], S_b[:]), (T_["ATm"][:, cs], T_["v"][:])])
                p.act(T_["osb"][:, c, :], ops_, AF.Identity)
            ups = V(b1, b1.h[:, 256:512])
            p.mm(ups, [(T_["kend"][cs, :], T_["v"][cs, :])])
            p.stt("dve", S_t[:], S_t[:], T_["Eend"][:, c:c + 1], ups, ALU.mult, ALU.add)
            if full:
                p.act(S_b[:], S_t[:], AF.Identity)
        if full:
            dst = M.gf_s if d == 0 else M.gb_s
            p.dma("sp", V(dst, dst.h[t, :, hd * 256:(hd + 1) * 256].rearrange("(c q) e -> q c e", q=64)), T_["osb"][:])

    def tiles_for(d, ctx):
        if ctx:
            return [NT, NT + 1] if d == 0 else [NT + 1, NT]
        return list(range(NT)) if d == 0 else list(range(NT - 1, -1, -1))

    slotcnt = [0, 0, 0, 0]

    def conv_gen():
        yT = p.sb("yT", [128, 8, TPC + 30], BF16)
        yTc = p.sb("yTc", [128, 8, CTX + 30], BF16)
        p.memset("dve", yTc[:], 0.0)
        ldy = [p.sb("ldy%d" % i, [128, 1024], BF16) for i in range(2)]
        for t in range(NTA):
            a = ldy[t % 2]
            p.dma("sp", a[:], V(o16, o16.h[t, :, OY:OY + 1024]))
            if t < NT:
                transpose_into(p, cx, a, yT, t, ident, 8, col0=15 + t * 128)
            else:
                transpose_into(p, cx, a, yTc, t, ident, 8, col0=15 + (t - NT) * 128)
            yield
        E = p.sb("E", [128, 1024], BF16)
        p.dma("sp", E[:], V(M.yrcv, M.yrcv.h.bitcast(BF16)))
        selm = p.sb("selm", [128, 30], BF16)
        p.dma("pool", selm[:], M.selm_d[:])
        for c in range(8):
            ps = cx.bank()
            p.mm(ps[:, 0:30], [(E[:, c * 128:(c + 1) * 128], selm[:])])
            p.act(yT.k(("hl", c), (slice(None), c, slice(0, 15))), ps[:, 0:15], AF.Identity)
            p.act(yT.k(("hr", c), (slice(None), c, slice(15 + TPC, 30 + TPC))), ps[:, 15:30], AF.Identity)
        yield
        cwl = p.sb("cwl", [32, 1024], F32)
        p.dma("sp", cwl[0:31, :], M.conv_w[:])
        p.dma("sp", cwl[31:32, :], M.conv_b[:])
        cwt = p.sb("cwt", [128, 8, 32], F32)
        for c in range(8):
            ps = cx.bank()
            p.tr(ps[:, 0:32], cwl[:, c * 128:(c + 1) * 128], identf[0:32, 0:32])
            p.act(cwt[:, c, :], ps[:, 0:32], AF.Identity)
        yield
        accs = p.sb("cacc", [128, TPC], F32)
        accc = p.sb("caccc", [128, CTX], F32)
        cvst = [p.sb("cvst%d" % i, [128, 4, 128], F32) for i in range(2)]
        dg = p.sb("cdiag", [128, CW, 128], BF16)
        cnt_ = {"ci": 0}

        def finish(c):
            for (a, n, t0) in ((accs, TPC, 0), (accc, CTX, NT)):
                for g0_ in range(0, n // 128, 4):
                    ng = min(4, n // 128 - g0_)
                    ps = cx.bank()
                    for k_ in range(ng):
                        p.tr(ps[:, k_ * 128:(k_ + 1) * 128], a[:, (g0_ + k_) * 128:(g0_ + k_ + 1) * 128], identf[:])
                    st_ = cvst[cnt_["ci"] % 2]
                    cnt_["ci"] += 1
                    p.act(V(st_, st_.h[:, 0:ng, :]), V(ps, ps.h[:, 0:ng * 128].rearrange("q (k f) -> q k f", f=128)), AF.Identity)
                    p.dma("sp", V(M.cv_s, M.cv_s.h[t0 + g0_:t0 + g0_ + ng, :, c * 128:(c + 1) * 128].rearrange("t q f -> q t f")),
                          V(st_, st_.h[:, 0:ng, :]))

        for c in range(8):
            for tap in range(CW):
                p.ts("dve", dg[:, tap, :], ident[:], cwt[:, c, tap:tap + 1], None, ALU.mult)
            yield
            for (a, ysrc, n) in ((accs, yT, TPC), (accc, yTc, CTX)):
                for b0_ in range(0, n, 512):
                    nb = min(512, n - b0_)
                    ps = cx.bank()
                    p.mm(ps[:, 0:nb], [(dg[:, tap, :], V(ysrc, ysrc.h[:, c, b0_ + tap:b0_ + tap + nb])) for tap in range(CW)])
                    p.act(a[:, b0_:b0_ + nb], ps[:, 0:nb], AF.Identity, bias=cwt[:, c, 31:32])
                    yield
            finish(c)
            yield


    def gla_group(items, full):
        cs_ = []
        for (slot, r, t, S_t, S_b) in items:
            hd, d = r // 2, r % 2
            j = slotcnt[slot] % 2
            slotcnt[slot] += 1
            T_ = {k_: v_[slot][j] for k_, v_ in gl.items()}
            b0 = banks[4 + slot]
            p.dma("sp", T_["k"][:], V(o16, o16.h[t, :, OKG + hd * 128:OKG + (hd + 1) * 128]))
            p.dma("sp", T_["v"][:], V(o16, o16.h[t, :, OVG + hd * 256:OVG + (hd + 1) * 256]))
            p.dma("sp", T_["g"][:], V(o32, o32.h[t, :, d * 512 + hd * 128:d * 512 + (hd + 1) * 128]))
            g = T_["g"]
            if full:
                p.mm(b0[:, 0:128], [(g[:], LM[d])])
            p.mm(b0[:, 128:256], [(UM[d], g[:])])
            p.mm(b0[:, 256:258], [(g[:], sel)])
            cs_.append((slot, r, t, S_t, S_b, hd, d, T_, b0))
        yield
        for (slot, r, t, S_t, S_b, hd, d, T_, b0) in cs_:
            if full:
                p.act(T_["EbT"][:], b0[:, 0:128], AF.Exp)
                p.act(T_["EnbT"][:], b0[:, 0:128], AF.Exp, scale=-1.0)
            p.act(T_["Erem"][:], b0[:, 128:256], AF.Exp)
            p.act(T_["Eend"][:], b0[:, 256:258], AF.Exp)
        yield
        for (slot, r, t, S_t, S_b, hd, d, T_, b0) in cs_:
            ts_ = slice(t * 128, (t + 1) * 128)
            if full:
                p.tt("dve", T_["qeT"][:], gqT[:, hd, ts_], T_["EbT"][:], ALU.mult)
                p.tt("dve", T_["keT"][:], gkT[:, hd, ts_], T_["EnbT"][:], ALU.mult)
            else:
                p.ts("dve", logD[:, r:r + 1], logD[:, r:r + 1], T_["Eend"][:, 0:1], T_["Eend"][:, 1:2], ALU.mult, ALU.mult)
            p.tt("dve", T_["kend"][:], T_["k"][:], T_["Erem"][:], ALU.mult)
        yield
        if full:
            for (slot, r, t, S_t, S_b, hd, d, T_, b0) in cs_:
                p.mm(b0[:, 384:512], [(T_["keT"][:], T_["qeT"][:])])
            yield
            for (slot, r, t, S_t, S_b, hd, d, T_, b0) in cs_:
                p.tt("dve", T_["ATm"][:], b0[:, 384:512], LM[d], ALU.mult)
            yield
        for ci in range(2):
            for (slot, r, t, S_t, S_b, hd, d, T_, b0) in cs_:
                c = ci if d == 0 else 1 - ci
                cs = slice(c * 64, (c + 1) * 64)
                if full:
                    ops_ = V(b0, b0.h[0:64, 0:256])
                    p.mm(ops_, [(T_["qeT"][:, cs], S_b[:]), (T_["ATm"][:, cs], T_["v"][:])])
                ups = V(b0, b0.h[:, 256:512])
                p.mm(ups, [(T_["kend"][cs, :], T_["v"][cs, :])])
            yield
            for (slot, r, t, S_t, S_b, hd, d, T_, b0) in cs_:
                c = ci if d == 0 else 1 - ci
                if full:
                    p.copy("dve", T_["osb"][:, c, :], V(b0, b0.h[0:64, 0:256]))
                p.stt("dve", S_t[:], S_t[:], T_["Eend"][:, c:c + 1], V(b0, b0.h[:, 256:512]), ALU.mult, ALU.add)
                if full:
                    p.act(S_b[:], S_t[:], AF.Identity)
            yield
        if full:
            for (slot, r, t, S_t, S_b, hd, d, T_, b0) in cs_:
                dst = M.gf_s if d == 0 else M.gb_s
                p.dma("sp", V(dst, dst.h[t, :, hd * 256:(hd + 1) * 256].rearrange("(c q) e -> q c e", q=64)), T_["osb"][:])

    def gla_pass_gen(ctx, full):
        n = 2 if ctx else NT
        for k_ in range(n):
            for g0_ in (0, 4):
                items = []
                for r in range(g0_, g0_ + 4):
                    d = r % 2
                    items.append((r % 4, r, tiles_for(d, ctx)[k_], St[r], Sb[r] if full else None))
                yield from gla_group(items, full)
                yield

    def gla_pass(ctx, full, side=None):
        for _ in gla_pass_gen(ctx, full):
            if side is not None:
                next(side, None)

    p.push_scope()
    ld = [p.sb("ld%d" % i, [128, 1280], BF16) for i in range(2)]
    for t in range(NTA):
        a = ld[t % 2]
        p.dma("sp", a[:, 0:256], V(o16, o16.h[t, :, OK_:OK_ + 256]))
        p.dma("sp", a[:, 256:768], V(o16, o16.h[t, :, OKG:OKG + 512]))
        p.dma("sp", a[:, 768:1280], V(o16, o16.h[t, :, OQG:OQG + 512]))
        transpose_into(p, cx, a, kTl, t, ident, 2, src0=0)
        transpose_into(p, cx, a, gkT, t, ident, 4, src0=256)
        transpose_into(p, cx, a, gqT, t, ident, 4, src0=768)
        p.dma("sp", veb.k(("t", t), (slice(None), t, slice(None), slice(0, 128))),
              V(o16, o16.h[t, :, OV_:OV_ + 256].rearrange("p (k e) -> p k e", k=2)))
    chk(0.05)
    ksb = M.ksnd.h.bitcast(BF16)
    for kvh in range(2):
        p.dma("sp", V(M.ksnd, ksb[kvh * 128:(kvh + 1) * 128, :]), kTl[:, kvh, 0:TPC])
    NH2 = NT // 2
    for hf in range(2):
        vs_ = M.vsnd[hf]
        vsb = vs_.h.bitcast(BF16)
        p.dma("sp", V(vs_, vsb.rearrange("p (t k e) -> p t k e", t=NH2, k=2)), veb[:, hf * NH2:(hf + 1) * NH2, :, :])
    chk(0.1)
    yed = p.sb("yed", [32, 1024], BF16)
    p.memset("dve", yed[:], 0.0)
    p.dma("sp", yed[0:15, :], V(o16, o16.h[0, 0:15, OY:OY + 1024]))
    p.dma("sp", yed[15:30, :], V(o16, o16.h[NT - 1, 113:128, OY:OY + 1024]))
    p.dma("sp", V(M.ysnd, M.ysnd.h.bitcast(BF16)), yed[:])
    chk(0.15)
    p.collective("AllGather", [M.ksnd[:]], [M.krcv[:]], groups)
    for hf in range(2):
        p.collective("AllGather", [M.vsnd[hf][:]], [M.vrcv[hf][:]], groups)
    p.collective("AllGather", [M.ysnd[:]], [M.yrcv[:]], groups)
    chk(0.2)
    for i in range(8):
        p.memset("dve", St[i][:], 0.0)
        p.memset("dve", Sb[i][:], 0.0)
    cgen = conv_gen()
    gla_pass(True, True, cgen)
    for r in range(8):
        p.copy("dve", Sctx[r][:], St[r][:])
        p.memset("dve", St[r][:], 0.0)
    chk(0.3)
    gla_pass(False, False, cgen)
    for _ in cgen:
        pass
    for r in range(8):
        p.dma("sp", V(M.gsnd, M.gsnd.h[r * 128:(r + 1) * 128, 0:256]), St[r][:])
    chk(0.4)
    dst_ = p.sb("dstage", [128, 64], F32)
    p.memset("dve", dst_[:], 0.0)
    p.copy("dve", dst_[:, 0:8], logD[:])
    p.dma("sp", M.dsnd[:], dst_[:])
    p.collective("AllGather", [M.gsnd[:]], [M.grcv[:]], groups)
    p.collective("AllGather", [M.dsnd[:]], [M.drcv[:]], groups)
    Dall = p.sb("Dall", [128, 4, 64], F32)
    p.dma("sp", Dall[:], V(M.drcv, M.drcv.h.rearrange("(s q) c -> q s c", s=4)))
    chk(0.5)
    G = [p.sb("G%d" % i, [128, 4, 256], F32) for i in range(1)]
    coef = p.sb("coef", [128, 2], F32)
    gv = M.grcv.h.rearrange("(s r q) c -> q s r c", s=4, r=8)
    for r in range(8):
        d = r % 2
        Gt = G[0]
        p.dma("sp", Gt[:], V(M.grcv, gv[:, :, r, :]))
        p.copy("dve", St[r][:], Sctx[r][:])
        for s_ in (range(4) if d == 0 else range(3, -1, -1)):
            m = masks[:, d * 4 + s_:d * 4 + s_ + 1]
            p.ts("dve", coef[:, 1:2], Dall[:, s_, r:r + 1], -1.0, m, ALU.add, ALU.mult)
            p.ts("dve", coef[:, 1:2], coef[:, 1:2], 1.0, None, ALU.add)
            p.ts("dve", St[r][:], St[r][:], coef[:, 1:2], None, ALU.mult)
            p.stt("dve", St[r][:], Gt[:, s_, 0:256], m, St[r][:], ALU.mult, ALU.add)
        p.act(Sb[r][:], St[r][:], AF.Identity)
    p.pop_scope()

    if getattr(M, "stop_after", 3) < 3:
        p.pop_scope()
        return
    p.push_scope()
    qT = p.sb("qT", [128, 8, NTOK], BF16)
    ld = [p.sb("ldq%d" % i, [128, 1024], BF16) for i in range(2)]
    for t in range(NTA):
        a = ld[t % 2]
        p.dma("sp", a[:], V(o16, o16.h[t, :, OQ:OQ + 1024]))
        transpose_into(p, cx, a, qT, t, ident, 8)
    kT = p.sb("kT", [128, L], BF16)
    vE = p.sb("vE", [128, NKT, 129], BF16)
    pTs = [p.sb("pTs%d" % i, [128, 512], BF16) for i in range(6)]
    rcp = p.sb("rcp", [128, 4], F32)
    ost = [p.sb("ost%d" % i, [128, 128], BF16) for i in range(4)]
    SC = HD ** -0.5
    krb = M.krcv.h.bitcast(BF16)
    NH2 = NT // 2
    vrb = [M.vrcv[hf].h.bitcast(BF16) for hf in range(2)]
    state = {"pi": 0, "sb": 0}

    def load_kv(kvh):
        for s_ in range(SEGS):
            p.dma("sp", kT[:, s_ * TPC:(s_ + 1) * TPC], V(M.krcv, krb[s_ * 256 + kvh * 128:s_ * 256 + (kvh + 1) * 128, :]))
            for hf in range(2):
                p.dma("sp", vE[:, s_ * NT + hf * NH2:s_ * NT + (hf + 1) * NH2, :],
                      V(M.vrcv[hf], vrb[hf][s_ * 128:(s_ + 1) * 128, :].rearrange("q (t k e) -> q t k e", t=NH2, k=2)[:, :, kvh, :]))
        p.act(kT[:, Sfull:L], kTl[:, kvh, TPC:NTOK], AF.Identity)
        p.copy("dve", vE[:, SEGS * NT:NKT, :], veb[:, NT:NTA, kvh, :])

    def att_block(blk):
        h, q0, nq, kt0, kt1 = blk
        nsub = nq // 128
        LOOK = 2
        pend = []

        def pv(kt, pt):
            def emit(e):
                ins = None
                for qs in range(nsub):
                    ob = banks[qs // 2]
                    ins = e.matmul(ob.h[:, (qs % 2) * 256:(qs % 2) * 256 + 129], pt.h[:, qs * 128:(qs + 1) * 128],
                                   vE.h[:, kt, :], start=(kt == kt0), stop=(kt == kt1 - 1))
                return ins
            writes = [V(banks[qs // 2], banks[qs // 2].h[:, (qs % 2) * 256:(qs % 2) * 256 + 129], ("o", qs % 2)) for qs in range(nsub)]
            p.op("pe", [pt[:, 0:nq], vE[:, kt, :]], writes, emit)

        for kt in range(kt0, kt1):
            sbk = banks[2 + state["sb"] % 6]
            state["sb"] += 1
            p.mm(sbk[:, 0:nq], [(kT[:, kt * 128:(kt + 1) * 128], qT[:, h, q0:q0 + nq])])
            pt = pTs[state["pi"] % 6]
            state["pi"] += 1
            p.act(pt[:, 0:nq], sbk[:, 0:nq], AF.Exp, scale=SC)
            pend.append((kt, pt))
            if len(pend) > LOOK:
                pv(*pend.pop(0))
        while pend:
            pv(*pend.pop(0))
        for qs in range(nsub):
            ob = banks[qs // 2]
            base = (qs % 2) * 256
            okey = V(ob, ob.h[:, base:base + 129], ("o", qs % 2))
            p.op("dve", [okey], [rcp[:, qs:qs + 1]],
                 lambda e, ob=ob, base=base, qs=qs: e.reciprocal(rcp.h[:, qs:qs + 1], ob.h[:, base + 128:base + 129]))
            p.ts("dve", ost[qs][:], V(ob, ob.h[:, base:base + 128], ("o", qs % 2)), rcp[:, qs:qs + 1], None, ALU.mult)
            p.dma("sp", V(M.att_s, M.att_s.h[(q0 // 128) + qs, :, h * 128:(h + 1) * 128]), ost[qs][:])

    for kvh in range(2):
        load_kv(kvh)
        for hh in range(4):
            h = kvh * 4 + hh
            for q0 in range(0, TPC, 512):
                att_block((h, q0, min(512, TPC - q0), 0, NKT))
            att_block((h, TPC, CTX, NKT - CTX // 128, NKT))
    gla_pass(False, True)
    p.pop_scope()
    p.pop_scope()
    cx.nrot = 8


def build_fused(NT=16):
    NCT = 2
    NTA = NT + NCT
    TPC = NT * 128
    nc = bass.Bass("TRN2", target_bir_lowering=False)
    es = ExitStack()
    with es:
        p = Prog(nc, es)
        cx = Ctx(p)
        X = lambda n, shp: p.dram(n, shp, F32, "ExternalInput")
        x_d = X("x", [NT, 128, D])
        ctx_d = X("ctxb", [NCT, 128, D])
        cvT = X("cvT", [128, 2, 8])
        cos_d = X("cos", [128, NT, 64])
        sin_d = X("sin", [128, NT, 64])
        ident_d = X("ident", [128, 128])
        gcst_d = X("gcst", [128, 514])
        masks_d = X("masks", [128, 8])
        selm_d = X("selm", [128, 30])
        W = {}
        for n, shp in (("w_ada", [DEPTH, D, 6 * D]), ("b_ada", [DEPTH, 6 * D]), ("w_in", [DEPTH, D, INC]),
                       ("q_norm", [DEPTH, HD]), ("k_norm", [DEPTH, HD]), ("w_att_o", [DEPTH, D, D]),
                       ("gla_w_a2", [DEPTH, 2, RANK, 512]), ("gla_b_a", [DEPTH, 2 * 512]), ("gla_norm", [DEPTH, GDV]),
                       ("w_gla_o", [DEPTH, D, D]), ("conv_w_dw", [DEPTH, CW, D]), ("conv_b_dw", [DEPTH, 1, D]),
                       ("conv_ln_g", [DEPTH, D]), ("conv_ln_b", [DEPTH, D]), ("w_conv_o", [DEPTH, D, D]),
                       ("w_out", [DEPTH, D, D]), ("ln1_g", [DEPTH, D]), ("ln1_b", [DEPTH, D]),
                       ("w_ff_gate", [DEPTH, D, DFF]), ("w_ff_up", [DEPTH, D, DFF]), ("w_ff_down", [DEPTH, DFF, D]),
                       ("ln2_g", [DEPTH, D]), ("ln2_b", [DEPTH, D])):
            W[n] = X(n, shp)
        out = p.dram("out", [NT, 128, D], F32, "ExternalOutput")
        I = lambda n, shp, dt: p.dram(n, shp, dt, "Internal")
        o16 = I("o16", [NTA, 128, C16], BF16)
        o32 = I("o32", [NTA, 128, 1024], F32)
        M = NS()
        M.o16, M.o32 = o16, o32
        M.att_s = I("att_s", [NTA, 128, D], BF16)
        M.gf_s = I("gf_s", [NTA, 128, D], F32)
        M.gb_s = I("gb_s", [NTA, 128, D], F32)
        M.cv_s = I("cv_s", [NTA, 128, D], F32)
        act_s = I("act_s", [NTA, 128, DFF // 128, 128], BF16)
        xa = I("xa", [NTA, 128, D], F32)
        xb = I("xb", [NTA, 128, D], F32)
        M.ksnd = I("ksnd", [256, TPC // 2], F32)
        M.krcv = I("krcv", [1024, TPC // 2], F32)
        M.vsnd = [I("vsnd%d" % i, [128, NT // 2 * 129], F32) for i in range(2)]
        M.vrcv = [I("vrcv%d" % i, [512, NT // 2 * 129], F32) for i in range(2)]
        M.gsnd = I("gsnd", [1024, 256], F32)
        M.grcv = I("grcv", [4096, 256], F32)
        M.dsnd = I("dsnd", [128, 64], F32)
        M.drcv = I("drcv", [512, 64], F32)
        M.ysnd = I("ysnd", [32, 512], F32)
        M.yrcv = I("yrcv", [128, 512], F32)
        M.ident_d, M.gcst_d, M.masks_d, M.selm_d = ident_d, gcst_d, masks_d, selm_d

        def lw(n, l):
            return T(p, W[n].h[l], "%s_l%d" % (n, l))

        for l in range(DEPTH):
            last = l == DEPTH - 1
            if l == 0:
                xs_fn = lambda t: (x_d[t] if t < NT else ctx_d[t - NT])
            else:
                xs_fn = lambda t: xb[t]

            class XS:
                def __getitem__(self, t):
                    return xs_fn(t)
            p.push_scope()
            wa2 = lw("gla_w_a2", l)
            emit_p1(p, cx, NT, NCT, XS(), cvT, lw("w_ada", l), lw("b_ada", l), lw("w_in", l), lw("q_norm", l),
                    lw("k_norm", l), wa2[0], wa2[1], lw("gla_b_a", l), cos_d, sin_d, ident_d, o16, o32, True)
            p.pop_scope()
            M.conv_w = lw("conv_w_dw", l)
            M.conv_b = lw("conv_b_dw", l)
            emit_mix_fused(p, cx, NT, M)
            nct = 0 if last else NCT
            A = NS()
            A.oatt = lambda t: M.att_s[t]
            A.ogf = lambda t: M.gf_s[t]
            A.ogb = lambda t: M.gb_s[t]
            A.sr = lambda t: V(o16, o16.h[t, :, OR:OR + 1024])
            A.cv = lambda t: M.cv_s[t]
            A.xs = xs_fn
            A.xo = lambda t: xa[t]
            A.gt = lambda t, c0, c1: V(o16, o16.h[t, :, OGT + c0:OGT + c1])
            A.g1_mi = 2
            A.ident_d, A.cvT, A.w_ada, A.b_ada = ident_d, cvT, lw("w_ada", l), lw("b_ada", l)
            A.vecs = {n: lw(n, l) for n in ("gla_norm", "conv_ln_g", "conv_ln_b", "ln1_g", "ln1_b")}
            A.wo = {n: lw(n, l) for n in ("w_att_o", "w_gla_o", "w_conv_o", "w_out")}
            p.push_scope()
            emit_posta(p, cx, NT, nct, A)
            p.pop_scope()
            Bn = NS()
            Bn.xs = lambda t: xa[t]
            Bn.xo = (lambda t: out[t]) if last else (lambda t: xb[t])
            Bn.mi0 = 3
            Bn.ident_d, Bn.cvT, Bn.w_ada, Bn.b_ada = ident_d, cvT, lw("w_ada", l), lw("b_ada", l)
            Bn.l2g_d, Bn.l2b_d = lw("ln2_g", l), lw("ln2_b", l)
            Bn.wg_d, Bn.wu_d, Bn.wd_d = lw("w_ff_gate", l), lw("w_ff_up", l), lw("w_ff_down", l)
            Bn.act_s = act_s
            p.push_scope()
            emit_postb2(p, cx, NT, nct, Bn)
            p.pop_scope()
        p.wait_all("sp", [out[:]])
        print("FUSED ops", p.n_ops, "waits", p.n_waits)
    return nc


def kernel_fused(P, S):
    TPC = S * B // NCORES
    NT = TPC // 128
    cos, sin = _rope_tables(S)
    gc = _gla_consts()
    Lm, Um, sel = gc[:, 0:128], gc[:, 128:256], gc[:, 256:258]
    gcst = np.ascontiguousarray(np.concatenate([Lm, Um, Lm.T, Um.T, sel], axis=1))
    ident = np.eye(128, dtype=np.float32)
    in_maps = []
    for core in range(NCORES):
        b, seg = core // 4, core % 4
        masks = np.zeros((128, 8), np.float32)
        for s_ in range(4):
            masks[:, s_] = 1.0 if s_ < seg else 0.0
            masks[:, 4 + s_] = 1.0 if s_ > seg else 0.0
        selm = np.zeros((128, 30), np.float32)
        for j in range(15):
            if seg > 0:
                selm[(seg - 1) * 32 + 15 + j, j] = 1.0
            if seg < 3:
                selm[(seg + 1) * 32 + j, 15 + j] = 1.0
        m = {
            "x": np.ascontiguousarray(P["x"][b, seg * TPC:(seg + 1) * TPC].reshape(NT, 128, D)),
            "ctxb": np.ascontiguousarray(P["ctx"][b].reshape(2, 128, D)),
            "cvT": np.ascontiguousarray(np.stack([P["c"][b], P["c_ctx"]]).reshape(2, 8, 128).transpose(2, 0, 1)),
            "cos": np.ascontiguousarray(cos[seg * TPC:(seg + 1) * TPC].reshape(NT, 128, 64).transpose(1, 0, 2)),
            "sin": np.ascontiguousarray(sin[seg * TPC:(seg + 1) * TPC].reshape(NT, 128, 64).transpose(1, 0, 2)),
            "ident": ident, "gcst": gcst, "masks": masks, "selm": selm,
        }
        for n in ("w_ada", "b_ada", "w_in", "q_norm", "k_norm", "w_att_o", "gla_w_a2", "gla_norm", "w_gla_o",
                  "conv_ln_g", "conv_ln_b", "w_conv_o", "w_out", "ln1_g", "ln1_b", "w_ff_gate", "w_ff_up",
                  "w_ff_down", "ln2_g", "ln2_b"):
            m[n] = P[n]
        m["gla_b_a"] = np.ascontiguousarray(P["gla_b_a"].reshape(DEPTH, 1024))
        m["conv_w_dw"] = np.ascontiguousarray(P["conv_w_dw"].reshape(DEPTH, CW, D))
        m["conv_b_dw"] = np.ascontiguousarray(P["conv_b_dw"].reshape(DEPTH, 1, D))
        in_maps.append(m)
    res = _run(("fused", NT), lambda: build_fused(NT), in_maps)
    out = np.stack([r["out"] for r in res])
    return np.ascontiguousarray(out.reshape(B, S, D))


def kernel(**inputs):
    P = {k: np.asarray(v) for k, v in inputs.items()}
    S = P["x"].shape[1]
    return kernel_fused(P, S).astype(np.float32)
```
